# Optimizing a Trainium2 kernel written in Bass

```python
import math
import jax, jax.numpy as jnp
from jax import lax
import numpy as np

D_MODEL = 1024
BATCH = 8
SEQ = 4096
DEPTH = 4

BRANCH_W = D_MODEL // 2
N_BRANCH = 4
EPS = 1e-6
LRU_HEADS = 8
LRU_HD = BRANCH_W // LRU_HEADS
CONV_W = 4
LRU_C = 8.0
S5_GROUP = 16
S5_GROUPS = BRANCH_W // S5_GROUP
S5_STATE = 64
RWKV_HD = 64
RWKV_HEADS = BRANCH_W // RWKV_HD
W_LORA = 64
A_LORA = 64
G_LORA = 128
RWKV_LN_EPS = 64e-5
MLA_HEADS = 8
QK_NOPE = 64
QK_ROPE = 32
QK_HD = QK_NOPE + QK_ROPE
V_HD = 64
Q_LORA = 256
KV_LORA = 128
ROPE_THETA = 10000.0
BLOCK_Q = 128
D_FF = 4 * D_MODEL

IN_WIDTHS = (BRANCH_W, BRANCH_W,
             BRANCH_W,
             BRANCH_W, BRANCH_W, BRANCH_W,
             W_LORA, A_LORA, G_LORA,
             Q_LORA, KV_LORA, QK_ROPE,
             N_BRANCH * D_MODEL)
D_IN = int(sum(IN_WIDTHS))
IN_SPLIT_POINTS = tuple(int(v) for v in np.cumsum(IN_WIDTHS)[:-1])

kernel_name = "hybrid_rglru_s5_rwkv7_mla_trunk"


def _rmsnorm(x, g, eps=EPS):
    x32 = x.astype(jnp.float32)
    y = x32 * lax.rsqrt(jnp.mean(x32 * x32, axis=-1, keepdims=True) + eps)
    return (y * g.astype(jnp.float32)).astype(x.dtype)


def _shift_mix(p, mu):
    prev = jnp.pad(p, ((0, 0), (1, 0), (0, 0)))[:, :-1]
    return p + (prev - p) * mu


def _linear_recurrence(a, u):
    def step(h, au):
        h = au[0] * h + au[1]
        return h, h
    _, hs = lax.scan(step, jnp.zeros_like(u[:, 0]), (jnp.swapaxes(a, 0, 1), jnp.swapaxes(u, 0, 1)))
    return jnp.swapaxes(hs, 0, 1)


def _rglru_branch(xb, gb, conv_w, conv_b, gate_w, gate_b, lam):
    f32 = jnp.float32
    bsz, seq, width = xb.shape
    xc = lax.conv_general_dilated(
        xb, conv_w[:, None, :], window_strides=(1,), padding=((CONV_W - 1, 0),),
        dimension_numbers=("NWC", "WIO", "NWC"), feature_group_count=width) + conv_b
    xh = xc.reshape(bsz, seq, LRU_HEADS, LRU_HD)
    gates = jnp.einsum("bthi,ghij->gbthj", xh, gate_w).reshape(2, bsz, seq, width) + gate_b[:, None, None, :]
    gates = gates.astype(f32)
    r = jax.nn.sigmoid(gates[0])
    i = jax.nn.sigmoid(gates[1])
    log_a = -LRU_C * r * jax.nn.softplus(-lam.astype(f32))
    a = jnp.exp(log_a)
    u = jnp.sqrt(-jnp.expm1(2.0 * log_a)) * (i * xc.astype(f32))
    h = _linear_recurrence(a, u)
    return (h * jax.nn.gelu(gb.astype(f32))).astype(xb.dtype)


def _s5_branch(u, a_re, a_im, b_re, b_im, c_re, c_im, d_skip, log_dt, w_glu):
    f32 = jnp.float32
    bsz, seq, width = u.shape
    lam = lax.complex(a_re.astype(f32), a_im.astype(f32))
    dt = jnp.exp(log_dt.astype(f32))[:, None]
    lam_bar = jnp.exp(lam * dt)
    b_bar = ((lam_bar - 1.0) / lam)[:, :, None] * lax.complex(b_re.astype(f32), b_im.astype(f32))
    u32 = u.astype(f32)
    ug = u32.reshape(bsz, seq, S5_GROUPS, S5_GROUP).astype(jnp.complex64)
    bu = jnp.einsum("gpm,btgm->tbgp", b_bar, ug)
    a_el = jnp.broadcast_to(lam_bar[None, None], (seq, 1, S5_GROUPS, S5_STATE))

    def combine(e1, e2):
        return e2[0] * e1[0], e2[0] * e1[1] + e2[1]

    _, states = lax.associative_scan(combine, (a_el, bu), axis=0)
    c = lax.complex(c_re.astype(f32), c_im.astype(f32))
    y = jnp.einsum("gmp,tbgp->btgm", c, states).real.reshape(bsz, seq, width) + d_skip.astype(f32) * u32
    y = jax.nn.gelu(y).astype(u.dtype)
    z = y @ w_glu
    return z[..., :width] * jax.nn.sigmoid(z[..., width:])


def _rwkv7_branch(r_in, k_in, v_in, wd_in, ad_in, gd_in, mu_rkv, mu_w, mu_a, mu_g, w0, w2, a0, a2, g2,
                  k_k, k_a, r_k, lnx_g, lnx_b):
    f32 = jnp.float32
    bsz, seq, width = r_in.shape
    r = _shift_mix(r_in, mu_rkv[0])
    k = _shift_mix(k_in, mu_rkv[1])
    v = _shift_mix(v_in, mu_rkv[2])
    wl = _shift_mix(wd_in, mu_w)
    al = _shift_mix(ad_in, mu_a)
    gl = _shift_mix(gd_in, mu_g)
    w = -jax.nn.softplus(-(w0 + jnp.tanh(wl) @ w2).astype(f32)) - 0.5
    decay = jnp.exp(-jnp.exp(w))
    a = jax.nn.sigmoid((a0 + al @ a2).astype(f32))
    g = jax.nn.sigmoid(gl) @ g2

    def heads(t):
        return t.astype(f32).reshape(bsz, seq, RWKV_HEADS, RWKV_HD)

    kk = heads(k * k_k)
    kk = kk * lax.rsqrt(jnp.sum(kk * kk, axis=-1, keepdims=True) + 1e-12)
    k_mod = k.astype(f32) * (1.0 + (a - 1.0) * k_a.astype(f32))
    r_h, k_h, v_h, w_h = heads(r), heads(k_mod), heads(v), heads(decay)
    a_vec = -kk
    b_vec = kk * heads(a)

    def step(S, inp):
        r_t, w_t, k_t, v_t, a_t, b_t = inp
        sa = jnp.einsum("bhvk,bhk->bhv", S, a_t)
        S = S * w_t[:, :, None, :] + sa[..., None] * b_t[:, :, None, :] + v_t[..., None] * k_t[:, :, None, :]
        return S, jnp.einsum("bhvk,bhk->bhv", S, r_t)

    s0 = jnp.zeros((bsz, RWKV_HEADS, RWKV_HD, RWKV_HD), f32)
    xs = tuple(jnp.swapaxes(t, 0, 1) for t in (r_h, w_h, k_h, v_h, a_vec, b_vec))
    _, y = lax.scan(step, s0, xs)
    y = jnp.swapaxes(y, 0, 1)
    mean = jnp.mean(y, axis=-1, keepdims=True)
    var = jnp.mean(jnp.square(y - mean), axis=-1, keepdims=True)
    y = ((y - mean) * lax.rsqrt(var + RWKV_LN_EPS)).reshape(bsz, seq, width)
    y = y * lnx_g.astype(f32) + lnx_b.astype(f32)
    bonus = jnp.sum(r_h * k_h * r_k.astype(f32), axis=-1, keepdims=True) * v_h
    y = y + bonus.reshape(bsz, seq, width)
    return (y * g.astype(f32)).astype(r_in.dtype)


def _rope(x, cos, sin):
    x1, x2 = jnp.split(x, 2, axis=-1)
    return jnp.concatenate([x1 * cos - x2 * sin, x1 * sin + x2 * cos], axis=-1)


def _causal_block_attention(q, k, v):
    bsz, seq, n_heads, _ = q.shape
    scale = QK_HD ** -0.5
    outs = []
    for blk in range(seq // BLOCK_Q):
        q0, q1 = blk * BLOCK_Q, (blk + 1) * BLOCK_Q
        s = jnp.einsum("bqhd,bkhd->bhqk", q[:, q0:q1], k[:, :q1]).astype(jnp.float32) * scale
        mask = jnp.arange(q1)[None, :] <= (q0 + jnp.arange(BLOCK_Q))[:, None]
        p = jax.nn.softmax(jnp.where(mask, s, -jnp.inf), axis=-1).astype(v.dtype)
        outs.append(jnp.einsum("bhqk,bkhd->bqhd", p, v[:, :q1]))
    return jnp.concatenate(outs, axis=1)


def _mla_branch(c_q, c_kv, k_rope_in, q_norm, w_uq, kv_norm, w_ukv, qk_norm_q, qk_norm_k):
    f32 = jnp.float32
    bsz, seq, _ = c_q.shape
    q = (_rmsnorm(c_q, q_norm) @ w_uq).reshape(bsz, seq, MLA_HEADS, QK_HD)
    kv = (_rmsnorm(c_kv, kv_norm) @ w_ukv).reshape(bsz, seq, MLA_HEADS, QK_NOPE + V_HD)
    k_nope, v = kv[..., :QK_NOPE], kv[..., QK_NOPE:]
    k_r = jnp.broadcast_to(k_rope_in[:, :, None, :], (bsz, seq, MLA_HEADS, QK_ROPE))
    k = jnp.concatenate([k_nope, k_r], axis=-1)
    q = _rmsnorm(q, qk_norm_q)
    k = _rmsnorm(k, qk_norm_k)
    pos = jnp.arange(seq, dtype=f32)
    inv_freq = ROPE_THETA ** (-jnp.arange(0, QK_ROPE, 2, dtype=f32) / QK_ROPE)
    ang = pos[:, None] * inv_freq[None, :]
    cos = jnp.cos(ang)[None, :, None, :]
    sin = jnp.sin(ang)[None, :, None, :]
    q = jnp.concatenate([q[..., :QK_NOPE], _rope(q[..., QK_NOPE:].astype(f32), cos, sin).astype(q.dtype)], axis=-1)
    k = jnp.concatenate([k[..., :QK_NOPE], _rope(k[..., QK_NOPE:].astype(f32), cos, sin).astype(k.dtype)], axis=-1)
    o = _causal_block_attention(q, k, v)
    return o.reshape(bsz, seq, MLA_HEADS * V_HD)


def setup_inputs(seed: int = 0) -> dict:
    key = jax.random.key(seed)
    keys = jax.random.split(key, 42)
    f32 = jnp.float32
    L, C, G, P, M = DEPTH, BRANCH_W, S5_GROUPS, S5_STATE, S5_GROUP

    def nrm(i, shape, scale):
        return scale * jax.random.normal(keys[i], shape, f32)

    def unif(i, shape, lo, hi):
        return jax.random.uniform(keys[i], shape, f32, lo, hi)

    a0_lru = unif(7, (L, C), 0.9, 0.999) ** (1.0 / LRU_C)
    return {
        "x": nrm(0, (BATCH, SEQ, D_MODEL), 1.0),
        "norm_mix": 1.0 + nrm(1, (L, D_MODEL), 0.05),
        "w_in": nrm(2, (L, D_MODEL, D_IN), D_MODEL ** -0.5),
        "lru_conv_w": nrm(3, (L, CONV_W, C), CONV_W ** -0.5),
        "lru_conv_b": nrm(4, (L, C), 0.02),
        "lru_gate_w": nrm(5, (L, 2, LRU_HEADS, LRU_HD, LRU_HD), LRU_HD ** -0.5),
        "lru_gate_b": nrm(6, (L, 2, C), 0.02),
        "lru_lambda": jnp.log(a0_lru) - jnp.log1p(-a0_lru),
        "s5_a_re": -0.5 + nrm(8, (L, G, P), 0.01),
        "s5_a_im": math.pi * jnp.arange(P, dtype=f32) + nrm(9, (L, G, P), 0.01),
        "s5_b_re": nrm(10, (L, G, P, M), (2.0 * M) ** -0.5),
        "s5_b_im": nrm(11, (L, G, P, M), (2.0 * M) ** -0.5),
        "s5_c_re": nrm(12, (L, G, M, P), 0.5 ** 0.5),
        "s5_c_im": nrm(13, (L, G, M, P), 0.5 ** 0.5),
        "s5_d": nrm(14, (L, C), 1.0),
        "s5_log_dt": unif(15, (L, G), math.log(0.001), math.log(0.1)),
        "s5_w_glu": nrm(16, (L, C, 2 * C), C ** -0.5),
        "rwkv_mu_rkv": unif(17, (L, 3, C), 0.0, 1.0),
        "rwkv_mu_w": unif(18, (L, W_LORA), 0.0, 1.0),
        "rwkv_mu_a": unif(19, (L, A_LORA), 0.0, 1.0),
        "rwkv_mu_g": unif(20, (L, G_LORA), 0.0, 1.0),
        "rwkv_w0": unif(21, (L, C), -5.0, 1.0),
        "rwkv_w2": nrm(22, (L, W_LORA, C), W_LORA ** -0.5),
        "rwkv_a0": nrm(23, (L, C), 0.1),
        "rwkv_a2": nrm(24, (L, A_LORA, C), A_LORA ** -0.5),
        "rwkv_g2": nrm(25, (L, G_LORA, C), G_LORA ** -0.5),
        "rwkv_k_k": 0.85 + nrm(26, (L, C), 0.05),
        "rwkv_k_a": unif(27, (L, C), 0.0, 1.0),
        "rwkv_r_k": nrm(28, (L, RWKV_HEADS, RWKV_HD), 0.1),
        "rwkv_lnx_g": 1.0 + nrm(29, (L, C), 0.05),
        "rwkv_lnx_b": nrm(30, (L, C), 0.02),
        "mla_q_norm": 1.0 + nrm(31, (L, Q_LORA), 0.05),
        "mla_w_uq": nrm(32, (L, Q_LORA, MLA_HEADS * QK_HD), Q_LORA ** -0.5),
        "mla_kv_norm": 1.0 + nrm(33, (L, KV_LORA), 0.05),
        "mla_w_ukv": nrm(34, (L, KV_LORA, MLA_HEADS * (QK_NOPE + V_HD)), KV_LORA ** -0.5),
        "mla_qk_norm_q": 1.0 + nrm(35, (L, QK_HD), 0.05),
        "mla_qk_norm_k": 1.0 + nrm(36, (L, QK_HD), 0.05),
        "w_branch": nrm(37, (L, N_BRANCH, C, D_MODEL), C ** -0.5),
        "w_out": nrm(38, (L, D_MODEL, D_MODEL), D_MODEL ** -0.5),
        "norm_mlp": 1.0 + nrm(39, (L, D_MODEL), 0.05),
        "w_ff1": nrm(40, (L, D_MODEL, D_FF), D_MODEL ** -0.5),
        "w_ff2": nrm(41, (L, D_FF, D_MODEL), D_FF ** -0.5),
    }


def reference(x, norm_mix, w_in, lru_conv_w, lru_conv_b, lru_gate_w, lru_gate_b, lru_lambda,
              s5_a_re, s5_a_im, s5_b_re, s5_b_im, s5_c_re, s5_c_im, s5_d, s5_log_dt, s5_w_glu,
              rwkv_mu_rkv, rwkv_mu_w, rwkv_mu_a, rwkv_mu_g, rwkv_w0, rwkv_w2, rwkv_a0, rwkv_a2, rwkv_g2,
              rwkv_k_k, rwkv_k_a, rwkv_r_k, rwkv_lnx_g, rwkv_lnx_b,
              mla_q_norm, mla_w_uq, mla_kv_norm, mla_w_ukv, mla_qk_norm_q, mla_qk_norm_k,
              w_branch, w_out, norm_mlp, w_ff1, w_ff2):
    bsz, seq, _ = x.shape
    for l in range(DEPTH):
        h = _rmsnorm(x, norm_mix[l])
        (lru_x, lru_g, s5_u, rw_r, rw_k, rw_v, rw_w, rw_a, rw_g,
         mla_cq, mla_ckv, mla_kr, gate_logits) = jnp.split(h @ w_in[l], IN_SPLIT_POINTS, axis=-1)
        y_lru = _rglru_branch(lru_x, lru_g, lru_conv_w[l], lru_conv_b[l], lru_gate_w[l], lru_gate_b[l], lru_lambda[l])
        y_s5 = _s5_branch(s5_u, s5_a_re[l], s5_a_im[l], s5_b_re[l], s5_b_im[l], s5_c_re[l], s5_c_im[l],
                          s5_d[l], s5_log_dt[l], s5_w_glu[l])
        y_rwkv = _rwkv7_branch(rw_r, rw_k, rw_v, rw_w, rw_a, rw_g, rwkv_mu_rkv[l], rwkv_mu_w[l], rwkv_mu_a[l],
                               rwkv_mu_g[l], rwkv_w0[l], rwkv_w2[l], rwkv_a0[l], rwkv_a2[l], rwkv_g2[l],
                               rwkv_k_k[l], rwkv_k_a[l], rwkv_r_k[l], rwkv_lnx_g[l], rwkv_lnx_b[l])
        y_mla = _mla_branch(mla_cq, mla_ckv, mla_kr, mla_q_norm[l], mla_w_uq[l], mla_kv_norm[l], mla_w_ukv[l],
                            mla_qk_norm_q[l], mla_qk_norm_k[l])
        ys = jnp.stack([y_lru, y_s5, y_rwkv, y_mla], axis=2)
        gates = jax.nn.sigmoid(gate_logits.reshape(bsz, seq, N_BRANCH, D_MODEL))
        merged = jnp.sum(jnp.einsum("btnc,ncd->btnd", ys, w_branch[l]) * gates, axis=2)
        x = x + merged @ w_out[l]
        h2 = _rmsnorm(x, norm_mlp[l])
        x = x + jnp.square(jax.nn.relu(h2 @ w_ff1[l])) @ w_ff2[l]
    return x
```

```python
import math
from contextlib import ExitStack
import numpy as np
import concourse.bass as bass
import concourse.mybir as mybir
from concourse.bass_utils import run_bass_kernel_spmd

F32 = mybir.dt.float32
BF16 = mybir.dt.bfloat16
AF = mybir.ActivationFunctionType
ALU = mybir.AluOpType
AX = mybir.AxisListType

D = 1024
T_FULL = 4096
DEPTH = 4
C = 512
D_IN = 7840
TT = 512
EPS = 1e-6
ENGS = ("pe", "act", "dve", "pool", "sp")
GELU_K = 1.5957691216057308


class Op:
    __slots__ = ("eng", "fn", "reads", "writes", "dma", "idx", "deps", "sig", "cnt", "sem", "semval")

    def __init__(self, eng, fn, reads, writes, dma):
        self.eng, self.fn, self.reads, self.writes, self.dma = eng, fn, reads, writes, dma
        self.deps = []
        self.sig = False
        self.cnt = 0
        self.sem = None
        self.semval = 0


class Prog:
    NDMA = 48

    def __init__(self, nc):
        self.nc = nc
        self.ops = []

    def op(self, eng, fn, reads=(), writes=(), dma=False):
        o = Op(eng, fn, tuple(reads), tuple(writes), dma)
        o.idx = len(self.ops)
        self.ops.append(o)
        return o

    def finalize(self):
        last_w, readers, children = {}, {}, {}
        dma_k = 0
        dma_last = [None] * self.NDMA
        alias = getattr(self, "alias", {})

        def expand(keys):
            out = []
            for k in keys:
                base, _, sub = k.partition("#")
                for s_ in alias.get(base, (base,)):
                    out.append((s_, sub))
            return out

        def related(s_, sub):
            if sub == "":
                return [(s_, "")] + [(s_, c) for c in children.get(s_, ())]
            return [(s_, sub), (s_, "")]

        for o in self.ops:
            deps = {}
            rd, wr = expand(o.reads), expand(o.writes)
            for (s_, sub) in rd + wr:
                if sub:
                    children.setdefault(s_, set()).add(sub)
            for (s_, sub) in rd:
                for kk in related(s_, sub):
                    w = last_w.get(kk)
                    if w is not None:
                        deps[w.idx] = (w, "raw")
            for (s_, sub) in wr:
                for kk in related(s_, sub):
                    w = last_w.get(kk)
                    if w is not None and w.idx not in deps:
                        deps[w.idx] = (w, "waw")
                    for r in readers.get(kk, ()):
                        if r.idx not in deps and r is not o:
                            deps[r.idx] = (r, "war")
            for kk in rd:
                readers.setdefault(kk, []).append(o)
            for (s_, sub) in wr:
                last_w[(s_, sub)] = o
                readers[(s_, sub)] = []
                if sub == "":
                    for c in children.get(s_, ()):
                        last_w[(s_, c)] = o
                        readers[(s_, c)] = []
            if o.dma:
                k = dma_k % self.NDMA
                dma_k += 1
                prev = dma_last[k]
                o.sem = k
                o.semval = (prev.semval if prev is not None else 0) + 16
                if prev is not None and prev.idx not in deps:
                    deps[prev.idx] = (prev, "raw")
                dma_last[k] = o
            for (p, kind) in deps.values():
                if p.dma:
                    o.deps.append(p)
                elif p.eng == o.eng and not o.dma:
                    if kind == "raw" and o.eng != "pe":
                        o.deps.append(p)
                        p.sig = True
                else:
                    o.deps.append(p)
                    p.sig = True
        cnt = {e: 0 for e in ENGS}
        for o in self.ops:
            if o.sig and not o.dma:
                cnt[o.eng] += 1
                o.cnt = cnt[o.eng]

    def emit(self, final_waits=()):
        nc = self.nc
        with ExitStack() as st:
            esem = {e: st.enter_context(nc.semaphore("s_" + e)) for e in ENGS}
            dsem = [st.enter_context(nc.semaphore("d%d" % i)) for i in range(self.NDMA)]
            block = st.enter_context(nc.Block())
            per = {e: [o for o in self.ops if o.eng == e] for e in ENGS}

            def run(e, engobj, extra_final=()):
                seen_e = {x: 0 for x in ENGS}
                seen_d = {}
                for o in per[e]:
                    for p in o.deps:
                        if p.dma:
                            if seen_d.get(p.sem, 0) < p.semval:
                                engobj.wait_ge(dsem[p.sem], p.semval)
                                seen_d[p.sem] = p.semval
                        elif seen_e[p.eng] < p.cnt:
                            engobj.wait_ge(esem[p.eng], p.cnt)
                            seen_e[p.eng] = p.cnt
                    ins = o.fn(engobj)
                    if o.dma:
                        ins.then_inc(dsem[o.sem], 16)
                    elif o.sig:
                        ins.then_inc(esem[o.eng], 1)
                for p in extra_final:
                    engobj.wait_ge(dsem[p.sem], p.semval)

            @block.tensor
            def _(e):
                run("pe", e)

            @block.scalar
            def _(e):
                run("act", e)

            @block.vector
            def _(e):
                run("dve", e)

            @block.gpsimd
            def _(e):
                run("pool", e)

            @block.sync
            def _(e):
                run("sp", e, extra_final=final_waits)


PV = {}
_o = 0
for _n, _w in [("conv_w", 16), ("conv_b", 4), ("gate_b", 8), ("lam", 4), ("s5_d", 4), ("mu_rkv", 12), ("w0", 4),
               ("a0", 4), ("k_k", 4), ("k_a", 4), ("r_k", 4), ("mu_w", 1), ("mu_a", 1), ("mu_g", 1), ("q_norm", 2),
               ("kv_norm", 1), ("qkn_q", 1), ("qkn_k", 1), ("s5_are", 16), ("s5_aim", 16), ("s5_ldt", 16)]:
    PV[_n] = (_o, _w)
    _o += _w
NPV = _o


def _chunks(v, n):
    return np.ascontiguousarray(v.reshape(n, 128).T)


def host_layout(inp):
    f = np.float32
    L = DEPTH
    pvec = np.zeros((L, 128, NPV), f)
    fvec = np.zeros((L, 4, 1024), f)
    lrug = np.zeros((L, 2, 4, 128, 128), f)
    bst = np.zeros((L, 2, 16, 128, 128), f)
    cst = np.zeros((L, 2, 16, 128, 128), f)
    for l in range(L):
        def put(name, arr):
            o, w = PV[name]
            pvec[l, :arr.shape[0], o:o + w] = arr
        put("conv_w", np.concatenate([_chunks(inp["lru_conv_w"][l, k], 4) for k in range(4)], axis=1))
        put("conv_b", _chunks(inp["lru_conv_b"][l], 4))
        put("gate_b", np.concatenate([_chunks(inp["lru_gate_b"][l, g], 4) for g in range(2)], axis=1))
        put("lam", _chunks(inp["lru_lambda"][l], 4))
        put("s5_d", _chunks(inp["s5_d"][l], 4))
        put("mu_rkv", np.concatenate([_chunks(inp["rwkv_mu_rkv"][l, j], 4) for j in range(3)], axis=1))
        put("w0", _chunks(inp["rwkv_w0"][l], 4))
        put("a0", _chunks(inp["rwkv_a0"][l], 4))
        put("k_k", _chunks(inp["rwkv_k_k"][l], 4))
        put("k_a", _chunks(inp["rwkv_k_a"][l], 4))
        put("r_k", _chunks(inp["rwkv_r_k"][l].reshape(-1), 4))
        put("mu_w", inp["rwkv_mu_w"][l].reshape(64, 1))
        put("mu_a", inp["rwkv_mu_a"][l].reshape(64, 1))
        put("mu_g", inp["rwkv_mu_g"][l].reshape(128, 1))
        put("q_norm", _chunks(inp["mla_q_norm"][l], 2))
        put("kv_norm", inp["mla_kv_norm"][l].reshape(128, 1))
        put("qkn_q", inp["mla_qk_norm_q"][l].reshape(96, 1))
        put("qkn_k", inp["mla_qk_norm_k"][l].reshape(96, 1))
        are = inp["s5_a_re"][l].reshape(16, 2, 64).transpose(1, 2, 0).reshape(128, 16)
        aim = inp["s5_a_im"][l].reshape(16, 2, 64).transpose(1, 2, 0).reshape(128, 16)
        ldt = np.broadcast_to(inp["s5_log_dt"][l].reshape(16, 2, 1), (16, 2, 64)).transpose(1, 2, 0).reshape(128, 16)
        put("s5_are", are)
        put("s5_aim", aim)
        put("s5_ldt", ldt)
        fvec[l, 0] = inp["norm_mix"][l]
        fvec[l, 1] = inp["norm_mlp"][l]
        fvec[l, 2, :512] = inp["rwkv_lnx_g"][l]
        fvec[l, 2, 512:] = inp["rwkv_lnx_b"][l]
        for g in range(2):
            for h in range(8):
                c, e = h // 2, h % 2
                lrug[l, g, c, e * 64:(e + 1) * 64, e * 64:(e + 1) * 64] = inp["lru_gate_w"][l, g, h]
        for ri, (bsrc, csrc) in enumerate([(inp["s5_b_re"][l], inp["s5_c_re"][l]), (inp["s5_b_im"][l], inp["s5_c_im"][l])]):
            for g in range(32):
                j, e = g // 2, g % 2
                r0 = 32 * (j % 4) + e * 16
                bst[l, ri, j, r0:r0 + 16, e * 64:(e + 1) * 64] = bsrc[g].T
                cst[l, ri, j, e * 64:(e + 1) * 64, r0:r0 + 16] = csrc[g].T
    return pvec, fvec, lrug, bst, cst


def host_consts():
    f = np.float32
    c = {}
    c["ident"] = np.eye(128, dtype=f)
    c["ones"] = np.ones((128, 128), f)
    blk = np.zeros((128, 128), f)
    blk[:64, :64] = 1
    blk[64:, 64:] = 1
    c["blk64"] = blk
    hs = np.zeros((128, 2), f)
    hs[:64, 0] = 1
    hs[64:, 1] = 1
    c["headsel"] = hs
    p = np.arange(128)[:, None]
    q = np.arange(512)[None, :]
    c["amask"] = np.stack([(q >= 128 * j + p) for j in range(4)], 1).astype(f)
    j = np.arange(64)[:, None]
    t = np.arange(64)[None, :]
    m = np.zeros((64, 3, 64), f)
    m[:, 0] = (t > j)
    m[:, 1] = (t >= j)
    m[:, 2] = (t < j)
    c["rmask"] = m
    rs = np.ones((128, 512), f)
    rs[:, ::64] = 0
    c["reset64"] = rs
    pos = np.arange(T_FULL, dtype=np.float64)
    inv = 10000.0 ** (-np.arange(0, 32, 2, dtype=np.float64) / 32)
    ang = pos[None, :] * inv[:, None]
    cos = np.ones((96, T_FULL), np.float64)
    sin = np.zeros((96, T_FULL), np.float64)
    cos[64:80] = np.cos(ang); cos[80:96] = np.cos(ang)
    sin[64:80] = np.sin(ang); sin[80:96] = np.sin(ang)
    c["ropec"] = cos.astype(f)
    c["ropes"] = sin.astype(f)
    pr = np.zeros((96, 96), f)
    for i in range(16):
        pr[80 + i, 64 + i] = -1.0
        pr[64 + i, 80 + i] = 1.0
    c["prot"] = pr
    return c


CONST_SHAPES = {"ident": [128, 128], "ones": [128, 128], "blk64": [128, 128], "headsel": [128, 2],
                "amask": [128, 4, 512], "rmask": [64, 3, 64], "reset64": [128, 512],
                "ropec": [96, T_FULL], "ropes": [96, T_FULL], "prot": [96, 96]}

BIGW = {"w_in": [D, D_IN], "s5_w_glu": [C, 2 * C], "w_branch": [4 * C, D], "w_out": [D, D], "w_ff1": [D, 4 * D],
        "w_ff2": [4 * D, D], "mla_w_uq": [256, 768], "mla_w_ukv": [128, 1024], "rwkv_w2": [64, C],
        "rwkv_a2": [64, C], "rwkv_g2": [128, C]}


class KB:
    def __init__(self, nc, L_RUN, T_RUN, dbg=False):
        self.nc, self.L, self.T, self.dbg = nc, L_RUN, T_RUN, dbg
        self.NT = T_RUN // TT
        self.P = Prog(nc)
        self.st = ExitStack()
        self.free_banks = []
        self.scr_free = {}
        self.scr_n = 0
        self.wk = 0
        self.outs = []

    def sb(self, name, shape, dt=F32):
        return self.st.enter_context(self.nc.sbuf_tensor(name, shape, dt))

    def bank(self):
        assert self.free_banks, "out of PSUM banks"
        return self.free_banks.pop(0)

    def unbank(self, b):
        self.free_banks.append(b)

    NSLOT = 40

    def scr(self, kb=2):
        n = kb // 2
        if not hasattr(self, "arena"):
            self.arena = self.sb("arena", [128, self.NSLOT * 512], F32)
            self.slot_used = [False] * self.NSLOT
            self.P.alias = {}
        for st in range(self.NSLOT - n + 1):
            if not any(self.slot_used[st:st + n]):
                for i in range(st, st + n):
                    self.slot_used[i] = True
                key = "arn_%d_%d" % (st, n)
                self.P.alias[key] = tuple("slot%d" % i for i in range(st, st + n))
                return (self.arena[:, st * 512:(st + n) * 512], key)
        raise AssertionError("out of scratch slots")

    def unscr(self, s, kb=2):
        _, st, n = s[1].split("_")
        for i in range(int(st), int(st) + int(n)):
            assert self.slot_used[i]
            self.slot_used[i] = False

    def mm(self, out, lhsT, rhs, start, stop, r, w):
        self.P.op("pe", lambda e: e.matmul(out, lhsT=lhsT, rhs=rhs, start=start, stop=stop), r, w)

    def trp(self, out, in_, ident, r, w):
        self.P.op("pe", lambda e: e.transpose(out=out, in_=in_, identity=ident), r, w)

    def act(self, out, in_, func, r, w, **kw):
        self.P.op("act", lambda e: e.activation(out=out, in_=in_, func=func, **kw), r, w)

    def tt(self, eng, out, in0, in1, op, r, w):
        self.P.op(eng, lambda e: e.tensor_tensor(out=out, in0=in0, in1=in1, op=op), r, w)

    def ts(self, eng, out, in0, s1, s2, op0, op1, r, w):
        if s2 is None:
            self.P.op(eng, lambda e: e.tensor_single_scalar(out=out, in_=in0, scalar=s1, op=op0), r, w)
        else:
            self.P.op(eng, lambda e: e.tensor_scalar(out=out, in0=in0, scalar1=s1, scalar2=s2, op0=op0, op1=op1), r, w)

    def stt(self, eng, out, in0, scalar, in1, op0, op1, r, w):
        self.P.op(eng, lambda e: e.scalar_tensor_tensor(out=out, in0=in0, scalar=scalar, in1=in1, op0=op0, op1=op1), r, w)

    def cp(self, eng, out, in_, r, w):
        if eng == "act":
            self.P.op("act", lambda e: e.copy(out=out, in_=in_), r, w)
        else:
            self.P.op(eng, lambda e: e.tensor_copy(out=out, in_=in_), r, w)

    def memset(self, eng, out, val, w):
        self.P.op(eng, lambda e: e.memset(out, val), [], w)

    def recip(self, out, in_, r, w):
        self.P.op("dve", lambda e: e.reciprocal(out=out, in_=in_), r, w)

    def scan(self, out, d0, d1, init, r, w):
        self.P.op("dve", lambda e: e.tensor_tensor_scan(out=out, data0=d0, data1=d1, initial=init, op0=ALU.mult, op1=ALU.add), r, w)

    def dma(self, eng, out, in_, r, w, **kw):
        return self.P.op(eng, lambda e: e.dma_start(out=out, in_=in_, **kw), r, w, dma=True)

    def wload(self, src, shape):
        i = self.wk % len(self.wring)
        self.wk += 1
        buf, key = self.wring[i]
        n = int(np.prod(shape[1:]))
        if len(shape) == 3:
            view = buf[:shape[0], 0:n].rearrange("p (k c) -> p k c", k=shape[1])
        else:
            view = buf[:shape[0], 0:n]
        self.dma("sp", view, src, [self.cur_wkey], [key])
        return view, key

    def gelu(self, src, srck, out, outk, tmp, tmpk):
        self.tt("dve", tmp, src, src, ALU.mult, [srck], [tmpk])
        self.ts("dve", tmp, tmp, 0.044715, 1.0, ALU.mult, ALU.add, [tmpk], [tmpk])
        self.tt("dve", tmp, tmp, src, ALU.mult, [tmpk, srck], [tmpk])
        self.act(tmp, tmp, AF.Sigmoid, [tmpk], [tmpk], scale=GELU_K)
        self.tt("dve", out, tmp, src, ALU.mult, [tmpk, srck], [outk])

    def build(self, branches=(0, 1, 2, 3)):
        nc, L, T = self.nc, self.L, self.T
        self.branches = branches
        dt = lambda name, shape, dty, kind: nc.dram_tensor(name, shape, dty, kind=kind).ap()
        self.x = dt("x", [T, D], F32, "ExternalInput")
        self.y = dt("y", [T, D], F32, "ExternalOutput")
        self.w32, self.wb = {}, {}
        for n, (r, c) in BIGW.items():
            self.w32[n] = dt(n, [DEPTH, r * c // 1024, 1024], F32, "ExternalInput")
            self.wb[n] = dt("b_" + n, [DEPTH, r, c], BF16, "Internal")
        self.pvec_d = dt("pvec", [DEPTH, 128, NPV], F32, "ExternalInput")
        self.fvec_d = dt("fvec", [DEPTH, 4, 1024], F32, "ExternalInput")
        for n, shp in [("lrug", [DEPTH, 8 * 128 * 128 // 1024, 1024]), ("bst", [DEPTH, 32 * 16, 1024]), ("cst", [DEPTH, 32 * 16, 1024])]:
            self.w32[n] = dt(n, shp, F32, "ExternalInput")
        self.wb["lrug"] = dt("b_lrug", [DEPTH, 8, 128, 128], BF16, "Internal")
        self.wb["bst"] = dt("b_bst", [DEPTH, 32, 128, 128], BF16, "Internal")
        self.wb["cst"] = dt("b_cst", [DEPTH, 32, 128, 128], BF16, "Internal")
        self.cd = {n: dt("c_" + n, s, F32, "ExternalInput") for n, s in CONST_SHAPES.items()}
        self.kc = dt("kcache", [8, 96, T], BF16, "Internal")
        self.vc = dt("vcache", [8, 128, T // 128, 65], BF16, "Internal")
        self.s5tab = dt("s5tab", [16, 128, 4, 128], F32, "Internal")
        if self.dbg:
            self.dbg_ys = dt("dbg_ys", [4, 128, 8, TT], F32, "ExternalOutput")

        for i in range(8):
            t = self.st.enter_context(nc.psum_tensor("bank%d" % i, [128, 512], F32))
            self.free_banks.append((t, "bank%d" % i))
        sb = self.sb
        self.identf = sb("identf", [128, 128]); self.identb = sb("identb", [128, 128], BF16)
        self.onesf = sb("onesf", [128, 128]); self.onesb = sb("onesb", [128, 128], BF16)
        self.blk64 = sb("blk64", [128, 128]); self.headsel = sb("headsel", [128, 2]); self.headselb = sb("headselb", [128, 2], BF16)
        self.amask = sb("amask", [128, 4, 512], BF16); self.rmask = sb("rmask", [64, 3, 64])
        self.reset64 = sb("reset64", [128, 512]); self.prot = sb("prot", [96, 96])
        amf, amfk = self.scr(8)
        for n, tl in [("ident", self.identf), ("ones", self.onesf), ("blk64", self.blk64), ("headsel", self.headsel),
                      ("rmask", self.rmask), ("reset64", self.reset64), ("prot", self.prot)]:
            self.dma("sp", tl[:], self.cd[n], [], [n])
        self.dma("sp", amf[:, 0:2048].rearrange("p (a b) -> p a b", a=4), self.cd["amask"], [], [amfk])
        self.cp("dve", self.amask[:], amf[:, 0:2048].rearrange("p (a b) -> p a b", a=4), [amfk], ["amask"])
        self.unscr((amf, amfk), 8)
        self.cp("dve", self.identb[:], self.identf[:], ["ident"], ["identb"])
        self.cp("dve", self.onesb[:], self.onesf[:], ["ones"], ["onesb"])
        self.cp("dve", self.headselb[:], self.headsel[:], ["headsel"], ["headselb"])
        self.wring = [(sb("wring%d" % i, [128, 4096], BF16), "wring%d" % i) for i in range(3)]
        self.xt = sb("xt", [128, 4, 1024]); self.hT = sb("hT", [128, 8, 512], BF16)
        self.gbc = sb("gbc", [128, 1024]); self.lnx = sb("lnx", [64, 1024])
        self.arA = sb("arA", [128, 4096]); self.arB = sb("arB", [128, 5120])
        self.hn = self.arA[:, 0:2048].bitcast(BF16).rearrange("p (s d) -> p s d", s=4)
        self.macc = self.arA[:, :].rearrange("p (c t) -> p c t", c=8)
        ysb = self.arB[:, :].bitcast(BF16)
        self.ys = [ysb[:, i * 2048:(i + 1) * 2048].rearrange("p (c t) -> p c t", c=4) for i in range(3)]
        self.ys.append(ysb[0:64, 6144:10240].rearrange("p (h t) -> p h t", h=8))
        self.a1T_lo = self.arA[:, :].bitcast(BF16).rearrange("p (c t) -> p c t", c=16)
        self.a1T_hi = ysb[:, 0:8192].rearrange("p (c t) -> p c t", c=16)
        self.pv = sb("pv", [128, NPV]); self.der = sb("der", [128, 16])
        self.lrug_sb = sb("lrug_sb", [128, 8, 128], BF16)
        self.w2_sb = sb("w2_sb", [64, 512], BF16); self.a2_sb = sb("a2_sb", [64, 512], BF16); self.g2_sb = sb("g2_sb", [128, 512], BF16)
        self.wuq_sb = sb("wuq_sb", [128, 2, 768], BF16); self.wukv_sb = sb("wukv_sb", [128, 1024], BF16)
        self.ss = sb("ss", [128, 8])
        self.xh = [sb("xh%d" % c, [128, 515]) for c in range(4)]
        self.hc = sb("hc", [128, 4])
        self.zc = sb("zc", [128, 2, 16]); self.el = sb("el", [128, 2, 16])
        self.carry = sb("carry", [128, 16])
        self.s0f = sb("s0f", [128, 4, 64]); self.s0bd = sb("s0bd", [128, 4, 128], BF16)

        for l in range(L):
            for n in list(BIGW) + ["lrug", "bst", "cst"]:
                dstv = self.wb[n][l]
                if n in ("lrug", "bst", "cst"):
                    dstv = dstv.rearrange("a r c -> (a r c)")
                else:
                    dstv = dstv.rearrange("r c -> (r c)")
                dstv = dstv.rearrange("(a b) -> a b", b=1024)
                self.dma("pool", dstv, self.w32[n][l], [], ["wb%d" % l])
        for l in range(L):
            self.layer(l)
        self.P.finalize()
        self.P.emit(final_waits=self.outs)
        self.st.close()

    def layer(self, l):
        self.l = l
        self.cur_wkey = "wb%d" % l
        wk = self.cur_wkey
        pv, der = self.pv, self.der
        self.dma("sp", pv[:], self.pvec_d[l], [], ["pv"])
        self.dma("sp", self.lnx[:], self.fvec_d[l, 2:3, :].broadcast_to([64, 1024]), [], ["lnx"])
        self.dma("sp", self.lrug_sb[:], self.wb["lrug"][l].rearrange("a r c -> r a c"), [wk], ["lrug_sb"])
        self.dma("sp", self.w2_sb[:], self.wb["rwkv_w2"][l], [wk], ["w2_sb"])
        self.dma("sp", self.a2_sb[:], self.wb["rwkv_a2"][l], [wk], ["a2_sb"])
        self.dma("sp", self.g2_sb[:], self.wb["rwkv_g2"][l], [wk], ["g2_sb"])
        self.dma("sp", self.wuq_sb[:], self.wb["mla_w_uq"][l].rearrange("(k p) c -> p k c", p=128), [wk], ["wuq_sb"])
        self.dma("sp", self.wukv_sb[:], self.wb["mla_w_ukv"][l], [wk], ["wukv_sb"])
        o = PV["lam"][0]
        self.act(der[:, 0:4], pv[:, o:o + 4], AF.Exp, ["pv"], ["der"], scale=-1.0)
        self.act(der[:, 0:4], der[:, 0:4], AF.Ln, ["der"], ["der"], bias=1.0)
        self.ts("dve", der[:, 0:4], der[:, 0:4], -8.0, None, ALU.mult, None, ["der"], ["der"])
        o = PV["w0"][0]
        self.ts("dve", der[:, 4:8], pv[:, o:o + 4], -1.0, None, ALU.mult, None, ["pv"], ["der"])
        o = PV["k_a"][0]
        self.ts("dve", der[:, 8:12], pv[:, o:o + 4], -1.0, 1.0, ALU.mult, ALU.add, ["pv"], ["der"])
        for c in range(4):
            self.memset("pool", self.xh[c][:, 0:3], 0.0, ["xh%d" % c])
        self.memset("pool", self.hc[:], 0.0, ["hc"])
        self.memset("pool", self.zc[:], 0.0, ["zc%d" % i for i in range(16)])
        self.memset("pool", self.carry[:], 0.0, ["carry%d" % i for i in range(16)])
        self.memset("pool", self.s0f[:], 0.0, ["s0_%df" % i for i in range(4)])
        self.memset("pool", self.s0bd[:], 0.0, ["s0_%d" % i for i in range(4)])
        if 1 in self.branches:
            self.s5_setup(l)
        for ti in range(self.NT):
            self.tile(l, ti)

    def pvc(self, name, i=0, rows=128):
        o = PV[name][0] + i
        return self.pv[0:rows, o:o + 1]

    def norm_T(self, gi):
        l = self.l
        self.dma("sp", self.gbc[:], self.fvec_d[l, gi:gi + 1, :].broadcast_to([128, 1024]), [], ["gbc"])
        junk, jk = self.scr(4)
        for s in range(4):
            self.act(junk[:, 0:1024], self.xt[:, s, :], AF.Square, ["xt"], [jk, "ss"], accum_out=self.ss[:, s:s + 1])
        self.unscr((junk, jk), 4)
        self.act(self.ss[:, 4:8], self.ss[:, 0:4], AF.Sqrt, ["ss"], ["ss"], scale=1.0 / D, bias=EPS)
        self.recip(self.ss[:, 4:8], self.ss[:, 4:8], ["ss"], ["ss"])
        for s in range(4):
            self.stt("dve", self.hn[:, s, :], self.xt[:, s, :], self.ss[:, 4 + s:5 + s], self.gbc[:], ALU.mult, ALU.mult,
                     ["xt", "ss", "gbc"], ["arA"])
        for kc in range(8):
            b, bk = self.bank()
            bb = b[:, :].bitcast(BF16)
            for s in range(4):
                self.trp(bb[:, s * 128:(s + 1) * 128], self.hn[:, s, kc * 128:(kc + 1) * 128], self.identb[:], ["arA", "identb"], [bk])
            self.cp("act" if kc % 2 else "dve", self.hT[:, kc, :], bb[:, 0:512], [bk], ["hT"])
            self.unbank((b, bk))

    def proj(self, out, slab, c0, m, nk, rhsf, r, w):
        for kc in range(nk):
            self.mm(out, slab[:, kc, c0:c0 + m], rhsf(kc), kc == 0, kc == nk - 1, r, w)

    def win(self, c0, n=512):
        return self.wload(self.wb["w_in"][self.l][:, c0:c0 + n].rearrange("(k p) c -> p k c", p=128), [128, 8, n])

    def tile(self, l, ti):
        t0 = ti * TT
        src = self.x if l == 0 else self.y
        self.dma("sp", self.xt[:], src[t0:t0 + TT, :].rearrange("(s p) d -> p s d", p=128), ["ydram"], ["xt"])
        self.norm_T(0)
        for bi, fn in enumerate([self.lru, self.s5, self.rwkv, self.mla]):
            if bi in self.branches:
                fn(l, ti)
            else:
                self.memset("pool", self.ys[bi], 0.0, ["arB"])
        if self.dbg and l == 0 and ti == self.dbg_tile:
            d, dk = self.scr(8)
            dv = d[:, :].rearrange("p (c t) -> p c t", c=4)
            for bi in range(4):
                np_ = 128 if bi < 3 else 64
                for hf_ in range(1 if bi < 3 else 2):
                    self.cp("dve", dv[0:np_], self.ys[bi][:, hf_ * 4:hf_ * 4 + 4, :], ["arB"], [dk])
                    self.outs.append(self.dma("sp", self.dbg_ys[bi, 0:np_, hf_ * 4:hf_ * 4 + 4, :], dv[0:np_], [dk], ["dbgout"]))
            self.unscr((d, dk), 8)
        self.merge(l, ti)
        self.mlp(l, ti)
        st = self.dma("pool", self.y[t0:t0 + TT, :].rearrange("(s p) d -> p s d", p=128), self.xt[:], ["xt"], ["ydram"])
        if l == self.L - 1:
            self.outs.append(st)

    def merge(self, l, ti):
        wbr = self.wb["w_branch"][l]
        sg, sgk = self.scr(2)
        tm, tmk = self.scr(2)
        for n in range(4):
            for half in range(2):
                if n < 3:
                    wsl, wslk = self.wload(wbr[n * 512:(n + 1) * 512, half * 512:(half + 1) * 512].rearrange("(k p) c -> p k c", p=128), [128, 4, 512])
                    nk = 4
                else:
                    wsl, wslk = self.wload(wbr[n * 512:(n + 1) * 512, half * 512:(half + 1) * 512].rearrange("(k p) c -> p k c", p=64), [64, 8, 512])
                    nk = 8
                gsl, gslk = self.win(3744 + n * 1024 + half * 512)
                for dc4 in range(4):
                    dc = half * 4 + dc4
                    bg, bgk = self.bank()
                    self.proj(bg[:, :], gsl, dc4 * 128, 128, 8, lambda kc: self.hT[:, kc, :], [gslk, "hT"], [bgk])
                    self.act(sg[:, 0:512], bg[:, :], AF.Sigmoid, [bgk], [sgk])
                    self.unbank((bg, bgk))
                    bz, bzk = self.bank()
                    ysn = self.ys[n]
                    self.proj(bz[:, :], wsl, dc4 * 128, 128, nk, lambda kc: ysn[:, kc, :], [wslk, "arB"], [bzk])
                    if n == 0:
                        self.tt("dve", self.macc[:, dc, :], bz[:, :], sg[:, 0:512], ALU.mult, [bzk, sgk], ["arA"])
                    else:
                        self.tt("dve", tm[:, 0:512], bz[:, :], sg[:, 0:512], ALU.mult, [bzk, sgk], [tmk])
                        self.tt("pool", self.macc[:, dc, :], self.macc[:, dc, :], tm[:, 0:512], ALU.add, ["arA", tmk], ["arA"])
                    self.unbank((bz, bzk))
        self.unscr((sg, sgk)); self.unscr((tm, tmk))
        for dc in range(8):
            self.cp("act" if dc % 2 else "dve", self.hT[:, dc, :], self.macc[:, dc, :], ["arA"], ["hT"])
        for half in range(2):
            wsl, wslk = self.wload(self.wb["w_out"][l][:, half * 512:(half + 1) * 512].rearrange("(k p) c -> p k c", p=128), [128, 8, 512])
            for s in range(4):
                b, bk = self.bank()
                for kc in range(8):
                    self.mm(b[:, :], self.hT[:, kc, s * 128:(s + 1) * 128], wsl[:, kc, :], kc == 0, kc == 7, ["hT", wslk], [bk])
                self.tt("dve", self.xt[:, s, half * 512:(half + 1) * 512], self.xt[:, s, half * 512:(half + 1) * 512], b[:, :], ALU.add, ["xt", bk], ["xt"])
                self.unbank((b, bk))

    def mlp(self, l, ti):
        self.norm_T(1)
        r1, r1k = self.scr(2)
        for sl in range(8):
            wsl, wslk = self.wload(self.wb["w_ff1"][l][:, sl * 512:(sl + 1) * 512].rearrange("(k p) c -> p k c", p=128), [128, 8, 512])
            for c4 in range(4):
                fc = sl * 4 + c4
                b, bk = self.bank()
                self.proj(b[:, :], wsl, c4 * 128, 128, 8, lambda kc: self.hT[:, kc, :], [wslk, "hT"], [bk])
                self.act(r1[:, 0:512], b[:, :], AF.Relu, [bk], [r1k])
                dst = self.a1T_lo[:, fc, :] if fc < 16 else self.a1T_hi[:, fc - 16, :]
                self.tt("dve" if fc % 2 else "pool", dst, r1[:, 0:512], r1[:, 0:512], ALU.mult, [r1k], ["arA" if fc < 16 else "arB"])
                self.unbank((b, bk))
        self.unscr((r1, r1k))
        for half in range(2):
            accs = [self.bank() for _ in range(4)]
            for g in range(4):
                wsl, wslk = self.wload(self.wb["w_ff2"][l][g * 1024:(g + 1) * 1024, half * 512:(half + 1) * 512].rearrange("(k p) c -> p k c", p=128), [128, 8, 512])
                for s in range(4):
                    for kc in range(8):
                        fc = g * 8 + kc
                        a = self.a1T_lo[:, fc, s * 128:(s + 1) * 128] if fc < 16 else self.a1T_hi[:, fc - 16, s * 128:(s + 1) * 128]
                        self.mm(accs[s][0][:, :], a, wsl[:, kc, :], fc == 0, fc == 31, ["arA", "arB", wslk], [accs[s][1]])
            for s in range(4):
                self.tt("dve", self.xt[:, s, half * 512:(half + 1) * 512], self.xt[:, s, half * 512:(half + 1) * 512], accs[s][0][:, :], ALU.add, ["xt", accs[s][1]], ["xt"])
                self.unbank(accs[s])

    def lru(self, l, ti):
        slx, slxk = self.win(0)
        slg, slgk = self.win(512)
        S = [self.scr() for _ in range(6)]
        (xc, xck), (rr, rrk), (ii, iik), (aa, aak), (t1, t1k), (hh, hhk) = S
        xcb, xcbk = self.scr()
        xcbv = xcb[:, 0:256].bitcast(BF16)
        hf = lambda kc: self.hT[:, kc, :]
        for c in range(4):
            xh, xhk = self.xh[c], "xh%d" % c
            b, bk = self.bank()
            self.proj(b[:, :], slx, c * 128, 128, 8, hf, [slxk, "hT"], [bk])
            self.cp("act", xh[:, 3:515], b[:, :], [bk], [xhk])
            self.unbank((b, bk))
            self.ts("dve", xc[:, 0:512], xh[:, 3:515], self.pvc("conv_w", 12 + c), self.pvc("conv_b", c), ALU.mult, ALU.add, [xhk, "pv"], [xck])
            for k in range(3):
                self.stt("dve", xc[:, 0:512], xh[:, k:k + 512], self.pvc("conv_w", 4 * k + c), xc[:, 0:512], ALU.mult, ALU.add, [xhk, "pv", xck], [xck])
            self.cp("pool", xh[:, 0:3], xh[:, 512:515], [xhk], [xhk])
            self.cp("dve", xcbv, xc[:, 0:512], [xck], [xcbk])
            for g, (dst, dstk) in enumerate([(rr, rrk), (ii, iik)]):
                b, bk = self.bank()
                self.mm(b[:, :], self.lrug_sb[:, g * 4 + c, :], xcbv, True, True, ["lrug_sb", xcbk], [bk])
                self.act(dst[:, 0:512], b[:, :], AF.Sigmoid, [bk, "pv"], [dstk], bias=self.pvc("gate_b", g * 4 + c))
                self.unbank((b, bk))
            self.act(aa[:, 0:512], rr[:, 0:512], AF.Exp, [rrk, "der"], [aak], scale=self.der[:, c:c + 1])
            self.tt("dve", t1[:, 0:512], aa[:, 0:512], aa[:, 0:512], ALU.mult, [aak], [t1k])
            self.ts("dve", t1[:, 0:512], t1[:, 0:512], -1.0, 1.0, ALU.mult, ALU.add, [t1k], [t1k])
            self.act(t1[:, 0:512], t1[:, 0:512], AF.Sqrt, [t1k], [t1k])
            self.tt("dve", ii[:, 0:512], ii[:, 0:512], xc[:, 0:512], ALU.mult, [iik, xck], [iik])
            self.tt("dve", t1[:, 0:512], t1[:, 0:512], ii[:, 0:512], ALU.mult, [t1k, iik], [t1k])
            self.scan(hh[:, 0:512], aa[:, 0:512], t1[:, 0:512], self.hc[:, c:c + 1], [aak, t1k, "hc"], [hhk])
            self.cp("dve", self.hc[:, c:c + 1], hh[:, 511:512], [hhk], ["hc"])
            b, bk = self.bank()
            self.proj(b[:, :], slg, c * 128, 128, 8, hf, [slgk, "hT"], [bk])
            self.cp("act", rr[:, 0:512], b[:, :], [bk], [rrk])
            self.unbank((b, bk))
            self.gelu(rr[:, 0:512], rrk, ii[:, 0:512], iik, t1[:, 0:512], t1k)
            self.tt("dve", self.ys[0][:, c, :], hh[:, 0:512], ii[:, 0:512], ALU.mult, [hhk, iik], ["arB"])
        for s_ in S:
            self.unscr(s_)
        self.unscr((xcb, xcbk))

    def s5_setup(self, l):
        sm, smk = self.scr()
        v = lambda i: sm[:, i * 16:(i + 1) * 16]
        pvs = lambda n: self.pv[:, PV[n][0]:PV[n][0] + 16]
        R, W = [smk, "pv"], [smk]
        tt = lambda o, a, b, op: self.tt("dve", o, a, b, op, R, W)
        self.act(v(0), pvs("s5_ldt"), AF.Exp, R, W)
        tt(v(1), pvs("s5_aim"), v(0), ALU.mult)
        tt(v(2), pvs("s5_are"), v(0), ALU.mult)
        self.act(v(3), v(2), AF.Exp, R, W)
        self.act(v(4), v(2), AF.Exp, R, W, scale=-1.0)
        self.act(v(5), v(1), AF.Sin, R, W, scale=1.0 / 16)
        self.ts("dve", v(6), v(1), 1.0 / 16, math.pi / 2, ALU.mult, ALU.add, R, W)
        self.act(v(6), v(6), AF.Sin, R, W)

        def csq(c, s_):
            tt(v(7), c, c, ALU.mult); tt(v(8), s_, s_, ALU.mult); tt(v(9), c, s_, ALU.mult)
            tt(c, v(7), v(8), ALU.subtract)
            self.ts("dve", s_, v(9), 2.0, None, ALU.mult, None, R, W)
        for _ in range(4):
            csq(v(6), v(5))
        tt(v(10), v(3), v(6), ALU.mult); tt(v(11), v(3), v(5), ALU.mult)
        tt(v(12), v(4), v(6), ALU.mult); tt(v(13), v(4), v(5), ALU.mult)
        self.ts("dve", v(13), v(13), -1.0, None, ALU.mult, None, R, W)
        self.ts("dve", v(14), v(10), -1.0, None, ALU.add, None, R, W)
        tt(v(7), pvs("s5_are"), pvs("s5_are"), ALU.mult); tt(v(8), pvs("s5_aim"), pvs("s5_aim"), ALU.mult)
        tt(v(7), v(7), v(8), ALU.add)
        self.recip(v(7), v(7), R, W)
        tt(v(8), v(14), pvs("s5_are"), ALU.mult); tt(v(9), v(11), pvs("s5_aim"), ALU.mult); tt(v(8), v(8), v(9), ALU.add)
        tt(v(15), v(8), v(7), ALU.mult)
        tt(v(8), v(11), pvs("s5_are"), ALU.mult); tt(v(9), v(14), pvs("s5_aim"), ALU.mult); tt(v(8), v(8), v(9), ALU.subtract)
        tt(v(14), v(8), v(7), ALU.mult)
        tabs = [self.scr(8) for _ in range(4)]
        tv = [t[0][:, :].rearrange("p (j t) -> p j t", j=16) for t in tabs]
        tk = [t[1] for t in tabs]
        tmp, tmpk = self.scr(8)
        tmv = tmp[:, :].rearrange("p (j t) -> p j t", j=16)
        self.cp("dve", tv[0][:, :, 0], v(15), R, [tk[0]]); self.cp("dve", tv[1][:, :, 0], v(14), R, [tk[1]])
        self.memset("dve", tv[2][:, :, 0], 1.0, [tk[2]]); self.memset("dve", tv[3][:, :, 0], 0.0, [tk[3]])
        for (ar_, ai_, pr, pi) in [(0, 1, v(12), v(13)), (2, 3, v(10), v(11))]:
            for k in range(7):
                n = 1 << k
                prb = pr.unsqueeze(2).broadcast_to([128, 16, n]) if n > 1 else pr.unsqueeze(2)
                pib = pi.unsqueeze(2).broadcast_to([128, 16, n]) if n > 1 else pi.unsqueeze(2)
                Ar, Ai = tv[ar_], tv[ai_]
                kk = [tk[ar_], tk[ai_], smk, tmpk]
                self.tt("dve", Ar[:, :, n:2 * n], Ar[:, :, 0:n], prb, ALU.mult, kk, [tk[ar_]])
                self.tt("dve", tmv[:, :, 0:n], Ai[:, :, 0:n], pib, ALU.mult, kk, [tmpk])
                self.tt("dve", Ar[:, :, n:2 * n], Ar[:, :, n:2 * n], tmv[:, :, 0:n], ALU.subtract, kk, [tk[ar_]])
                self.tt("dve", tmv[:, :, 0:n], Ar[:, :, 0:n], pib, ALU.mult, kk, [tmpk])
                self.tt("dve", Ai[:, :, n:2 * n], Ai[:, :, 0:n], prb, ALU.mult, kk, [tk[ai_]])
                self.tt("dve", Ai[:, :, n:2 * n], Ai[:, :, n:2 * n], tmv[:, :, 0:n], ALU.add, kk, [tk[ai_]])
                csq(pr, pi)
        self.cp("dve", self.el[:, 0, :], v(10), R, ["el"]); self.cp("dve", self.el[:, 1, :], v(11), R, ["el"])
        for i in range(4):
            self.dma("sp", self.s5tab[:, :, i, :].rearrange("j p t -> p j t"), tv[i], [tk[i]], ["s5tab"])
        for t in tabs:
            self.unscr(t, 8)
        self.unscr((tmp, tmpk), 8); self.unscr((sm, smk))

    def s5(self, l, ti):
        import os
        if os.environ.get('S5SKIP'):
            return
        hf = lambda kc: self.hT[:, kc, :]
        slu, sluk = self.win(1024)
        (usb, usbk), (ubf, ubfk), (tmp, tmpk), (yv, yvk), (tc, tck) = [self.scr() for _ in range(5)]
        ubv = ubf[:, 0:256].bitcast(BF16)
        (yg, ygk), (sre, srek), (nsi, nsik) = [self.scr(4) for _ in range(3)]
        ygv = yg[:, :].bitcast(BF16).rearrange("p (c t) -> p c t", c=4)
        srv = sre[:, :].bitcast(BF16).rearrange("p (c t) -> p c t", c=4)
        nsv = nsi[:, :].bitcast(BF16).rearrange("p (c t) -> p c t", c=4)
        Z = [self.scr(8) for _ in range(4)]
        zr, zi, zor, zoi = [z[0][:, :].rearrange("p (j t) -> p j t", j=4) for z in Z]
        zrk, zik, zork, zoik = [z[1] for z in Z]
        (bs, bsk), (cs, csk) = self.scr(), self.scr()
        bsv = bs[:, :].bitcast(BF16).rearrange("p (r j c) -> p r j c", r=2, j=4)
        csv = cs[:, :].bitcast(BF16).rearrange("p (r j c) -> p r j c", r=2, j=4)
        tabs = [self.scr() for _ in range(4)]
        q4 = lambda ap: ap.rearrange("p (q t) -> p q t", q=4)
        for c in range(4):
            b, bk = self.bank()
            self.proj(b[:, :], slu, c * 128, 128, 8, hf, [sluk, "hT"], [bk])
            self.cp("act", usb[:, 0:512], b[:, :], [bk], [usbk])
            self.cp("dve", ubv, usb[:, 0:512], [usbk], [ubfk])
            self.unbank((b, bk))
            for r in range(2):
                self.dma("sp", bsv[:, r], self.wb["bst"][l][r * 16 + 4 * c:r * 16 + 4 * c + 4].rearrange("a r c -> r a c"), [self.cur_wkey], [bsk])
                self.dma("sp", csv[:, r], self.wb["cst"][l][r * 16 + 4 * c:r * 16 + 4 * c + 4].rearrange("a r c -> r a c"), [self.cur_wkey], [csk])
            for jj in range(4):
                j = 4 * c + jj
                tb, tbk = tabs[jj]
                tbv = tb[:, 0:512].rearrange("p (i t) -> p i t", i=4)
                self.dma("sp", tbv, self.s5tab[j], ["s5tab"], [tbk])
                Fr = tbv[:, 0:1, :].broadcast_to([128, 4, 128]); Fi = tbv[:, 1:2, :].broadcast_to([128, 4, 128])
                bre, brek = self.bank(); bim, bimk = self.bank()
                self.mm(bre[:, :], bsv[:, 0, jj, :], ubv, True, True, [bsk, ubfk], [brek])
                self.mm(bim[:, :], bsv[:, 1, jj, :], ubv, True, True, [bsk, ubfk], [bimk])
                for q in range(4):
                    sl = slice(q * 128, (q + 1) * 128)
                    self.tt("dve", zr[:, jj, sl], bre[:, sl], tbv[:, 0, :], ALU.mult, [brek, tbk], [zrk])
                    self.tt("dve", tmp[:, sl], bim[:, sl], tbv[:, 1, :], ALU.mult, [bimk, tbk], [tmpk])
                    self.tt("dve", zi[:, jj, sl], bre[:, sl], tbv[:, 1, :], ALU.mult, [brek, tbk], [zik])
                    self.tt("dve", tc[:, 16:144], bim[:, sl], tbv[:, 0, :], ALU.mult, [bimk, tbk], [tck + "#x"])
                    self.tt("dve", zi[:, jj, sl], zi[:, jj, sl], tc[:, 16:144], ALU.add, [zik, tck + "#x"], [zik])
                self.tt("dve", zr[:, jj, :], zr[:, jj, :], tmp[:, 0:512], ALU.subtract, [zrk, tmpk], [zrk])
                self.unbank((bre, brek)); self.unbank((bim, bimk))
            for q in range(4):
                for jj in range(4):
                    j = 4 * c + jj
                    zk = "zc%d" % j
                    sl = slice(q * 128, (q + 1) * 128)
                    e = q * 128 + 127
                    self.scan(zor[:, jj, sl], self.onesf[:, 0:128], zr[:, jj, sl], self.zc[:, 0, j:j + 1], ["ones", zrk, zk], [zork + "#%d" % jj])
                    self.scan(zoi[:, jj, sl], self.onesf[:, 0:128], zi[:, jj, sl], self.zc[:, 1, j:j + 1], ["ones", zik, zk], [zoik + "#%d" % jj])
                    tcr = [zork + "#%d" % jj, zoik + "#%d" % jj, "el", tck + "#%d" % jj]
                    self.tt("dve", tc[:, 2 * jj:2 * jj + 1], zoi[:, jj, e:e + 1], self.el[:, 1, j:j + 1], ALU.mult, tcr, [tck + "#%d" % jj])
                    self.tt("dve", tc[:, 2 * jj + 1:2 * jj + 2], zor[:, jj, e:e + 1], self.el[:, 1, j:j + 1], ALU.mult, tcr, [tck + "#%d" % jj])
                    self.stt("dve", self.zc[:, 0, j:j + 1], zor[:, jj, e:e + 1], self.el[:, 0, j:j + 1], tc[:, 2 * jj:2 * jj + 1], ALU.mult, ALU.subtract, tcr, [zk])
                    self.stt("dve", self.zc[:, 1, j:j + 1], zoi[:, jj, e:e + 1], self.el[:, 0, j:j + 1], tc[:, 2 * jj + 1:2 * jj + 2], ALU.mult, ALU.add, tcr, [zk])
            yb, ybk = self.bank()
            for jj in range(4):
                tb, tbk = tabs[jj]
                tbv = tb[:, 0:512].rearrange("p (i t) -> p i t", i=4)
                Rr = tbv[:, 2:3, :].broadcast_to([128, 4, 128]); Ri = tbv[:, 3:4, :].broadcast_to([128, 4, 128])
                kr_ = [zork + "#%d" % jj, zoik + "#%d" % jj, tbk]
                p1, p1k = self.scr(); p2, p2k = self.scr()
                for q in range(4):
                    sl = slice(q * 128, (q + 1) * 128)
                    self.tt("dve", p1[:, sl], zor[:, jj, sl], tbv[:, 2, :], ALU.mult, kr_, [p1k])
                    self.tt("dve", p2[:, sl], zoi[:, jj, sl], tbv[:, 3, :], ALU.mult, kr_, [p2k])
                self.tt("dve", srv[:, jj, :], p1[:, 0:512], p2[:, 0:512], ALU.subtract, [p1k, p2k], [srek])
                for q in range(4):
                    sl = slice(q * 128, (q + 1) * 128)
                    self.tt("dve", p1[:, sl], zor[:, jj, sl], tbv[:, 3, :], ALU.mult, kr_ + [p1k], [p1k])
                    self.tt("dve", p2[:, sl], zoi[:, jj, sl], tbv[:, 2, :], ALU.mult, kr_ + [p2k], [p2k])
                self.tt("dve", p1[:, 0:512], p1[:, 0:512], p2[:, 0:512], ALU.add, [p1k, p2k], [p1k])
                self.ts("dve", nsv[:, jj, :], p1[:, 0:512], -1.0, None, ALU.mult, None, [p1k], [nsik])
                self.unscr((p1, p1k)); self.unscr((p2, p2k))
                self.mm(yb[:, :], csv[:, 0, jj, :], srv[:, jj, :], jj == 0, False, [csk, srek], [ybk])
                self.mm(yb[:, :], csv[:, 1, jj, :], nsv[:, jj, :], False, jj == 3, [csk, nsik], [ybk])
            self.stt("dve", yv[:, 0:512], usb[:, 0:512], self.pvc("s5_d", c), yb[:, :], ALU.mult, ALU.add, [usbk, "pv", ybk], [yvk])
            self.unbank((yb, ybk))
            self.gelu(yv[:, 0:512], yvk, ygv[:, c, :], ygk, tmp[:, 0:512], tmpk)
        wg, wgk = self.wload(self.wb["s5_w_glu"][l].rearrange("(k p) c -> p k c", p=128), [128, 4, 1024])
        for oc in range(4):
            za, zak = self.bank(); zb, zbk = self.bank()
            self.proj(za[:, :], wg, oc * 128, 128, 4, lambda kc: ygv[:, kc, :], [wgk, ygk], [zak])
            self.proj(zb[:, :], wg, 512 + oc * 128, 128, 4, lambda kc: ygv[:, kc, :], [wgk, ygk], [zbk])
            self.act(tmp[:, 0:512], zb[:, :], AF.Sigmoid, [zbk], [tmpk])
            self.tt("dve", self.ys[1][:, oc, :], za[:, :], tmp[:, 0:512], ALU.mult, [zak, tmpk], ["arB"])
            self.unbank((za, zak)); self.unbank((zb, zbk))
        for s_ in [(usb, usbk), (ubf, ubfk), (tmp, tmpk), (yv, yvk), (tc, tck), (bs, bsk), (cs, csk)] + tabs:
            self.unscr(s_)
        for s_ in [(yg, ygk), (sre, srek), (nsi, nsik)]:
            self.unscr(s_, 4)
        for z in Z:
            self.unscr(z, 8)

    def proj_shift(self, slab, slk, c0, m, mu_ap, ccol):
        hf = lambda kc: self.hT[:, kc, :]
        b, bk = self.bank()
        self.proj(b[0:m, :], slab, c0, m, 8, hf, [slk, "hT"], [bk])
        R, Rk = self.scr(4)
        self.cp("act", R[0:m, 1:513], b[0:m, :], [bk], [Rk])
        self.unbank((b, bk))
        ck = "carry%d" % ccol
        self.cp("dve", R[0:m, 0:1], self.carry[0:m, ccol:ccol + 1], [ck], [Rk])
        d, dk = self.scr()
        o, ok_ = self.scr()
        self.tt("dve", d[0:m, 0:512], R[0:m, 0:512], R[0:m, 1:513], ALU.subtract, [Rk], [dk])
        self.stt("dve", o[0:m, 0:512], d[0:m, 0:512], mu_ap, R[0:m, 1:513], ALU.mult, ALU.add, [dk, "pv", Rk], [ok_])
        self.cp("dve", self.carry[0:m, ccol:ccol + 1], R[0:m, 512:513], [Rk], [ck])
        self.unscr((R, Rk), 4); self.unscr((d, dk))
        return o, ok_

    def rwkv(self, l, ti):
        import os
        self.rstage = int(os.environ.get('RSTAGE', '99'))
        slA, slAk = self.win(3072)
        bfv = lambda t, n=256: t[:, 0:n].bitcast(BF16)
        wl, wlk = self.proj_shift(slA, slAk, 0, 64, self.pvc("mu_w", 0, 64), 12)
        twl, twlk = self.scr()
        self.act(bfv(twl)[0:64], wl[0:64, 0:512], AF.Tanh, [wlk], [twlk]); self.unscr((wl, wlk))
        al, alk = self.proj_shift(slA, slAk, 64, 64, self.pvc("mu_a", 0, 64), 13)
        alb, albk = self.scr()
        self.cp("dve", bfv(alb)[0:64], al[0:64, 0:512], [alk], [albk]); self.unscr((al, alk))
        gl, glk = self.proj_shift(slA, slAk, 128, 128, self.pvc("mu_g"), 14)
        sgl, sglk = self.scr()
        self.act(bfv(sgl), gl[:, 0:512], AF.Sigmoid, [glk], [sglk]); self.unscr((gl, glk))
        slr, slrk = self.win(1536); slk_, slkk = self.win(2048); slv, slvk = self.win(2560)
        m3 = lambda ap: ap.rearrange("p (q t) -> p q t", q=8)
        for cc in range(4):
            rm, rmk = self.proj_shift(slr, slrk, cc * 128, 128, self.pvc("mu_rkv", cc), cc)
            km, kmk = self.proj_shift(slk_, slkk, cc * 128, 128, self.pvc("mu_rkv", 4 + cc), 4 + cc)
            vm, vmk = self.proj_shift(slv, slvk, cc * 128, 128, self.pvc("mu_rkv", 8 + cc), 8 + cc)
            ew, ewk = self.scr(); cs, csk = self.scr(); t1, t1k = self.scr()
            b, bk = self.bank()
            self.mm(b[:, :], self.w2_sb[:, cc * 128:(cc + 1) * 128], bfv(twl)[0:64], True, True, ["w2_sb", twlk], [bk])
            self.act(ew[:, 0:512], b[:, :], AF.Exp, [bk, "der"], [ewk], scale=-1.0, bias=self.der[:, 4 + cc:5 + cc])
            self.unbank((b, bk))
            self.act(ew[:, 0:512], ew[:, 0:512], AF.Ln, [ewk], [ewk], bias=1.0)
            self.ts("dve", ew[:, 0:512], ew[:, 0:512], -1.0, -0.5, ALU.mult, ALU.add, [ewk], [ewk])
            self.act(ew[:, 0:512], ew[:, 0:512], AF.Exp, [ewk], [ewk])
            self.scan(cs[:, 0:512], self.reset64[:, :], ew[:, 0:512], 0.0, ["reset64", ewk], [csk])
            E1, E1k = self.scr(); E2, E2k = self.scr(); E3, E3k = self.scr(); Ex, Exk = self.scr()
            self.act(E1[:, 0:512], cs[:, 0:512], AF.Exp, [csk], [E1k], scale=-1.0)
            self.act(E2[:, 0:512], cs[:, 0:512], AF.Exp, [csk], [E2k])
            self.tt("dve", t1[:, 0:512], cs[:, 0:512], ew[:, 0:512], ALU.subtract, [csk, ewk], [t1k])
            self.act(Ex[:, 0:512], t1[:, 0:512], AF.Exp, [t1k], [Exk], scale=-1.0)
            self.tt("dve", m3(t1[:, 0:512]), m3(cs[:, 0:512])[:, :, 63:64].broadcast_to([128, 8, 64]), m3(cs[:, 0:512]), ALU.subtract, [csk], [t1k])
            self.act(E3[:, 0:512], t1[:, 0:512], AF.Exp, [t1k], [E3k], scale=-1.0)
            ag, agk = self.scr()
            b, bk = self.bank()
            self.mm(b[:, :], self.a2_sb[:, cc * 128:(cc + 1) * 128], bfv(alb)[0:64], True, True, ["a2_sb", albk], [bk])
            self.act(ag[:, 0:512], b[:, :], AF.Sigmoid, [bk, "pv"], [agk], bias=self.pvc("a0", cc))
            self.unbank((b, bk))
            gg, ggk = self.scr()
            b, bk = self.bank()
            self.mm(b[:, :], self.g2_sb[:, cc * 128:(cc + 1) * 128], bfv(sgl), True, True, ["g2_sb", sglk], [bk])
            self.cp("act", gg[:, 0:512], b[:, :], [bk], [ggk])
            self.unbank((b, bk))
            kk, kkk = self.scr()
            self.ts("dve", kk[:, 0:512], km[:, 0:512], self.pvc("k_k", cc), None, ALU.mult, None, [kmk, "pv"], [kkk])
            self.tt("dve", t1[:, 0:512], kk[:, 0:512], kk[:, 0:512], ALU.mult, [kkk], [t1k])
            b, bk = self.bank()
            self.mm(b[:, :], self.blk64[:, :], t1[:, 0:512], True, True, ["blk64", t1k], [bk])
            self.act(t1[:, 0:512], b[:, :], AF.Sqrt, [bk], [t1k], bias=1e-12)
            self.unbank((b, bk))
            self.recip(t1[:, 0:512], t1[:, 0:512], [t1k], [t1k])
            self.tt("dve", kk[:, 0:512], kk[:, 0:512], t1[:, 0:512], ALU.mult, [kkk, t1k], [kkk])
            self.ts("dve", t1[:, 0:512], ag[:, 0:512], self.pvc("k_a", cc), self.der[:, 8 + cc:9 + cc], ALU.mult, ALU.add, [agk, "pv", "der"], [t1k])
            self.tt("dve", km[:, 0:512], km[:, 0:512], t1[:, 0:512], ALU.mult, [kmk, t1k], [kmk])
            self.tt("dve", ag[:, 0:512], ag[:, 0:512], kk[:, 0:512], ALU.mult, [agk, kkk], [agk])
            AR, ARk = self.scr(); BT, BTk = self.scr(); KT, KTk = self.scr(); BH, BHk = self.scr(); KH, KHk = self.scr(); RK, RKk = self.scr()
            ARv = AR[:, :].bitcast(BF16).rearrange("p (q a t) -> p q a t", q=8, a=2)
            self.stt("dve", ARv[:, :, 0, :], m3(kk[:, 0:512]), -1.0, m3(Ex[:, 0:512]), ALU.mult, ALU.mult, [kkk, Exk], [ARk])
            self.tt("dve", ARv[:, :, 1, :], m3(rm[:, 0:512]), m3(E1[:, 0:512]), ALU.mult, [rmk, E1k], [ARk])
            self.tt("dve", bfv(BT), ag[:, 0:512], E2[:, 0:512], ALU.mult, [agk, E2k], [BTk])
            self.tt("dve", bfv(KT), km[:, 0:512], E2[:, 0:512], ALU.mult, [kmk, E2k], [KTk])
            self.tt("dve", bfv(BH), ag[:, 0:512], E3[:, 0:512], ALU.mult, [agk, E3k], [BHk])
            self.tt("dve", bfv(KH), km[:, 0:512], E3[:, 0:512], ALU.mult, [kmk, E3k], [KHk])
            self.stt("dve", bfv(RK), rm[:, 0:512], self.pvc("r_k", cc), km[:, 0:512], ALU.mult, ALU.mult, [rmk, "pv", kmk], [RKk])
            for s_ in [(rm, rmk), (km, kmk), (ew, ewk), (cs, csk), (t1, t1k), (E2, E2k), (E3, E3k), (Ex, Exk), (ag, agk), (kk, kkk)]:
                self.unscr(s_)
            if self.rstage <= 1:
                self.memset("pool", self.ys[2], 0.0, ["arB"]); return
            BHt, BHtk = self.scr(); KHt, KHtk = self.scr(); Vt, Vtk = self.scr(4); Vtb, Vtbk = self.scr()
            tm = lambda t: t[0:64, :].bitcast(BF16).rearrange("p (q c) -> p q c", q=8)
            for (src, srck, dst, dstk) in [(BH, BHk, BHt, BHtk), (KH, KHk, KHt, KHtk)]:
                b, bk = self.bank()
                bb = b[:, :].bitcast(BF16)
                for q in range(8):
                    self.trp(bb[0:64, q * 128:(q + 1) * 128], bfv(src)[:, q * 64:(q + 1) * 64], self.identb[:], [srck, "identb"], [bk])
                self.cp("act", tm(dst), bb[0:64, 0:1024].rearrange("p (q c) -> p q c", q=8), [bk], [dstk])
                self.unbank((b, bk))
            Vtv = Vt[0:64, :].rearrange("p (q c) -> p q c", q=8)
            for hf_ in range(2):
                b, bk = self.bank()
                for q4_ in range(4):
                    q = hf_ * 4 + q4_
                    self.trp(b[0:64, q4_ * 128:(q4_ + 1) * 128], vm[:, q * 64:(q + 1) * 64], self.identf[:], [vmk, "ident"], [bk])
                self.cp("act", Vtv[:, hf_ * 4:hf_ * 4 + 4, :], b[0:64, :].rearrange("p (q c) -> p q c", q=4), [bk], [Vtk])
                self.unbank((b, bk))
            self.cp("dve", tm(Vtb), Vtv, [Vtk], [Vtbk])
            self.unscr((vm, vmk))
            if self.rstage <= 2:
                self.memset("pool", self.ys[2], 0.0, ["arB"]); return
            mk = lambda i: self.rmask[:, i:i + 1, :].broadcast_to([64, 8, 64])
            NP = [self.scr() for _ in range(6)]; XP = [self.scr() for _ in range(5)]
            ARB, ARBk = self.scr(); AAK, AAKk = self.scr(); ARK, ARKk = self.scr()
            mt = lambda t: t[0:64, :].bitcast(BF16).rearrange("p (m t) -> p m t", m=16)
            for hf_ in range(2):
                BN = [self.bank(), self.bank()]; BK = [self.bank(), self.bank()]; BX = [self.bank(), self.bank()]
                for q4_ in range(4):
                    q = hf_ * 4 + q4_
                    for e in range(2):
                        rows = slice(e * 64, (e + 1) * 64)
                        arhs = ARv[rows, q, :, :]
                        co = q4_ * 128
                        self.mm(BN[e][0][0:64, co:co + 128], bfv(BT)[rows, q * 64:(q + 1) * 64], arhs, True, True, [BTk, ARk], [BN[e][1]])
                        self.mm(BK[e][0][0:64, co:co + 128], bfv(KT)[rows, q * 64:(q + 1) * 64], arhs, True, True, [KTk, ARk], [BK[e][1]])
                        self.mm(BX[e][0][0:64, q4_ * 64:(q4_ + 1) * 64], ARv[rows, q, 0, :], bfv(BT)[rows, q * 64:(q + 1) * 64], True, True, [ARk, BTk], [BX[e][1]])
                v4 = lambda bnk: bnk[0:64, :].rearrange("p (m a t) -> p m a t", m=4, a=2)
                mk4 = lambda i: self.rmask[:, i:i + 1, :].broadcast_to([64, 4, 64])
                for e in range(2):
                    ms = slice(hf_ * 8 + e, hf_ * 8 + 8, 2)
                    self.tt("dve", mt(NP[0][0])[:, ms, :], v4(BN[e][0])[:, :, 0, :], mk4(0), ALU.mult, [BN[e][1], "rmask"], [NP[0][1]])
                    self.tt("dve", mt(ARB)[:, ms, :], v4(BN[e][0])[:, :, 1, :], mk4(1), ALU.mult, [BN[e][1], "rmask"], [ARBk])
                    self.tt("dve", mt(AAK)[:, ms, :], v4(BK[e][0])[:, :, 0, :], mk4(0), ALU.mult, [BK[e][1], "rmask"], [AAKk])
                    self.tt("dve", mt(ARK)[:, ms, :], v4(BK[e][0])[:, :, 1, :], mk4(1), ALU.mult, [BK[e][1], "rmask"], [ARKk])
                    self.tt("dve", mt(XP[0][0])[:, ms, :], BX[e][0][0:64, 0:256].rearrange("p (m t) -> p m t", m=4), mk4(2), ALU.mult, [BX[e][1], "rmask"], [XP[0][1]])
                for bb_ in BN + BK + BX:
                    self.unbank(bb_)
            if self.rstage <= 3:
                self.memset("pool", self.ys[2], 0.0, ["arB"]); return
            for k in range(5):
                for hf_ in range(2):
                    bn_, bnk_ = self.bank(); bx_, bxk_ = self.bank()
                    for m8 in range(8):
                        m = hf_ * 8 + m8
                        self.mm(bn_[0:64, m8 * 64:(m8 + 1) * 64], mt(XP[k][0])[:, m, :], mt(NP[k][0])[:, m, :], True, True, [XP[k][1], NP[k][1]], [bnk_])
                        if k < 4:
                            self.mm(bx_[0:64, m8 * 64:(m8 + 1) * 64], mt(NP[k][0])[:, m, :], mt(XP[k][0])[:, m, :], True, True, [XP[k][1], NP[k][1]], [bxk_])
                    self.cp("act", mt(NP[k + 1][0])[:, hf_ * 8:hf_ * 8 + 8, :], bn_[0:64, :].rearrange("p (m t) -> p m t", m=8), [bnk_], [NP[k + 1][1]])
                    if k < 4:
                        self.cp("act", mt(XP[k + 1][0])[:, hf_ * 8:hf_ * 8 + 8, :], bx_[0:64, :].rearrange("p (m t) -> p m t", m=8), [bxk_], [XP[k + 1][1]])
                    self.unbank((bn_, bnk_)); self.unbank((bx_, bxk_))
            if self.rstage <= 4:
                self.memset("pool", self.ys[2], 0.0, ["arB"]); return
            ytm, ytmk = self.scr(4)
            ytv = ytm[0:64, :].rearrange("p (q c) -> p q c", q=8)
            U = [self.scr(), self.scr()]
            ub = lambda i: U[i][0][0:64, 0:64].bitcast(BF16)
            s0k = "s0_%d" % cc
            for q in range(8):
                b, bk = self.bank()
                self.mm(b[0:64, 0:128], ARv[:, q, 0, :], self.s0bd[:, cc, :], True, False, [ARk, s0k], [bk])
                for e in range(2):
                    self.mm(b[0:64, e * 64:(e + 1) * 64], mt(AAK)[:, 2 * q + e, :], tm(Vtb)[:, q, e * 64:(e + 1) * 64], False, e == 1, [AAKk, Vtbk], [bk])
                self.cp("dve", ub(0), b[0:64, 0:128], [bk], [U[0][1]])
                self.unbank((b, bk))
                cur = 0
                for k in range(6):
                    b, bk = self.bank()
                    for e in range(2):
                        self.mm(b[0:64, e * 64:(e + 1) * 64], mt(NP[k][0])[:, 2 * q + e, :], ub(cur)[:, e * 64:(e + 1) * 64], True, True, [NP[k][1], U[cur][1]], [bk])
                    self.tt("dve", ub(1 - cur), b[0:64, 0:128], ub(cur), ALU.add, [bk, U[cur][1]], [U[1 - cur][1]])
                    self.unbank((b, bk))
                    cur = 1 - cur
                uf, ufk = ub(cur), U[cur][1]
                b, bk = self.bank()
                self.mm(b[0:64, 0:128], ARv[:, q, 1, :], self.s0bd[:, cc, :], True, False, [ARk, s0k], [bk])
                for e in range(2):
                    self.mm(b[0:64, e * 64:(e + 1) * 64], mt(ARB)[:, 2 * q + e, :], uf[:, e * 64:(e + 1) * 64], False, False, [ARBk, ufk], [bk])
                    self.mm(b[0:64, e * 64:(e + 1) * 64], mt(ARK)[:, 2 * q + e, :], tm(Vtb)[:, q, e * 64:(e + 1) * 64], False, e == 1, [ARKk, Vtbk], [bk])
                self.cp("act", ytv[:, q, :], b[0:64, 0:128], [bk], [ytmk])
                self.unbank((b, bk))
                b, bk = self.bank()
                self.mm(b[:, 0:128], tm(BHt)[:, q, :], uf, True, False, [BHtk, ufk], [bk])
                self.mm(b[:, 0:128], tm(KHt)[:, q, :], tm(Vtb)[:, q, :], False, True, [KHtk, Vtbk], [bk])
                for e in range(2):
                    rows = slice(e * 64, (e + 1) * 64)
                    self.stt("dve", self.s0f[rows, cc, :], self.s0f[rows, cc, :], E1[rows, q * 64 + 63:q * 64 + 64], b[rows, e * 64:(e + 1) * 64], ALU.mult, ALU.add, [s0k + "f", E1k, bk], [s0k + "f"])
                    self.cp("dve", self.s0bd[rows, cc, e * 64:(e + 1) * 64], self.s0f[rows, cc, :], [s0k + "f"], [s0k])
                self.unbank((b, bk))
            if self.rstage <= 5:
                self.memset("pool", self.ys[2], 0.0, ["arB"]); return
            g16 = lambda ap: ap.rearrange("p (g v) -> p g v", g=16)
            yv = g16(ytm[0:64, :])
            st_, stk = self.scr(); sq, sqk = self.scr(4)
            self.P.op("dve", lambda e, o=st_[0:64, 0:16], i=yv: e.tensor_reduce(out=o, in_=i, axis=AX.X, op=ALU.add), [ytmk], [stk])
            self.tt("dve", sq[0:64, :], ytm[0:64, :], ytm[0:64, :], ALU.mult, [ytmk], [sqk])
            self.P.op("dve", lambda e, o=st_[0:64, 16:32], i=g16(sq[0:64, :]): e.tensor_reduce(out=o, in_=i, axis=AX.X, op=ALU.add), [sqk], [stk])
            self.ts("dve", st_[0:64, 0:32], st_[0:64, 0:32], 1.0 / 64, None, ALU.mult, None, [stk], [stk])
            self.tt("dve", st_[0:64, 32:48], st_[0:64, 0:16], st_[0:64, 0:16], ALU.mult, [stk], [stk])
            self.tt("dve", st_[0:64, 16:32], st_[0:64, 16:32], st_[0:64, 32:48], ALU.subtract, [stk], [stk])
            self.act(st_[0:64, 16:32], st_[0:64, 16:32], AF.Sqrt, [stk], [stk], bias=64e-5)
            self.recip(st_[0:64, 16:32], st_[0:64, 16:32], [stk], [stk])
            bc = lambda ap: ap.unsqueeze(2).broadcast_to([64, 16, 64])
            self.tt("dve", yv, yv, bc(st_[0:64, 0:16]), ALU.subtract, [ytmk, stk], [ytmk])
            self.tt("dve", yv, yv, bc(st_[0:64, 16:32]), ALU.mult, [ytmk, stk], [ytmk])
            lg = self.lnx[:, cc * 128:(cc + 1) * 128].unsqueeze(1).broadcast_to([64, 8, 128])
            lb = self.lnx[:, 512 + cc * 128:512 + (cc + 1) * 128].unsqueeze(1).broadcast_to([64, 8, 128])
            self.tt("dve", ytv, ytv, lg, ALU.mult, [ytmk, "lnx"], [ytmk])
            self.tt("dve", ytv, ytv, lb, ALU.add, [ytmk, "lnx"], [ytmk])
            b, bk = self.bank()
            for q in range(8):
                self.mm(b[0:64, q * 2:q * 2 + 2], bfv(RK)[:, q * 64:(q + 1) * 64], self.headselb[:, :], True, True, [RKk, "headselb"], [bk])
            self.cp("act", st_[0:64, 0:16], b[0:64, 0:16], [bk], [stk])
            self.unbank((b, bk))
            self.tt("dve", g16(sq[0:64, :]), g16(Vt[0:64, :]), bc(st_[0:64, 0:16]), ALU.mult, [Vtk, stk], [sqk])
            self.tt("dve", ytm[0:64, :], ytm[0:64, :], sq[0:64, :], ALU.add, [ytmk, sqk], [ytmk])
            yb_, ybk_ = self.scr()
            self.cp("dve", tm(yb_), ytv, [ytmk], [ybk_])
            b, bk = self.bank()
            bb = b[:, :].bitcast(BF16)
            for q in range(8):
                self.trp(bb[:, q * 64:(q + 1) * 64], tm(yb_)[:, q, :], self.identb[0:64, 0:64], [ybk_, "identb"], [bk])
            self.tt("dve", self.ys[2][:, cc, :], bb[:, 0:512], gg[:, 0:512], ALU.mult, [bk, ggk], ["arB"])
            self.unbank((b, bk))
            for s_ in [(E1, E1k), (gg, ggk),
                       (AR, ARk), (BT, BTk), (KT, KTk), (BH, BHk), (KH, KHk), (RK, RKk), (BHt, BHtk), (KHt, KHtk), (Vtb, Vtbk), (ARB, ARBk), (AAK, AAKk), (ARK, ARKk),
                       (st_, stk), (yb_, ybk_)] + NP + XP + U:
                self.unscr(s_)
            for s_ in [(Vt, Vtk), (ytm, ytmk), (sq, sqk)]:
                self.unscr(s_, 4)
        for s_ in [(twl, twlk), (alb, albk), (sgl, sglk)]:
            self.unscr(s_)

    def rstd_bcast(self, dst, dstk, srcs, nfeat, nparts, ones_lhsT):
        sq, sqk = self.scr()
        b, bk = self.bank()
        for i, (ap, k) in enumerate(srcs):
            self.act(sq[0:ap.shape[0], 0:512], ap, AF.Square, [k], [sqk])
            self.mm(b[0:nparts, :], ones_lhsT(ap.shape[0]), sq[0:ap.shape[0], 0:512], i == 0, i == len(srcs) - 1, [sqk, "ones"], [bk])
        self.act(dst, b[0:nparts, :], AF.Sqrt, [bk], [dstk], scale=1.0 / nfeat, bias=EPS)
        self.recip(dst, dst, [dstk], [dstk])
        self.unbank((b, bk))
        self.unscr((sq, sqk))

    def qk_finish(self, src, srck, gname, out, outk, t0):
        rs, rsk = self.scr()
        self.rstd_bcast(rs[0:96, 0:512], rsk, [(src, srck)], 96, 96, lambda n: self.onesf[0:96, 0:96])
        self.stt("dve", src, src, self.pvc(gname, 0, 96), rs[0:96, 0:512], ALU.mult, ALU.mult, [srck, "pv", rsk], [srck])
        b, bk = self.bank()
        self.mm(b[0:96, :], self.prot[:, :], src, True, True, ["prot", srck], [bk])
        self.tt("dve", rs[0:96, 0:512], b[0:96, :], self.rsin[0:96, 0:512], ALU.mult, [bk, self.rsink], [rsk])
        self.unbank((b, bk))
        self.tt("pool", src, src, self.rcos[0:96, 0:512], ALU.mult, [srck, self.rcosk], [srck])
        self.tt("dve", out, src, rs[0:96, 0:512], ALU.add, [srck, rsk], [outk])
        self.unscr((rs, rsk))

    def mla(self, l, ti):
        t0 = ti * TT
        hf = lambda kc: self.hT[:, kc, :]
        slA, slAk = self.win(3072)
        slB, slBk = self.win(3584, 160)
        (self.rcos, self.rcosk), (self.rsin, self.rsink) = self.scr(), self.scr()
        self.dma("sp", self.rcos[0:96, 0:512], self.cd["ropec"][:, t0:t0 + 512], [], [self.rcosk])
        self.dma("sp", self.rsin[0:96, 0:512], self.cd["ropes"][:, t0:t0 + 512], [], [self.rsink])
        cq = [self.scr() for _ in range(2)]
        for c in range(2):
            b, bk = self.bank()
            self.proj(b[:, :], slA, 256 + c * 128, 128, 8, hf, [slAk, "hT"], [bk])
            self.cp("act", cq[c][0][:, 0:512], b[:, :], [bk], [cq[c][1]])
            self.unbank((b, bk))
        rs, rsk = self.scr()
        self.rstd_bcast(rs[:, 0:512], rsk, [(cq[0][0][:, 0:512], cq[0][1]), (cq[1][0][:, 0:512], cq[1][1])], 256, 128, lambda n: self.onesf[:, :])
        cqn, cqnk = self.scr()
        cqnv = cqn[:, 0:512].bitcast(BF16).rearrange("p (c t) -> p c t", c=2)
        for c in range(2):
            self.stt("dve", cqnv[:, c, :], cq[c][0][:, 0:512], self.pvc("q_norm", c), rs[:, 0:512], ALU.mult, ALU.mult, [cq[c][1], "pv", rsk], [cqnk])
        ckv, ckvk = cq[0]
        b, bk = self.bank()
        self.proj(b[:, :], slB, 0, 128, 8, hf, [slBk, "hT"], [bk])
        self.cp("act", ckv[:, 0:512], b[:, :], [bk], [ckvk])
        self.unbank((b, bk))
        self.rstd_bcast(rs[:, 0:512], rsk, [(ckv[:, 0:512], ckvk)], 128, 128, lambda n: self.onesf[:, :])
        ckvn, ckvnk = self.scr()
        ckvnv = ckvn[:, 0:256].bitcast(BF16)
        self.stt("dve", ckvnv, ckv[:, 0:512], self.pvc("kv_norm"), rs[:, 0:512], ALU.mult, ALU.mult, [ckvk, "pv", rsk], [ckvnk])
        kr, krk = cq[1]
        b, bk = self.bank()
        self.proj(b[0:32, :], slB, 128, 32, 8, hf, [slBk, "hT"], [bk])
        self.cp("act", kr[64:96, 0:512], b[0:32, :], [bk], [krk])
        self.unbank((b, bk))
        self.unscr((rs, rsk))
        vt, vtk = self.scr(8)
        vtv = vt[:, 0:1040].bitcast(BF16).rearrange("p (s h e) -> p s h e", s=4, h=8)
        self.memset("pool", vtv[:, :, :, 64:65], 1.0, [vtk])
        wv = self.wukv_sb[:, :].rearrange("p (h e) -> p h e", h=8)[:, :, 64:128]
        for s in range(4):
            b, bk = self.bank()
            self.mm(b[:, :].rearrange("p (h e) -> p h e", h=8), ckvnv[:, s * 128:(s + 1) * 128], wv, True, True, [ckvnk, "wukv_sb"], [bk])
            self.cp("act" if s % 2 else "dve", vtv[:, s, :, 0:64], b[:, :].rearrange("p (h e) -> p h e", h=8), [bk], [vtk])
            self.unbank((b, bk))
        for h in range(8):
            self.dma("pool", self.vc[h, :, 4 * ti:4 * ti + 4, :], vtv[:, :, h, :], [vtk], ["vc"])
        kt, ktk = self.scr()
        kb_, kbk = self.scr()
        kbv = kb_[0:96, 0:256].bitcast(BF16)
        for h in range(8):
            b, bk = self.bank()
            self.mm(b[0:64, :], self.wukv_sb[:, h * 128:h * 128 + 64], ckvnv, True, True, ["wukv_sb", ckvnk], [bk])
            self.cp("act", kt[0:64, 0:512], b[0:64, :], [bk], [ktk])
            self.unbank((b, bk))
            self.cp("pool", kt[64:96, 0:512], kr[64:96, 0:512], [krk], [ktk])
            self.qk_finish(kt[0:96, 0:512], ktk, "qkn_k", kbv, kbk, t0)
            self.dma("pool", self.kc[h, :, t0:t0 + 512], kbv, [kbk], ["kc"])
        self.unscr((kb_, kbk))
        nkt = 4 * (ti + 1)
        qb, qbk = self.scr()
        qbv = qb[0:96, 0:256].bitcast(BF16)
        kbuf, kbufk = self.scr(8)
        kbv2 = kbuf[0:96, 0:2048].bitcast(BF16)
        vbuf, vbufk = self.scr(8)
        vbv = vbuf[:, 0:1040].bitcast(BF16).rearrange("p (k e) -> p k e", k=32)
        pts = [self.scr() for _ in range(3)]
        rl, rlk = self.scr()
        for h in range(8):
            b, bk = self.bank()
            for c in range(2):
                self.mm(b[0:96, :], self.wuq_sb[:, c, h * 96:(h + 1) * 96], cqnv[:, c, :], c == 0, c == 1, ["wuq_sb", cqnk], [bk])
            self.cp("act", kt[0:96, 0:512], b[0:96, :], [bk], [ktk])
            self.unbank((b, bk))
            self.qk_finish(kt[0:96, 0:512], ktk, "qkn_q", qbv, qbk, t0)
            self.dma("sp", kbv2[:, 0:nkt * 128], self.kc[h, :, 0:nkt * 128], ["kc"], [kbufk])
            self.dma("sp", vbv[:, 0:nkt, :], self.vc[h, :, 0:nkt, :], ["vc"], [vbufk])
            ob, obk = self.bank()
            for k in range(nkt):
                sb_, sbk = self.bank()
                self.mm(sb_[:, :], kbv2[:, k * 128:(k + 1) * 128], qbv, True, True, [kbufk, qbk], [sbk])
                pt, ptk = pts[k % 3]
                ptv = pt[:, 0:256].bitcast(BF16)
                self.act(ptv, sb_[:, :], AF.Exp, [sbk], [ptk], scale=96.0 ** -0.5)
                self.unbank((sb_, sbk))
                if k >= 4 * ti:
                    self.tt("pool", ptv, ptv, self.amask[:, k - 4 * ti, :], ALU.mult, [ptk, "amask"], [ptk])
                self.mm(ob[0:65, :], vbv[:, k, :], ptv, k == 0, k == nkt - 1, [vbufk, ptk], [obk])
            self.recip(rl[64:65, 0:512], ob[64:65, :], [obk], [rlk])
            bc, bck = self.bank()
            self.mm(bc[0:64, :], self.onesf[64:65, 0:64], rl[64:65, 0:512], True, True, ["ones", rlk], [bck])
            self.cp("act", rl[0:64, 0:512], bc[0:64, :], [bck], [rlk])
            self.unbank((bc, bck))
            self.tt("dve", self.ys[3][:, h, :], ob[0:64, :], rl[0:64, 0:512], ALU.mult, [obk, rlk], ["arB"])
            self.unbank((ob, obk))
        for s_ in pts + [(rl, rlk), (qb, qbk), (kt, ktk), (cqn, cqnk), (ckvn, ckvnk), cq[0], cq[1], (self.rcos, self.rcosk), (self.rsin, self.rsink)]:
            self.unscr(s_)
        self.unscr((kbuf, kbufk), 8); self.unscr((vbuf, vbufk), 8); self.unscr((vt, vtk), 8)


def _prep_inputs(inputs):
    pvec, fvec, lrug, bst, cst = host_layout(inputs)
    shared = {"pvec": pvec, "fvec": fvec,
              "lrug": lrug.reshape(DEPTH, -1, 1024), "bst": bst.reshape(DEPTH, -1, 1024), "cst": cst.reshape(DEPTH, -1, 1024)}
    for n in BIGW:
        shared[n] = np.ascontiguousarray(np.asarray(inputs[n], np.float32)).reshape(DEPTH, -1, 1024)
    for n, v in host_consts().items():
        shared["c_" + n] = v
    return shared


def run(inputs, L_RUN=DEPTH, T_RUN=T_FULL, n_cores=8, branches=(0, 1, 2, 3), dbg=False, dbg_tile=0, trace=False):
    inputs = {k: np.asarray(v) for k, v in inputs.items()}
    shared = _prep_inputs(inputs)
    nc = bass.Bass("TRN2", target_bir_lowering=False)
    kb = KB(nc, L_RUN, T_RUN, dbg=dbg)
    kb.dbg_tile = dbg_tile
    kb.build(branches=branches)
    in_maps = []
    for b in range(n_cores):
        m = dict(shared)
        m["x"] = np.ascontiguousarray(inputs["x"][b, :T_RUN].astype(np.float32))
        in_maps.append(m)
    res = run_bass_kernel_spmd(nc, in_maps, core_ids=list(range(n_cores)), trace=trace)
    return res


DEFAULT_BRANCHES = (0, 1, 2, 3)


def kernel(**inputs):
    res = run(inputs, branches=DEFAULT_BRANCHES)
    return np.stack([r["y"] for r in res.results], axis=0).astype(np.float32)
```

```python
import math
from contextlib import ExitStack
import numpy as np
import concourse.bass as bass
import concourse.mybir as mybir
from concourse.bass_utils import run_bass_kernel_spmd

F32 = mybir.dt.float32
BF16 = mybir.dt.bfloat16
AF = mybir.ActivationFunctionType
ALU = mybir.AluOpType
AX = mybir.AxisListType

D = 1024
T_FULL = 4096
DEPTH = 4
C = 512
D_IN = 7840
TT = 512
EPS = 1e-6
ENGS = ("pe", "act", "dve", "pool", "sp")
GELU_K = 1.5957691216057308


class Op:
    __slots__ = ("eng", "fn", "reads", "writes", "dma", "idx", "deps", "sig", "cnt", "sem", "semval")

    def __init__(self, eng, fn, reads, writes, dma):
        self.eng, self.fn, self.reads, self.writes, self.dma = eng, fn, reads, writes, dma
        self.deps = []
        self.sig = False
        self.cnt = 0
        self.sem = None
        self.semval = 0


class Prog:
    NDMA = 48

    def __init__(self, nc):
        self.nc = nc
        self.ops = []

    def op(self, eng, fn, reads=(), writes=(), dma=False):
        o = Op(eng, fn, tuple(reads), tuple(writes), dma)
        o.idx = len(self.ops)
        self.ops.append(o)
        return o

    def finalize(self):
        last_w, readers, children = {}, {}, {}
        dma_k = 0
        dma_last = [None] * self.NDMA
        alias = getattr(self, "alias", {})

        def expand(keys):
            out = []
            for k in keys:
                base, _, sub = k.partition("#")
                for s_ in alias.get(base, (base,)):
                    out.append((s_, sub))
            return out

        def related(s_, sub):
            if sub == "":
                return [(s_, "")] + [(s_, c) for c in children.get(s_, ())]
            return [(s_, sub), (s_, "")]

        for o in self.ops:
            deps = {}
            rd, wr = expand(o.reads), expand(o.writes)
            for (s_, sub) in rd + wr:
                if sub:
                    children.setdefault(s_, set()).add(sub)
            for (s_, sub) in rd:
                for kk in related(s_, sub):
                    w = last_w.get(kk)
                    if w is not None:
                        deps[w.idx] = (w, "raw")
            for (s_, sub) in wr:
                for kk in related(s_, sub):
                    w = last_w.get(kk)
                    if w is not None and w.idx not in deps:
                        deps[w.idx] = (w, "waw")
                    for r in readers.get(kk, ()):
                        if r.idx not in deps and r is not o:
                            deps[r.idx] = (r, "war")
            for kk in rd:
                readers.setdefault(kk, []).append(o)
            for (s_, sub) in wr:
                last_w[(s_, sub)] = o
                readers[(s_, sub)] = []
                if sub == "":
                    for c in children.get(s_, ()):
                        last_w[(s_, c)] = o
                        readers[(s_, c)] = []
            if o.dma:
                k = dma_k % self.NDMA
                dma_k += 1
                prev = dma_last[k]
                o.sem = k
                o.semval = (prev.semval if prev is not None else 0) + 16
                if prev is not None and prev.idx not in deps:
                    deps[prev.idx] = (prev, "raw")
                dma_last[k] = o
            for (p, kind) in deps.values():
                if p.dma:
                    o.deps.append(p)
                elif p.eng == o.eng and not o.dma:
                    if kind == "raw" and o.eng != "pe":
                        o.deps.append(p)
                        p.sig = True
                else:
                    o.deps.append(p)
                    p.sig = True
        cnt = {e: 0 for e in ENGS}
        for o in self.ops:
            if o.sig and not o.dma:
                cnt[o.eng] += 1
                o.cnt = cnt[o.eng]

    def emit(self, final_waits=()):
        nc = self.nc
        with ExitStack() as st:
            esem = {e: st.enter_context(nc.semaphore("s_" + e)) for e in ENGS}
            dsem = [st.enter_context(nc.semaphore("d%d" % i)) for i in range(self.NDMA)]
            block = st.enter_context(nc.Block())
            per = {e: [o for o in self.ops if o.eng == e] for e in ENGS}

            def run(e, engobj, extra_final=()):
                seen_e = {x: 0 for x in ENGS}
                seen_d = {}
                for o in per[e]:
                    for p in o.deps:
                        if p.dma:
                            if seen_d.get(p.sem, 0) < p.semval:
                                engobj.wait_ge(dsem[p.sem], p.semval)
                                seen_d[p.sem] = p.semval
                        elif seen_e[p.eng] < p.cnt:
                            engobj.wait_ge(esem[p.eng], p.cnt)
                            seen_e[p.eng] = p.cnt
                    ins = o.fn(engobj)
                    if o.dma:
                        ins.then_inc(dsem[o.sem], 16)
                    elif o.sig:
                        ins.then_inc(esem[o.eng], 1)
                for p in extra_final:
                    engobj.wait_ge(dsem[p.sem], p.semval)

            @block.tensor
            def _(e):
                run("pe", e)

            @block.scalar
            def _(e):
                run("act", e)

            @block.vector
            def _(e):
                run("dve", e)

            @block.gpsimd
            def _(e):
                run("pool", e)

            @block.sync
            def _(e):
                run("sp", e, extra_final=final_waits)


PV = {}
_o = 0
for _n, _w in [("conv_w", 16), ("conv_b", 4), ("gate_b", 8), ("lam", 4), ("s5_d", 4), ("mu_rkv", 12), ("w0", 4),
               ("a0", 4), ("k_k", 4), ("k_a", 4), ("r_k", 4), ("mu_w", 1), ("mu_a", 1), ("mu_g", 1), ("q_norm", 2),
               ("kv_norm", 1), ("qkn_q", 1), ("qkn_k", 1), ("s5_are", 16), ("s5_aim", 16), ("s5_ldt", 16)]:
    PV[_n] = (_o, _w)
    _o += _w
NPV = _o


def _chunks(v, n):
    return np.ascontiguousarray(v.reshape(n, 128).T)


def host_layout(inp):
    f = np.float32
    L = DEPTH
    pvec = np.zeros((L, 128, NPV), f)
    fvec = np.zeros((L, 4, 1024), f)
    lrug = np.zeros((L, 2, 4, 128, 128), f)
    bst = np.zeros((L, 2, 16, 128, 128), f)
    cst = np.zeros((L, 2, 16, 128, 128), f)
    for l in range(L):
        def put(name, arr):
            o, w = PV[name]
            pvec[l, :arr.shape[0], o:o + w] = arr
        put("conv_w", np.concatenate([_chunks(inp["lru_conv_w"][l, k], 4) for k in range(4)], axis=1))
        put("conv_b", _chunks(inp["lru_conv_b"][l], 4))
        put("gate_b", np.concatenate([_chunks(inp["lru_gate_b"][l, g], 4) for g in range(2)], axis=1))
        put("lam", _chunks(inp["lru_lambda"][l], 4))
        put("s5_d", _chunks(inp["s5_d"][l], 4))
        put("mu_rkv", np.concatenate([_chunks(inp["rwkv_mu_rkv"][l, j], 4) for j in range(3)], axis=1))
        put("w0", _chunks(inp["rwkv_w0"][l], 4))
        put("a0", _chunks(inp["rwkv_a0"][l], 4))
        put("k_k", _chunks(inp["rwkv_k_k"][l], 4))
        put("k_a", _chunks(inp["rwkv_k_a"][l], 4))
        put("r_k", _chunks(inp["rwkv_r_k"][l].reshape(-1), 4))
        put("mu_w", inp["rwkv_mu_w"][l].reshape(64, 1))
        put("mu_a", inp["rwkv_mu_a"][l].reshape(64, 1))
        put("mu_g", inp["rwkv_mu_g"][l].reshape(128, 1))
        put("q_norm", _chunks(inp["mla_q_norm"][l], 2))
        put("kv_norm", inp["mla_kv_norm"][l].reshape(128, 1))
        put("qkn_q", inp["mla_qk_norm_q"][l].reshape(96, 1))
        put("qkn_k", inp["mla_qk_norm_k"][l].reshape(96, 1))
        are = inp["s5_a_re"][l].reshape(16, 2, 64).transpose(1, 2, 0).reshape(128, 16)
        aim = inp["s5_a_im"][l].reshape(16, 2, 64).transpose(1, 2, 0).reshape(128, 16)
        ldt = np.broadcast_to(inp["s5_log_dt"][l].reshape(16, 2, 1), (16, 2, 64)).transpose(1, 2, 0).reshape(128, 16)
        put("s5_are", are)
        put("s5_aim", aim)
        put("s5_ldt", ldt)
        fvec[l, 0] = inp["norm_mix"][l]
        fvec[l, 1] = inp["norm_mlp"][l]
        fvec[l, 2, :512] = inp["rwkv_lnx_g"][l]
        fvec[l, 2, 512:] = inp["rwkv_lnx_b"][l]
        for g in range(2):
            for h in range(8):
                c, e = h // 2, h % 2
                lrug[l, g, c, e * 64:(e + 1) * 64, e * 64:(e + 1) * 64] = inp["lru_gate_w"][l, g, h]
        for ri, (bsrc, csrc) in enumerate([(inp["s5_b_re"][l], inp["s5_c_re"][l]), (inp["s5_b_im"][l], inp["s5_c_im"][l])]):
            for g in range(32):
                j, e = g // 2, g % 2
                r0 = 32 * (j % 4) + e * 16
                bst[l, ri, j, r0:r0 + 16, e * 64:(e + 1) * 64] = bsrc[g].T
                cst[l, ri, j, e * 64:(e + 1) * 64, r0:r0 + 16] = csrc[g].T
    return pvec, fvec, lrug, bst, cst


def host_consts():
    f = np.float32
    c = {}
    c["ident"] = np.eye(128, dtype=f)
    c["ones"] = np.ones((128, 128), f)
    blk = np.zeros((128, 128), f)
    blk[:64, :64] = 1
    blk[64:, 64:] = 1
    c["blk64"] = blk
    hs = np.zeros((128, 2), f)
    hs[:64, 0] = 1
    hs[64:, 1] = 1
    c["headsel"] = hs
    p = np.arange(128)[:, None]
    q = np.arange(512)[None, :]
    c["amask"] = np.stack([(q >= 128 * j + p) for j in range(4)], 1).astype(f)
    j = np.arange(64)[:, None]
    t = np.arange(64)[None, :]
    m = np.zeros((64, 3, 64), f)
    m[:, 0] = (t > j)
    m[:, 1] = (t >= j)
    m[:, 2] = (t < j)
    c["rmask"] = m
    rs = np.ones((128, 512), f)
    rs[:, ::64] = 0
    c["reset64"] = rs
    pos = np.arange(T_FULL, dtype=np.float64)
    inv = 10000.0 ** (-np.arange(0, 32, 2, dtype=np.float64) / 32)
    ang = pos[None, :] * inv[:, None]
    cos = np.ones((96, T_FULL), np.float64)
    sin = np.zeros((96, T_FULL), np.float64)
    cos[64:80] = np.cos(ang); cos[80:96] = np.cos(ang)
    sin[64:80] = np.sin(ang); sin[80:96] = np.sin(ang)
    c["ropec"] = cos.astype(f)
    c["ropes"] = sin.astype(f)
    pr = np.zeros((96, 96), f)
    for i in range(16):
        pr[80 + i, 64 + i] = -1.0
        pr[64 + i, 80 + i] = 1.0
    c["prot"] = pr
    return c


CONST_SHAPES = {"ident": [128, 128], "ones": [128, 128], "blk64": [128, 128], "headsel": [128, 2],
                "amask": [128, 4, 512], "rmask": [64, 3, 64], "reset64": [128, 512],
                "ropec": [96, T_FULL], "ropes": [96, T_FULL], "prot": [96, 96]}

BIGW = {"w_in": [D, D_IN], "s5_w_glu": [C, 2 * C], "w_branch": [4 * C, D], "w_out": [D, D], "w_ff1": [D, 4 * D],
        "w_ff2": [4 * D, D], "mla_w_uq": [256, 768], "mla_w_ukv": [128, 1024], "rwkv_w2": [64, C],
        "rwkv_a2": [64, C], "rwkv_g2": [128, C]}


class KB:
    def __init__(self, nc, L_RUN, T_RUN, dbg=False):
        self.nc, self.L, self.T, self.dbg = nc, L_RUN, T_RUN, dbg
        self.NT = T_RUN // TT
        self.P = Prog(nc)
        self.st = ExitStack()
        self.free_banks = []
        self.scr_free = {}
        self.scr_n = 0
        self.wk = 0
        self.outs = []

    def sb(self, name, shape, dt=F32):
        return self.st.enter_context(self.nc.sbuf_tensor(name, shape, dt))

    def bank(self):
        assert self.free_banks, "out of PSUM banks"
        return self.free_banks.pop(0)

    def unbank(self, b):
        self.free_banks.append(b)

    NSLOT = 40

    def scr(self, kb=2):
        n = kb // 2
        if not hasattr(self, "arena"):
            self.arena = self.sb("arena", [128, self.NSLOT * 512], F32)
            self.slot_used = [False] * self.NSLOT
            self.P.alias = {}
        for st in range(self.NSLOT - n + 1):
            if not any(self.slot_used[st:st + n]):
                for i in range(st, st + n):
                    self.slot_used[i] = True
                key = "arn_%d_%d" % (st, n)
                self.P.alias[key] = tuple("slot%d" % i for i in range(st, st + n))
                return (self.arena[:, st * 512:(st + n) * 512], key)
        raise AssertionError("out of scratch slots")

    def unscr(self, s, kb=2):
        _, st, n = s[1].split("_")
        for i in range(int(st), int(st) + int(n)):
            assert self.slot_used[i]
            self.slot_used[i] = False

    def mm(self, out, lhsT, rhs, start, stop, r, w):
        self.P.op("pe", lambda e: e.matmul(out, lhsT=lhsT, rhs=rhs, start=start, stop=stop), r, w)

    def trp(self, out, in_, ident, r, w):
        self.P.op("pe", lambda e: e.transpose(out=out, in_=in_, identity=ident), r, w)

    def act(self, out, in_, func, r, w, **kw):
        self.P.op("act", lambda e: e.activation(out=out, in_=in_, func=func, **kw), r, w)

    def tt(self, eng, out, in0, in1, op, r, w):
        self.P.op(eng, lambda e: e.tensor_tensor(out=out, in0=in0, in1=in1, op=op), r, w)

    def ts(self, eng, out, in0, s1, s2, op0, op1, r, w):
        if s2 is None:
            self.P.op(eng, lambda e: e.tensor_single_scalar(out=out, in_=in0, scalar=s1, op=op0), r, w)
        else:
            self.P.op(eng, lambda e: e.tensor_scalar(out=out, in0=in0, scalar1=s1, scalar2=s2, op0=op0, op1=op1), r, w)

    def stt(self, eng, out, in0, scalar, in1, op0, op1, r, w):
        self.P.op(eng, lambda e: e.scalar_tensor_tensor(out=out, in0=in0, scalar=scalar, in1=in1, op0=op0, op1=op1), r, w)

    def cp(self, eng, out, in_, r, w):
        if eng == "act":
            self.P.op("act", lambda e: e.copy(out=out, in_=in_), r, w)
        else:
            self.P.op(eng, lambda e: e.tensor_copy(out=out, in_=in_), r, w)

    def memset(self, eng, out, val, w):
        self.P.op(eng, lambda e: e.memset(out, val), [], w)

    def recip(self, out, in_, r, w):
        self.P.op("dve", lambda e: e.reciprocal(out=out, in_=in_), r, w)

    def scan(self, out, d0, d1, init, r, w):
        self.P.op("dve", lambda e: e.tensor_tensor_scan(out=out, data0=d0, data1=d1, initial=init, op0=ALU.mult, op1=ALU.add), r, w)

    def dma(self, eng, out, in_, r, w, **kw):
        return self.P.op(eng, lambda e: e.dma_start(out=out, in_=in_, **kw), r, w, dma=True)

    def wload(self, src, shape):
        i = self.wk % len(self.wring)
        self.wk += 1
        buf, key = self.wring[i]
        n = int(np.prod(shape[1:]))
        if len(shape) == 3:
            view = buf[:shape[0], 0:n].rearrange("p (k c) -> p k c", k=shape[1])
        else:
            view = buf[:shape[0], 0:n]
        self.dma("sp", view, src, [self.cur_wkey], [key])
        return view, key

    def gelu(self, src, srck, out, outk, tmp, tmpk):
        self.tt("dve", tmp, src, src, ALU.mult, [srck], [tmpk])
        self.ts("dve", tmp, tmp, 0.044715, 1.0, ALU.mult, ALU.add, [tmpk], [tmpk])
        self.tt("dve", tmp, tmp, src, ALU.mult, [tmpk, srck], [tmpk])
        self.act(tmp, tmp, AF.Sigmoid, [tmpk], [tmpk], scale=GELU_K)
        self.tt("dve", out, tmp, src, ALU.mult, [tmpk, srck], [outk])

    def build(self, branches=(0, 1, 2, 3)):
        nc, L, T = self.nc, self.L, self.T
        self.branches = branches
        dt = lambda name, shape, dty, kind: nc.dram_tensor(name, shape, dty, kind=kind).ap()
        self.x = dt("x", [T, D], F32, "ExternalInput")
        self.y = dt("y", [T, D], F32, "ExternalOutput")
        self.w32, self.wb = {}, {}
        for n, (r, c) in BIGW.items():
            self.w32[n] = dt(n, [DEPTH, r * c // 1024, 1024], F32, "ExternalInput")
            self.wb[n] = dt("b_" + n, [DEPTH, r, c], BF16, "Internal")
        self.pvec_d = dt("pvec", [DEPTH, 128, NPV], F32, "ExternalInput")
        self.fvec_d = dt("fvec", [DEPTH, 4, 1024], F32, "ExternalInput")
        for n, shp in [("lrug", [DEPTH, 8 * 128 * 128 // 1024, 1024]), ("bst", [DEPTH, 32 * 16, 1024]), ("cst", [DEPTH, 32 * 16, 1024])]:
            self.w32[n] = dt(n, shp, F32, "ExternalInput")
        self.wb["lrug"] = dt("b_lrug", [DEPTH, 8, 128, 128], BF16, "Internal")
        self.wb["bst"] = dt("b_bst", [DEPTH, 32, 128, 128], BF16, "Internal")
        self.wb["cst"] = dt("b_cst", [DEPTH, 32, 128, 128], BF16, "Internal")
        self.cd = {n: dt("c_" + n, s, F32, "ExternalInput") for n, s in CONST_SHAPES.items()}
        self.kc = dt("kcache", [8, 96, T], BF16, "Internal")
        self.vc = dt("vcache", [8, 128, T // 128, 65], BF16, "Internal")
        self.s5tab = dt("s5tab", [16, 128, 4, 128], F32, "Internal")
        if self.dbg:
            self.dbg_ys = dt("dbg_ys", [4, 128, 8, TT], F32, "ExternalOutput")

        for i in range(8):
            t = self.st.enter_context(nc.psum_tensor("bank%d" % i, [128, 512], F32))
            self.free_banks.append((t, "bank%d" % i))
        sb = self.sb
        self.identf = sb("identf", [128, 128]); self.identb = sb("identb", [128, 128], BF16)
        self.onesf = sb("onesf", [128, 128]); self.onesb = sb("onesb", [128, 128], BF16)
        self.blk64 = sb("blk64", [128, 128]); self.headsel = sb("headsel", [128, 2]); self.headselb = sb("headselb", [128, 2], BF16)
        self.amask = sb("amask", [128, 4, 512], BF16); self.rmask = sb("rmask", [64, 3, 64])
        self.reset64 = sb("reset64", [128, 512]); self.prot = sb("prot", [96, 96])
        amf, amfk = self.scr(8)
        for n, tl in [("ident", self.identf), ("ones", self.onesf), ("blk64", self.blk64), ("headsel", self.headsel),
                      ("rmask", self.rmask), ("reset64", self.reset64), ("prot", self.prot)]:
            self.dma("sp", tl[:], self.cd[n], [], [n])
        self.dma("sp", amf[:, 0:2048].rearrange("p (a b) -> p a b", a=4), self.cd["amask"], [], [amfk])
        self.cp("dve", self.amask[:], amf[:, 0:2048].rearrange("p (a b) -> p a b", a=4), [amfk], ["amask"])
        self.unscr((amf, amfk), 8)
        self.cp("dve", self.identb[:], self.identf[:], ["ident"], ["identb"])
        self.cp("dve", self.onesb[:], self.onesf[:], ["ones"], ["onesb"])
        self.cp("dve", self.headselb[:], self.headsel[:], ["headsel"], ["headselb"])
        self.wring = [(sb("wring%d" % i, [128, 4096], BF16), "wring%d" % i) for i in range(3)]
        self.xt = sb("xt", [128, 4, 1024]); self.hT = sb("hT", [128, 8, 512], BF16)
        self.gbc = sb("gbc", [128, 1024]); self.lnx = sb("lnx", [64, 1024])
        self.arA = sb("arA", [128, 4096]); self.arB = sb("arB", [128, 5120])
        self.hn = self.arA[:, 0:2048].bitcast(BF16).rearrange("p (s d) -> p s d", s=4)
        self.macc = self.arA[:, :].rearrange("p (c t) -> p c t", c=8)
        ysb = self.arB[:, :].bitcast(BF16)
        self.ys = [ysb[:, i * 2048:(i + 1) * 2048].rearrange("p (c t) -> p c t", c=4) for i in range(3)]
        self.ys.append(ysb[0:64, 6144:10240].rearrange("p (h t) -> p h t", h=8))
        self.a1T_lo = self.arA[:, :].bitcast(BF16).rearrange("p (c t) -> p c t", c=16)
        self.a1T_hi = ysb[:, 0:8192].rearrange("p (c t) -> p c t", c=16)
        self.pv = sb("pv", [128, NPV]); self.der = sb("der", [128, 16])
        self.lrug_sb = sb("lrug_sb", [128, 8, 128], BF16)
        self.w2_sb = sb("w2_sb", [64, 512], BF16); self.a2_sb = sb("a2_sb", [64, 512], BF16); self.g2_sb = sb("g2_sb", [128, 512], BF16)
        self.wuq_sb = sb("wuq_sb", [128, 2, 768], BF16); self.wukv_sb = sb("wukv_sb", [128, 1024], BF16)
        self.ss = sb("ss", [128, 8])
        self.xh = [sb("xh%d" % c, [128, 515]) for c in range(4)]
        self.hc = sb("hc", [128, 4])
        self.zc = sb("zc", [128, 2, 16]); self.el = sb("el", [128, 2, 16])
        self.carry = sb("carry", [128, 16])
        self.s0f = sb("s0f", [128, 4, 64]); self.s0bd = sb("s0bd", [128, 4, 128], BF16)

        for l in range(L):
            for n in list(BIGW) + ["lrug", "bst", "cst"]:
                dstv = self.wb[n][l]
                if n in ("lrug", "bst", "cst"):
                    dstv = dstv.rearrange("a r c -> (a r c)")
                else:
                    dstv = dstv.rearrange("r c -> (r c)")
                dstv = dstv.rearrange("(a b) -> a b", b=1024)
                self.dma("pool", dstv, self.w32[n][l], [], ["wb%d" % l])
        for l in range(L):
            self.layer(l)
        self.P.finalize()
        self.P.emit(final_waits=self.outs)
        self.st.close()

    def layer(self, l):
        self.l = l
        self.cur_wkey = "wb%d" % l
        wk = self.cur_wkey
        pv, der = self.pv, self.der
        self.dma("sp", pv[:], self.pvec_d[l], [], ["pv"])
        self.dma("sp", self.lnx[:], self.fvec_d[l, 2:3, :].broadcast_to([64, 1024]), [], ["lnx"])
        self.dma("sp", self.lrug_sb[:], self.wb["lrug"][l].rearrange("a r c -> r a c"), [wk], ["lrug_sb"])
        self.dma("sp", self.w2_sb[:], self.wb["rwkv_w2"][l], [wk], ["w2_sb"])
        self.dma("sp", self.a2_sb[:], self.wb["rwkv_a2"][l], [wk], ["a2_sb"])
        self.dma("sp", self.g2_sb[:], self.wb["rwkv_g2"][l], [wk], ["g2_sb"])
        self.dma("sp", self.wuq_sb[:], self.wb["mla_w_uq"][l].rearrange("(k p) c -> p k c", p=128), [wk], ["wuq_sb"])
        self.dma("sp", self.wukv_sb[:], self.wb["mla_w_ukv"][l], [wk], ["wukv_sb"])
        o = PV["lam"][0]
        self.act(der[:, 0:4], pv[:, o:o + 4], AF.Exp, ["pv"], ["der"], scale=-1.0)
        self.act(der[:, 0:4], der[:, 0:4], AF.Ln, ["der"], ["der"], bias=1.0)
        self.ts("dve", der[:, 0:4], der[:, 0:4], -8.0, None, ALU.mult, None, ["der"], ["der"])
        o = PV["w0"][0]
        self.ts("dve", der[:, 4:8], pv[:, o:o + 4], -1.0, None, ALU.mult, None, ["pv"], ["der"])
        o = PV["k_a"][0]
        self.ts("dve", der[:, 8:12], pv[:, o:o + 4], -1.0, 1.0, ALU.mult, ALU.add, ["pv"], ["der"])
        for c in range(4):
            self.memset("pool", self.xh[c][:, 0:3], 0.0, ["xh%d" % c])
        self.memset("pool", self.hc[:], 0.0, ["hc"])
        self.memset("pool", self.zc[:], 0.0, ["zc%d" % i for i in range(16)])
        self.memset("pool", self.carry[:], 0.0, ["carry%d" % i for i in range(16)])
        self.memset("pool", self.s0f[:], 0.0, ["s0_%df" % i for i in range(4)])
        self.memset("pool", self.s0bd[:], 0.0, ["s0_%d" % i for i in range(4)])
        if 1 in self.branches:
            self.s5_setup(l)
        for ti in range(self.NT):
            self.tile(l, ti)

    def pvc(self, name, i=0, rows=128):
        o = PV[name][0] + i
        return self.pv[0:rows, o:o + 1]

    def norm_T(self, gi):
        l = self.l
        self.dma("sp", self.gbc[:], self.fvec_d[l, gi:gi + 1, :].broadcast_to([128, 1024]), [], ["gbc"])
        junk, jk = self.scr(4)
        for s in range(4):
            self.act(junk[:, 0:1024], self.xt[:, s, :], AF.Square, ["xt"], [jk, "ss"], accum_out=self.ss[:, s:s + 1])
        self.unscr((junk, jk), 4)
        self.act(self.ss[:, 4:8], self.ss[:, 0:4], AF.Sqrt, ["ss"], ["ss"], scale=1.0 / D, bias=EPS)
        self.recip(self.ss[:, 4:8], self.ss[:, 4:8], ["ss"], ["ss"])
        for s in range(4):
            self.stt("dve", self.hn[:, s, :], self.xt[:, s, :], self.ss[:, 4 + s:5 + s], self.gbc[:], ALU.mult, ALU.mult,
                     ["xt", "ss", "gbc"], ["arA"])
        for kc in range(8):
            b, bk = self.bank()
            bb = b[:, :].bitcast(BF16)
            for s in range(4):
                self.trp(bb[:, s * 128:(s + 1) * 128], self.hn[:, s, kc * 128:(kc + 1) * 128], self.identb[:], ["arA", "identb"], [bk])
            self.cp("act" if kc % 2 else "dve", self.hT[:, kc, :], bb[:, 0:512], [bk], ["hT"])
            self.unbank((b, bk))

    def proj(self, out, slab, c0, m, nk, rhsf, r, w):
        for kc in range(nk):
            self.mm(out, slab[:, kc, c0:c0 + m], rhsf(kc), kc == 0, kc == nk - 1, r, w)

    def win(self, c0, n=512):
        return self.wload(self.wb["w_in"][self.l][:, c0:c0 + n].rearrange("(k p) c -> p k c", p=128), [128, 8, n])

    def tile(self, l, ti):
        t0 = ti * TT
        src = self.x if l == 0 else self.y
        self.dma("sp", self.xt[:], src[t0:t0 + TT, :].rearrange("(s p) d -> p s d", p=128), ["ydram"], ["xt"])
        self.norm_T(0)
        for bi, fn in enumerate([self.lru, self.s5, self.rwkv, self.mla]):
            if bi in self.branches:
                fn(l, ti)
            else:
                self.memset("pool", self.ys[bi], 0.0, ["arB"])
        if self.dbg and l == 0 and ti == self.dbg_tile:
            d, dk = self.scr(8)
            dv = d[:, :].rearrange("p (c t) -> p c t", c=4)
            for bi in range(4):
                np_ = 128 if bi < 3 else 64
                for hf_ in range(1 if bi < 3 else 2):
                    self.cp("dve", dv[0:np_], self.ys[bi][:, hf_ * 4:hf_ * 4 + 4, :], ["arB"], [dk])
                    self.outs.append(self.dma("sp", self.dbg_ys[bi, 0:np_, hf_ * 4:hf_ * 4 + 4, :], dv[0:np_], [dk], ["dbgout"]))
            self.unscr((d, dk), 8)
        self.merge(l, ti)
        self.mlp(l, ti)
        st = self.dma("pool", self.y[t0:t0 + TT, :].rearrange("(s p) d -> p s d", p=128), self.xt[:], ["xt"], ["ydram"])
        if l == self.L - 1:
            self.outs.append(st)

    def merge(self, l, ti):
        wbr = self.wb["w_branch"][l]
        SG = [self.scr() for _ in range(3)]
        TM = [self.scr() for _ in range(3)]
        it = 0
        for n in range(4):
            for half in range(2):
                if n < 3:
                    wsl, wslk = self.wload(wbr[n * 512:(n + 1) * 512, half * 512:(half + 1) * 512].rearrange("(k p) c -> p k c", p=128), [128, 4, 512])
                    nk = 4
                else:
                    wsl, wslk = self.wload(wbr[n * 512:(n + 1) * 512, half * 512:(half + 1) * 512].rearrange("(k p) c -> p k c", p=64), [64, 8, 512])
                    nk = 8
                gsl, gslk = self.win(3744 + n * 1024 + half * 512)
                for dc4 in range(4):
                    dc = half * 4 + dc4
                    sg, sgk = SG[it % 3]; tm, tmk = TM[it % 3]; it += 1
                    bg, bgk = self.bank()
                    self.proj(bg[:, :], gsl, dc4 * 128, 128, 8, lambda kc: self.hT[:, kc, :], [gslk, "hT"], [bgk])
                    self.act(sg[:, 0:512], bg[:, :], AF.Sigmoid, [bgk], [sgk])
                    self.unbank((bg, bgk))
                    bz, bzk = self.bank()
                    ysn = self.ys[n]
                    self.proj(bz[:, :], wsl, dc4 * 128, 128, nk, lambda kc: ysn[:, kc, :], [wslk, "arB"], [bzk])
                    mk_ = "arA#m%d" % dc
                    if n == 0:
                        self.tt("dve", self.macc[:, dc, :], bz[:, :], sg[:, 0:512], ALU.mult, [bzk, sgk], [mk_])
                    else:
                        self.tt("dve", tm[:, 0:512], bz[:, :], sg[:, 0:512], ALU.mult, [bzk, sgk], [tmk])
                        self.tt("pool", self.macc[:, dc, :], self.macc[:, dc, :], tm[:, 0:512], ALU.add, [mk_, tmk], [mk_])
                    self.unbank((bz, bzk))
        for s_ in SG + TM:
            self.unscr(s_)
        for dc in range(8):
            self.cp("act" if dc % 2 else "dve", self.hT[:, dc, :], self.macc[:, dc, :], ["arA"], ["hT"])
        for half in range(2):
            wsl, wslk = self.wload(self.wb["w_out"][l][:, half * 512:(half + 1) * 512].rearrange("(k p) c -> p k c", p=128), [128, 8, 512])
            for s in range(4):
                b, bk = self.bank()
                for kc in range(8):
                    self.mm(b[:, :], self.hT[:, kc, s * 128:(s + 1) * 128], wsl[:, kc, :], kc == 0, kc == 7, ["hT", wslk], [bk])
                self.tt("dve", self.xt[:, s, half * 512:(half + 1) * 512], self.xt[:, s, half * 512:(half + 1) * 512], b[:, :], ALU.add, ["xt", bk], ["xt"])
                self.unbank((b, bk))

    def mlp(self, l, ti):
        self.norm_T(1)
        r1, r1k = self.scr(2)
        for sl in range(8):
            wsl, wslk = self.wload(self.wb["w_ff1"][l][:, sl * 512:(sl + 1) * 512].rearrange("(k p) c -> p k c", p=128), [128, 8, 512])
            for c4 in range(4):
                fc = sl * 4 + c4
                b, bk = self.bank()
                self.proj(b[:, :], wsl, c4 * 128, 128, 8, lambda kc: self.hT[:, kc, :], [wslk, "hT"], [bk])
                self.act(r1[:, 0:512], b[:, :], AF.Relu, [bk], [r1k])
                dst = self.a1T_lo[:, fc, :] if fc < 16 else self.a1T_hi[:, fc - 16, :]
                self.tt("dve" if fc % 2 else "pool", dst, r1[:, 0:512], r1[:, 0:512], ALU.mult, [r1k], ["arA" if fc < 16 else "arB"])
                self.unbank((b, bk))
        self.unscr((r1, r1k))
        for half in range(2):
            accs = [self.bank() for _ in range(4)]
            for g in range(4):
                wsl, wslk = self.wload(self.wb["w_ff2"][l][g * 1024:(g + 1) * 1024, half * 512:(half + 1) * 512].rearrange("(k p) c -> p k c", p=128), [128, 8, 512])
                for s in range(4):
                    for kc in range(8):
                        fc = g * 8 + kc
                        a = self.a1T_lo[:, fc, s * 128:(s + 1) * 128] if fc < 16 else self.a1T_hi[:, fc - 16, s * 128:(s + 1) * 128]
                        self.mm(accs[s][0][:, :], a, wsl[:, kc, :], fc == 0, fc == 31, ["arA", "arB", wslk], [accs[s][1]])
            for s in range(4):
                self.tt("dve", self.xt[:, s, half * 512:(half + 1) * 512], self.xt[:, s, half * 512:(half + 1) * 512], accs[s][0][:, :], ALU.add, ["xt", accs[s][1]], ["xt"])
                self.unbank(accs[s])

    def lru(self, l, ti):
        slx, slxk = self.win(0)
        slg, slgk = self.win(512)
        S = [self.scr() for _ in range(6)]
        (xc, xck), (rr, rrk), (ii, iik), (aa, aak), (t1, t1k), (hh, hhk) = S
        xcb, xcbk = self.scr()
        xcbv = xcb[:, 0:256].bitcast(BF16)
        hf = lambda kc: self.hT[:, kc, :]
        for c in range(4):
            xh, xhk = self.xh[c], "xh%d" % c
            b, bk = self.bank()
            self.proj(b[:, :], slx, c * 128, 128, 8, hf, [slxk, "hT"], [bk])
            self.cp("act", xh[:, 3:515], b[:, :], [bk], [xhk])
            self.unbank((b, bk))
            self.ts("dve", xc[:, 0:512], xh[:, 3:515], self.pvc("conv_w", 12 + c), self.pvc("conv_b", c), ALU.mult, ALU.add, [xhk, "pv"], [xck])
            for k in range(3):
                self.stt("dve", xc[:, 0:512], xh[:, k:k + 512], self.pvc("conv_w", 4 * k + c), xc[:, 0:512], ALU.mult, ALU.add, [xhk, "pv", xck], [xck])
            self.cp("pool", xh[:, 0:3], xh[:, 512:515], [xhk], [xhk])
            self.cp("dve", xcbv, xc[:, 0:512], [xck], [xcbk])
            for g, (dst, dstk) in enumerate([(rr, rrk), (ii, iik)]):
                b, bk = self.bank()
                self.mm(b[:, :], self.lrug_sb[:, g * 4 + c, :], xcbv, True, True, ["lrug_sb", xcbk], [bk])
                self.act(dst[:, 0:512], b[:, :], AF.Sigmoid, [bk, "pv"], [dstk], bias=self.pvc("gate_b", g * 4 + c))
                self.unbank((b, bk))
            self.act(aa[:, 0:512], rr[:, 0:512], AF.Exp, [rrk, "der"], [aak], scale=self.der[:, c:c + 1])
            self.tt("dve", t1[:, 0:512], aa[:, 0:512], aa[:, 0:512], ALU.mult, [aak], [t1k])
            self.ts("dve", t1[:, 0:512], t1[:, 0:512], -1.0, 1.0, ALU.mult, ALU.add, [t1k], [t1k])
            self.act(t1[:, 0:512], t1[:, 0:512], AF.Sqrt, [t1k], [t1k])
            self.tt("dve", ii[:, 0:512], ii[:, 0:512], xc[:, 0:512], ALU.mult, [iik, xck], [iik])
            self.tt("dve", t1[:, 0:512], t1[:, 0:512], ii[:, 0:512], ALU.mult, [t1k, iik], [t1k])
            self.scan(hh[:, 0:512], aa[:, 0:512], t1[:, 0:512], self.hc[:, c:c + 1], [aak, t1k, "hc"], [hhk])
            self.cp("dve", self.hc[:, c:c + 1], hh[:, 511:512], [hhk], ["hc"])
            b, bk = self.bank()
            self.proj(b[:, :], slg, c * 128, 128, 8, hf, [slgk, "hT"], [bk])
            self.cp("act", rr[:, 0:512], b[:, :], [bk], [rrk])
            self.unbank((b, bk))
            self.gelu(rr[:, 0:512], rrk, ii[:, 0:512], iik, t1[:, 0:512], t1k)
            self.tt("dve", self.ys[0][:, c, :], hh[:, 0:512], ii[:, 0:512], ALU.mult, [hhk, iik], ["arB"])
        for s_ in S:
            self.unscr(s_)
        self.unscr((xcb, xcbk))

    def s5_setup(self, l):
        sm, smk = self.scr()
        v = lambda i: sm[:, i * 16:(i + 1) * 16]
        pvs = lambda n: self.pv[:, PV[n][0]:PV[n][0] + 16]
        R, W = [smk, "pv"], [smk]
        tt = lambda o, a, b, op: self.tt("dve", o, a, b, op, R, W)
        self.act(v(0), pvs("s5_ldt"), AF.Exp, R, W)
        tt(v(1), pvs("s5_aim"), v(0), ALU.mult)
        tt(v(2), pvs("s5_are"), v(0), ALU.mult)
        self.act(v(3), v(2), AF.Exp, R, W)
        self.act(v(4), v(2), AF.Exp, R, W, scale=-1.0)
        self.act(v(5), v(1), AF.Sin, R, W, scale=1.0 / 16)
        self.ts("dve", v(6), v(1), 1.0 / 16, math.pi / 2, ALU.mult, ALU.add, R, W)
        self.act(v(6), v(6), AF.Sin, R, W)

        def csq(c, s_):
            tt(v(7), c, c, ALU.mult); tt(v(8), s_, s_, ALU.mult); tt(v(9), c, s_, ALU.mult)
            tt(c, v(7), v(8), ALU.subtract)
            self.ts("dve", s_, v(9), 2.0, None, ALU.mult, None, R, W)
        for _ in range(4):
            csq(v(6), v(5))
        tt(v(10), v(3), v(6), ALU.mult); tt(v(11), v(3), v(5), ALU.mult)
        tt(v(12), v(4), v(6), ALU.mult); tt(v(13), v(4), v(5), ALU.mult)
        self.ts("dve", v(13), v(13), -1.0, None, ALU.mult, None, R, W)
        self.ts("dve", v(14), v(10), -1.0, None, ALU.add, None, R, W)
        tt(v(7), pvs("s5_are"), pvs("s5_are"), ALU.mult); tt(v(8), pvs("s5_aim"), pvs("s5_aim"), ALU.mult)
        tt(v(7), v(7), v(8), ALU.add)
        self.recip(v(7), v(7), R, W)
        tt(v(8), v(14), pvs("s5_are"), ALU.mult); tt(v(9), v(11), pvs("s5_aim"), ALU.mult); tt(v(8), v(8), v(9), ALU.add)
        tt(v(15), v(8), v(7), ALU.mult)
        tt(v(8), v(11), pvs("s5_are"), ALU.mult); tt(v(9), v(14), pvs("s5_aim"), ALU.mult); tt(v(8), v(8), v(9), ALU.subtract)
        tt(v(14), v(8), v(7), ALU.mult)
        tabs = [self.scr(8) for _ in range(4)]
        tv = [t[0][:, :].rearrange("p (j t) -> p j t", j=16) for t in tabs]
        tk = [t[1] for t in tabs]
        tmp, tmpk = self.scr(8)
        tmv = tmp[:, :].rearrange("p (j t) -> p j t", j=16)
        self.cp("dve", tv[0][:, :, 0], v(15), R, [tk[0]]); self.cp("dve", tv[1][:, :, 0], v(14), R, [tk[1]])
        self.memset("dve", tv[2][:, :, 0], 1.0, [tk[2]]); self.memset("dve", tv[3][:, :, 0], 0.0, [tk[3]])
        for (ar_, ai_, pr, pi) in [(0, 1, v(12), v(13)), (2, 3, v(10), v(11))]:
            for k in range(7):
                n = 1 << k
                prb = pr.unsqueeze(2).broadcast_to([128, 16, n]) if n > 1 else pr.unsqueeze(2)
                pib = pi.unsqueeze(2).broadcast_to([128, 16, n]) if n > 1 else pi.unsqueeze(2)
                Ar, Ai = tv[ar_], tv[ai_]
                kk = [tk[ar_], tk[ai_], smk, tmpk]
                self.tt("dve", Ar[:, :, n:2 * n], Ar[:, :, 0:n], prb, ALU.mult, kk, [tk[ar_]])
                self.tt("dve", tmv[:, :, 0:n], Ai[:, :, 0:n], pib, ALU.mult, kk, [tmpk])
                self.tt("dve", Ar[:, :, n:2 * n], Ar[:, :, n:2 * n], tmv[:, :, 0:n], ALU.subtract, kk, [tk[ar_]])
                self.tt("dve", tmv[:, :, 0:n], Ar[:, :, 0:n], pib, ALU.mult, kk, [tmpk])
                self.tt("dve", Ai[:, :, n:2 * n], Ai[:, :, 0:n], prb, ALU.mult, kk, [tk[ai_]])
                self.tt("dve", Ai[:, :, n:2 * n], Ai[:, :, n:2 * n], tmv[:, :, 0:n], ALU.add, kk, [tk[ai_]])
                csq(pr, pi)
        self.cp("dve", self.el[:, 0, :], v(10), R, ["el"]); self.cp("dve", self.el[:, 1, :], v(11), R, ["el"])
        for i in range(4):
            self.dma("sp", self.s5tab[:, :, i, :].rearrange("j p t -> p j t"), tv[i], [tk[i]], ["s5tab"])
        for t in tabs:
            self.unscr(t, 8)
        self.unscr((tmp, tmpk), 8); self.unscr((sm, smk))

    def s5(self, l, ti):
        import os
        if os.environ.get('S5SKIP'):
            return
        hf = lambda kc: self.hT[:, kc, :]
        slu, sluk = self.win(1024)
        (usb, usbk), (ubf, ubfk), (tmp, tmpk), (yv, yvk), (tc, tck) = [self.scr() for _ in range(5)]
        ubv = ubf[:, 0:256].bitcast(BF16)
        (yg, ygk), (sre, srek), (nsi, nsik) = [self.scr(4) for _ in range(3)]
        ygv = yg[:, :].bitcast(BF16).rearrange("p (c t) -> p c t", c=4)
        srv = sre[:, :].bitcast(BF16).rearrange("p (c t) -> p c t", c=4)
        nsv = nsi[:, :].bitcast(BF16).rearrange("p (c t) -> p c t", c=4)
        Z = [self.scr(8) for _ in range(4)]
        zr, zi, zor, zoi = [z[0][:, :].rearrange("p (j t) -> p j t", j=4) for z in Z]
        zrk, zik, zork, zoik = [z[1] for z in Z]
        (bs, bsk), (cs, csk) = self.scr(), self.scr()
        bsv = bs[:, :].bitcast(BF16).rearrange("p (r j c) -> p r j c", r=2, j=4)
        csv = cs[:, :].bitcast(BF16).rearrange("p (r j c) -> p r j c", r=2, j=4)
        tabs = [self.scr() for _ in range(4)]
        q4 = lambda ap: ap.rearrange("p (q t) -> p q t", q=4)
        for c in range(4):
            b, bk = self.bank()
            self.proj(b[:, :], slu, c * 128, 128, 8, hf, [sluk, "hT"], [bk])
            self.cp("act", usb[:, 0:512], b[:, :], [bk], [usbk])
            self.cp("dve", ubv, usb[:, 0:512], [usbk], [ubfk])
            self.unbank((b, bk))
            for r in range(2):
                self.dma("sp", bsv[:, r], self.wb["bst"][l][r * 16 + 4 * c:r * 16 + 4 * c + 4].rearrange("a r c -> r a c"), [self.cur_wkey], [bsk])
                self.dma("sp", csv[:, r], self.wb["cst"][l][r * 16 + 4 * c:r * 16 + 4 * c + 4].rearrange("a r c -> r a c"), [self.cur_wkey], [csk])
            for jj in range(4):
                j = 4 * c + jj
                tb, tbk = tabs[jj]
                tbv = tb[:, 0:512].rearrange("p (i t) -> p i t", i=4)
                self.dma("sp", tbv, self.s5tab[j], ["s5tab"], [tbk])
                Fr = tbv[:, 0:1, :].broadcast_to([128, 4, 128]); Fi = tbv[:, 1:2, :].broadcast_to([128, 4, 128])
                bre, brek = self.bank(); bim, bimk = self.bank()
                self.mm(bre[:, :], bsv[:, 0, jj, :], ubv, True, True, [bsk, ubfk], [brek])
                self.mm(bim[:, :], bsv[:, 1, jj, :], ubv, True, True, [bsk, ubfk], [bimk])
                for q in range(4):
                    sl = slice(q * 128, (q + 1) * 128)
                    self.tt("dve", zr[:, jj, sl], bre[:, sl], tbv[:, 0, :], ALU.mult, [brek, tbk], [zrk])
                    self.tt("dve", tmp[:, sl], bim[:, sl], tbv[:, 1, :], ALU.mult, [bimk, tbk], [tmpk])
                    self.tt("dve", zi[:, jj, sl], bre[:, sl], tbv[:, 1, :], ALU.mult, [brek, tbk], [zik])
                    self.tt("dve", tc[:, 16:144], bim[:, sl], tbv[:, 0, :], ALU.mult, [bimk, tbk], [tck + "#x"])
                    self.tt("dve", zi[:, jj, sl], zi[:, jj, sl], tc[:, 16:144], ALU.add, [zik, tck + "#x"], [zik])
                self.tt("dve", zr[:, jj, :], zr[:, jj, :], tmp[:, 0:512], ALU.subtract, [zrk, tmpk], [zrk])
                self.unbank((bre, brek)); self.unbank((bim, bimk))
            for q in range(4):
                for jj in range(4):
                    j = 4 * c + jj
                    zk = "zc%d" % j
                    sl = slice(q * 128, (q + 1) * 128)
                    e = q * 128 + 127
                    self.scan(zor[:, jj, sl], self.onesf[:, 0:128], zr[:, jj, sl], self.zc[:, 0, j:j + 1], ["ones", zrk, zk], [zork + "#%d" % jj])
                    self.scan(zoi[:, jj, sl], self.onesf[:, 0:128], zi[:, jj, sl], self.zc[:, 1, j:j + 1], ["ones", zik, zk], [zoik + "#%d" % jj])
                    tcr = [zork + "#%d" % jj, zoik + "#%d" % jj, "el", tck + "#%d" % jj]
                    self.tt("dve", tc[:, 2 * jj:2 * jj + 1], zoi[:, jj, e:e + 1], self.el[:, 1, j:j + 1], ALU.mult, tcr, [tck + "#%d" % jj])
                    self.tt("dve", tc[:, 2 * jj + 1:2 * jj + 2], zor[:, jj, e:e + 1], self.el[:, 1, j:j + 1], ALU.mult, tcr, [tck + "#%d" % jj])
                    self.stt("dve", self.zc[:, 0, j:j + 1], zor[:, jj, e:e + 1], self.el[:, 0, j:j + 1], tc[:, 2 * jj:2 * jj + 1], ALU.mult, ALU.subtract, tcr, [zk])
                    self.stt("dve", self.zc[:, 1, j:j + 1], zoi[:, jj, e:e + 1], self.el[:, 0, j:j + 1], tc[:, 2 * jj + 1:2 * jj + 2], ALU.mult, ALU.add, tcr, [zk])
            yb, ybk = self.bank()
            for jj in range(4):
                tb, tbk = tabs[jj]
                tbv = tb[:, 0:512].rearrange("p (i t) -> p i t", i=4)
                Rr = tbv[:, 2:3, :].broadcast_to([128, 4, 128]); Ri = tbv[:, 3:4, :].broadcast_to([128, 4, 128])
                kr_ = [zork + "#%d" % jj, zoik + "#%d" % jj, tbk]
                p1, p1k = self.scr(); p2, p2k = self.scr()
                for q in range(4):
                    sl = slice(q * 128, (q + 1) * 128)
                    self.tt("pool", p1[:, sl], zor[:, jj, sl], tbv[:, 2, :], ALU.mult, kr_, [p1k])
                    self.tt("pool", p2[:, sl], zoi[:, jj, sl], tbv[:, 3, :], ALU.mult, kr_, [p2k])
                self.tt("pool", srv[:, jj, :], p1[:, 0:512], p2[:, 0:512], ALU.subtract, [p1k, p2k], [srek])
                for q in range(4):
                    sl = slice(q * 128, (q + 1) * 128)
                    self.tt("pool", p1[:, sl], zor[:, jj, sl], tbv[:, 3, :], ALU.mult, kr_ + [p1k], [p1k])
                    self.tt("pool", p2[:, sl], zoi[:, jj, sl], tbv[:, 2, :], ALU.mult, kr_ + [p2k], [p2k])
                self.tt("pool", p1[:, 0:512], p1[:, 0:512], p2[:, 0:512], ALU.add, [p1k, p2k], [p1k])
                self.ts("pool", nsv[:, jj, :], p1[:, 0:512], -1.0, None, ALU.mult, None, [p1k], [nsik])
                self.unscr((p1, p1k)); self.unscr((p2, p2k))
                self.mm(yb[:, :], csv[:, 0, jj, :], srv[:, jj, :], jj == 0, False, [csk, srek], [ybk])
                self.mm(yb[:, :], csv[:, 1, jj, :], nsv[:, jj, :], False, jj == 3, [csk, nsik], [ybk])
            self.stt("dve", yv[:, 0:512], usb[:, 0:512], self.pvc("s5_d", c), yb[:, :], ALU.mult, ALU.add, [usbk, "pv", ybk], [yvk])
            self.unbank((yb, ybk))
            self.gelu(yv[:, 0:512], yvk, ygv[:, c, :], ygk, tmp[:, 0:512], tmpk)
        wg, wgk = self.wload(self.wb["s5_w_glu"][l].rearrange("(k p) c -> p k c", p=128), [128, 4, 1024])
        for oc in range(4):
            za, zak = self.bank(); zb, zbk = self.bank()
            self.proj(za[:, :], wg, oc * 128, 128, 4, lambda kc: ygv[:, kc, :], [wgk, ygk], [zak])
            self.proj(zb[:, :], wg, 512 + oc * 128, 128, 4, lambda kc: ygv[:, kc, :], [wgk, ygk], [zbk])
            self.act(tmp[:, 0:512], zb[:, :], AF.Sigmoid, [zbk], [tmpk])
            self.tt("dve", self.ys[1][:, oc, :], za[:, :], tmp[:, 0:512], ALU.mult, [zak, tmpk], ["arB"])
            self.unbank((za, zak)); self.unbank((zb, zbk))
        for s_ in [(usb, usbk), (ubf, ubfk), (tmp, tmpk), (yv, yvk), (tc, tck), (bs, bsk), (cs, csk)] + tabs:
            self.unscr(s_)
        for s_ in [(yg, ygk), (sre, srek), (nsi, nsik)]:
            self.unscr(s_, 4)
        for z in Z:
            self.unscr(z, 8)

    def proj_shift(self, slab, slk, c0, m, mu_ap, ccol):
        hf = lambda kc: self.hT[:, kc, :]
        b, bk = self.bank()
        self.proj(b[0:m, :], slab, c0, m, 8, hf, [slk, "hT"], [bk])
        R, Rk = self.scr(4)
        self.cp("act", R[0:m, 1:513], b[0:m, :], [bk], [Rk])
        self.unbank((b, bk))
        ck = "carry%d" % ccol
        self.cp("dve", R[0:m, 0:1], self.carry[0:m, ccol:ccol + 1], [ck], [Rk])
        d, dk = self.scr()
        o, ok_ = self.scr()
        self.tt("dve", d[0:m, 0:512], R[0:m, 0:512], R[0:m, 1:513], ALU.subtract, [Rk], [dk])
        self.stt("dve", o[0:m, 0:512], d[0:m, 0:512], mu_ap, R[0:m, 1:513], ALU.mult, ALU.add, [dk, "pv", Rk], [ok_])
        self.cp("dve", self.carry[0:m, ccol:ccol + 1], R[0:m, 512:513], [Rk], [ck])
        self.unscr((R, Rk), 4); self.unscr((d, dk))
        return o, ok_

    def rwkv(self, l, ti):
        import os
        self.rstage = int(os.environ.get('RSTAGE', '99'))
        slA, slAk = self.win(3072)
        bfv = lambda t, n=256: t[:, 0:n].bitcast(BF16)
        wl, wlk = self.proj_shift(slA, slAk, 0, 64, self.pvc("mu_w", 0, 64), 12)
        twl, twlk = self.scr()
        self.act(bfv(twl)[0:64], wl[0:64, 0:512], AF.Tanh, [wlk], [twlk]); self.unscr((wl, wlk))
        al, alk = self.proj_shift(slA, slAk, 64, 64, self.pvc("mu_a", 0, 64), 13)
        alb, albk = self.scr()
        self.cp("dve", bfv(alb)[0:64], al[0:64, 0:512], [alk], [albk]); self.unscr((al, alk))
        gl, glk = self.proj_shift(slA, slAk, 128, 128, self.pvc("mu_g"), 14)
        sgl, sglk = self.scr()
        self.act(bfv(sgl), gl[:, 0:512], AF.Sigmoid, [glk], [sglk]); self.unscr((gl, glk))
        slr, slrk = self.win(1536); slk_, slkk = self.win(2048); slv, slvk = self.win(2560)
        m3 = lambda ap: ap.rearrange("p (q t) -> p q t", q=8)
        for cc in range(4):
            rm, rmk = self.proj_shift(slr, slrk, cc * 128, 128, self.pvc("mu_rkv", cc), cc)
            km, kmk = self.proj_shift(slk_, slkk, cc * 128, 128, self.pvc("mu_rkv", 4 + cc), 4 + cc)
            vm, vmk = self.proj_shift(slv, slvk, cc * 128, 128, self.pvc("mu_rkv", 8 + cc), 8 + cc)
            ew, ewk = self.scr(); cs, csk = self.scr(); t1, t1k = self.scr()
            b, bk = self.bank()
            self.mm(b[:, :], self.w2_sb[:, cc * 128:(cc + 1) * 128], bfv(twl)[0:64], True, True, ["w2_sb", twlk], [bk])
            self.act(ew[:, 0:512], b[:, :], AF.Exp, [bk, "der"], [ewk], scale=-1.0, bias=self.der[:, 4 + cc:5 + cc])
            self.unbank((b, bk))
            self.act(ew[:, 0:512], ew[:, 0:512], AF.Ln, [ewk], [ewk], bias=1.0)
            self.ts("dve", ew[:, 0:512], ew[:, 0:512], -1.0, -0.5, ALU.mult, ALU.add, [ewk], [ewk])
            self.act(ew[:, 0:512], ew[:, 0:512], AF.Exp, [ewk], [ewk])
            self.scan(cs[:, 0:512], self.reset64[:, :], ew[:, 0:512], 0.0, ["reset64", ewk], [csk])
            E1, E1k = self.scr(); E2, E2k = self.scr(); E3, E3k = self.scr(); Ex, Exk = self.scr()
            self.act(E1[:, 0:512], cs[:, 0:512], AF.Exp, [csk], [E1k], scale=-1.0)
            self.act(E2[:, 0:512], cs[:, 0:512], AF.Exp, [csk], [E2k])
            self.tt("dve", t1[:, 0:512], cs[:, 0:512], ew[:, 0:512], ALU.subtract, [csk, ewk], [t1k])
            self.act(Ex[:, 0:512], t1[:, 0:512], AF.Exp, [t1k], [Exk], scale=-1.0)
            self.tt("dve", m3(t1[:, 0:512]), m3(cs[:, 0:512])[:, :, 63:64].broadcast_to([128, 8, 64]), m3(cs[:, 0:512]), ALU.subtract, [csk], [t1k])
            self.act(E3[:, 0:512], t1[:, 0:512], AF.Exp, [t1k], [E3k], scale=-1.0)
            ag, agk = self.scr()
            b, bk = self.bank()
            self.mm(b[:, :], self.a2_sb[:, cc * 128:(cc + 1) * 128], bfv(alb)[0:64], True, True, ["a2_sb", albk], [bk])
            self.act(ag[:, 0:512], b[:, :], AF.Sigmoid, [bk, "pv"], [agk], bias=self.pvc("a0", cc))
            self.unbank((b, bk))
            gg, ggk = self.scr()
            b, bk = self.bank()
            self.mm(b[:, :], self.g2_sb[:, cc * 128:(cc + 1) * 128], bfv(sgl), True, True, ["g2_sb", sglk], [bk])
            self.cp("act", gg[:, 0:512], b[:, :], [bk], [ggk])
            self.unbank((b, bk))
            kk, kkk = self.scr()
            self.ts("dve", kk[:, 0:512], km[:, 0:512], self.pvc("k_k", cc), None, ALU.mult, None, [kmk, "pv"], [kkk])
            self.tt("dve", t1[:, 0:512], kk[:, 0:512], kk[:, 0:512], ALU.mult, [kkk], [t1k])
            b, bk = self.bank()
            self.mm(b[:, :], self.blk64[:, :], t1[:, 0:512], True, True, ["blk64", t1k], [bk])
            self.act(t1[:, 0:512], b[:, :], AF.Sqrt, [bk], [t1k], bias=1e-12)
            self.unbank((b, bk))
            self.recip(t1[:, 0:512], t1[:, 0:512], [t1k], [t1k])
            self.tt("dve", kk[:, 0:512], kk[:, 0:512], t1[:, 0:512], ALU.mult, [kkk, t1k], [kkk])
            self.ts("dve", t1[:, 0:512], ag[:, 0:512], self.pvc("k_a", cc), self.der[:, 8 + cc:9 + cc], ALU.mult, ALU.add, [agk, "pv", "der"], [t1k])
            self.tt("dve", km[:, 0:512], km[:, 0:512], t1[:, 0:512], ALU.mult, [kmk, t1k], [kmk])
            self.tt("dve", ag[:, 0:512], ag[:, 0:512], kk[:, 0:512], ALU.mult, [agk, kkk], [agk])
            AR, ARk = self.scr(); BT, BTk = self.scr(); KT, KTk = self.scr(); BH, BHk = self.scr(); KH, KHk = self.scr(); RK, RKk = self.scr()
            ARv = AR[:, :].bitcast(BF16).rearrange("p (q a t) -> p q a t", q=8, a=2)
            self.stt("dve", ARv[:, :, 0, :], m3(kk[:, 0:512]), -1.0, m3(Ex[:, 0:512]), ALU.mult, ALU.mult, [kkk, Exk], [ARk])
            self.tt("dve", ARv[:, :, 1, :], m3(rm[:, 0:512]), m3(E1[:, 0:512]), ALU.mult, [rmk, E1k], [ARk])
            self.tt("dve", bfv(BT), ag[:, 0:512], E2[:, 0:512], ALU.mult, [agk, E2k], [BTk])
            self.tt("dve", bfv(KT), km[:, 0:512], E2[:, 0:512], ALU.mult, [kmk, E2k], [KTk])
            self.tt("dve", bfv(BH), ag[:, 0:512], E3[:, 0:512], ALU.mult, [agk, E3k], [BHk])
            self.tt("dve", bfv(KH), km[:, 0:512], E3[:, 0:512], ALU.mult, [kmk, E3k], [KHk])
            self.stt("dve", bfv(RK), rm[:, 0:512], self.pvc("r_k", cc), km[:, 0:512], ALU.mult, ALU.mult, [rmk, "pv", kmk], [RKk])
            for s_ in [(rm, rmk), (km, kmk), (ew, ewk), (cs, csk), (t1, t1k), (E2, E2k), (E3, E3k), (Ex, Exk), (ag, agk), (kk, kkk)]:
                self.unscr(s_)
            if self.rstage <= 1:
                self.memset("pool", self.ys[2], 0.0, ["arB"]); return
            BHt, BHtk = self.scr(); KHt, KHtk = self.scr(); Vt, Vtk = self.scr(4); Vtb, Vtbk = self.scr()
            tm = lambda t: t[0:64, :].bitcast(BF16).rearrange("p (q c) -> p q c", q=8)
            for (src, srck, dst, dstk) in [(BH, BHk, BHt, BHtk), (KH, KHk, KHt, KHtk)]:
                b, bk = self.bank()
                bb = b[:, :].bitcast(BF16)
                for q in range(8):
                    self.trp(bb[0:64, q * 128:(q + 1) * 128], bfv(src)[:, q * 64:(q + 1) * 64], self.identb[:], [srck, "identb"], [bk])
                self.cp("act", tm(dst), bb[0:64, 0:1024].rearrange("p (q c) -> p q c", q=8), [bk], [dstk])
                self.unbank((b, bk))
            Vtv = Vt[0:64, :].rearrange("p (q c) -> p q c", q=8)
            for hf_ in range(2):
                b, bk = self.bank()
                for q4_ in range(4):
                    q = hf_ * 4 + q4_
                    self.trp(b[0:64, q4_ * 128:(q4_ + 1) * 128], vm[:, q * 64:(q + 1) * 64], self.identf[:], [vmk, "ident"], [bk])
                self.cp("act", Vtv[:, hf_ * 4:hf_ * 4 + 4, :], b[0:64, :].rearrange("p (q c) -> p q c", q=4), [bk], [Vtk])
                self.unbank((b, bk))
            self.cp("dve", tm(Vtb), Vtv, [Vtk], [Vtbk])
            self.unscr((vm, vmk))
            if self.rstage <= 2:
                self.memset("pool", self.ys[2], 0.0, ["arB"]); return
            mk = lambda i: self.rmask[:, i:i + 1, :].broadcast_to([64, 8, 64])
            NP = [self.scr() for _ in range(2)]; XP = [self.scr() for _ in range(2)]; PP = [self.scr() for _ in range(2)]; QQ = [self.scr() for _ in range(2)]
            ARB, ARBk = self.scr(); AAK, AAKk = self.scr(); ARK, ARKk = self.scr()
            mt = lambda t: t[0:64, :].bitcast(BF16).rearrange("p (m t) -> p m t", m=16)
            for hf_ in range(2):
                BN = [self.bank(), self.bank()]; BK = [self.bank(), self.bank()]; BX = [self.bank(), self.bank()]
                for q4_ in range(4):
                    q = hf_ * 4 + q4_
                    for e in range(2):
                        rows = slice(e * 64, (e + 1) * 64)
                        arhs = ARv[rows, q, :, :]
                        co = q4_ * 128
                        self.mm(BN[e][0][0:64, co:co + 128], bfv(BT)[rows, q * 64:(q + 1) * 64], arhs, True, True, [BTk, ARk], [BN[e][1]])
                        self.mm(BK[e][0][0:64, co:co + 128], bfv(KT)[rows, q * 64:(q + 1) * 64], arhs, True, True, [KTk, ARk], [BK[e][1]])
                        self.mm(BX[e][0][0:64, q4_ * 64:(q4_ + 1) * 64], ARv[rows, q, 0, :], bfv(BT)[rows, q * 64:(q + 1) * 64], True, True, [ARk, BTk], [BX[e][1]])
                v4 = lambda bnk: bnk[0:64, :].rearrange("p (m a t) -> p m a t", m=4, a=2)
                mk4 = lambda i: self.rmask[:, i:i + 1, :].broadcast_to([64, 4, 64])
                for e in range(2):
                    ms = slice(hf_ * 8 + e, hf_ * 8 + 8, 2)
                    self.tt("dve", mt(NP[0][0])[:, ms, :], v4(BN[e][0])[:, :, 0, :], mk4(0), ALU.mult, [BN[e][1], "rmask"], [NP[0][1]])
                    self.tt("dve", mt(ARB)[:, ms, :], v4(BN[e][0])[:, :, 1, :], mk4(1), ALU.mult, [BN[e][1], "rmask"], [ARBk])
                    self.tt("dve", mt(AAK)[:, ms, :], v4(BK[e][0])[:, :, 0, :], mk4(0), ALU.mult, [BK[e][1], "rmask"], [AAKk])
                    self.tt("dve", mt(ARK)[:, ms, :], v4(BK[e][0])[:, :, 1, :], mk4(1), ALU.mult, [BK[e][1], "rmask"], [ARKk])
                    self.tt("dve", mt(XP[0][0])[:, ms, :], BX[e][0][0:64, 0:256].rearrange("p (m t) -> p m t", m=4), mk4(2), ALU.mult, [BX[e][1], "rmask"], [XP[0][1]])
                for bb_ in BN + BK + BX:
                    self.unbank(bb_)
            if self.rstage <= 3:
                self.memset("pool", self.ys[2], 0.0, ["arB"]); return
            idb = self.identb[0:64, 0:64].unsqueeze(1).broadcast_to([64, 16, 64])
            self.tt("dve", mt(PP[0][0]), mt(NP[0][0]), idb, ALU.add, [NP[0][1], "identb"], [PP[0][1]])
            self.tt("dve", mt(QQ[0][0]), mt(XP[0][0]), idb, ALU.add, [XP[0][1], "identb"], [QQ[0][1]])
            for k in range(5):
                c_, n_ = k % 2, (k + 1) % 2
                for hf_ in range(2):
                    hs = slice(hf_ * 8, hf_ * 8 + 8)
                    bn_, bnk_ = self.bank(); bx_, bxk_ = self.bank()
                    for m8 in range(8):
                        m = hf_ * 8 + m8
                        self.mm(bn_[0:64, m8 * 64:(m8 + 1) * 64], mt(XP[c_][0])[:, m, :], mt(NP[c_][0])[:, m, :], True, True, [XP[c_][1], NP[c_][1]], [bnk_])
                        if k < 4:
                            self.mm(bx_[0:64, m8 * 64:(m8 + 1) * 64], mt(NP[c_][0])[:, m, :], mt(XP[c_][0])[:, m, :], True, True, [XP[c_][1], NP[c_][1]], [bxk_])
                    self.cp("act", mt(NP[n_][0])[:, hs, :], bn_[0:64, :].rearrange("p (m t) -> p m t", m=8), [bnk_], [NP[n_][1] + "#%d" % hf_])
                    if k < 4:
                        self.cp("act", mt(XP[n_][0])[:, hs, :], bx_[0:64, :].rearrange("p (m t) -> p m t", m=8), [bxk_], [XP[n_][1] + "#%d" % hf_])
                    self.unbank((bn_, bnk_)); self.unbank((bx_, bxk_))
                for hf_ in range(2):
                    hs = slice(hf_ * 8, hf_ * 8 + 8)
                    bp_, bpk_ = self.bank(); bq_, bqk_ = self.bank()
                    for m8 in range(8):
                        m = hf_ * 8 + m8
                        self.mm(bp_[0:64, m8 * 64:(m8 + 1) * 64], mt(QQ[c_][0])[:, m, :], mt(NP[n_][0])[:, m, :], True, True, [QQ[c_][1], NP[n_][1] + "#%d" % hf_], [bpk_])
                        if k < 4:
                            self.mm(bq_[0:64, m8 * 64:(m8 + 1) * 64], mt(PP[c_][0])[:, m, :], mt(XP[n_][0])[:, m, :], True, True, [PP[c_][1], XP[n_][1] + "#%d" % hf_], [bqk_])
                    self.tt("dve", mt(PP[n_][0])[:, hs, :], bp_[0:64, :].rearrange("p (m t) -> p m t", m=8), mt(PP[c_][0])[:, hs, :], ALU.add, [bpk_, PP[c_][1]], [PP[n_][1] + "#%d" % hf_])
                    if k < 4:
                        self.tt("dve", mt(QQ[n_][0])[:, hs, :], bq_[0:64, :].rearrange("p (m t) -> p m t", m=8), mt(QQ[c_][0])[:, hs, :], ALU.add, [bqk_, QQ[c_][1]], [QQ[n_][1] + "#%d" % hf_])
                    self.unbank((bp_, bpk_)); self.unbank((bq_, bqk_))
            PF, PFk = PP[1]
            if self.rstage <= 4:
                self.memset("pool", self.ys[2], 0.0, ["arB"]); return
            ytm, ytmk = self.scr(4)
            ytv = ytm[0:64, :].rearrange("p (q c) -> p q c", q=8)
            U = [self.scr(), self.scr()]
            ub = lambda i: U[i][0][0:64, 0:64].bitcast(BF16)
            s0k = "s0_%d" % cc
            for q in range(8):
                b, bk = self.bank()
                self.mm(b[0:64, 0:128], ARv[:, q, 0, :], self.s0bd[:, cc, :], True, False, [ARk, s0k], [bk])
                for e in range(2):
                    self.mm(b[0:64, e * 64:(e + 1) * 64], mt(AAK)[:, 2 * q + e, :], tm(Vtb)[:, q, e * 64:(e + 1) * 64], False, e == 1, [AAKk, Vtbk], [bk])
                self.cp("dve", ub(0), b[0:64, 0:128], [bk], [U[0][1]])
                self.unbank((b, bk))
                b, bk = self.bank()
                for e in range(2):
                    self.mm(b[0:64, e * 64:(e + 1) * 64], mt(PF)[:, 2 * q + e, :], ub(0)[:, e * 64:(e + 1) * 64], True, True, [PFk, U[0][1]], [bk])
                self.cp("dve", ub(1), b[0:64, 0:128], [bk], [U[1][1]])
                self.unbank((b, bk))
                cur = 1
                uf, ufk = ub(cur), U[cur][1]
                b, bk = self.bank()
                self.mm(b[0:64, 0:128], ARv[:, q, 1, :], self.s0bd[:, cc, :], True, False, [ARk, s0k], [bk])
                for e in range(2):
                    self.mm(b[0:64, e * 64:(e + 1) * 64], mt(ARB)[:, 2 * q + e, :], uf[:, e * 64:(e + 1) * 64], False, False, [ARBk, ufk], [bk])
                    self.mm(b[0:64, e * 64:(e + 1) * 64], mt(ARK)[:, 2 * q + e, :], tm(Vtb)[:, q, e * 64:(e + 1) * 64], False, e == 1, [ARKk, Vtbk], [bk])
                self.cp("act", ytv[:, q, :], b[0:64, 0:128], [bk], [ytmk])
                self.unbank((b, bk))
                b, bk = self.bank()
                self.mm(b[:, 0:128], tm(BHt)[:, q, :], uf, True, False, [BHtk, ufk], [bk])
                self.mm(b[:, 0:128], tm(KHt)[:, q, :], tm(Vtb)[:, q, :], False, True, [KHtk, Vtbk], [bk])
                for e in range(2):
                    rows = slice(e * 64, (e + 1) * 64)
                    self.stt("dve", self.s0f[rows, cc, :], self.s0f[rows, cc, :], E1[rows, q * 64 + 63:q * 64 + 64], b[rows, e * 64:(e + 1) * 64], ALU.mult, ALU.add, [s0k + "f", E1k, bk], [s0k + "f"])
                    self.cp("dve", self.s0bd[rows, cc, e * 64:(e + 1) * 64], self.s0f[rows, cc, :], [s0k + "f"], [s0k])
                self.unbank((b, bk))
            if self.rstage <= 5:
                self.memset("pool", self.ys[2], 0.0, ["arB"]); return
            g16 = lambda ap: ap.rearrange("p (g v) -> p g v", g=16)
            yv = g16(ytm[0:64, :])
            st_, stk = self.scr(); sq, sqk = self.scr(4)
            self.P.op("dve", lambda e, o=st_[0:64, 0:16], i=yv: e.tensor_reduce(out=o, in_=i, axis=AX.X, op=ALU.add), [ytmk], [stk])
            self.tt("dve", sq[0:64, :], ytm[0:64, :], ytm[0:64, :], ALU.mult, [ytmk], [sqk])
            self.P.op("dve", lambda e, o=st_[0:64, 16:32], i=g16(sq[0:64, :]): e.tensor_reduce(out=o, in_=i, axis=AX.X, op=ALU.add), [sqk], [stk])
            self.ts("dve", st_[0:64, 0:32], st_[0:64, 0:32], 1.0 / 64, None, ALU.mult, None, [stk], [stk])
            self.tt("dve", st_[0:64, 32:48], st_[0:64, 0:16], st_[0:64, 0:16], ALU.mult, [stk], [stk])
            self.tt("dve", st_[0:64, 16:32], st_[0:64, 16:32], st_[0:64, 32:48], ALU.subtract, [stk], [stk])
            self.act(st_[0:64, 16:32], st_[0:64, 16:32], AF.Sqrt, [stk], [stk], bias=64e-5)
            self.recip(st_[0:64, 16:32], st_[0:64, 16:32], [stk], [stk])
            bc = lambda ap: ap.unsqueeze(2).broadcast_to([64, 16, 64])
            self.tt("dve", yv, yv, bc(st_[0:64, 0:16]), ALU.subtract, [ytmk, stk], [ytmk])
            self.tt("dve", yv, yv, bc(st_[0:64, 16:32]), ALU.mult, [ytmk, stk], [ytmk])
            lg = self.lnx[:, cc * 128:(cc + 1) * 128].unsqueeze(1).broadcast_to([64, 8, 128])
            lb = self.lnx[:, 512 + cc * 128:512 + (cc + 1) * 128].unsqueeze(1).broadcast_to([64, 8, 128])
            self.tt("dve", ytv, ytv, lg, ALU.mult, [ytmk, "lnx"], [ytmk])
            self.tt("dve", ytv, ytv, lb, ALU.add, [ytmk, "lnx"], [ytmk])
            b, bk = self.bank()
            for q in range(8):
                self.mm(b[0:64, q * 2:q * 2 + 2], bfv(RK)[:, q * 64:(q + 1) * 64], self.headselb[:, :], True, True, [RKk, "headselb"], [bk])
            self.cp("act", st_[0:64, 0:16], b[0:64, 0:16], [bk], [stk])
            self.unbank((b, bk))
            self.tt("dve", g16(sq[0:64, :]), g16(Vt[0:64, :]), bc(st_[0:64, 0:16]), ALU.mult, [Vtk, stk], [sqk])
            self.tt("dve", ytm[0:64, :], ytm[0:64, :], sq[0:64, :], ALU.add, [ytmk, sqk], [ytmk])
            yb_, ybk_ = self.scr()
            self.cp("dve", tm(yb_), ytv, [ytmk], [ybk_])
            b, bk = self.bank()
            bb = b[:, :].bitcast(BF16)
            for q in range(8):
                self.trp(bb[:, q * 64:(q + 1) * 64], tm(yb_)[:, q, :], self.identb[0:64, 0:64], [ybk_, "identb"], [bk])
            self.tt("dve", self.ys[2][:, cc, :], bb[:, 0:512], gg[:, 0:512], ALU.mult, [bk, ggk], ["arB"])
            self.unbank((b, bk))
            for s_ in [(E1, E1k), (gg, ggk),
                       (AR, ARk), (BT, BTk), (KT, KTk), (BH, BHk), (KH, KHk), (RK, RKk), (BHt, BHtk), (KHt, KHtk), (Vtb, Vtbk), (ARB, ARBk), (AAK, AAKk), (ARK, ARKk),
                       (st_, stk), (yb_, ybk_)] + NP + XP + PP + QQ + U:
                self.unscr(s_)
            for s_ in [(Vt, Vtk), (ytm, ytmk), (sq, sqk)]:
                self.unscr(s_, 4)
        for s_ in [(twl, twlk), (alb, albk), (sgl, sglk)]:
            self.unscr(s_)

    def rstd_bcast(self, dst, dstk, srcs, nfeat, nparts, ones_lhsT):
        sq, sqk = self.scr()
        b, bk = self.bank()
        for i, (ap, k) in enumerate(srcs):
            self.act(sq[0:ap.shape[0], 0:512], ap, AF.Square, [k], [sqk])
            self.mm(b[0:nparts, :], ones_lhsT(ap.shape[0]), sq[0:ap.shape[0], 0:512], i == 0, i == len(srcs) - 1, [sqk, "ones"], [bk])
        self.act(dst, b[0:nparts, :], AF.Sqrt, [bk], [dstk], scale=1.0 / nfeat, bias=EPS)
        self.recip(dst, dst, [dstk], [dstk])
        self.unbank((b, bk))
        self.unscr((sq, sqk))

    def qk_stages(self, pre, src, srck, gname, out, outk, post):
        st = {}
        S = list(pre)

        def s1():
            st["sq"] = self.scr(); st["rs"] = self.scr()
            self.act(st["sq"][0][0:96, 0:512], src, AF.Square, [srck], [st["sq"][1]])

        def s2():
            st["b"] = self.bank()
            self.mm(st["b"][0][0:96, :], self.onesf[0:96, 0:96], st["sq"][0][0:96, 0:512], True, True, [st["sq"][1], "ones"], [st["b"][1]])

        def s3():
            self.act(st["rs"][0][0:96, 0:512], st["b"][0][0:96, :], AF.Sqrt, [st["b"][1]], [st["rs"][1]], scale=1.0 / 96, bias=EPS)
            self.unbank(st["b"]); self.unscr(st["sq"])

        def s4():
            self.recip(st["rs"][0][0:96, 0:512], st["rs"][0][0:96, 0:512], [st["rs"][1]], [st["rs"][1]])

        def s5():
            self.stt("dve", src, src, self.pvc(gname, 0, 96), st["rs"][0][0:96, 0:512], ALU.mult, ALU.mult, [srck, "pv", st["rs"][1]], [srck])

        def s6():
            st["b"] = self.bank()
            self.mm(st["b"][0][0:96, :], self.prot[:, :], src, True, True, ["prot", srck], [st["b"][1]])

        def s7():
            self.tt("dve", st["rs"][0][0:96, 0:512], st["b"][0][0:96, :], self.rsin[0:96, 0:512], ALU.mult, [st["b"][1], self.rsink], [st["rs"][1]])
            self.unbank(st["b"])

        def s8():
            self.tt("pool", src, src, self.rcos[0:96, 0:512], ALU.mult, [srck, self.rcosk], [srck])

        def s9():
            self.tt("dve", out, src, st["rs"][0][0:96, 0:512], ALU.add, [srck, st["rs"][1]], [outk])
            self.unscr(st["rs"])
        return S + [s1, s2, s3, s4, s5, s6, s7, s8, s9] + list(post)

    @staticmethod
    def interleave(chains):
        n = max(len(c) for c in chains)
        for i in range(n):
            for c in chains:
                if i < len(c):
                    c[i]()

    def mla(self, l, ti):
        t0 = ti * TT
        hf = lambda kc: self.hT[:, kc, :]
        slA, slAk = self.win(3072)
        slB, slBk = self.win(3584, 160)
        (self.rcos, self.rcosk), (self.rsin, self.rsink) = self.scr(), self.scr()
        self.dma("sp", self.rcos[0:96, 0:512], self.cd["ropec"][:, t0:t0 + 512], [], [self.rcosk])
        self.dma("sp", self.rsin[0:96, 0:512], self.cd["ropes"][:, t0:t0 + 512], [], [self.rsink])
        cq = [self.scr() for _ in range(2)]
        for c in range(2):
            b, bk = self.bank()
            self.proj(b[:, :], slA, 256 + c * 128, 128, 8, hf, [slAk, "hT"], [bk])
            self.cp("act", cq[c][0][:, 0:512], b[:, :], [bk], [cq[c][1]])
            self.unbank((b, bk))
        rs, rsk = self.scr()
        self.rstd_bcast(rs[:, 0:512], rsk, [(cq[0][0][:, 0:512], cq[0][1]), (cq[1][0][:, 0:512], cq[1][1])], 256, 128, lambda n: self.onesf[:, :])
        cqn, cqnk = self.scr()
        cqnv = cqn[:, 0:512].bitcast(BF16).rearrange("p (c t) -> p c t", c=2)
        for c in range(2):
            self.stt("dve", cqnv[:, c, :], cq[c][0][:, 0:512], self.pvc("q_norm", c), rs[:, 0:512], ALU.mult, ALU.mult, [cq[c][1], "pv", rsk], [cqnk])
        ckv, ckvk = cq[0]
        b, bk = self.bank()
        self.proj(b[:, :], slB, 0, 128, 8, hf, [slBk, "hT"], [bk])
        self.cp("act", ckv[:, 0:512], b[:, :], [bk], [ckvk])
        self.unbank((b, bk))
        self.rstd_bcast(rs[:, 0:512], rsk, [(ckv[:, 0:512], ckvk)], 128, 128, lambda n: self.onesf[:, :])
        ckvn, ckvnk = self.scr()
        ckvnv = ckvn[:, 0:256].bitcast(BF16)
        self.stt("dve", ckvnv, ckv[:, 0:512], self.pvc("kv_norm"), rs[:, 0:512], ALU.mult, ALU.mult, [ckvk, "pv", rsk], [ckvnk])
        kr, krk = cq[1]
        b, bk = self.bank()
        self.proj(b[0:32, :], slB, 128, 32, 8, hf, [slBk, "hT"], [bk])
        self.cp("act", kr[64:96, 0:512], b[0:32, :], [bk], [krk])
        self.unbank((b, bk))
        self.unscr((rs, rsk))
        vt, vtk = self.scr(8)
        vtv = vt[:, 0:1040].bitcast(BF16).rearrange("p (s h e) -> p s h e", s=4, h=8)
        self.memset("pool", vtv[:, :, :, 64:65], 1.0, [vtk])
        wv = self.wukv_sb[:, :].rearrange("p (h e) -> p h e", h=8)[:, :, 64:128]
        for s in range(4):
            b, bk = self.bank()
            self.mm(b[:, :].rearrange("p (h e) -> p h e", h=8), ckvnv[:, s * 128:(s + 1) * 128], wv, True, True, [ckvnk, "wukv_sb"], [bk])
            self.cp("act" if s % 2 else "dve", vtv[:, s, :, 0:64], b[:, :].rearrange("p (h e) -> p h e", h=8), [bk], [vtk])
            self.unbank((b, bk))
        for h in range(8):
            self.dma("pool", self.vc[h, :, 4 * ti:4 * ti + 4, :], vtv[:, :, h, :], [vtk], ["vc"])
        self.unscr((vt, vtk), 8)
        def kchain(h, kt, ktk, kb_, kbk):
            kbv = kb_[0:96, 0:256].bitcast(BF16)
            st = {}

            def p1():
                st["b"] = self.bank()
                self.mm(st["b"][0][0:64, :], self.wukv_sb[:, h * 128:h * 128 + 64], ckvnv, True, True, ["wukv_sb", ckvnk], [st["b"][1]])

            def p2():
                self.cp("act", kt[0:64, 0:512], st["b"][0][0:64, :], [st["b"][1]], [ktk])
                self.unbank(st["b"])

            def p3():
                self.cp("pool", kt[64:96, 0:512], kr[64:96, 0:512], [krk], [ktk])

            def post():
                self.dma("pool", self.kc[h, :, t0:t0 + 512], kbv, [kbk], ["kc"])
            return self.qk_stages([p1, p2, p3], kt[0:96, 0:512], ktk, "qkn_k", kbv, kbk, [post])
        KB4 = [(self.scr(), self.scr()) for _ in range(4)]
        for hg in range(2):
            self.interleave([kchain(hg * 4 + i, KB4[i][0][0], KB4[i][0][1], KB4[i][1][0], KB4[i][1][1]) for i in range(4)])
        for (a_, b_) in KB4:
            self.unscr(a_); self.unscr(b_)
        nkt = 4 * (ti + 1)
        QB = [self.scr(), self.scr()]
        QF = [self.scr(), self.scr()]
        qbv_ = lambda i: QB[i][0][0:96, 0:256].bitcast(BF16)
        KBUF = [self.scr(8), self.scr(8)]
        VBUF = [self.scr(8), self.scr(8)]
        pts = [self.scr() for _ in range(3)]
        rl, rlk = self.scr()

        def qchain(h):
            i = h % 2
            qf, qfk = QF[i]
            st = {}

            def p1():
                st["b"] = self.bank()
                for c in range(2):
                    self.mm(st["b"][0][0:96, :], self.wuq_sb[:, c, h * 96:(h + 1) * 96], cqnv[:, c, :], c == 0, c == 1, ["wuq_sb", cqnk], [st["b"][1]])

            def p2():
                self.cp("act", qf[0:96, 0:512], st["b"][0][0:96, :], [st["b"][1]], [qfk])
                self.unbank(st["b"])

            def p3():
                kbv2 = KBUF[i][0][0:96, 0:2048].bitcast(BF16)
                vbv = VBUF[i][0][:, 0:1040].bitcast(BF16).rearrange("p (k e) -> p k e", k=32)
                self.dma("sp", kbv2[:, 0:nkt * 128], self.kc[h, :, 0:nkt * 128], ["kc"], [KBUF[i][1]])
                self.dma("sp", vbv[:, 0:nkt, :], self.vc[h, :, 0:nkt, :], ["vc"], [VBUF[i][1]])
            return self.qk_stages([p1, p2, p3], qf[0:96, 0:512], qfk, "qkn_q", qbv_(i), QB[i][1], [])
        for f in qchain(0):
            f()
        for h in range(8):
            i = h % 2
            qbv, qbk = qbv_(i), QB[i][1]
            kbv2 = KBUF[i][0][0:96, 0:2048].bitcast(BF16)
            kbufk = KBUF[i][1]
            vbv = VBUF[i][0][:, 0:1040].bitcast(BF16).rearrange("p (k e) -> p k e", k=32)
            vbufk = VBUF[i][1]
            nxt = qchain(h + 1) if h < 7 else []
            per = -(-len(nxt) // nkt) if nxt else 0
            ob, obk = self.bank()
            for k in range(nkt):
                sb_, sbk = self.bank()
                self.mm(sb_[:, :], kbv2[:, k * 128:(k + 1) * 128], qbv, True, True, [kbufk, qbk], [sbk])
                pt, ptk = pts[k % 3]
                ptv = pt[:, 0:256].bitcast(BF16)
                self.act(ptv, sb_[:, :], AF.Exp, [sbk], [ptk], scale=96.0 ** -0.5)
                self.unbank((sb_, sbk))
                if k >= 4 * ti:
                    self.tt("pool", ptv, ptv, self.amask[:, k - 4 * ti, :], ALU.mult, [ptk, "amask"], [ptk])
                self.mm(ob[0:65, :], vbv[:, k, :], ptv, k == 0, k == nkt - 1, [vbufk, ptk], [obk])
                for f in nxt[k * per:(k + 1) * per]:
                    f()
            self.recip(rl[64:65, 0:512], ob[64:65, :], [obk], [rlk])
            bc, bck = self.bank()
            self.mm(bc[0:64, :], self.onesf[64:65, 0:64], rl[64:65, 0:512], True, True, ["ones", rlk], [bck])
            self.cp("act", rl[0:64, 0:512], bc[0:64, :], [bck], [rlk])
            self.unbank((bc, bck))
            self.tt("dve", self.ys[3][:, h, :], ob[0:64, :], rl[0:64, 0:512], ALU.mult, [obk, rlk], ["arB"])
            self.unbank((ob, obk))
        for s_ in QB + QF:
            self.unscr(s_)
        for s_ in KBUF + VBUF:
            self.unscr(s_, 8)
        for s_ in pts + [(rl, rlk), (cqn, cqnk), (ckvn, ckvnk), cq[0], cq[1], (self.rcos, self.rcosk), (self.rsin, self.rsink)]:
            self.unscr(s_)


def _prep_inputs(inputs):
    pvec, fvec, lrug, bst, cst = host_layout(inputs)
    shared = {"pvec": pvec, "fvec": fvec,
              "lrug": lrug.reshape(DEPTH, -1, 1024), "bst": bst.reshape(DEPTH, -1, 1024), "cst": cst.reshape(DEPTH, -1, 1024)}
    for n in BIGW:
        shared[n] = np.ascontiguousarray(np.asarray(inputs[n], np.float32)).reshape(DEPTH, -1, 1024)
    for n, v in host_consts().items():
        shared["c_" + n] = v
    return shared


def run(inputs, L_RUN=DEPTH, T_RUN=T_FULL, n_cores=8, branches=(0, 1, 2, 3), dbg=False, dbg_tile=0, trace=False):
    inputs = {k: np.asarray(v) for k, v in inputs.items()}
    shared = _prep_inputs(inputs)
    nc = bass.Bass("TRN2", target_bir_lowering=False)
    kb = KB(nc, L_RUN, T_RUN, dbg=dbg)
    kb.dbg_tile = dbg_tile
    kb.build(branches=branches)
    in_maps = []
    for b in range(n_cores):
        m = dict(shared)
        m["x"] = np.ascontiguousarray(inputs["x"][b, :T_RUN].astype(np.float32))
        in_maps.append(m)
    res = run_bass_kernel_spmd(nc, in_maps, core_ids=list(range(n_cores)), trace=trace)
    return res


DEFAULT_BRANCHES = (0, 1, 2, 3)


def kernel(**inputs):
    res = run(inputs, branches=DEFAULT_BRANCHES)
    return np.stack([r["y"] for r in res.results], axis=0).astype(np.float32)
```

```python
import math
from contextlib import ExitStack
import numpy as np
import concourse.bass as bass
import concourse.mybir as mybir
from concourse.bass_utils import run_bass_kernel_spmd

F32 = mybir.dt.float32
BF16 = mybir.dt.bfloat16
AF = mybir.ActivationFunctionType
ALU = mybir.AluOpType
AX = mybir.AxisListType

D = 1024
T_FULL = 4096
DEPTH = 4
C = 512
D_IN = 7840
TT = 512
EPS = 1e-6
ENGS = ("pe", "act", "dve", "pool", "sp")
GELU_K = 1.5957691216057308


class Op:
    __slots__ = ("eng", "fn", "reads", "writes", "dma", "idx", "deps", "sig", "cnt", "sem", "semval")

    def __init__(self, eng, fn, reads, writes, dma):
        self.eng, self.fn, self.reads, self.writes, self.dma = eng, fn, reads, writes, dma
        self.deps = []
        self.sig = False
        self.cnt = 0
        self.sem = None
        self.semval = 0


class Prog:
    NDMA = 48

    def __init__(self, nc):
        self.nc = nc
        self.ops = []

    def op(self, eng, fn, reads=(), writes=(), dma=False):
        o = Op(eng, fn, tuple(reads), tuple(writes), dma)
        o.idx = len(self.ops)
        self.ops.append(o)
        return o

    def finalize(self):
        last_w, readers, children = {}, {}, {}
        dma_k = 0
        dma_last = [None] * self.NDMA
        alias = getattr(self, "alias", {})

        def expand(keys):
            out = []
            for k in keys:
                base, _, sub = k.partition("#")
                for s_ in alias.get(base, (base,)):
                    out.append((s_, sub))
            return out

        def related(s_, sub):
            if sub == "":
                return [(s_, "")] + [(s_, c) for c in children.get(s_, ())]
            return [(s_, sub), (s_, "")]

        for o in self.ops:
            deps = {}
            rd, wr = expand(o.reads), expand(o.writes)
            for (s_, sub) in rd + wr:
                if sub:
                    children.setdefault(s_, set()).add(sub)
            for (s_, sub) in rd:
                for kk in related(s_, sub):
                    w = last_w.get(kk)
                    if w is not None:
                        deps[w.idx] = (w, "raw")
            for (s_, sub) in wr:
                for kk in related(s_, sub):
                    w = last_w.get(kk)
                    if w is not None and w.idx not in deps:
                        deps[w.idx] = (w, "waw")
                    for r in readers.get(kk, ()):
                        if r.idx not in deps and r is not o:
                            deps[r.idx] = (r, "war")
            for kk in rd:
                readers.setdefault(kk, []).append(o)
            for (s_, sub) in wr:
                last_w[(s_, sub)] = o
                readers[(s_, sub)] = []
                if sub == "":
                    for c in children.get(s_, ()):
                        last_w[(s_, c)] = o
                        readers[(s_, c)] = []
            if o.dma:
                k = dma_k % self.NDMA
                dma_k += 1
                prev = dma_last[k]
                o.sem = k
                o.semval = (prev.semval if prev is not None else 0) + 16
                if prev is not None and prev.idx not in deps:
                    deps[prev.idx] = (prev, "raw")
                dma_last[k] = o
            for (p, kind) in deps.values():
                if p.dma:
                    o.deps.append(p)
                elif p.eng == o.eng and not o.dma:
                    if kind == "raw" and o.eng != "pe":
                        o.deps.append(p)
                        p.sig = True
                else:
                    o.deps.append(p)
                    p.sig = True
        cnt = {e: 0 for e in ENGS}
        for o in self.ops:
            if o.sig and not o.dma:
                cnt[o.eng] += 1
                o.cnt = cnt[o.eng]

    def emit(self, final_waits=()):
        nc = self.nc
        with ExitStack() as st:
            esem = {e: st.enter_context(nc.semaphore("s_" + e)) for e in ENGS}
            dsem = [st.enter_context(nc.semaphore("d%d" % i)) for i in range(self.NDMA)]
            block = st.enter_context(nc.Block())
            per = {e: [o for o in self.ops if o.eng == e] for e in ENGS}

            def run(e, engobj, extra_final=()):
                seen_e = {x: 0 for x in ENGS}
                seen_d = {}
                for o in per[e]:
                    for p in o.deps:
                        if p.dma:
                            if seen_d.get(p.sem, 0) < p.semval:
                                engobj.wait_ge(dsem[p.sem], p.semval)
                                seen_d[p.sem] = p.semval
                        elif seen_e[p.eng] < p.cnt:
                            engobj.wait_ge(esem[p.eng], p.cnt)
                            seen_e[p.eng] = p.cnt
                    ins = o.fn(engobj)
                    if o.dma:
                        ins.then_inc(dsem[o.sem], 16)
                    elif o.sig:
                        ins.then_inc(esem[o.eng], 1)
                for p in extra_final:
                    engobj.wait_ge(dsem[p.sem], p.semval)

            @block.tensor
            def _(e):
                run("pe", e)

            @block.scalar
            def _(e):
                run("act", e)

            @block.vector
            def _(e):
                run("dve", e)

            @block.gpsimd
            def _(e):
                run("pool", e)

            @block.sync
            def _(e):
                run("sp", e, extra_final=final_waits)


PV = {}
_o = 0
for _n, _w in [("conv_w", 16), ("conv_b", 4), ("gate_b", 8), ("lam", 4), ("s5_d", 4), ("mu_rkv", 12), ("w0", 4),
               ("a0", 4), ("k_k", 4), ("k_a", 4), ("r_k", 4), ("mu_w", 1), ("mu_a", 1), ("mu_g", 1), ("q_norm", 2),
               ("kv_norm", 1), ("qkn_q", 1), ("qkn_k", 1), ("s5_are", 16), ("s5_aim", 16), ("s5_ldt", 16)]:
    PV[_n] = (_o, _w)
    _o += _w
NPV = _o


def _chunks(v, n):
    return np.ascontiguousarray(v.reshape(n, 128).T)


def host_layout(inp):
    f = np.float32
    L = DEPTH
    pvec = np.zeros((L, 128, NPV), f)
    fvec = np.zeros((L, 4, 1024), f)
    lrug = np.zeros((L, 2, 4, 128, 128), f)
    bst = np.zeros((L, 2, 16, 128, 128), f)
    cst = np.zeros((L, 2, 16, 128, 128), f)
    for l in range(L):
        def put(name, arr):
            o, w = PV[name]
            pvec[l, :arr.shape[0], o:o + w] = arr
        put("conv_w", np.concatenate([_chunks(inp["lru_conv_w"][l, k], 4) for k in range(4)], axis=1))
        put("conv_b", _chunks(inp["lru_conv_b"][l], 4))
        put("gate_b", np.concatenate([_chunks(inp["lru_gate_b"][l, g], 4) for g in range(2)], axis=1))
        put("lam", _chunks(inp["lru_lambda"][l], 4))
        put("s5_d", _chunks(inp["s5_d"][l], 4))
        put("mu_rkv", np.concatenate([_chunks(inp["rwkv_mu_rkv"][l, j], 4) for j in range(3)], axis=1))
        put("w0", _chunks(inp["rwkv_w0"][l], 4))
        put("a0", _chunks(inp["rwkv_a0"][l], 4))
        put("k_k", _chunks(inp["rwkv_k_k"][l], 4))
        put("k_a", _chunks(inp["rwkv_k_a"][l], 4))
        put("r_k", _chunks(inp["rwkv_r_k"][l].reshape(-1), 4))
        put("mu_w", inp["rwkv_mu_w"][l].reshape(64, 1))
        put("mu_a", inp["rwkv_mu_a"][l].reshape(64, 1))
        put("mu_g", inp["rwkv_mu_g"][l].reshape(128, 1))
        put("q_norm", _chunks(inp["mla_q_norm"][l], 2))
        put("kv_norm", inp["mla_kv_norm"][l].reshape(128, 1))
        put("qkn_q", inp["mla_qk_norm_q"][l].reshape(96, 1))
        put("qkn_k", inp["mla_qk_norm_k"][l].reshape(96, 1))
        are = inp["s5_a_re"][l].reshape(16, 2, 64).transpose(1, 2, 0).reshape(128, 16)
        aim = inp["s5_a_im"][l].reshape(16, 2, 64).transpose(1, 2, 0).reshape(128, 16)
        ldt = np.broadcast_to(inp["s5_log_dt"][l].reshape(16, 2, 1), (16, 2, 64)).transpose(1, 2, 0).reshape(128, 16)
        put("s5_are", are)
        put("s5_aim", aim)
        put("s5_ldt", ldt)
        fvec[l, 0] = inp["norm_mix"][l]
        fvec[l, 1] = inp["norm_mlp"][l]
        fvec[l, 2, :512] = inp["rwkv_lnx_g"][l]
        fvec[l, 2, 512:] = inp["rwkv_lnx_b"][l]
        for g in range(2):
            for h in range(8):
                c, e = h // 2, h % 2
                lrug[l, g, c, e * 64:(e + 1) * 64, e * 64:(e + 1) * 64] = inp["lru_gate_w"][l, g, h]
        for ri, (bsrc, csrc) in enumerate([(inp["s5_b_re"][l], inp["s5_c_re"][l]), (inp["s5_b_im"][l], inp["s5_c_im"][l])]):
            for g in range(32):
                j, e = g // 2, g % 2
                r0 = 32 * (j % 4) + e * 16
                bst[l, ri, j, r0:r0 + 16, e * 64:(e + 1) * 64] = bsrc[g].T
                cst[l, ri, j, e * 64:(e + 1) * 64, r0:r0 + 16] = csrc[g].T
    return pvec, fvec, lrug, bst, cst


def host_consts():
    f = np.float32
    c = {}
    c["ident"] = np.eye(128, dtype=f)
    c["ones"] = np.ones((128, 128), f)
    blk = np.zeros((128, 128), f)
    blk[:64, :64] = 1
    blk[64:, 64:] = 1
    c["blk64"] = blk
    hs = np.zeros((128, 2), f)
    hs[:64, 0] = 1
    hs[64:, 1] = 1
    c["headsel"] = hs
    p = np.arange(128)[:, None]
    q = np.arange(512)[None, :]
    c["amask"] = np.stack([(q >= 128 * j + p) for j in range(4)], 1).astype(f)
    j = np.arange(64)[:, None]
    t = np.arange(64)[None, :]
    m = np.zeros((64, 3, 64), f)
    m[:, 0] = (t > j)
    m[:, 1] = (t >= j)
    m[:, 2] = (t < j)
    c["rmask"] = m
    rs = np.ones((128, 512), f)
    rs[:, ::64] = 0
    c["reset64"] = rs
    pos = np.arange(T_FULL, dtype=np.float64)
    inv = 10000.0 ** (-np.arange(0, 32, 2, dtype=np.float64) / 32)
    ang = pos[None, :] * inv[:, None]
    cos = np.ones((96, T_FULL), np.float64)
    sin = np.zeros((96, T_FULL), np.float64)
    cos[64:80] = np.cos(ang); cos[80:96] = np.cos(ang)
    sin[64:80] = np.sin(ang); sin[80:96] = np.sin(ang)
    c["ropec"] = cos.astype(f)
    c["ropes"] = sin.astype(f)
    pr = np.zeros((96, 96), f)
    for i in range(16):
        pr[80 + i, 64 + i] = -1.0
        pr[64 + i, 80 + i] = 1.0
    c["prot"] = pr
    return c


CONST_SHAPES = {"ident": [128, 128], "ones": [128, 128], "blk64": [128, 128], "headsel": [128, 2],
                "amask": [128, 4, 512], "rmask": [64, 3, 64], "reset64": [128, 512],
                "ropec": [96, T_FULL], "ropes": [96, T_FULL], "prot": [96, 96]}

BIGW = {"w_in": [D, D_IN], "s5_w_glu": [C, 2 * C], "w_branch": [4 * C, D], "w_out": [D, D], "w_ff1": [D, 4 * D],
        "w_ff2": [4 * D, D], "mla_w_uq": [256, 768], "mla_w_ukv": [128, 1024], "rwkv_w2": [64, C],
        "rwkv_a2": [64, C], "rwkv_g2": [128, C]}


class KB:
    def __init__(self, nc, L_RUN, T_RUN, dbg=False):
        self.nc, self.L, self.T, self.dbg = nc, L_RUN, T_RUN, dbg
        self.NT = T_RUN // TT
        self.P = Prog(nc)
        self.st = ExitStack()
        self.free_banks = []
        self.scr_free = {}
        self.scr_n = 0
        self.wk = 0
        self.outs = []

    def sb(self, name, shape, dt=F32):
        return self.st.enter_context(self.nc.sbuf_tensor(name, shape, dt))

    def bank(self):
        assert self.free_banks, "out of PSUM banks"
        return self.free_banks.pop(0)

    def unbank(self, b):
        self.free_banks.append(b)

    NSLOT = 40

    def scr(self, kb=2):
        n = kb // 2
        if not hasattr(self, "arena"):
            self.arena = self.sb("arena", [128, self.NSLOT * 512], F32)
            self.slot_used = [False] * self.NSLOT
            self.P.alias = {}
        for st in range(self.NSLOT - n + 1):
            if not any(self.slot_used[st:st + n]):
                for i in range(st, st + n):
                    self.slot_used[i] = True
                key = "arn_%d_%d" % (st, n)
                self.P.alias[key] = tuple("slot%d" % i for i in range(st, st + n))
                return (self.arena[:, st * 512:(st + n) * 512], key)
        raise AssertionError("out of scratch slots")

    def unscr(self, s, kb=2):
        _, st, n = s[1].split("_")
        for i in range(int(st), int(st) + int(n)):
            assert self.slot_used[i]
            self.slot_used[i] = False

    def mm(self, out, lhsT, rhs, start, stop, r, w):
        self.P.op("pe", lambda e: e.matmul(out, lhsT=lhsT, rhs=rhs, start=start, stop=stop), r, w)

    def trp(self, out, in_, ident, r, w):
        self.P.op("pe", lambda e: e.transpose(out=out, in_=in_, identity=ident), r, w)

    def act(self, out, in_, func, r, w, **kw):
        self.P.op("act", lambda e: e.activation(out=out, in_=in_, func=func, **kw), r, w)

    def tt(self, eng, out, in0, in1, op, r, w):
        self.P.op(eng, lambda e: e.tensor_tensor(out=out, in0=in0, in1=in1, op=op), r, w)

    def ts(self, eng, out, in0, s1, s2, op0, op1, r, w):
        if s2 is None:
            self.P.op(eng, lambda e: e.tensor_single_scalar(out=out, in_=in0, scalar=s1, op=op0), r, w)
        else:
            self.P.op(eng, lambda e: e.tensor_scalar(out=out, in0=in0, scalar1=s1, scalar2=s2, op0=op0, op1=op1), r, w)

    def stt(self, eng, out, in0, scalar, in1, op0, op1, r, w):
        self.P.op(eng, lambda e: e.scalar_tensor_tensor(out=out, in0=in0, scalar=scalar, in1=in1, op0=op0, op1=op1), r, w)

    def cp(self, eng, out, in_, r, w):
        if eng == "act":
            self.P.op("act", lambda e: e.copy(out=out, in_=in_), r, w)
        else:
            self.P.op(eng, lambda e: e.tensor_copy(out=out, in_=in_), r, w)

    def memset(self, eng, out, val, w):
        self.P.op(eng, lambda e: e.memset(out, val), [], w)

    def recip(self, out, in_, r, w):
        self.P.op("dve", lambda e: e.reciprocal(out=out, in_=in_), r, w)

    def scan(self, out, d0, d1, init, r, w):
        self.P.op("dve", lambda e: e.tensor_tensor_scan(out=out, data0=d0, data1=d1, initial=init, op0=ALU.mult, op1=ALU.add), r, w)

    def dma(self, eng, out, in_, r, w, **kw):
        return self.P.op(eng, lambda e: e.dma_start(out=out, in_=in_, **kw), r, w, dma=True)

    def wload(self, src, shape):
        i = self.wk % len(self.wring)
        self.wk += 1
        buf, key = self.wring[i]
        n = int(np.prod(shape[1:]))
        if len(shape) == 3:
            view = buf[:shape[0], 0:n].rearrange("p (k c) -> p k c", k=shape[1])
        else:
            view = buf[:shape[0], 0:n]
        self.dma("sp", view, src, [self.cur_wkey], [key])
        return view, key

    def gelu(self, src, srck, out, outk, tmp, tmpk):
        self.tt("dve", tmp, src, src, ALU.mult, [srck], [tmpk])
        self.ts("dve", tmp, tmp, 0.044715, 1.0, ALU.mult, ALU.add, [tmpk], [tmpk])
        self.tt("dve", tmp, tmp, src, ALU.mult, [tmpk, srck], [tmpk])
        self.act(tmp, tmp, AF.Sigmoid, [tmpk], [tmpk], scale=GELU_K)
        self.tt("dve", out, tmp, src, ALU.mult, [tmpk, srck], [outk])

    def build(self, branches=(0, 1, 2, 3)):
        nc, L, T = self.nc, self.L, self.T
        self.branches = branches
        dt = lambda name, shape, dty, kind: nc.dram_tensor(name, shape, dty, kind=kind).ap()
        self.x = dt("x", [T, D], F32, "ExternalInput")
        self.y = dt("y", [T, D], F32, "ExternalOutput")
        self.w32, self.wb = {}, {}
        for n, (r, c) in BIGW.items():
            self.w32[n] = dt(n, [DEPTH, r * c // 1024, 1024], F32, "ExternalInput")
            self.wb[n] = dt("b_" + n, [DEPTH, r, c], BF16, "Internal")
        self.pvec_d = dt("pvec", [DEPTH, 128, NPV], F32, "ExternalInput")
        self.fvec_d = dt("fvec", [DEPTH, 4, 1024], F32, "ExternalInput")
        for n, shp in [("lrug", [DEPTH, 8 * 128 * 128 // 1024, 1024]), ("bst", [DEPTH, 32 * 16, 1024]), ("cst", [DEPTH, 32 * 16, 1024])]:
            self.w32[n] = dt(n, shp, F32, "ExternalInput")
        self.wb["lrug"] = dt("b_lrug", [DEPTH, 8, 128, 128], BF16, "Internal")
        self.wb["bst"] = dt("b_bst", [DEPTH, 32, 128, 128], BF16, "Internal")
        self.wb["cst"] = dt("b_cst", [DEPTH, 32, 128, 128], BF16, "Internal")
        self.cd = {n: dt("c_" + n, s, F32, "ExternalInput") for n, s in CONST_SHAPES.items()}
        self.kc = dt("kcache", [8, 96, T], BF16, "Internal")
        self.vc = dt("vcache", [8, 128, T // 128, 65], BF16, "Internal")
        self.s5tab = dt("s5tab", [16, 128, 4, 128], F32, "Internal")
        if self.dbg:
            self.dbg_ys = dt("dbg_ys", [4, 128, 8, TT], F32, "ExternalOutput")

        for i in range(8):
            t = self.st.enter_context(nc.psum_tensor("bank%d" % i, [128, 512], F32))
            self.free_banks.append((t, "bank%d" % i))
        sb = self.sb
        self.identf = sb("identf", [128, 128]); self.identb = sb("identb", [128, 128], BF16)
        self.onesf = sb("onesf", [128, 128]); self.onesb = sb("onesb", [128, 128], BF16)
        self.blk64 = sb("blk64", [128, 128]); self.headsel = sb("headsel", [128, 2]); self.headselb = sb("headselb", [128, 2], BF16)
        self.amask = sb("amask", [128, 4, 512], BF16); self.rmask = sb("rmask", [64, 3, 64])
        self.reset64 = sb("reset64", [128, 512]); self.prot = sb("prot", [96, 96])
        amf, amfk = self.scr(8)
        for n, tl in [("ident", self.identf), ("ones", self.onesf), ("blk64", self.blk64), ("headsel", self.headsel),
                      ("rmask", self.rmask), ("reset64", self.reset64), ("prot", self.prot)]:
            self.dma("sp", tl[:], self.cd[n], [], [n])
        self.dma("sp", amf[:, 0:2048].rearrange("p (a b) -> p a b", a=4), self.cd["amask"], [], [amfk])
        self.cp("dve", self.amask[:], amf[:, 0:2048].rearrange("p (a b) -> p a b", a=4), [amfk], ["amask"])
        self.unscr((amf, amfk), 8)
        self.cp("dve", self.identb[:], self.identf[:], ["ident"], ["identb"])
        self.cp("dve", self.onesb[:], self.onesf[:], ["ones"], ["onesb"])
        self.cp("dve", self.headselb[:], self.headsel[:], ["headsel"], ["headselb"])
        self.wring = [(sb("wring%d" % i, [128, 4096], BF16), "wring%d" % i) for i in range(3)]
        self.xt = sb("xt", [128, 4, 1024]); self.hT = sb("hT", [128, 8, 512], BF16)
        self.gbc = sb("gbc", [128, 1024]); self.lnx = sb("lnx", [64, 1024])
        self.arA = sb("arA", [128, 4096]); self.arB = sb("arB", [128, 5120])
        self.hn = self.arA[:, 0:2048].bitcast(BF16).rearrange("p (s d) -> p s d", s=4)
        self.macc = self.arA[:, :].rearrange("p (c t) -> p c t", c=8)
        ysb = self.arB[:, :].bitcast(BF16)
        self.ys = [ysb[:, i * 2048:(i + 1) * 2048].rearrange("p (c t) -> p c t", c=4) for i in range(3)]
        self.ys.append(ysb[0:64, 6144:10240].rearrange("p (h t) -> p h t", h=8))
        self.a1T_lo = self.arA[:, :].bitcast(BF16).rearrange("p (c t) -> p c t", c=16)
        self.a1T_hi = ysb[:, 0:8192].rearrange("p (c t) -> p c t", c=16)
        self.pv = sb("pv", [128, NPV]); self.der = sb("der", [128, 16])
        self.lrug_sb = sb("lrug_sb", [128, 8, 128], BF16)
        self.w2_sb = sb("w2_sb", [64, 512], BF16); self.a2_sb = sb("a2_sb", [64, 512], BF16); self.g2_sb = sb("g2_sb", [128, 512], BF16)
        self.wuq_sb = sb("wuq_sb", [128, 2, 768], BF16); self.wukv_sb = sb("wukv_sb", [128, 1024], BF16)
        self.ss = sb("ss", [128, 8])
        self.xh = [sb("xh%d" % c, [128, 515]) for c in range(4)]
        self.hc = sb("hc", [128, 4])
        self.zc = sb("zc", [128, 2, 16]); self.el = sb("el", [128, 2, 16])
        self.carry = sb("carry", [128, 16])
        self.s0f = sb("s0f", [128, 4, 64]); self.s0bd = sb("s0bd", [128, 4, 128], BF16)

        for l in range(L):
            for n in list(BIGW) + ["lrug", "bst", "cst"]:
                dstv = self.wb[n][l]
                if n in ("lrug", "bst", "cst"):
                    dstv = dstv.rearrange("a r c -> (a r c)")
                else:
                    dstv = dstv.rearrange("r c -> (r c)")
                dstv = dstv.rearrange("(a b) -> a b", b=1024)
                self.dma("pool", dstv, self.w32[n][l], [], ["wb%d" % l])
        for l in range(L):
            self.layer(l)
        self.P.finalize()
        self.P.emit(final_waits=self.outs)
        self.st.close()

    def layer(self, l):
        self.l = l
        self.cur_wkey = "wb%d" % l
        wk = self.cur_wkey
        pv, der = self.pv, self.der
        self.dma("sp", pv[:], self.pvec_d[l], [], ["pv"])
        self.dma("sp", self.lnx[:], self.fvec_d[l, 2:3, :].broadcast_to([64, 1024]), [], ["lnx"])
        self.dma("sp", self.lrug_sb[:], self.wb["lrug"][l].rearrange("a r c -> r a c"), [wk], ["lrug_sb"])
        self.dma("sp", self.w2_sb[:], self.wb["rwkv_w2"][l], [wk], ["w2_sb"])
        self.dma("sp", self.a2_sb[:], self.wb["rwkv_a2"][l], [wk], ["a2_sb"])
        self.dma("sp", self.g2_sb[:], self.wb["rwkv_g2"][l], [wk], ["g2_sb"])
        self.dma("sp", self.wuq_sb[:], self.wb["mla_w_uq"][l].rearrange("(k p) c -> p k c", p=128), [wk], ["wuq_sb"])
        self.dma("sp", self.wukv_sb[:], self.wb["mla_w_ukv"][l], [wk], ["wukv_sb"])
        o = PV["lam"][0]
        self.act(der[:, 0:4], pv[:, o:o + 4], AF.Exp, ["pv"], ["der"], scale=-1.0)
        self.act(der[:, 0:4], der[:, 0:4], AF.Ln, ["der"], ["der"], bias=1.0)
        self.ts("dve", der[:, 0:4], der[:, 0:4], -8.0, None, ALU.mult, None, ["der"], ["der"])
        o = PV["w0"][0]
        self.ts("dve", der[:, 4:8], pv[:, o:o + 4], -1.0, None, ALU.mult, None, ["pv"], ["der"])
        o = PV["k_a"][0]
        self.ts("dve", der[:, 8:12], pv[:, o:o + 4], -1.0, 1.0, ALU.mult, ALU.add, ["pv"], ["der"])
        for c in range(4):
            self.memset("pool", self.xh[c][:, 0:3], 0.0, ["xh%d" % c])
        self.memset("pool", self.hc[:], 0.0, ["hc"])
        self.memset("pool", self.zc[:], 0.0, ["zc%d" % i for i in range(16)])
        self.memset("pool", self.carry[:], 0.0, ["carry%d" % i for i in range(16)])
        self.memset("pool", self.s0f[:], 0.0, ["s0_%df" % i for i in range(4)])
        self.memset("pool", self.s0bd[:], 0.0, ["s0_%d" % i for i in range(4)])
        if 1 in self.branches:
            self.s5_setup(l)
        for ti in range(self.NT):
            self.tile(l, ti)

    def pvc(self, name, i=0, rows=128):
        o = PV[name][0] + i
        return self.pv[0:rows, o:o + 1]

    def norm_T(self, gi):
        l = self.l
        self.dma("sp", self.gbc[:], self.fvec_d[l, gi:gi + 1, :].broadcast_to([128, 1024]), [], ["gbc"])
        junk, jk = self.scr(4)
        for s in range(4):
            self.act(junk[:, 0:1024], self.xt[:, s, :], AF.Square, ["xt"], [jk, "ss"], accum_out=self.ss[:, s:s + 1])
        self.unscr((junk, jk), 4)
        self.act(self.ss[:, 4:8], self.ss[:, 0:4], AF.Sqrt, ["ss"], ["ss"], scale=1.0 / D, bias=EPS)
        self.recip(self.ss[:, 4:8], self.ss[:, 4:8], ["ss"], ["ss"])
        for s in range(4):
            self.stt("dve", self.hn[:, s, :], self.xt[:, s, :], self.ss[:, 4 + s:5 + s], self.gbc[:], ALU.mult, ALU.mult,
                     ["xt", "ss", "gbc"], ["arA"])
        for kc in range(8):
            b, bk = self.bank()
            bb = b[:, :].bitcast(BF16)
            for s in range(4):
                self.trp(bb[:, s * 128:(s + 1) * 128], self.hn[:, s, kc * 128:(kc + 1) * 128], self.identb[:], ["arA", "identb"], [bk])
            self.cp("act" if kc % 2 else "dve", self.hT[:, kc, :], bb[:, 0:512], [bk], ["hT"])
            self.unbank((b, bk))

    def proj(self, out, slab, c0, m, nk, rhsf, r, w):
        for kc in range(nk):
            self.mm(out, slab[:, kc, c0:c0 + m], rhsf(kc), kc == 0, kc == nk - 1, r, w)

    def win(self, c0, n=512):
        return self.wload(self.wb["w_in"][self.l][:, c0:c0 + n].rearrange("(k p) c -> p k c", p=128), [128, 8, n])

    def tile(self, l, ti):
        t0 = ti * TT
        src = self.x if l == 0 else self.y
        self.dma("sp", self.xt[:], src[t0:t0 + TT, :].rearrange("(s p) d -> p s d", p=128), ["ydram"], ["xt"])
        self.norm_T(0)
        for bi, fn in enumerate([self.lru, self.s5, self.rwkv, self.mla]):
            if bi in self.branches:
                fn(l, ti)
            else:
                self.memset("pool", self.ys[bi], 0.0, ["arB"])
        if self.dbg and l == 0 and ti == self.dbg_tile:
            d, dk = self.scr(8)
            dv = d[:, :].rearrange("p (c t) -> p c t", c=4)
            for bi in range(4):
                np_ = 128 if bi < 3 else 64
                for hf_ in range(1 if bi < 3 else 2):
                    self.cp("dve", dv[0:np_], self.ys[bi][:, hf_ * 4:hf_ * 4 + 4, :], ["arB"], [dk])
                    self.outs.append(self.dma("sp", self.dbg_ys[bi, 0:np_, hf_ * 4:hf_ * 4 + 4, :], dv[0:np_], [dk], ["dbgout"]))
            self.unscr((d, dk), 8)
        self.merge(l, ti)
        self.mlp(l, ti)
        st = self.dma("pool", self.y[t0:t0 + TT, :].rearrange("(s p) d -> p s d", p=128), self.xt[:], ["xt"], ["ydram"])
        if l == self.L - 1:
            self.outs.append(st)

    def merge(self, l, ti):
        wbr = self.wb["w_branch"][l]
        SG = [self.scr() for _ in range(3)]
        TM = [self.scr() for _ in range(3)]
        it = 0
        for n in range(4):
            for half in range(2):
                if n < 3:
                    wsl, wslk = self.wload(wbr[n * 512:(n + 1) * 512, half * 512:(half + 1) * 512].rearrange("(k p) c -> p k c", p=128), [128, 4, 512])
                    nk = 4
                else:
                    wsl, wslk = self.wload(wbr[n * 512:(n + 1) * 512, half * 512:(half + 1) * 512].rearrange("(k p) c -> p k c", p=64), [64, 8, 512])
                    nk = 8
                gsl, gslk = self.win(3744 + n * 1024 + half * 512)
                for dc4 in range(4):
                    dc = half * 4 + dc4
                    sg, sgk = SG[it % 3]; tm, tmk = TM[it % 3]; it += 1
                    bg, bgk = self.bank()
                    self.proj(bg[:, :], gsl, dc4 * 128, 128, 8, lambda kc: self.hT[:, kc, :], [gslk, "hT"], [bgk])
                    self.act(sg[:, 0:512], bg[:, :], AF.Sigmoid, [bgk], [sgk])
                    self.unbank((bg, bgk))
                    bz, bzk = self.bank()
                    ysn = self.ys[n]
                    self.proj(bz[:, :], wsl, dc4 * 128, 128, nk, lambda kc: ysn[:, kc, :], [wslk, "arB"], [bzk])
                    mk_ = "arA#m%d" % dc
                    if n == 0:
                        self.tt("dve", self.macc[:, dc, :], bz[:, :], sg[:, 0:512], ALU.mult, [bzk, sgk], [mk_])
                    else:
                        self.tt("dve", tm[:, 0:512], bz[:, :], sg[:, 0:512], ALU.mult, [bzk, sgk], [tmk])
                        self.tt("dve", self.macc[:, dc, :], self.macc[:, dc, :], tm[:, 0:512], ALU.add, [mk_, tmk], [mk_])
                    self.unbank((bz, bzk))
        for s_ in SG + TM:
            self.unscr(s_)
        for dc in range(8):
            self.cp("act" if dc % 2 else "dve", self.hT[:, dc, :], self.macc[:, dc, :], ["arA"], ["hT"])
        for half in range(2):
            wsl, wslk = self.wload(self.wb["w_out"][l][:, half * 512:(half + 1) * 512].rearrange("(k p) c -> p k c", p=128), [128, 8, 512])
            for s in range(4):
                b, bk = self.bank()
                for kc in range(8):
                    self.mm(b[:, :], self.hT[:, kc, s * 128:(s + 1) * 128], wsl[:, kc, :], kc == 0, kc == 7, ["hT", wslk], [bk])
                self.tt("dve", self.xt[:, s, half * 512:(half + 1) * 512], self.xt[:, s, half * 512:(half + 1) * 512], b[:, :], ALU.add, ["xt", bk], ["xt"])
                self.unbank((b, bk))

    def mlp(self, l, ti):
        self.norm_T(1)
        R1 = [self.scr(), self.scr()]
        for sl in range(8):
            wsl, wslk = self.wload(self.wb["w_ff1"][l][:, sl * 512:(sl + 1) * 512].rearrange("(k p) c -> p k c", p=128), [128, 8, 512])
            for c4 in range(4):
                fc = sl * 4 + c4
                r1, r1k = R1[fc % 2]
                b, bk = self.bank()
                self.proj(b[:, :], wsl, c4 * 128, 128, 8, lambda kc: self.hT[:, kc, :], [wslk, "hT"], [bk])
                self.act(r1[:, 0:512], b[:, :], AF.Relu, [bk], [r1k])
                dst = self.a1T_lo[:, fc, :] if fc < 16 else self.a1T_hi[:, fc - 16, :]
                self.tt("dve", dst, r1[:, 0:512], r1[:, 0:512], ALU.mult, [r1k], ["arA" if fc < 16 else "arB"])
                self.unbank((b, bk))
        self.unscr(R1[0]); self.unscr(R1[1])
        for half in range(2):
            accs = [self.bank() for _ in range(4)]
            for g in range(4):
                wsl, wslk = self.wload(self.wb["w_ff2"][l][g * 1024:(g + 1) * 1024, half * 512:(half + 1) * 512].rearrange("(k p) c -> p k c", p=128), [128, 8, 512])
                for s in range(4):
                    for kc in range(8):
                        fc = g * 8 + kc
                        a = self.a1T_lo[:, fc, s * 128:(s + 1) * 128] if fc < 16 else self.a1T_hi[:, fc - 16, s * 128:(s + 1) * 128]
                        self.mm(accs[s][0][:, :], a, wsl[:, kc, :], fc == 0, fc == 31, ["arA", "arB", wslk], [accs[s][1]])
            for s in range(4):
                self.tt("dve", self.xt[:, s, half * 512:(half + 1) * 512], self.xt[:, s, half * 512:(half + 1) * 512], accs[s][0][:, :], ALU.add, ["xt", accs[s][1]], ["xt"])
                self.unbank(accs[s])

    def lru(self, l, ti):
        slx, slxk = self.win(0)
        slg, slgk = self.win(512)
        S = [self.scr() for _ in range(6)]
        (xc, xck), (rr, rrk), (ii, iik), (aa, aak), (t1, t1k), (hh, hhk) = S
        xcb, xcbk = self.scr()
        xcbv = xcb[:, 0:256].bitcast(BF16)
        hf = lambda kc: self.hT[:, kc, :]
        for c in range(4):
            xh, xhk = self.xh[c], "xh%d" % c
            b, bk = self.bank()
            self.proj(b[:, :], slx, c * 128, 128, 8, hf, [slxk, "hT"], [bk])
            self.cp("act", xh[:, 3:515], b[:, :], [bk], [xhk])
            self.unbank((b, bk))
            self.ts("dve", xc[:, 0:512], xh[:, 3:515], self.pvc("conv_w", 12 + c), self.pvc("conv_b", c), ALU.mult, ALU.add, [xhk, "pv"], [xck])
            for k in range(3):
                self.stt("dve", xc[:, 0:512], xh[:, k:k + 512], self.pvc("conv_w", 4 * k + c), xc[:, 0:512], ALU.mult, ALU.add, [xhk, "pv", xck], [xck])
            self.cp("pool", xh[:, 0:3], xh[:, 512:515], [xhk], [xhk])
            self.cp("dve", xcbv, xc[:, 0:512], [xck], [xcbk])
            for g, (dst, dstk) in enumerate([(rr, rrk), (ii, iik)]):
                b, bk = self.bank()
                self.mm(b[:, :], self.lrug_sb[:, g * 4 + c, :], xcbv, True, True, ["lrug_sb", xcbk], [bk])
                self.act(dst[:, 0:512], b[:, :], AF.Sigmoid, [bk, "pv"], [dstk], bias=self.pvc("gate_b", g * 4 + c))
                self.unbank((b, bk))
            self.act(aa[:, 0:512], rr[:, 0:512], AF.Exp, [rrk, "der"], [aak], scale=self.der[:, c:c + 1])
            self.tt("dve", t1[:, 0:512], aa[:, 0:512], aa[:, 0:512], ALU.mult, [aak], [t1k])
            self.ts("dve", t1[:, 0:512], t1[:, 0:512], -1.0, 1.0, ALU.mult, ALU.add, [t1k], [t1k])
            self.act(t1[:, 0:512], t1[:, 0:512], AF.Sqrt, [t1k], [t1k])
            self.tt("dve", ii[:, 0:512], ii[:, 0:512], xc[:, 0:512], ALU.mult, [iik, xck], [iik])
            self.tt("dve", t1[:, 0:512], t1[:, 0:512], ii[:, 0:512], ALU.mult, [t1k, iik], [t1k])
            self.scan(hh[:, 0:512], aa[:, 0:512], t1[:, 0:512], self.hc[:, c:c + 1], [aak, t1k, "hc"], [hhk])
            self.cp("dve", self.hc[:, c:c + 1], hh[:, 511:512], [hhk], ["hc"])
            b, bk = self.bank()
            self.proj(b[:, :], slg, c * 128, 128, 8, hf, [slgk, "hT"], [bk])
            self.cp("act", rr[:, 0:512], b[:, :], [bk], [rrk])
            self.unbank((b, bk))
            self.gelu(rr[:, 0:512], rrk, ii[:, 0:512], iik, t1[:, 0:512], t1k)
            self.tt("dve", self.ys[0][:, c, :], hh[:, 0:512], ii[:, 0:512], ALU.mult, [hhk, iik], ["arB"])
        for s_ in S:
            self.unscr(s_)
        self.unscr((xcb, xcbk))

    def s5_setup(self, l):
        sm, smk = self.scr()
        v = lambda i: sm[:, i * 16:(i + 1) * 16]
        pvs = lambda n: self.pv[:, PV[n][0]:PV[n][0] + 16]
        R, W = [smk, "pv"], [smk]
        tt = lambda o, a, b, op: self.tt("dve", o, a, b, op, R, W)
        self.act(v(0), pvs("s5_ldt"), AF.Exp, R, W)
        tt(v(1), pvs("s5_aim"), v(0), ALU.mult)
        tt(v(2), pvs("s5_are"), v(0), ALU.mult)
        self.act(v(3), v(2), AF.Exp, R, W)
        self.act(v(4), v(2), AF.Exp, R, W, scale=-1.0)
        self.act(v(5), v(1), AF.Sin, R, W, scale=1.0 / 16)
        self.ts("dve", v(6), v(1), 1.0 / 16, math.pi / 2, ALU.mult, ALU.add, R, W)
        self.act(v(6), v(6), AF.Sin, R, W)

        def csq(c, s_):
            tt(v(7), c, c, ALU.mult); tt(v(8), s_, s_, ALU.mult); tt(v(9), c, s_, ALU.mult)
            tt(c, v(7), v(8), ALU.subtract)
            self.ts("dve", s_, v(9), 2.0, None, ALU.mult, None, R, W)
        for _ in range(4):
            csq(v(6), v(5))
        tt(v(10), v(3), v(6), ALU.mult); tt(v(11), v(3), v(5), ALU.mult)
        tt(v(12), v(4), v(6), ALU.mult); tt(v(13), v(4), v(5), ALU.mult)
        self.ts("dve", v(13), v(13), -1.0, None, ALU.mult, None, R, W)
        self.ts("dve", v(14), v(10), -1.0, None, ALU.add, None, R, W)
        tt(v(7), pvs("s5_are"), pvs("s5_are"), ALU.mult); tt(v(8), pvs("s5_aim"), pvs("s5_aim"), ALU.mult)
        tt(v(7), v(7), v(8), ALU.add)
        self.recip(v(7), v(7), R, W)
        tt(v(8), v(14), pvs("s5_are"), ALU.mult); tt(v(9), v(11), pvs("s5_aim"), ALU.mult); tt(v(8), v(8), v(9), ALU.add)
        tt(v(15), v(8), v(7), ALU.mult)
        tt(v(8), v(11), pvs("s5_are"), ALU.mult); tt(v(9), v(14), pvs("s5_aim"), ALU.mult); tt(v(8), v(8), v(9), ALU.subtract)
        tt(v(14), v(8), v(7), ALU.mult)
        tabs = [self.scr(8) for _ in range(4)]
        tv = [t[0][:, :].rearrange("p (j t) -> p j t", j=16) for t in tabs]
        tk = [t[1] for t in tabs]
        tmp, tmpk = self.scr(8)
        tmv = tmp[:, :].rearrange("p (j t) -> p j t", j=16)
        self.cp("dve", tv[0][:, :, 0], v(15), R, [tk[0]]); self.cp("dve", tv[1][:, :, 0], v(14), R, [tk[1]])
        self.memset("dve", tv[2][:, :, 0], 1.0, [tk[2]]); self.memset("dve", tv[3][:, :, 0], 0.0, [tk[3]])
        for (ar_, ai_, pr, pi) in [(0, 1, v(12), v(13)), (2, 3, v(10), v(11))]:
            for k in range(7):
                n = 1 << k
                prb = pr.unsqueeze(2).broadcast_to([128, 16, n]) if n > 1 else pr.unsqueeze(2)
                pib = pi.unsqueeze(2).broadcast_to([128, 16, n]) if n > 1 else pi.unsqueeze(2)
                Ar, Ai = tv[ar_], tv[ai_]
                kk = [tk[ar_], tk[ai_], smk, tmpk]
                self.tt("dve", Ar[:, :, n:2 * n], Ar[:, :, 0:n], prb, ALU.mult, kk, [tk[ar_]])
                self.tt("dve", tmv[:, :, 0:n], Ai[:, :, 0:n], pib, ALU.mult, kk, [tmpk])
                self.tt("dve", Ar[:, :, n:2 * n], Ar[:, :, n:2 * n], tmv[:, :, 0:n], ALU.subtract, kk, [tk[ar_]])
                self.tt("dve", tmv[:, :, 0:n], Ar[:, :, 0:n], pib, ALU.mult, kk, [tmpk])
                self.tt("dve", Ai[:, :, n:2 * n], Ai[:, :, 0:n], prb, ALU.mult, kk, [tk[ai_]])
                self.tt("dve", Ai[:, :, n:2 * n], Ai[:, :, n:2 * n], tmv[:, :, 0:n], ALU.add, kk, [tk[ai_]])
                csq(pr, pi)
        self.cp("dve", self.el[:, 0, :], v(10), R, ["el"]); self.cp("dve", self.el[:, 1, :], v(11), R, ["el"])
        for i in range(4):
            self.dma("sp", self.s5tab[:, :, i, :].rearrange("j p t -> p j t"), tv[i], [tk[i]], ["s5tab"])
        for t in tabs:
            self.unscr(t, 8)
        self.unscr((tmp, tmpk), 8); self.unscr((sm, smk))

    def s5(self, l, ti):
        import os
        if os.environ.get('S5SKIP'):
            return
        hf = lambda kc: self.hT[:, kc, :]
        slu, sluk = self.win(1024)
        (usb, usbk), (ubf, ubfk), (tmp, tmpk), (yv, yvk), (tc, tck) = [self.scr() for _ in range(5)]
        ubv = ubf[:, 0:256].bitcast(BF16)
        (yg, ygk), (sre, srek), (nsi, nsik) = [self.scr(4) for _ in range(3)]
        ygv = yg[:, :].bitcast(BF16).rearrange("p (c t) -> p c t", c=4)
        srv = sre[:, :].bitcast(BF16).rearrange("p (c t) -> p c t", c=4)
        nsv = nsi[:, :].bitcast(BF16).rearrange("p (c t) -> p c t", c=4)
        Z = [self.scr(8) for _ in range(4)]
        zr, zi, zor, zoi = [z[0][:, :].rearrange("p (j t) -> p j t", j=4) for z in Z]
        zrk, zik, zork, zoik = [z[1] for z in Z]
        (bs, bsk), (cs, csk) = self.scr(), self.scr()
        bsv = bs[:, :].bitcast(BF16).rearrange("p (r j c) -> p r j c", r=2, j=4)
        csv = cs[:, :].bitcast(BF16).rearrange("p (r j c) -> p r j c", r=2, j=4)
        tabs = [self.scr() for _ in range(4)]
        q4 = lambda ap: ap.rearrange("p (q t) -> p q t", q=4)
        for c in range(4):
            b, bk = self.bank()
            self.proj(b[:, :], slu, c * 128, 128, 8, hf, [sluk, "hT"], [bk])
            self.cp("act", usb[:, 0:512], b[:, :], [bk], [usbk])
            self.cp("dve", ubv, usb[:, 0:512], [usbk], [ubfk])
            self.unbank((b, bk))
            for r in range(2):
                self.dma("sp", bsv[:, r], self.wb["bst"][l][r * 16 + 4 * c:r * 16 + 4 * c + 4].rearrange("a r c -> r a c"), [self.cur_wkey], [bsk])
                self.dma("sp", csv[:, r], self.wb["cst"][l][r * 16 + 4 * c:r * 16 + 4 * c + 4].rearrange("a r c -> r a c"), [self.cur_wkey], [csk])
            for jj in range(4):
                j = 4 * c + jj
                tb, tbk = tabs[jj]
                tbv = tb[:, 0:512].rearrange("p (i t) -> p i t", i=4)
                self.dma("sp", tbv, self.s5tab[j], ["s5tab"], [tbk])
                Fr = tbv[:, 0:1, :].broadcast_to([128, 4, 128]); Fi = tbv[:, 1:2, :].broadcast_to([128, 4, 128])
                bre, brek = self.bank(); bim, bimk = self.bank()
                self.mm(bre[:, :], bsv[:, 0, jj, :], ubv, True, True, [bsk, ubfk], [brek])
                self.mm(bim[:, :], bsv[:, 1, jj, :], ubv, True, True, [bsk, ubfk], [bimk])
                for q in range(4):
                    sl = slice(q * 128, (q + 1) * 128)
                    self.tt("dve", zr[:, jj, sl], bre[:, sl], tbv[:, 0, :], ALU.mult, [brek, tbk], [zrk])
                    self.tt("dve", tmp[:, sl], bim[:, sl], tbv[:, 1, :], ALU.mult, [bimk, tbk], [tmpk])
                    self.tt("dve", zi[:, jj, sl], bre[:, sl], tbv[:, 1, :], ALU.mult, [brek, tbk], [zik])
                    self.tt("dve", tc[:, 16:144], bim[:, sl], tbv[:, 0, :], ALU.mult, [bimk, tbk], [tck + "#x"])
                    self.tt("dve", zi[:, jj, sl], zi[:, jj, sl], tc[:, 16:144], ALU.add, [zik, tck + "#x"], [zik])
                self.tt("dve", zr[:, jj, :], zr[:, jj, :], tmp[:, 0:512], ALU.subtract, [zrk, tmpk], [zrk])
                self.unbank((bre, brek)); self.unbank((bim, bimk))
            for q in range(4):
                for jj in range(4):
                    j = 4 * c + jj
                    zk = "zc%d" % j
                    sl = slice(q * 128, (q + 1) * 128)
                    e = q * 128 + 127
                    self.scan(zor[:, jj, sl], self.onesf[:, 0:128], zr[:, jj, sl], self.zc[:, 0, j:j + 1], ["ones", zrk, zk], [zork + "#%d" % jj])
                    self.scan(zoi[:, jj, sl], self.onesf[:, 0:128], zi[:, jj, sl], self.zc[:, 1, j:j + 1], ["ones", zik, zk], [zoik + "#%d" % jj])
                    tcr = [zork + "#%d" % jj, zoik + "#%d" % jj, "el", tck + "#%d" % jj]
                    self.tt("dve", tc[:, 2 * jj:2 * jj + 1], zoi[:, jj, e:e + 1], self.el[:, 1, j:j + 1], ALU.mult, tcr, [tck + "#%d" % jj])
                    self.tt("dve", tc[:, 2 * jj + 1:2 * jj + 2], zor[:, jj, e:e + 1], self.el[:, 1, j:j + 1], ALU.mult, tcr, [tck + "#%d" % jj])
                    self.stt("dve", self.zc[:, 0, j:j + 1], zor[:, jj, e:e + 1], self.el[:, 0, j:j + 1], tc[:, 2 * jj:2 * jj + 1], ALU.mult, ALU.subtract, tcr, [zk])
                    self.stt("dve", self.zc[:, 1, j:j + 1], zoi[:, jj, e:e + 1], self.el[:, 0, j:j + 1], tc[:, 2 * jj + 1:2 * jj + 2], ALU.mult, ALU.add, tcr, [zk])
            yb, ybk = self.bank()
            for jj in range(4):
                tb, tbk = tabs[jj]
                tbv = tb[:, 0:512].rearrange("p (i t) -> p i t", i=4)
                Rr = tbv[:, 2:3, :].broadcast_to([128, 4, 128]); Ri = tbv[:, 3:4, :].broadcast_to([128, 4, 128])
                kr_ = [zork + "#%d" % jj, zoik + "#%d" % jj, tbk]
                p1, p1k = self.scr(); p2, p2k = self.scr()
                for q in range(4):
                    sl = slice(q * 128, (q + 1) * 128)
                    self.tt("dve", p1[:, sl], zor[:, jj, sl], tbv[:, 2, :], ALU.mult, kr_, [p1k])
                    self.tt("dve", p2[:, sl], zoi[:, jj, sl], tbv[:, 3, :], ALU.mult, kr_, [p2k])
                self.tt("dve", srv[:, jj, :], p1[:, 0:512], p2[:, 0:512], ALU.subtract, [p1k, p2k], [srek])
                for q in range(4):
                    sl = slice(q * 128, (q + 1) * 128)
                    self.tt("dve", p1[:, sl], zor[:, jj, sl], tbv[:, 3, :], ALU.mult, kr_ + [p1k], [p1k])
                    self.tt("dve", p2[:, sl], zoi[:, jj, sl], tbv[:, 2, :], ALU.mult, kr_ + [p2k], [p2k])
                self.tt("dve", p1[:, 0:512], p1[:, 0:512], p2[:, 0:512], ALU.add, [p1k, p2k], [p1k])
                self.ts("dve", nsv[:, jj, :], p1[:, 0:512], -1.0, None, ALU.mult, None, [p1k], [nsik])
                self.unscr((p1, p1k)); self.unscr((p2, p2k))
                self.mm(yb[:, :], csv[:, 0, jj, :], srv[:, jj, :], jj == 0, False, [csk, srek], [ybk])
                self.mm(yb[:, :], csv[:, 1, jj, :], nsv[:, jj, :], False, jj == 3, [csk, nsik], [ybk])
            self.stt("dve", yv[:, 0:512], usb[:, 0:512], self.pvc("s5_d", c), yb[:, :], ALU.mult, ALU.add, [usbk, "pv", ybk], [yvk])
            self.unbank((yb, ybk))
            self.gelu(yv[:, 0:512], yvk, ygv[:, c, :], ygk, tmp[:, 0:512], tmpk)
        wg, wgk = self.wload(self.wb["s5_w_glu"][l].rearrange("(k p) c -> p k c", p=128), [128, 4, 1024])
        for oc in range(4):
            za, zak = self.bank(); zb, zbk = self.bank()
            self.proj(za[:, :], wg, oc * 128, 128, 4, lambda kc: ygv[:, kc, :], [wgk, ygk], [zak])
            self.proj(zb[:, :], wg, 512 + oc * 128, 128, 4, lambda kc: ygv[:, kc, :], [wgk, ygk], [zbk])
            self.act(tmp[:, 0:512], zb[:, :], AF.Sigmoid, [zbk], [tmpk])
            self.tt("dve", self.ys[1][:, oc, :], za[:, :], tmp[:, 0:512], ALU.mult, [zak, tmpk], ["arB"])
            self.unbank((za, zak)); self.unbank((zb, zbk))
        for s_ in [(usb, usbk), (ubf, ubfk), (tmp, tmpk), (yv, yvk), (tc, tck), (bs, bsk), (cs, csk)] + tabs:
            self.unscr(s_)
        for s_ in [(yg, ygk), (sre, srek), (nsi, nsik)]:
            self.unscr(s_, 4)
        for z in Z:
            self.unscr(z, 8)

    def proj_shift(self, slab, slk, c0, m, mu_ap, ccol):
        hf = lambda kc: self.hT[:, kc, :]
        b, bk = self.bank()
        self.proj(b[0:m, :], slab, c0, m, 8, hf, [slk, "hT"], [bk])
        R, Rk = self.scr(4)
        self.cp("act", R[0:m, 1:513], b[0:m, :], [bk], [Rk])
        self.unbank((b, bk))
        ck = "carry%d" % ccol
        self.cp("dve", R[0:m, 0:1], self.carry[0:m, ccol:ccol + 1], [ck], [Rk])
        d, dk = self.scr()
        o, ok_ = self.scr()
        self.tt("dve", d[0:m, 0:512], R[0:m, 0:512], R[0:m, 1:513], ALU.subtract, [Rk], [dk])
        self.stt("dve", o[0:m, 0:512], d[0:m, 0:512], mu_ap, R[0:m, 1:513], ALU.mult, ALU.add, [dk, "pv", Rk], [ok_])
        self.cp("dve", self.carry[0:m, ccol:ccol + 1], R[0:m, 512:513], [Rk], [ck])
        self.unscr((R, Rk), 4); self.unscr((d, dk))
        return o, ok_

    def rwkv(self, l, ti):
        import os
        self.rstage = int(os.environ.get('RSTAGE', '99'))
        slA, slAk = self.win(3072)
        bfv = lambda t, n=256: t[:, 0:n].bitcast(BF16)
        wl, wlk = self.proj_shift(slA, slAk, 0, 64, self.pvc("mu_w", 0, 64), 12)
        twl, twlk = self.scr()
        self.act(bfv(twl)[0:64], wl[0:64, 0:512], AF.Tanh, [wlk], [twlk]); self.unscr((wl, wlk))
        al, alk = self.proj_shift(slA, slAk, 64, 64, self.pvc("mu_a", 0, 64), 13)
        alb, albk = self.scr()
        self.cp("dve", bfv(alb)[0:64], al[0:64, 0:512], [alk], [albk]); self.unscr((al, alk))
        gl, glk = self.proj_shift(slA, slAk, 128, 128, self.pvc("mu_g"), 14)
        sgl, sglk = self.scr()
        self.act(bfv(sgl), gl[:, 0:512], AF.Sigmoid, [glk], [sglk]); self.unscr((gl, glk))
        slr, slrk = self.win(1536); slk_, slkk = self.win(2048); slv, slvk = self.win(2560)
        m3 = lambda ap: ap.rearrange("p (q t) -> p q t", q=8)
        for cc in range(4):
            rm, rmk = self.proj_shift(slr, slrk, cc * 128, 128, self.pvc("mu_rkv", cc), cc)
            km, kmk = self.proj_shift(slk_, slkk, cc * 128, 128, self.pvc("mu_rkv", 4 + cc), 4 + cc)
            vm, vmk = self.proj_shift(slv, slvk, cc * 128, 128, self.pvc("mu_rkv", 8 + cc), 8 + cc)
            ew, ewk = self.scr(); cs, csk = self.scr(); t1, t1k = self.scr()
            b, bk = self.bank()
            self.mm(b[:, :], self.w2_sb[:, cc * 128:(cc + 1) * 128], bfv(twl)[0:64], True, True, ["w2_sb", twlk], [bk])
            self.act(ew[:, 0:512], b[:, :], AF.Exp, [bk, "der"], [ewk], scale=-1.0, bias=self.der[:, 4 + cc:5 + cc])
            self.unbank((b, bk))
            self.act(ew[:, 0:512], ew[:, 0:512], AF.Ln, [ewk], [ewk], bias=1.0)
            self.ts("dve", ew[:, 0:512], ew[:, 0:512], -1.0, -0.5, ALU.mult, ALU.add, [ewk], [ewk])
            self.act(ew[:, 0:512], ew[:, 0:512], AF.Exp, [ewk], [ewk])
            self.scan(cs[:, 0:512], self.reset64[:, :], ew[:, 0:512], 0.0, ["reset64", ewk], [csk])
            E1, E1k = self.scr(); E2, E2k = self.scr(); E3, E3k = self.scr(); Ex, Exk = self.scr()
            self.act(E1[:, 0:512], cs[:, 0:512], AF.Exp, [csk], [E1k], scale=-1.0)
            self.act(E2[:, 0:512], cs[:, 0:512], AF.Exp, [csk], [E2k])
            self.tt("dve", t1[:, 0:512], cs[:, 0:512], ew[:, 0:512], ALU.subtract, [csk, ewk], [t1k])
            self.act(Ex[:, 0:512], t1[:, 0:512], AF.Exp, [t1k], [Exk], scale=-1.0)
            self.tt("dve", m3(t1[:, 0:512]), m3(cs[:, 0:512])[:, :, 63:64].broadcast_to([128, 8, 64]), m3(cs[:, 0:512]), ALU.subtract, [csk], [t1k])
            self.act(E3[:, 0:512], t1[:, 0:512], AF.Exp, [t1k], [E3k], scale=-1.0)
            ag, agk = self.scr()
            b, bk = self.bank()
            self.mm(b[:, :], self.a2_sb[:, cc * 128:(cc + 1) * 128], bfv(alb)[0:64], True, True, ["a2_sb", albk], [bk])
            self.act(ag[:, 0:512], b[:, :], AF.Sigmoid, [bk, "pv"], [agk], bias=self.pvc("a0", cc))
            self.unbank((b, bk))
            gg, ggk = self.scr()
            b, bk = self.bank()
            self.mm(b[:, :], self.g2_sb[:, cc * 128:(cc + 1) * 128], bfv(sgl), True, True, ["g2_sb", sglk], [bk])
            self.cp("act", gg[:, 0:512], b[:, :], [bk], [ggk])
            self.unbank((b, bk))
            kk, kkk = self.scr()
            self.ts("dve", kk[:, 0:512], km[:, 0:512], self.pvc("k_k", cc), None, ALU.mult, None, [kmk, "pv"], [kkk])
            self.tt("dve", t1[:, 0:512], kk[:, 0:512], kk[:, 0:512], ALU.mult, [kkk], [t1k])
            b, bk = self.bank()
            self.mm(b[:, :], self.blk64[:, :], t1[:, 0:512], True, True, ["blk64", t1k], [bk])
            self.act(t1[:, 0:512], b[:, :], AF.Sqrt, [bk], [t1k], bias=1e-12)
            self.unbank((b, bk))
            self.recip(t1[:, 0:512], t1[:, 0:512], [t1k], [t1k])
            self.tt("dve", kk[:, 0:512], kk[:, 0:512], t1[:, 0:512], ALU.mult, [kkk, t1k], [kkk])
            self.ts("dve", t1[:, 0:512], ag[:, 0:512], self.pvc("k_a", cc), self.der[:, 8 + cc:9 + cc], ALU.mult, ALU.add, [agk, "pv", "der"], [t1k])
            self.tt("dve", km[:, 0:512], km[:, 0:512], t1[:, 0:512], ALU.mult, [kmk, t1k], [kmk])
            self.tt("dve", ag[:, 0:512], ag[:, 0:512], kk[:, 0:512], ALU.mult, [agk, kkk], [agk])
            AR, ARk = self.scr(); BT, BTk = self.scr(); KT, KTk = self.scr(); BH, BHk = self.scr(); KH, KHk = self.scr(); RK, RKk = self.scr()
            ARv = AR[:, :].bitcast(BF16).rearrange("p (q a t) -> p q a t", q=8, a=2)
            self.stt("dve", ARv[:, :, 0, :], m3(kk[:, 0:512]), -1.0, m3(Ex[:, 0:512]), ALU.mult, ALU.mult, [kkk, Exk], [ARk])
            self.tt("dve", ARv[:, :, 1, :], m3(rm[:, 0:512]), m3(E1[:, 0:512]), ALU.mult, [rmk, E1k], [ARk])
            self.tt("dve", bfv(BT), ag[:, 0:512], E2[:, 0:512], ALU.mult, [agk, E2k], [BTk])
            self.tt("dve", bfv(KT), km[:, 0:512], E2[:, 0:512], ALU.mult, [kmk, E2k], [KTk])
            self.tt("dve", bfv(BH), ag[:, 0:512], E3[:, 0:512], ALU.mult, [agk, E3k], [BHk])
            self.tt("dve", bfv(KH), km[:, 0:512], E3[:, 0:512], ALU.mult, [kmk, E3k], [KHk])
            self.stt("dve", bfv(RK), rm[:, 0:512], self.pvc("r_k", cc), km[:, 0:512], ALU.mult, ALU.mult, [rmk, "pv", kmk], [RKk])
            for s_ in [(rm, rmk), (km, kmk), (ew, ewk), (cs, csk), (t1, t1k), (E2, E2k), (E3, E3k), (Ex, Exk), (ag, agk), (kk, kkk)]:
                self.unscr(s_)
            if self.rstage <= 1:
                self.memset("pool", self.ys[2], 0.0, ["arB"]); return
            BHt, BHtk = self.scr(); KHt, KHtk = self.scr(); Vt, Vtk = self.scr(4); Vtb, Vtbk = self.scr()
            tm = lambda t: t[0:64, :].bitcast(BF16).rearrange("p (q c) -> p q c", q=8)
            for (src, srck, dst, dstk) in [(BH, BHk, BHt, BHtk), (KH, KHk, KHt, KHtk)]:
                b, bk = self.bank()
                bb = b[:, :].bitcast(BF16)
                for q in range(8):
                    self.trp(bb[0:64, q * 128:(q + 1) * 128], bfv(src)[:, q * 64:(q + 1) * 64], self.identb[:], [srck, "identb"], [bk])
                self.cp("act", tm(dst), bb[0:64, 0:1024].rearrange("p (q c) -> p q c", q=8), [bk], [dstk])
                self.unbank((b, bk))
            Vtv = Vt[0:64, :].rearrange("p (q c) -> p q c", q=8)
            for hf_ in range(2):
                b, bk = self.bank()
                for q4_ in range(4):
                    q = hf_ * 4 + q4_
                    self.trp(b[0:64, q4_ * 128:(q4_ + 1) * 128], vm[:, q * 64:(q + 1) * 64], self.identf[:], [vmk, "ident"], [bk])
                self.cp("act", Vtv[:, hf_ * 4:hf_ * 4 + 4, :], b[0:64, :].rearrange("p (q c) -> p q c", q=4), [bk], [Vtk])
                self.unbank((b, bk))
            self.cp("dve", tm(Vtb), Vtv, [Vtk], [Vtbk])
            self.unscr((vm, vmk))
            if self.rstage <= 2:
                self.memset("pool", self.ys[2], 0.0, ["arB"]); return
            mk = lambda i: self.rmask[:, i:i + 1, :].broadcast_to([64, 8, 64])
            NP = [self.scr() for _ in range(2)]; XP = [self.scr() for _ in range(2)]; PP = [self.scr() for _ in range(2)]; QQ = [self.scr() for _ in range(2)]
            ARB, ARBk = self.scr(); AAK, AAKk = self.scr(); ARK, ARKk = self.scr()
            mt = lambda t: t[0:64, :].bitcast(BF16).rearrange("p (m t) -> p m t", m=16)
            for hf_ in range(2):
                BN = [self.bank(), self.bank()]; BK = [self.bank(), self.bank()]; BX = [self.bank(), self.bank()]
                for q4_ in range(4):
                    q = hf_ * 4 + q4_
                    for e in range(2):
                        rows = slice(e * 64, (e + 1) * 64)
                        arhs = ARv[rows, q, :, :]
                        co = q4_ * 128
                        self.mm(BN[e][0][0:64, co:co + 128], bfv(BT)[rows, q * 64:(q + 1) * 64], arhs, True, True, [BTk, ARk], [BN[e][1]])
                        self.mm(BK[e][0][0:64, co:co + 128], bfv(KT)[rows, q * 64:(q + 1) * 64], arhs, True, True, [KTk, ARk], [BK[e][1]])
                        self.mm(BX[e][0][0:64, q4_ * 64:(q4_ + 1) * 64], ARv[rows, q, 0, :], bfv(BT)[rows, q * 64:(q + 1) * 64], True, True, [ARk, BTk], [BX[e][1]])
                v4 = lambda bnk: bnk[0:64, :].rearrange("p (m a t) -> p m a t", m=4, a=2)
                mk4 = lambda i: self.rmask[:, i:i + 1, :].broadcast_to([64, 4, 64])
                for e in range(2):
                    ms = slice(hf_ * 8 + e, hf_ * 8 + 8, 2)
                    self.tt("dve", mt(NP[0][0])[:, ms, :], v4(BN[e][0])[:, :, 0, :], mk4(0), ALU.mult, [BN[e][1], "rmask"], [NP[0][1]])
                    self.tt("dve", mt(ARB)[:, ms, :], v4(BN[e][0])[:, :, 1, :], mk4(1), ALU.mult, [BN[e][1], "rmask"], [ARBk])
                    self.tt("dve", mt(AAK)[:, ms, :], v4(BK[e][0])[:, :, 0, :], mk4(0), ALU.mult, [BK[e][1], "rmask"], [AAKk])
                    self.tt("dve", mt(ARK)[:, ms, :], v4(BK[e][0])[:, :, 1, :], mk4(1), ALU.mult, [BK[e][1], "rmask"], [ARKk])
                    self.tt("dve", mt(XP[0][0])[:, ms, :], BX[e][0][0:64, 0:256].rearrange("p (m t) -> p m t", m=4), mk4(2), ALU.mult, [BX[e][1], "rmask"], [XP[0][1]])
                for bb_ in BN + BK + BX:
                    self.unbank(bb_)
            if self.rstage <= 3:
                self.memset("pool", self.ys[2], 0.0, ["arB"]); return
            idb = self.identb[0:64, 0:64].unsqueeze(1).broadcast_to([64, 16, 64])
            self.tt("dve", mt(PP[0][0]), mt(NP[0][0]), idb, ALU.add, [NP[0][1], "identb"], [PP[0][1]])
            self.tt("dve", mt(QQ[0][0]), mt(XP[0][0]), idb, ALU.add, [XP[0][1], "identb"], [QQ[0][1]])
            for k in range(5):
                c_, n_ = k % 2, (k + 1) % 2
                for hf_ in range(2):
                    hs = slice(hf_ * 8, hf_ * 8 + 8)
                    bn_, bnk_ = self.bank(); bx_, bxk_ = self.bank()
                    for m8 in range(8):
                        m = hf_ * 8 + m8
                        self.mm(bn_[0:64, m8 * 64:(m8 + 1) * 64], mt(XP[c_][0])[:, m, :], mt(NP[c_][0])[:, m, :], True, True, [XP[c_][1], NP[c_][1]], [bnk_])
                        if k < 4:
                            self.mm(bx_[0:64, m8 * 64:(m8 + 1) * 64], mt(NP[c_][0])[:, m, :], mt(XP[c_][0])[:, m, :], True, True, [XP[c_][1], NP[c_][1]], [bxk_])
                    self.cp("act", mt(NP[n_][0])[:, hs, :], bn_[0:64, :].rearrange("p (m t) -> p m t", m=8), [bnk_], [NP[n_][1] + "#%d" % hf_])
                    if k < 4:
                        self.cp("act", mt(XP[n_][0])[:, hs, :], bx_[0:64, :].rearrange("p (m t) -> p m t", m=8), [bxk_], [XP[n_][1] + "#%d" % hf_])
                    self.unbank((bn_, bnk_)); self.unbank((bx_, bxk_))
                for hf_ in range(2):
                    hs = slice(hf_ * 8, hf_ * 8 + 8)
                    bp_, bpk_ = self.bank(); bq_, bqk_ = self.bank()
                    for m8 in range(8):
                        m = hf_ * 8 + m8
                        self.mm(bp_[0:64, m8 * 64:(m8 + 1) * 64], mt(QQ[c_][0])[:, m, :], mt(NP[n_][0])[:, m, :], True, True, [QQ[c_][1], NP[n_][1] + "#%d" % hf_], [bpk_])
                        if k < 4:
                            self.mm(bq_[0:64, m8 * 64:(m8 + 1) * 64], mt(PP[c_][0])[:, m, :], mt(XP[n_][0])[:, m, :], True, True, [PP[c_][1], XP[n_][1] + "#%d" % hf_], [bqk_])
                    self.tt("dve", mt(PP[n_][0])[:, hs, :], bp_[0:64, :].rearrange("p (m t) -> p m t", m=8), mt(PP[c_][0])[:, hs, :], ALU.add, [bpk_, PP[c_][1]], [PP[n_][1] + "#%d" % hf_])
                    if k < 4:
                        self.tt("dve", mt(QQ[n_][0])[:, hs, :], bq_[0:64, :].rearrange("p (m t) -> p m t", m=8), mt(QQ[c_][0])[:, hs, :], ALU.add, [bqk_, QQ[c_][1]], [QQ[n_][1] + "#%d" % hf_])
                    self.unbank((bp_, bpk_)); self.unbank((bq_, bqk_))
            PF, PFk = PP[1]
            if self.rstage <= 4:
                self.memset("pool", self.ys[2], 0.0, ["arB"]); return
            ytm, ytmk = self.scr(4)
            ytv = ytm[0:64, :].rearrange("p (q c) -> p q c", q=8)
            U = [self.scr(), self.scr()]
            ub = lambda i: U[i][0][0:64, 0:64].bitcast(BF16)
            s0k = "s0_%d" % cc
            for q in range(8):
                b, bk = self.bank()
                self.mm(b[0:64, 0:128], ARv[:, q, 0, :], self.s0bd[:, cc, :], True, False, [ARk, s0k], [bk])
                for e in range(2):
                    self.mm(b[0:64, e * 64:(e + 1) * 64], mt(AAK)[:, 2 * q + e, :], tm(Vtb)[:, q, e * 64:(e + 1) * 64], False, e == 1, [AAKk, Vtbk], [bk])
                self.cp("act", ub(0), b[0:64, 0:128], [bk], [U[0][1]])
                self.unbank((b, bk))
                b, bk = self.bank()
                for e in range(2):
                    self.mm(b[0:64, e * 64:(e + 1) * 64], mt(PF)[:, 2 * q + e, :], ub(0)[:, e * 64:(e + 1) * 64], True, True, [PFk, U[0][1]], [bk])
                self.cp("act", ub(1), b[0:64, 0:128], [bk], [U[1][1]])
                self.unbank((b, bk))
                cur = 1
                uf, ufk = ub(cur), U[cur][1]
                b, bk = self.bank()
                self.mm(b[0:64, 0:128], ARv[:, q, 1, :], self.s0bd[:, cc, :], True, False, [ARk, s0k], [bk])
                for e in range(2):
                    self.mm(b[0:64, e * 64:(e + 1) * 64], mt(ARB)[:, 2 * q + e, :], uf[:, e * 64:(e + 1) * 64], False, False, [ARBk, ufk], [bk])
                    self.mm(b[0:64, e * 64:(e + 1) * 64], mt(ARK)[:, 2 * q + e, :], tm(Vtb)[:, q, e * 64:(e + 1) * 64], False, e == 1, [ARKk, Vtbk], [bk])
                self.cp("act", ytv[:, q, :], b[0:64, 0:128], [bk], [ytmk])
                self.unbank((b, bk))
                b, bk = self.bank()
                self.mm(b[:, 0:128], tm(BHt)[:, q, :], uf, True, False, [BHtk, ufk], [bk])
                self.mm(b[:, 0:128], tm(KHt)[:, q, :], tm(Vtb)[:, q, :], False, True, [KHtk, Vtbk], [bk])
                for e in range(2):
                    rows = slice(e * 64, (e + 1) * 64)
                    self.stt("dve", self.s0f[rows, cc, :], self.s0f[rows, cc, :], E1[rows, q * 64 + 63:q * 64 + 64], b[rows, e * 64:(e + 1) * 64], ALU.mult, ALU.add, [s0k + "f", E1k, bk], [s0k + "f"])
                    self.cp("dve", self.s0bd[rows, cc, e * 64:(e + 1) * 64], self.s0f[rows, cc, :], [s0k + "f"], [s0k])
                self.unbank((b, bk))
            if self.rstage <= 5:
                self.memset("pool", self.ys[2], 0.0, ["arB"]); return
            g16 = lambda ap: ap.rearrange("p (g v) -> p g v", g=16)
            yv = g16(ytm[0:64, :])
            st_, stk = self.scr(); sq, sqk = self.scr(4)
            self.P.op("dve", lambda e, o=st_[0:64, 0:16], i=yv: e.tensor_reduce(out=o, in_=i, axis=AX.X, op=ALU.add), [ytmk], [stk])
            self.tt("dve", sq[0:64, :], ytm[0:64, :], ytm[0:64, :], ALU.mult, [ytmk], [sqk])
            self.P.op("dve", lambda e, o=st_[0:64, 16:32], i=g16(sq[0:64, :]): e.tensor_reduce(out=o, in_=i, axis=AX.X, op=ALU.add), [sqk], [stk])
            self.ts("dve", st_[0:64, 0:32], st_[0:64, 0:32], 1.0 / 64, None, ALU.mult, None, [stk], [stk])
            self.tt("dve", st_[0:64, 32:48], st_[0:64, 0:16], st_[0:64, 0:16], ALU.mult, [stk], [stk])
            self.tt("dve", st_[0:64, 16:32], st_[0:64, 16:32], st_[0:64, 32:48], ALU.subtract, [stk], [stk])
            self.act(st_[0:64, 16:32], st_[0:64, 16:32], AF.Sqrt, [stk], [stk], bias=64e-5)
            self.recip(st_[0:64, 16:32], st_[0:64, 16:32], [stk], [stk])
            bc = lambda ap: ap.unsqueeze(2).broadcast_to([64, 16, 64])
            self.tt("dve", yv, yv, bc(st_[0:64, 0:16]), ALU.subtract, [ytmk, stk], [ytmk])
            self.tt("dve", yv, yv, bc(st_[0:64, 16:32]), ALU.mult, [ytmk, stk], [ytmk])
            lg = self.lnx[:, cc * 128:(cc + 1) * 128].unsqueeze(1).broadcast_to([64, 8, 128])
            lb = self.lnx[:, 512 + cc * 128:512 + (cc + 1) * 128].unsqueeze(1).broadcast_to([64, 8, 128])
            self.tt("dve", ytv, ytv, lg, ALU.mult, [ytmk, "lnx"], [ytmk])
            self.tt("dve", ytv, ytv, lb, ALU.add, [ytmk, "lnx"], [ytmk])
            b, bk = self.bank()
            for q in range(8):
                self.mm(b[0:64, q * 2:q * 2 + 2], bfv(RK)[:, q * 64:(q + 1) * 64], self.headselb[:, :], True, True, [RKk, "headselb"], [bk])
            self.cp("act", st_[0:64, 0:16], b[0:64, 0:16], [bk], [stk])
            self.unbank((b, bk))
            self.tt("dve", g16(sq[0:64, :]), g16(Vt[0:64, :]), bc(st_[0:64, 0:16]), ALU.mult, [Vtk, stk], [sqk])
            self.tt("dve", ytm[0:64, :], ytm[0:64, :], sq[0:64, :], ALU.add, [ytmk, sqk], [ytmk])
            yb_, ybk_ = self.scr()
            self.cp("dve", tm(yb_), ytv, [ytmk], [ybk_])
            b, bk = self.bank()
            bb = b[:, :].bitcast(BF16)
            for q in range(8):
                self.trp(bb[:, q * 64:(q + 1) * 64], tm(yb_)[:, q, :], self.identb[0:64, 0:64], [ybk_, "identb"], [bk])
            self.tt("dve", self.ys[2][:, cc, :], bb[:, 0:512], gg[:, 0:512], ALU.mult, [bk, ggk], ["arB"])
            self.unbank((b, bk))
            for s_ in [(E1, E1k), (gg, ggk),
                       (AR, ARk), (BT, BTk), (KT, KTk), (BH, BHk), (KH, KHk), (RK, RKk), (BHt, BHtk), (KHt, KHtk), (Vtb, Vtbk), (ARB, ARBk), (AAK, AAKk), (ARK, ARKk),
                       (st_, stk), (yb_, ybk_)] + NP + XP + PP + QQ + U:
                self.unscr(s_)
            for s_ in [(Vt, Vtk), (ytm, ytmk), (sq, sqk)]:
                self.unscr(s_, 4)
        for s_ in [(twl, twlk), (alb, albk), (sgl, sglk)]:
            self.unscr(s_)

    def rstd_bcast(self, dst, dstk, srcs, nfeat, nparts, ones_lhsT):
        sq, sqk = self.scr()
        b, bk = self.bank()
        for i, (ap, k) in enumerate(srcs):
            self.act(sq[0:ap.shape[0], 0:512], ap, AF.Square, [k], [sqk])
            self.mm(b[0:nparts, :], ones_lhsT(ap.shape[0]), sq[0:ap.shape[0], 0:512], i == 0, i == len(srcs) - 1, [sqk, "ones"], [bk])
        self.act(dst, b[0:nparts, :], AF.Sqrt, [bk], [dstk], scale=1.0 / nfeat, bias=EPS)
        self.recip(dst, dst, [dstk], [dstk])
        self.unbank((b, bk))
        self.unscr((sq, sqk))

    def qk_stages(self, pre, src, srck, gname, out, outk, post):
        st = {}
        S = list(pre)

        def s1():
            st["sq"] = self.scr(); st["rs"] = self.scr()
            self.act(st["sq"][0][0:96, 0:512], src, AF.Square, [srck], [st["sq"][1]])

        def s2():
            st["b"] = self.bank()
            self.mm(st["b"][0][0:96, :], self.onesf[0:96, 0:96], st["sq"][0][0:96, 0:512], True, True, [st["sq"][1], "ones"], [st["b"][1]])

        def s3():
            self.act(st["rs"][0][0:96, 0:512], st["b"][0][0:96, :], AF.Sqrt, [st["b"][1]], [st["rs"][1]], scale=1.0 / 96, bias=EPS)
            self.unbank(st["b"]); self.unscr(st["sq"])

        def s4():
            self.recip(st["rs"][0][0:96, 0:512], st["rs"][0][0:96, 0:512], [st["rs"][1]], [st["rs"][1]])

        def s5():
            self.stt("dve", src, src, self.pvc(gname, 0, 96), st["rs"][0][0:96, 0:512], ALU.mult, ALU.mult, [srck, "pv", st["rs"][1]], [srck])

        def s6():
            st["b"] = self.bank()
            self.mm(st["b"][0][0:96, :], self.prot[:, :], src, True, True, ["prot", srck], [st["b"][1]])

        def s7():
            self.tt("dve", st["rs"][0][0:96, 0:512], st["b"][0][0:96, :], self.rsin[0:96, 0:512], ALU.mult, [st["b"][1], self.rsink], [st["rs"][1]])
            self.unbank(st["b"])

        def s8():
            self.tt("dve", src, src, self.rcos[0:96, 0:512], ALU.mult, [srck, self.rcosk], [srck])

        def s9():
            self.tt("dve", out, src, st["rs"][0][0:96, 0:512], ALU.add, [srck, st["rs"][1]], [outk])
            self.unscr(st["rs"])
        return S + [s1, s2, s3, s4, s5, s6, s7, s8, s9] + list(post)

    @staticmethod
    def interleave(chains):
        n = max(len(c) for c in chains)
        for i in range(n):
            for c in chains:
                if i < len(c):
                    c[i]()

    def mla(self, l, ti):
        t0 = ti * TT
        hf = lambda kc: self.hT[:, kc, :]
        slA, slAk = self.win(3072)
        slB, slBk = self.win(3584, 160)
        (self.rcos, self.rcosk), (self.rsin, self.rsink) = self.scr(), self.scr()
        self.dma("sp", self.rcos[0:96, 0:512], self.cd["ropec"][:, t0:t0 + 512], [], [self.rcosk])
        self.dma("sp", self.rsin[0:96, 0:512], self.cd["ropes"][:, t0:t0 + 512], [], [self.rsink])
        cq = [self.scr() for _ in range(2)]
        for c in range(2):
            b, bk = self.bank()
            self.proj(b[:, :], slA, 256 + c * 128, 128, 8, hf, [slAk, "hT"], [bk])
            self.cp("act", cq[c][0][:, 0:512], b[:, :], [bk], [cq[c][1]])
            self.unbank((b, bk))
        rs, rsk = self.scr()
        self.rstd_bcast(rs[:, 0:512], rsk, [(cq[0][0][:, 0:512], cq[0][1]), (cq[1][0][:, 0:512], cq[1][1])], 256, 128, lambda n: self.onesf[:, :])
        cqn, cqnk = self.scr()
        cqnv = cqn[:, 0:512].bitcast(BF16).rearrange("p (c t) -> p c t", c=2)
        for c in range(2):
            self.stt("dve", cqnv[:, c, :], cq[c][0][:, 0:512], self.pvc("q_norm", c), rs[:, 0:512], ALU.mult, ALU.mult, [cq[c][1], "pv", rsk], [cqnk])
        ckv, ckvk = cq[0]
        b, bk = self.bank()
        self.proj(b[:, :], slB, 0, 128, 8, hf, [slBk, "hT"], [bk])
        self.cp("act", ckv[:, 0:512], b[:, :], [bk], [ckvk])
        self.unbank((b, bk))
        self.rstd_bcast(rs[:, 0:512], rsk, [(ckv[:, 0:512], ckvk)], 128, 128, lambda n: self.onesf[:, :])
        ckvn, ckvnk = self.scr()
        ckvnv = ckvn[:, 0:256].bitcast(BF16)
        self.stt("dve", ckvnv, ckv[:, 0:512], self.pvc("kv_norm"), rs[:, 0:512], ALU.mult, ALU.mult, [ckvk, "pv", rsk], [ckvnk])
        kr, krk = cq[1]
        b, bk = self.bank()
        self.proj(b[0:32, :], slB, 128, 32, 8, hf, [slBk, "hT"], [bk])
        self.cp("act", kr[64:96, 0:512], b[0:32, :], [bk], [krk])
        self.unbank((b, bk))
        self.unscr((rs, rsk))
        vt, vtk = self.scr(8)
        vtv = vt[:, 0:1040].bitcast(BF16).rearrange("p (s h e) -> p s h e", s=4, h=8)
        self.memset("pool", vtv[:, :, :, 64:65], 1.0, [vtk])
        wv = self.wukv_sb[:, :].rearrange("p (h e) -> p h e", h=8)[:, :, 64:128]
        for s in range(4):
            b, bk = self.bank()
            self.mm(b[:, :].rearrange("p (h e) -> p h e", h=8), ckvnv[:, s * 128:(s + 1) * 128], wv, True, True, [ckvnk, "wukv_sb"], [bk])
            self.cp("act" if s % 2 else "dve", vtv[:, s, :, 0:64], b[:, :].rearrange("p (h e) -> p h e", h=8), [bk], [vtk])
            self.unbank((b, bk))
        for h in range(8):
            self.dma("pool", self.vc[h, :, 4 * ti:4 * ti + 4, :], vtv[:, :, h, :], [vtk], ["vc"])
        self.unscr((vt, vtk), 8)
        def kchain(h, kt, ktk, kb_, kbk):
            kbv = kb_[0:96, 0:256].bitcast(BF16)
            st = {}

            def p1():
                st["b"] = self.bank()
                self.mm(st["b"][0][0:64, :], self.wukv_sb[:, h * 128:h * 128 + 64], ckvnv, True, True, ["wukv_sb", ckvnk], [st["b"][1]])

            def p2():
                self.cp("act", kt[0:64, 0:512], st["b"][0][0:64, :], [st["b"][1]], [ktk])
                self.unbank(st["b"])

            def p3():
                self.cp("pool", kt[64:96, 0:512], kr[64:96, 0:512], [krk], [ktk])

            def post():
                self.dma("pool", self.kc[h, :, t0:t0 + 512], kbv, [kbk], ["kc"])
            return self.qk_stages([p1, p2, p3], kt[0:96, 0:512], ktk, "qkn_k", kbv, kbk, [post])
        KB4 = [(self.scr(), self.scr()) for _ in range(4)]
        for hg in range(2):
            self.interleave([kchain(hg * 4 + i, KB4[i][0][0], KB4[i][0][1], KB4[i][1][0], KB4[i][1][1]) for i in range(4)])
        for (a_, b_) in KB4:
            self.unscr(a_); self.unscr(b_)
        nkt = 4 * (ti + 1)
        QB = [self.scr(), self.scr()]
        QF = [self.scr(), self.scr()]
        qbv_ = lambda i: QB[i][0][0:96, 0:256].bitcast(BF16)
        KBUF = [self.scr(8), self.scr(8)]
        VBUF = [self.scr(8), self.scr(8)]
        pts = [self.scr() for _ in range(3)]
        rl, rlk = self.scr()

        def qchain(h):
            i = h % 2
            qf, qfk = QF[i]
            st = {}

            def p1():
                st["b"] = self.bank()
                for c in range(2):
                    self.mm(st["b"][0][0:96, :], self.wuq_sb[:, c, h * 96:(h + 1) * 96], cqnv[:, c, :], c == 0, c == 1, ["wuq_sb", cqnk], [st["b"][1]])

            def p2():
                self.cp("act", qf[0:96, 0:512], st["b"][0][0:96, :], [st["b"][1]], [qfk])
                self.unbank(st["b"])

            def p3():
                kbv2 = KBUF[i][0][0:96, 0:2048].bitcast(BF16)
                vbv = VBUF[i][0][:, 0:1040].bitcast(BF16).rearrange("p (k e) -> p k e", k=32)
                self.dma("sp", kbv2[:, 0:nkt * 128], self.kc[h, :, 0:nkt * 128], ["kc"], [KBUF[i][1]])
                self.dma("sp", vbv[:, 0:nkt, :], self.vc[h, :, 0:nkt, :], ["vc"], [VBUF[i][1]])
            return self.qk_stages([p1, p2, p3], qf[0:96, 0:512], qfk, "qkn_q", qbv_(i), QB[i][1], [])
        for f in qchain(0):
            f()
        for h in range(8):
            i = h % 2
            qbv, qbk = qbv_(i), QB[i][1]
            kbv2 = KBUF[i][0][0:96, 0:2048].bitcast(BF16)
            kbufk = KBUF[i][1]
            vbv = VBUF[i][0][:, 0:1040].bitcast(BF16).rearrange("p (k e) -> p k e", k=32)
            vbufk = VBUF[i][1]
            nxt = qchain(h + 1) if h < 7 else []
            per = -(-len(nxt) // nkt) if nxt else 0
            ob, obk = self.bank()
            for k in range(nkt):
                sb_, sbk = self.bank()
                self.mm(sb_[:, :], kbv2[:, k * 128:(k + 1) * 128], qbv, True, True, [kbufk, qbk], [sbk])
                pt, ptk = pts[k % 3]
                ptv = pt[:, 0:256].bitcast(BF16)
                self.act(ptv, sb_[:, :], AF.Exp, [sbk], [ptk], scale=96.0 ** -0.5)
                self.unbank((sb_, sbk))
                if k >= 4 * ti:
                    self.tt("dve", ptv, ptv, self.amask[:, k - 4 * ti, :], ALU.mult, [ptk, "amask"], [ptk])
                self.mm(ob[0:65, :], vbv[:, k, :], ptv, k == 0, k == nkt - 1, [vbufk, ptk], [obk])
                for f in nxt[k * per:(k + 1) * per]:
                    f()
            self.recip(rl[64:65, 0:512], ob[64:65, :], [obk], [rlk])
            bc, bck = self.bank()
            self.mm(bc[0:64, :], self.onesf[64:65, 0:64], rl[64:65, 0:512], True, True, ["ones", rlk], [bck])
            self.cp("act", rl[0:64, 0:512], bc[0:64, :], [bck], [rlk])
            self.unbank((bc, bck))
            self.tt("dve", self.ys[3][:, h, :], ob[0:64, :], rl[0:64, 0:512], ALU.mult, [obk, rlk], ["arB"])
            self.unbank((ob, obk))
        for s_ in QB + QF:
            self.unscr(s_)
        for s_ in KBUF + VBUF:
            self.unscr(s_, 8)
        for s_ in pts + [(rl, rlk), (cqn, cqnk), (ckvn, ckvnk), cq[0], cq[1], (self.rcos, self.rcosk), (self.rsin, self.rsink)]:
            self.unscr(s_)


def _prep_inputs(inputs):
    pvec, fvec, lrug, bst, cst = host_layout(inputs)
    shared = {"pvec": pvec, "fvec": fvec,
              "lrug": lrug.reshape(DEPTH, -1, 1024), "bst": bst.reshape(DEPTH, -1, 1024), "cst": cst.reshape(DEPTH, -1, 1024)}
    for n in BIGW:
        shared[n] = np.ascontiguousarray(np.asarray(inputs[n], np.float32)).reshape(DEPTH, -1, 1024)
    for n, v in host_consts().items():
        shared["c_" + n] = v
    return shared


def run(inputs, L_RUN=DEPTH, T_RUN=T_FULL, n_cores=8, branches=(0, 1, 2, 3), dbg=False, dbg_tile=0, trace=False):
    inputs = {k: np.asarray(v) for k, v in inputs.items()}
    shared = _prep_inputs(inputs)
    nc = bass.Bass("TRN2", target_bir_lowering=False)
    kb = KB(nc, L_RUN, T_RUN, dbg=dbg)
    kb.dbg_tile = dbg_tile
    kb.build(branches=branches)
    in_maps = []
    for b in range(n_cores):
        m = dict(shared)
        m["x"] = np.ascontiguousarray(inputs["x"][b, :T_RUN].astype(np.float32))
        in_maps.append(m)
    res = run_bass_kernel_spmd(nc, in_maps, core_ids=list(range(n_cores)), trace=trace)
    return res


DEFAULT_BRANCHES = (0, 1, 2, 3)


def kernel(**inputs):
    res = run(inputs, branches=DEFAULT_BRANCHES)
    return np.stack([r["y"] for r in res.results], axis=0).astype(np.float32)
```

```python
import math
from contextlib import ExitStack
import numpy as np
import concourse.bass as bass
import concourse.mybir as mybir
from concourse.bass_utils import run_bass_kernel_spmd

F32 = mybir.dt.float32
BF16 = mybir.dt.bfloat16
AF = mybir.ActivationFunctionType
ALU = mybir.AluOpType
AX = mybir.AxisListType

D = 1024
T_FULL = 4096
DEPTH = 4
C = 512
D_IN = 7840
TT = 512
EPS = 1e-6
ENGS = ("pe", "act", "dve", "pool", "sp")
GELU_K = 1.5957691216057308


class Op:
    __slots__ = ("eng", "fn", "reads", "writes", "dma", "idx", "deps", "sig", "cnt", "sem", "semval")

    def __init__(self, eng, fn, reads, writes, dma):
        self.eng, self.fn, self.reads, self.writes, self.dma = eng, fn, reads, writes, dma
        self.deps = []
        self.sig = False
        self.cnt = 0
        self.sem = None
        self.semval = 0


class Prog:
    NDMA = 48

    def __init__(self, nc):
        self.nc = nc
        self.ops = []

    def op(self, eng, fn, reads=(), writes=(), dma=False):
        o = Op(eng, fn, tuple(reads), tuple(writes), dma)
        o.idx = len(self.ops)
        self.ops.append(o)
        return o

    def finalize(self):
        last_w, readers, children = {}, {}, {}
        dma_k = 0
        dma_last = [None] * self.NDMA
        alias = getattr(self, "alias", {})

        def expand(keys):
            out = []
            for k in keys:
                base, _, sub = k.partition("#")
                for s_ in alias.get(base, (base,)):
                    out.append((s_, sub))
            return out

        def related(s_, sub):
            if sub == "":
                return [(s_, "")] + [(s_, c) for c in children.get(s_, ())]
            return [(s_, sub), (s_, "")]

        for o in self.ops:
            deps = {}
            rd, wr = expand(o.reads), expand(o.writes)
            for (s_, sub) in rd + wr:
                if sub:
                    children.setdefault(s_, set()).add(sub)
            for (s_, sub) in rd:
                for kk in related(s_, sub):
                    w = last_w.get(kk)
                    if w is not None:
                        deps[w.idx] = (w, "raw")
            for (s_, sub) in wr:
                for kk in related(s_, sub):
                    w = last_w.get(kk)
                    if w is not None and w.idx not in deps:
                        deps[w.idx] = (w, "waw")
                    for r in readers.get(kk, ()):
                        if r.idx not in deps and r is not o:
                            deps[r.idx] = (r, "war")
            for kk in rd:
                readers.setdefault(kk, []).append(o)
            for (s_, sub) in wr:
                last_w[(s_, sub)] = o
                readers[(s_, sub)] = []
                if sub == "":
                    for c in children.get(s_, ()):
                        last_w[(s_, c)] = o
                        readers[(s_, c)] = []
            if o.dma:
                k = dma_k % self.NDMA
                dma_k += 1
                prev = dma_last[k]
                o.sem = k
                o.semval = (prev.semval if prev is not None else 0) + 16
                if prev is not None and prev.idx not in deps:
                    deps[prev.idx] = (prev, "raw")
                dma_last[k] = o
            for (p, kind) in deps.values():
                if p.dma:
                    o.deps.append(p)
                elif p.eng == o.eng and not o.dma:
                    if kind == "raw" and o.eng != "pe":
                        o.deps.append(p)
                        p.sig = True
                else:
                    o.deps.append(p)
                    p.sig = True
        cnt = {e: 0 for e in ENGS}
        for o in self.ops:
            if o.sig and not o.dma:
                cnt[o.eng] += 1
                o.cnt = cnt[o.eng]

    def emit(self, final_waits=()):
        nc = self.nc
        with ExitStack() as st:
            esem = {e: st.enter_context(nc.semaphore("s_" + e)) for e in ENGS}
            dsem = [st.enter_context(nc.semaphore("d%d" % i)) for i in range(self.NDMA)]
            block = st.enter_context(nc.Block())
            per = {e: [o for o in self.ops if o.eng == e] for e in ENGS}

            def run(e, engobj, extra_final=()):
                seen_e = {x: 0 for x in ENGS}
                seen_d = {}
                for o in per[e]:
                    for p in o.deps:
                        if p.dma:
                            if seen_d.get(p.sem, 0) < p.semval:
                                engobj.wait_ge(dsem[p.sem], p.semval)
                                seen_d[p.sem] = p.semval
                        elif seen_e[p.eng] < p.cnt:
                            engobj.wait_ge(esem[p.eng], p.cnt)
                            seen_e[p.eng] = p.cnt
                    ins = o.fn(engobj)
                    if o.dma:
                        ins.then_inc(dsem[o.sem], 16)
                    elif o.sig:
                        ins.then_inc(esem[o.eng], 1)
                for p in extra_final:
                    engobj.wait_ge(dsem[p.sem], p.semval)

            @block.tensor
            def _(e):
                run("pe", e)

            @block.scalar
            def _(e):
                run("act", e)

            @block.vector
            def _(e):
                run("dve", e)

            @block.gpsimd
            def _(e):
                run("pool", e)

            @block.sync
            def _(e):
                run("sp", e, extra_final=final_waits)


PV = {}
_o = 0
for _n, _w in [("conv_w", 16), ("conv_b", 4), ("gate_b", 8), ("lam", 4), ("s5_d", 4), ("mu_rkv", 12), ("w0", 4),
               ("a0", 4), ("k_k", 4), ("k_a", 4), ("r_k", 4), ("mu_w", 1), ("mu_a", 1), ("mu_g", 1), ("q_norm", 2),
               ("kv_norm", 1), ("qkn_q", 1), ("qkn_k", 1), ("s5_are", 16), ("s5_aim", 16), ("s5_ldt", 16)]:
    PV[_n] = (_o, _w)
    _o += _w
NPV = _o


def _chunks(v, n):
    return np.ascontiguousarray(v.reshape(n, 128).T)


def host_layout(inp):
    f = np.float32
    L = DEPTH
    pvec = np.zeros((L, 128, NPV), f)
    fvec = np.zeros((L, 4, 1024), f)
    lrug = np.zeros((L, 2, 4, 128, 128), f)
    bst = np.zeros((L, 2, 16, 128, 128), f)
    cst = np.zeros((L, 2, 16, 128, 128), f)
    for l in range(L):
        def put(name, arr):
            o, w = PV[name]
            pvec[l, :arr.shape[0], o:o + w] = arr
        put("conv_w", np.concatenate([_chunks(inp["lru_conv_w"][l, k], 4) for k in range(4)], axis=1))
        put("conv_b", _chunks(inp["lru_conv_b"][l], 4))
        put("gate_b", np.concatenate([_chunks(inp["lru_gate_b"][l, g], 4) for g in range(2)], axis=1))
        put("lam", _chunks(inp["lru_lambda"][l], 4))
        put("s5_d", _chunks(inp["s5_d"][l], 4))
        put("mu_rkv", np.concatenate([_chunks(inp["rwkv_mu_rkv"][l, j], 4) for j in range(3)], axis=1))
        put("w0", _chunks(inp["rwkv_w0"][l], 4))
        put("a0", _chunks(inp["rwkv_a0"][l], 4))
        put("k_k", _chunks(inp["rwkv_k_k"][l], 4))
        put("k_a", _chunks(inp["rwkv_k_a"][l], 4))
        put("r_k", _chunks(inp["rwkv_r_k"][l].reshape(-1), 4))
        put("mu_w", inp["rwkv_mu_w"][l].reshape(64, 1))
        put("mu_a", inp["rwkv_mu_a"][l].reshape(64, 1))
        put("mu_g", inp["rwkv_mu_g"][l].reshape(128, 1))
        put("q_norm", _chunks(inp["mla_q_norm"][l], 2))
        put("kv_norm", inp["mla_kv_norm"][l].reshape(128, 1))
        put("qkn_q", inp["mla_qk_norm_q"][l].reshape(96, 1))
        put("qkn_k", inp["mla_qk_norm_k"][l].reshape(96, 1))
        are = inp["s5_a_re"][l].reshape(16, 2, 64).transpose(1, 2, 0).reshape(128, 16)
        aim = inp["s5_a_im"][l].reshape(16, 2, 64).transpose(1, 2, 0).reshape(128, 16)
        ldt = np.broadcast_to(inp["s5_log_dt"][l].reshape(16, 2, 1), (16, 2, 64)).transpose(1, 2, 0).reshape(128, 16)
        put("s5_are", are)
        put("s5_aim", aim)
        put("s5_ldt", ldt)
        fvec[l, 0] = inp["norm_mix"][l]
        fvec[l, 1] = inp["norm_mlp"][l]
        fvec[l, 2, :512] = inp["rwkv_lnx_g"][l]
        fvec[l, 2, 512:] = inp["rwkv_lnx_b"][l]
        for g in range(2):
            for h in range(8):
                c, e = h // 2, h % 2
                lrug[l, g, c, e * 64:(e + 1) * 64, e * 64:(e + 1) * 64] = inp["lru_gate_w"][l, g, h]
        for ri, (bsrc, csrc) in enumerate([(inp["s5_b_re"][l], inp["s5_c_re"][l]), (inp["s5_b_im"][l], inp["s5_c_im"][l])]):
            for g in range(32):
                j, e = g // 2, g % 2
                r0 = 32 * (j % 4) + e * 16
                bst[l, ri, j, r0:r0 + 16, e * 64:(e + 1) * 64] = bsrc[g].T
                cst[l, ri, j, e * 64:(e + 1) * 64, r0:r0 + 16] = csrc[g].T
    return pvec, fvec, lrug, bst, cst


def host_consts():
    f = np.float32
    c = {}
    c["ident"] = np.eye(128, dtype=f)
    c["ones"] = np.ones((128, 128), f)
    blk = np.zeros((128, 128), f)
    blk[:64, :64] = 1
    blk[64:, 64:] = 1
    c["blk64"] = blk
    hs = np.zeros((128, 2), f)
    hs[:64, 0] = 1
    hs[64:, 1] = 1
    c["headsel"] = hs
    p = np.arange(128)[:, None]
    q = np.arange(512)[None, :]
    c["amask"] = np.stack([(q >= 128 * j + p) for j in range(4)], 1).astype(f)
    j = np.arange(64)[:, None]
    t = np.arange(64)[None, :]
    m = np.zeros((64, 3, 64), f)
    m[:, 0] = (t > j)
    m[:, 1] = (t >= j)
    m[:, 2] = (t < j)
    c["rmask"] = m
    rs = np.ones((128, 512), f)
    rs[:, ::64] = 0
    c["reset64"] = rs
    pos = np.arange(T_FULL, dtype=np.float64)
    inv = 10000.0 ** (-np.arange(0, 32, 2, dtype=np.float64) / 32)
    ang = pos[None, :] * inv[:, None]
    cos = np.ones((96, T_FULL), np.float64)
    sin = np.zeros((96, T_FULL), np.float64)
    cos[64:80] = np.cos(ang); cos[80:96] = np.cos(ang)
    sin[64:80] = np.sin(ang); sin[80:96] = np.sin(ang)
    c["ropec"] = cos.astype(f)
    c["ropes"] = sin.astype(f)
    pr = np.zeros((96, 96), f)
    for i in range(16):
        pr[80 + i, 64 + i] = -1.0
        pr[64 + i, 80 + i] = 1.0
    c["prot"] = pr
    return c


CONST_SHAPES = {"ident": [128, 128], "ones": [128, 128], "blk64": [128, 128], "headsel": [128, 2],
                "amask": [128, 4, 512], "rmask": [64, 3, 64], "reset64": [128, 512],
                "ropec": [96, T_FULL], "ropes": [96, T_FULL], "prot": [96, 96]}

BIGW = {"w_in": [D, D_IN], "s5_w_glu": [C, 2 * C], "w_branch": [4 * C, D], "w_out": [D, D], "w_ff1": [D, 4 * D],
        "w_ff2": [4 * D, D], "mla_w_uq": [256, 768], "mla_w_ukv": [128, 1024], "rwkv_w2": [64, C],
        "rwkv_a2": [64, C], "rwkv_g2": [128, C]}


class KB:
    def __init__(self, nc, L_RUN, T_RUN, dbg=False):
        self.nc, self.L, self.T, self.dbg = nc, L_RUN, T_RUN, dbg
        self.NT = T_RUN // TT
        self.P = Prog(nc)
        self.st = ExitStack()
        self.free_banks = []
        self.scr_free = {}
        self.scr_n = 0
        self.wk = 0
        self.outs = []

    def sb(self, name, shape, dt=F32):
        return self.st.enter_context(self.nc.sbuf_tensor(name, shape, dt))

    def bank(self):
        assert self.free_banks, "out of PSUM banks"
        return self.free_banks.pop(0)

    def unbank(self, b):
        self.free_banks.append(b)

    NSLOT = 42

    def scr(self, kb=2):
        n = kb // 2
        if not hasattr(self, "arena"):
            self.arena = self.sb("arena", [128, self.NSLOT * 512], F32)
            self.slot_used = [False] * self.NSLOT
            self.P.alias = {}
        for st in range(self.NSLOT - n + 1):
            if not any(self.slot_used[st:st + n]):
                for i in range(st, st + n):
                    self.slot_used[i] = True
                key = "arn_%d_%d" % (st, n)
                self.P.alias[key] = tuple("slot%d" % i for i in range(st, st + n))
                return (self.arena[:, st * 512:(st + n) * 512], key)
        raise AssertionError("out of scratch slots")

    def unscr(self, s, kb=2):
        _, st, n = s[1].split("_")
        for i in range(int(st), int(st) + int(n)):
            assert self.slot_used[i]
            self.slot_used[i] = False

    def mm(self, out, lhsT, rhs, start, stop, r, w):
        self.P.op("pe", lambda e: e.matmul(out, lhsT=lhsT, rhs=rhs, start=start, stop=stop), r, w)

    def trp(self, out, in_, ident, r, w):
        self.P.op("pe", lambda e: e.transpose(out=out, in_=in_, identity=ident), r, w)

    def act(self, out, in_, func, r, w, **kw):
        self.P.op("act", lambda e: e.activation(out=out, in_=in_, func=func, **kw), r, w)

    def tt(self, eng, out, in0, in1, op, r, w):
        self.P.op(eng, lambda e: e.tensor_tensor(out=out, in0=in0, in1=in1, op=op), r, w)

    def ts(self, eng, out, in0, s1, s2, op0, op1, r, w):
        if s2 is None:
            self.P.op(eng, lambda e: e.tensor_single_scalar(out=out, in_=in0, scalar=s1, op=op0), r, w)
        else:
            self.P.op(eng, lambda e: e.tensor_scalar(out=out, in0=in0, scalar1=s1, scalar2=s2, op0=op0, op1=op1), r, w)

    def stt(self, eng, out, in0, scalar, in1, op0, op1, r, w):
        self.P.op(eng, lambda e: e.scalar_tensor_tensor(out=out, in0=in0, scalar=scalar, in1=in1, op0=op0, op1=op1), r, w)

    def cp(self, eng, out, in_, r, w):
        if eng == "act":
            self.P.op("act", lambda e: e.copy(out=out, in_=in_), r, w)
        else:
            self.P.op(eng, lambda e: e.tensor_copy(out=out, in_=in_), r, w)

    def memset(self, eng, out, val, w):
        self.P.op(eng, lambda e: e.memset(out, val), [], w)

    def recip(self, out, in_, r, w):
        self.P.op("dve", lambda e: e.reciprocal(out=out, in_=in_), r, w)

    def scan(self, out, d0, d1, init, r, w):
        self.P.op("dve", lambda e: e.tensor_tensor_scan(out=out, data0=d0, data1=d1, initial=init, op0=ALU.mult, op1=ALU.add), r, w)

    def dma(self, eng, out, in_, r, w, **kw):
        return self.P.op(eng, lambda e: e.dma_start(out=out, in_=in_, **kw), r, w, dma=True)

    def wload(self, src, shape):
        i = self.wk % len(self.wring)
        self.wk += 1
        buf, key = self.wring[i]
        n = int(np.prod(shape[1:]))
        if len(shape) == 3:
            view = buf[:shape[0], 0:n].rearrange("p (k c) -> p k c", k=shape[1])
        else:
            view = buf[:shape[0], 0:n]
        self.dma("sp", view, src, [self.cur_wkey], [key])
        return view, key

    def gelu(self, src, srck, out, outk, tmp, tmpk):
        self.tt("dve", tmp, src, src, ALU.mult, [srck], [tmpk])
        self.ts("dve", tmp, tmp, 0.044715, 1.0, ALU.mult, ALU.add, [tmpk], [tmpk])
        self.tt("dve", tmp, tmp, src, ALU.mult, [tmpk, srck], [tmpk])
        self.act(tmp, tmp, AF.Sigmoid, [tmpk], [tmpk], scale=GELU_K)
        self.tt("dve", out, tmp, src, ALU.mult, [tmpk, srck], [outk])

    def build(self, branches=(0, 1, 2, 3)):
        nc, L, T = self.nc, self.L, self.T
        self.branches = branches
        dt = lambda name, shape, dty, kind: nc.dram_tensor(name, shape, dty, kind=kind).ap()
        self.x = dt("x", [T, D], F32, "ExternalInput")
        self.y = dt("y", [T, D], F32, "ExternalOutput")
        self.w32, self.wb = {}, {}
        for n, (r, c) in BIGW.items():
            self.w32[n] = dt(n, [DEPTH, r * c // 1024, 1024], F32, "ExternalInput")
            self.wb[n] = dt("b_" + n, [DEPTH, r, c], BF16, "Internal")
        self.pvec_d = dt("pvec", [DEPTH, 128, NPV], F32, "ExternalInput")
        self.fvec_d = dt("fvec", [DEPTH, 4, 1024], F32, "ExternalInput")
        for n, shp in [("lrug", [DEPTH, 8 * 128 * 128 // 1024, 1024]), ("bst", [DEPTH, 32 * 16, 1024]), ("cst", [DEPTH, 32 * 16, 1024])]:
            self.w32[n] = dt(n, shp, F32, "ExternalInput")
        self.wb["lrug"] = dt("b_lrug", [DEPTH, 8, 128, 128], BF16, "Internal")
        self.wb["bst"] = dt("b_bst", [DEPTH, 32, 128, 128], BF16, "Internal")
        self.wb["cst"] = dt("b_cst", [DEPTH, 32, 128, 128], BF16, "Internal")
        self.cd = {n: dt("c_" + n, s, F32, "ExternalInput") for n, s in CONST_SHAPES.items()}
        self.kc = dt("kcache", [8, 96, T], BF16, "Internal")
        self.vc = dt("vcache", [8, 128, T // 128, 65], BF16, "Internal")
        self.s5tab = dt("s5tab", [16, 128, 4, 128], F32, "Internal")
        if self.dbg:
            self.dbg_ys = dt("dbg_ys", [4, 128, 8, TT], F32, "ExternalOutput")

        for i in range(8):
            t = self.st.enter_context(nc.psum_tensor("bank%d" % i, [128, 512], F32))
            self.free_banks.append((t, "bank%d" % i))
        sb = self.sb
        self.identf = sb("identf", [128, 128]); self.identb = sb("identb", [128, 128], BF16)
        self.onesf = sb("onesf", [128, 128]); self.onesb = sb("onesb", [128, 128], BF16)
        self.blk64 = sb("blk64", [128, 128]); self.headsel = sb("headsel", [128, 2]); self.headselb = sb("headselb", [128, 2], BF16)
        self.amask = sb("amask", [128, 4, 512], BF16); self.rmask = sb("rmask", [64, 3, 64])
        self.reset64 = sb("reset64", [128, 512]); self.prot = sb("prot", [96, 96])
        amf, amfk = self.scr(8)
        for n, tl in [("ident", self.identf), ("ones", self.onesf), ("blk64", self.blk64), ("headsel", self.headsel),
                      ("rmask", self.rmask), ("reset64", self.reset64), ("prot", self.prot)]:
            self.dma("sp", tl[:], self.cd[n], [], [n])
        self.dma("sp", amf[:, 0:2048].rearrange("p (a b) -> p a b", a=4), self.cd["amask"], [], [amfk])
        self.cp("dve", self.amask[:], amf[:, 0:2048].rearrange("p (a b) -> p a b", a=4), [amfk], ["amask"])
        self.unscr((amf, amfk), 8)
        self.cp("dve", self.identb[:], self.identf[:], ["ident"], ["identb"])
        self.cp("dve", self.onesb[:], self.onesf[:], ["ones"], ["onesb"])
        self.cp("dve", self.headselb[:], self.headsel[:], ["headsel"], ["headselb"])
        self.wring = [(sb("wring%d" % i, [128, 4096], BF16), "wring%d" % i) for i in range(3)]
        self.xt = sb("xt", [128, 4, 1024]); self.hT = sb("hT", [128, 8, 512], BF16)
        self.gbc = sb("gbc", [128, 1024]); self.lnx = sb("lnx", [64, 1024])
        self.arA = sb("arA", [128, 4096]); self.arB = sb("arB", [128, 5120])
        self.hn = self.arA[:, 0:2048].bitcast(BF16).rearrange("p (s d) -> p s d", s=4)
        self.macc = self.arA[:, :].rearrange("p (c t) -> p c t", c=8)
        ysb = self.arB[:, :].bitcast(BF16)
        self.ys = [ysb[:, i * 2048:(i + 1) * 2048].rearrange("p (c t) -> p c t", c=4) for i in range(3)]
        self.ys.append(ysb[0:64, 6144:10240].rearrange("p (h t) -> p h t", h=8))
        self.a1T_lo = self.arA[:, :].bitcast(BF16).rearrange("p (c t) -> p c t", c=16)
        self.a1T_hi = ysb[:, 0:8192].rearrange("p (c t) -> p c t", c=16)
        self.pv = sb("pv", [128, NPV]); self.der = sb("der", [128, 16])
        self.lrug_sb = sb("lrug_sb", [128, 8, 128], BF16)
        self.w2_sb = sb("w2_sb", [64, 512], BF16); self.a2_sb = sb("a2_sb", [64, 512], BF16); self.g2_sb = sb("g2_sb", [128, 512], BF16)
        self.wuq_sb = sb("wuq_sb", [128, 2, 768], BF16); self.wukv_sb = sb("wukv_sb", [128, 1024], BF16)
        self.ss = sb("ss", [128, 8])
        self.xh = [sb("xh%d" % c, [128, 515]) for c in range(4)]
        self.hc = sb("hc", [128, 4])
        self.zc = sb("zc", [128, 2, 16]); self.el = sb("el", [128, 2, 16])
        self.carry = sb("carry", [128, 16])
        self.pc_t = sb("pc_t", [128, 4, 8])
        self.s0f = sb("s0f", [128, 4, 64]); self.s0bd = sb("s0bd", [128, 4, 128], BF16)

        for l in range(L):
            for n in list(BIGW) + ["lrug", "bst", "cst"]:
                dstv = self.wb[n][l]
                if n in ("lrug", "bst", "cst"):
                    dstv = dstv.rearrange("a r c -> (a r c)")
                else:
                    dstv = dstv.rearrange("r c -> (r c)")
                dstv = dstv.rearrange("(a b) -> a b", b=1024)
                self.dma("pool", dstv, self.w32[n][l], [], ["wb%d" % l])
        for l in range(L):
            self.layer(l)
        self.P.finalize()
        self.P.emit(final_waits=self.outs)
        self.st.close()

    def layer(self, l):
        self.l = l
        self.cur_wkey = "wb%d" % l
        wk = self.cur_wkey
        pv, der = self.pv, self.der
        self.dma("sp", pv[:], self.pvec_d[l], [], ["pv"])
        self.dma("sp", self.lnx[:], self.fvec_d[l, 2:3, :].broadcast_to([64, 1024]), [], ["lnx"])
        self.dma("sp", self.lrug_sb[:], self.wb["lrug"][l].rearrange("a r c -> r a c"), [wk], ["lrug_sb"])
        self.dma("sp", self.w2_sb[:], self.wb["rwkv_w2"][l], [wk], ["w2_sb"])
        self.dma("sp", self.a2_sb[:], self.wb["rwkv_a2"][l], [wk], ["a2_sb"])
        self.dma("sp", self.g2_sb[:], self.wb["rwkv_g2"][l], [wk], ["g2_sb"])
        self.dma("sp", self.wuq_sb[:], self.wb["mla_w_uq"][l].rearrange("(k p) c -> p k c", p=128), [wk], ["wuq_sb"])
        self.dma("sp", self.wukv_sb[:], self.wb["mla_w_ukv"][l], [wk], ["wukv_sb"])
        o = PV["lam"][0]
        self.act(der[:, 0:4], pv[:, o:o + 4], AF.Exp, ["pv"], ["der"], scale=-1.0)
        self.act(der[:, 0:4], der[:, 0:4], AF.Ln, ["der"], ["der"], bias=1.0)
        self.ts("dve", der[:, 0:4], der[:, 0:4], -8.0, None, ALU.mult, None, ["der"], ["der"])
        o = PV["w0"][0]
        self.ts("dve", der[:, 4:8], pv[:, o:o + 4], -1.0, None, ALU.mult, None, ["pv"], ["der"])
        o = PV["k_a"][0]
        self.ts("dve", der[:, 8:12], pv[:, o:o + 4], -1.0, 1.0, ALU.mult, ALU.add, ["pv"], ["der"])
        for c in range(4):
            self.memset("pool", self.xh[c][:, 0:3], 0.0, ["xh%d" % c])
        self.memset("pool", self.hc[:], 0.0, ["hc"])
        self.memset("pool", self.zc[:], 0.0, ["zc%d" % i for i in range(16)])
        self.memset("pool", self.carry[:], 0.0, ["carry%d" % i for i in range(16)])
        self.memset("pool", self.s0f[:], 0.0, ["s0_%df" % i for i in range(4)])
        self.memset("pool", self.s0bd[:], 0.0, ["s0_%d" % i for i in range(4)])
        if 1 in self.branches:
            self.s5_setup(l)
        for ti in range(self.NT):
            self.tile(l, ti)

    def pvc(self, name, i=0, rows=128):
        o = PV[name][0] + i
        return self.pv[0:rows, o:o + 1]

    def norm_T(self, gi):
        l = self.l
        self.dma("sp", self.gbc[:], self.fvec_d[l, gi:gi + 1, :].broadcast_to([128, 1024]), [], ["gbc"])
        junk, jk = self.scr(4)
        for s in range(4):
            self.act(junk[:, 0:1024], self.xt[:, s, :], AF.Square, ["xt"], [jk, "ss"], accum_out=self.ss[:, s:s + 1])
        self.unscr((junk, jk), 4)
        self.act(self.ss[:, 4:8], self.ss[:, 0:4], AF.Sqrt, ["ss"], ["ss"], scale=1.0 / D, bias=EPS)
        self.recip(self.ss[:, 4:8], self.ss[:, 4:8], ["ss"], ["ss"])
        for s in range(4):
            self.stt("dve", self.hn[:, s, :], self.xt[:, s, :], self.ss[:, 4 + s:5 + s], self.gbc[:], ALU.mult, ALU.mult,
                     ["xt", "ss", "gbc"], ["arA"])
        for kc in range(8):
            b, bk = self.bank()
            bb = b[:, :].bitcast(BF16)
            for s in range(4):
                self.trp(bb[:, s * 128:(s + 1) * 128], self.hn[:, s, kc * 128:(kc + 1) * 128], self.identb[:], ["arA", "identb"], [bk])
            self.cp("act" if kc % 2 else "dve", self.hT[:, kc, :], bb[:, 0:512], [bk], ["hT"])
            self.unbank((b, bk))

    def proj(self, out, slab, c0, m, nk, rhsf, r, w):
        for kc in range(nk):
            self.mm(out, slab[:, kc, c0:c0 + m], rhsf(kc), kc == 0, kc == nk - 1, r, w)

    def win(self, c0, n=512):
        return self.wload(self.wb["w_in"][self.l][:, c0:c0 + n].rearrange("(k p) c -> p k c", p=128), [128, 8, n])

    def tile(self, l, ti):
        t0 = ti * TT
        src = self.x if l == 0 else self.y
        self.dma("sp", self.xt[:], src[t0:t0 + TT, :].rearrange("(s p) d -> p s d", p=128), ["ydram"], ["xt"])
        self.norm_T(0)
        for bi, fn in enumerate([self.lru, self.s5, self.rwkv, self.mla]):
            if bi in self.branches:
                fn(l, ti)
            else:
                self.memset("pool", self.ys[bi], 0.0, ["arB"])
        if self.dbg and l == 0 and ti == self.dbg_tile:
            d, dk = self.scr(8)
            dv = d[:, :].rearrange("p (c t) -> p c t", c=4)
            for bi in range(4):
                np_ = 128 if bi < 3 else 64
                for hf_ in range(1 if bi < 3 else 2):
                    self.cp("dve", dv[0:np_], self.ys[bi][:, hf_ * 4:hf_ * 4 + 4, :], ["arB"], [dk])
                    self.outs.append(self.dma("sp", self.dbg_ys[bi, 0:np_, hf_ * 4:hf_ * 4 + 4, :], dv[0:np_], [dk], ["dbgout"]))
            self.unscr((d, dk), 8)
        self.merge(l, ti)
        self.mlp(l, ti)
        st = self.dma("pool", self.y[t0:t0 + TT, :].rearrange("(s p) d -> p s d", p=128), self.xt[:], ["xt"], ["ydram"])
        if l == self.L - 1:
            self.outs.append(st)

    def merge(self, l, ti):
        wbr = self.wb["w_branch"][l]
        SG = [self.scr() for _ in range(3)]
        TM = [self.scr() for _ in range(3)]
        it = 0
        for n in range(4):
            for half in range(2):
                if n < 3:
                    wsl, wslk = self.wload(wbr[n * 512:(n + 1) * 512, half * 512:(half + 1) * 512].rearrange("(k p) c -> p k c", p=128), [128, 4, 512])
                    nk = 4
                else:
                    wsl, wslk = self.wload(wbr[n * 512:(n + 1) * 512, half * 512:(half + 1) * 512].rearrange("(k p) c -> p k c", p=64), [64, 8, 512])
                    nk = 8
                gsl, gslk = self.win(3744 + n * 1024 + half * 512)
                for dc4 in range(4):
                    dc = half * 4 + dc4
                    sg, sgk = SG[it % 3]; tm, tmk = TM[it % 3]; it += 1
                    bg, bgk = self.bank()
                    self.proj(bg[:, :], gsl, dc4 * 128, 128, 8, lambda kc: self.hT[:, kc, :], [gslk, "hT"], [bgk])
                    self.act(sg[:, 0:512], bg[:, :], AF.Sigmoid, [bgk], [sgk])
                    self.unbank((bg, bgk))
                    bz, bzk = self.bank()
                    ysn = self.ys[n]
                    self.proj(bz[:, :], wsl, dc4 * 128, 128, nk, lambda kc: ysn[:, kc, :], [wslk, "arB"], [bzk])
                    mk_ = "arA#m%d" % dc
                    if n == 0:
                        self.tt("dve", self.macc[:, dc, :], bz[:, :], sg[:, 0:512], ALU.mult, [bzk, sgk], [mk_])
                    else:
                        self.tt("dve", tm[:, 0:512], bz[:, :], sg[:, 0:512], ALU.mult, [bzk, sgk], [tmk])
                        self.tt("dve", self.macc[:, dc, :], self.macc[:, dc, :], tm[:, 0:512], ALU.add, [mk_, tmk], [mk_])
                    self.unbank((bz, bzk))
        for s_ in SG + TM:
            self.unscr(s_)
        for dc in range(8):
            self.cp("act" if dc % 2 else "dve", self.hT[:, dc, :], self.macc[:, dc, :], ["arA"], ["hT"])
        for half in range(2):
            wsl, wslk = self.wload(self.wb["w_out"][l][:, half * 512:(half + 1) * 512].rearrange("(k p) c -> p k c", p=128), [128, 8, 512])
            for s in range(4):
                b, bk = self.bank()
                for kc in range(8):
                    self.mm(b[:, :], self.hT[:, kc, s * 128:(s + 1) * 128], wsl[:, kc, :], kc == 0, kc == 7, ["hT", wslk], [bk])
                self.tt("dve", self.xt[:, s, half * 512:(half + 1) * 512], self.xt[:, s, half * 512:(half + 1) * 512], b[:, :], ALU.add, ["xt", bk], ["xt"])
                self.unbank((b, bk))

    def mlp(self, l, ti):
        self.norm_T(1)
        R1 = [self.scr(), self.scr()]
        for sl in range(8):
            wsl, wslk = self.wload(self.wb["w_ff1"][l][:, sl * 512:(sl + 1) * 512].rearrange("(k p) c -> p k c", p=128), [128, 8, 512])
            for c4 in range(4):
                fc = sl * 4 + c4
                r1, r1k = R1[fc % 2]
                b, bk = self.bank()
                self.proj(b[:, :], wsl, c4 * 128, 128, 8, lambda kc: self.hT[:, kc, :], [wslk, "hT"], [bk])
                self.act(r1[:, 0:512], b[:, :], AF.Relu, [bk], [r1k])
                dst = self.a1T_lo[:, fc, :] if fc < 16 else self.a1T_hi[:, fc - 16, :]
                self.tt("dve", dst, r1[:, 0:512], r1[:, 0:512], ALU.mult, [r1k], ["arA" if fc < 16 else "arB"])
                self.unbank((b, bk))
        self.unscr(R1[0]); self.unscr(R1[1])
        for half in range(2):
            accs = [self.bank() for _ in range(4)]
            for g in range(4):
                wsl, wslk = self.wload(self.wb["w_ff2"][l][g * 1024:(g + 1) * 1024, half * 512:(half + 1) * 512].rearrange("(k p) c -> p k c", p=128), [128, 8, 512])
                for s in range(4):
                    for kc in range(8):
                        fc = g * 8 + kc
                        a = self.a1T_lo[:, fc, s * 128:(s + 1) * 128] if fc < 16 else self.a1T_hi[:, fc - 16, s * 128:(s + 1) * 128]
                        self.mm(accs[s][0][:, :], a, wsl[:, kc, :], fc == 0, fc == 31, ["arA", "arB", wslk], [accs[s][1]])
            for s in range(4):
                self.tt("dve", self.xt[:, s, half * 512:(half + 1) * 512], self.xt[:, s, half * 512:(half + 1) * 512], accs[s][0][:, :], ALU.add, ["xt", accs[s][1]], ["xt"])
                self.unbank(accs[s])

    def lru(self, l, ti):
        slx, slxk = self.win(0)
        slg, slgk = self.win(512)
        S = [self.scr() for _ in range(6)]
        (xc, xck), (rr, rrk), (ii, iik), (aa, aak), (t1, t1k), (hh, hhk) = S
        xcb, xcbk = self.scr()
        xcbv = xcb[:, 0:256].bitcast(BF16)
        hf = lambda kc: self.hT[:, kc, :]
        for c in range(4):
            xh, xhk = self.xh[c], "xh%d" % c
            b, bk = self.bank()
            self.proj(b[:, :], slx, c * 128, 128, 8, hf, [slxk, "hT"], [bk])
            self.cp("act", xh[:, 3:515], b[:, :], [bk], [xhk])
            self.unbank((b, bk))
            self.ts("dve", xc[:, 0:512], xh[:, 3:515], self.pvc("conv_w", 12 + c), self.pvc("conv_b", c), ALU.mult, ALU.add, [xhk, "pv"], [xck])
            for k in range(3):
                self.stt("dve", xc[:, 0:512], xh[:, k:k + 512], self.pvc("conv_w", 4 * k + c), xc[:, 0:512], ALU.mult, ALU.add, [xhk, "pv", xck], [xck])
            self.cp("pool", xh[:, 0:3], xh[:, 512:515], [xhk], [xhk])
            self.cp("dve", xcbv, xc[:, 0:512], [xck], [xcbk])
            for g, (dst, dstk) in enumerate([(rr, rrk), (ii, iik)]):
                b, bk = self.bank()
                self.mm(b[:, :], self.lrug_sb[:, g * 4 + c, :], xcbv, True, True, ["lrug_sb", xcbk], [bk])
                self.act(dst[:, 0:512], b[:, :], AF.Sigmoid, [bk, "pv"], [dstk], bias=self.pvc("gate_b", g * 4 + c))
                self.unbank((b, bk))
            self.act(aa[:, 0:512], rr[:, 0:512], AF.Exp, [rrk, "der"], [aak], scale=self.der[:, c:c + 1])
            self.tt("dve", t1[:, 0:512], aa[:, 0:512], aa[:, 0:512], ALU.mult, [aak], [t1k])
            self.ts("dve", t1[:, 0:512], t1[:, 0:512], -1.0, 1.0, ALU.mult, ALU.add, [t1k], [t1k])
            self.act(t1[:, 0:512], t1[:, 0:512], AF.Sqrt, [t1k], [t1k])
            self.tt("dve", ii[:, 0:512], ii[:, 0:512], xc[:, 0:512], ALU.mult, [iik, xck], [iik])
            self.tt("dve", t1[:, 0:512], t1[:, 0:512], ii[:, 0:512], ALU.mult, [t1k, iik], [t1k])
            self.scan(hh[:, 0:512], aa[:, 0:512], t1[:, 0:512], self.hc[:, c:c + 1], [aak, t1k, "hc"], [hhk])
            self.cp("dve", self.hc[:, c:c + 1], hh[:, 511:512], [hhk], ["hc"])
            b, bk = self.bank()
            self.proj(b[:, :], slg, c * 128, 128, 8, hf, [slgk, "hT"], [bk])
            self.cp("act", rr[:, 0:512], b[:, :], [bk], [rrk])
            self.unbank((b, bk))
            self.gelu(rr[:, 0:512], rrk, ii[:, 0:512], iik, t1[:, 0:512], t1k)
            self.tt("dve", self.ys[0][:, c, :], hh[:, 0:512], ii[:, 0:512], ALU.mult, [hhk, iik], ["arB"])
        for s_ in S:
            self.unscr(s_)
        self.unscr((xcb, xcbk))

    def s5_setup(self, l):
        sm, smk = self.scr()
        v = lambda i: sm[:, i * 16:(i + 1) * 16]
        pvs = lambda n: self.pv[:, PV[n][0]:PV[n][0] + 16]
        R, W = [smk, "pv"], [smk]
        tt = lambda o, a, b, op: self.tt("dve", o, a, b, op, R, W)
        self.act(v(0), pvs("s5_ldt"), AF.Exp, R, W)
        tt(v(1), pvs("s5_aim"), v(0), ALU.mult)
        tt(v(2), pvs("s5_are"), v(0), ALU.mult)
        self.act(v(3), v(2), AF.Exp, R, W)
        self.act(v(4), v(2), AF.Exp, R, W, scale=-1.0)
        self.act(v(5), v(1), AF.Sin, R, W, scale=1.0 / 16)
        self.ts("dve", v(6), v(1), 1.0 / 16, math.pi / 2, ALU.mult, ALU.add, R, W)
        self.act(v(6), v(6), AF.Sin, R, W)

        def csq(c, s_):
            tt(v(7), c, c, ALU.mult); tt(v(8), s_, s_, ALU.mult); tt(v(9), c, s_, ALU.mult)
            tt(c, v(7), v(8), ALU.subtract)
            self.ts("dve", s_, v(9), 2.0, None, ALU.mult, None, R, W)
        for _ in range(4):
            csq(v(6), v(5))
        tt(v(10), v(3), v(6), ALU.mult); tt(v(11), v(3), v(5), ALU.mult)
        tt(v(12), v(4), v(6), ALU.mult); tt(v(13), v(4), v(5), ALU.mult)
        self.ts("dve", v(13), v(13), -1.0, None, ALU.mult, None, R, W)
        self.ts("dve", v(14), v(10), -1.0, None, ALU.add, None, R, W)
        tt(v(7), pvs("s5_are"), pvs("s5_are"), ALU.mult); tt(v(8), pvs("s5_aim"), pvs("s5_aim"), ALU.mult)
        tt(v(7), v(7), v(8), ALU.add)
        self.recip(v(7), v(7), R, W)
        tt(v(8), v(14), pvs("s5_are"), ALU.mult); tt(v(9), v(11), pvs("s5_aim"), ALU.mult); tt(v(8), v(8), v(9), ALU.add)
        tt(v(15), v(8), v(7), ALU.mult)
        tt(v(8), v(11), pvs("s5_are"), ALU.mult); tt(v(9), v(14), pvs("s5_aim"), ALU.mult); tt(v(8), v(8), v(9), ALU.subtract)
        tt(v(14), v(8), v(7), ALU.mult)
        tabs = [self.scr(8) for _ in range(4)]
        tv = [t[0][:, :].rearrange("p (j t) -> p j t", j=16) for t in tabs]
        tk = [t[1] for t in tabs]
        tmp, tmpk = self.scr(8)
        tmv = tmp[:, :].rearrange("p (j t) -> p j t", j=16)
        self.cp("dve", tv[0][:, :, 0], v(15), R, [tk[0]]); self.cp("dve", tv[1][:, :, 0], v(14), R, [tk[1]])
        self.memset("dve", tv[2][:, :, 0], 1.0, [tk[2]]); self.memset("dve", tv[3][:, :, 0], 0.0, [tk[3]])
        for (ar_, ai_, pr, pi) in [(0, 1, v(12), v(13)), (2, 3, v(10), v(11))]:
            for k in range(7):
                n = 1 << k
                prb = pr.unsqueeze(2).broadcast_to([128, 16, n]) if n > 1 else pr.unsqueeze(2)
                pib = pi.unsqueeze(2).broadcast_to([128, 16, n]) if n > 1 else pi.unsqueeze(2)
                Ar, Ai = tv[ar_], tv[ai_]
                kk = [tk[ar_], tk[ai_], smk, tmpk]
                self.tt("dve", Ar[:, :, n:2 * n], Ar[:, :, 0:n], prb, ALU.mult, kk, [tk[ar_]])
                self.tt("dve", tmv[:, :, 0:n], Ai[:, :, 0:n], pib, ALU.mult, kk, [tmpk])
                self.tt("dve", Ar[:, :, n:2 * n], Ar[:, :, n:2 * n], tmv[:, :, 0:n], ALU.subtract, kk, [tk[ar_]])
                self.tt("dve", tmv[:, :, 0:n], Ar[:, :, 0:n], pib, ALU.mult, kk, [tmpk])
                self.tt("dve", Ai[:, :, n:2 * n], Ai[:, :, 0:n], prb, ALU.mult, kk, [tk[ai_]])
                self.tt("dve", Ai[:, :, n:2 * n], Ai[:, :, n:2 * n], tmv[:, :, 0:n], ALU.add, kk, [tk[ai_]])
                csq(pr, pi)
        self.cp("dve", self.el[:, 0, :], v(10), R, ["el"]); self.cp("dve", self.el[:, 1, :], v(11), R, ["el"])
        for i in range(4):
            self.dma("sp", self.s5tab[:, :, i, :].rearrange("j p t -> p j t"), tv[i], [tk[i]], ["s5tab"])
        for t in tabs:
            self.unscr(t, 8)
        self.unscr((tmp, tmpk), 8); self.unscr((sm, smk))

    def s5(self, l, ti):
        import os
        if os.environ.get('S5SKIP'):
            return
        hf = lambda kc: self.hT[:, kc, :]
        slu, sluk = self.win(1024)
        (usb, usbk), (ubf, ubfk), (tmp, tmpk), (yv, yvk), (tc, tck) = [self.scr() for _ in range(5)]
        ubv = ubf[:, 0:256].bitcast(BF16)
        (yg, ygk), (sre, srek), (nsi, nsik) = [self.scr(4) for _ in range(3)]
        ygv = yg[:, :].bitcast(BF16).rearrange("p (c t) -> p c t", c=4)
        srv = sre[:, :].bitcast(BF16).rearrange("p (c t) -> p c t", c=4)
        nsv = nsi[:, :].bitcast(BF16).rearrange("p (c t) -> p c t", c=4)
        Z = [self.scr(8) for _ in range(4)]
        zr, zi, zor, zoi = [z[0][:, :].rearrange("p (j t) -> p j t", j=4) for z in Z]
        zrk, zik, zork, zoik = [z[1] for z in Z]
        (bs, bsk), (cs, csk) = self.scr(), self.scr()
        bsv = bs[:, :].bitcast(BF16).rearrange("p (r j c) -> p r j c", r=2, j=4)
        csv = cs[:, :].bitcast(BF16).rearrange("p (r j c) -> p r j c", r=2, j=4)
        tabs = [self.scr() for _ in range(4)]
        q4 = lambda ap: ap.rearrange("p (q t) -> p q t", q=4)
        for c in range(4):
            b, bk = self.bank()
            self.proj(b[:, :], slu, c * 128, 128, 8, hf, [sluk, "hT"], [bk])
            self.cp("act", usb[:, 0:512], b[:, :], [bk], [usbk])
            self.cp("dve", ubv, usb[:, 0:512], [usbk], [ubfk])
            self.unbank((b, bk))
            for r in range(2):
                self.dma("sp", bsv[:, r], self.wb["bst"][l][r * 16 + 4 * c:r * 16 + 4 * c + 4].rearrange("a r c -> r a c"), [self.cur_wkey], [bsk])
                self.dma("sp", csv[:, r], self.wb["cst"][l][r * 16 + 4 * c:r * 16 + 4 * c + 4].rearrange("a r c -> r a c"), [self.cur_wkey], [csk])
            for jj in range(4):
                j = 4 * c + jj
                tb, tbk = tabs[jj]
                tbv = tb[:, 0:512].rearrange("p (i t) -> p i t", i=4)
                self.dma("sp", tbv, self.s5tab[j], ["s5tab"], [tbk])
                Fr = tbv[:, 0:1, :].broadcast_to([128, 4, 128]); Fi = tbv[:, 1:2, :].broadcast_to([128, 4, 128])
                bre, brek = self.bank(); bim, bimk = self.bank()
                self.mm(bre[:, :], bsv[:, 0, jj, :], ubv, True, True, [bsk, ubfk], [brek])
                self.mm(bim[:, :], bsv[:, 1, jj, :], ubv, True, True, [bsk, ubfk], [bimk])
                for q in range(4):
                    sl = slice(q * 128, (q + 1) * 128)
                    self.tt("dve", zr[:, jj, sl], bre[:, sl], tbv[:, 0, :], ALU.mult, [brek, tbk], [zrk])
                    self.tt("dve", tmp[:, sl], bim[:, sl], tbv[:, 1, :], ALU.mult, [bimk, tbk], [tmpk])
                    self.tt("dve", zi[:, jj, sl], bre[:, sl], tbv[:, 1, :], ALU.mult, [brek, tbk], [zik])
                    self.tt("dve", tc[:, 16:144], bim[:, sl], tbv[:, 0, :], ALU.mult, [bimk, tbk], [tck + "#x"])
                    self.tt("dve", zi[:, jj, sl], zi[:, jj, sl], tc[:, 16:144], ALU.add, [zik, tck + "#x"], [zik])
                self.tt("dve", zr[:, jj, :], zr[:, jj, :], tmp[:, 0:512], ALU.subtract, [zrk, tmpk], [zrk])
                self.unbank((bre, brek)); self.unbank((bim, bimk))
            for q in range(4):
                for jj in range(4):
                    j = 4 * c + jj
                    zk = "zc%d" % j
                    sl = slice(q * 128, (q + 1) * 128)
                    e = q * 128 + 127
                    self.scan(zor[:, jj, sl], self.onesf[:, 0:128], zr[:, jj, sl], self.zc[:, 0, j:j + 1], ["ones", zrk, zk], [zork + "#%d" % jj])
                    self.scan(zoi[:, jj, sl], self.onesf[:, 0:128], zi[:, jj, sl], self.zc[:, 1, j:j + 1], ["ones", zik, zk], [zoik + "#%d" % jj])
                    tcr = [zork + "#%d" % jj, zoik + "#%d" % jj, "el", tck + "#%d" % jj]
                    self.tt("dve", tc[:, 2 * jj:2 * jj + 1], zoi[:, jj, e:e + 1], self.el[:, 1, j:j + 1], ALU.mult, tcr, [tck + "#%d" % jj])
                    self.tt("dve", tc[:, 2 * jj + 1:2 * jj + 2], zor[:, jj, e:e + 1], self.el[:, 1, j:j + 1], ALU.mult, tcr, [tck + "#%d" % jj])
                    self.stt("dve", self.zc[:, 0, j:j + 1], zor[:, jj, e:e + 1], self.el[:, 0, j:j + 1], tc[:, 2 * jj:2 * jj + 1], ALU.mult, ALU.subtract, tcr, [zk])
                    self.stt("dve", self.zc[:, 1, j:j + 1], zoi[:, jj, e:e + 1], self.el[:, 0, j:j + 1], tc[:, 2 * jj + 1:2 * jj + 2], ALU.mult, ALU.add, tcr, [zk])
            yb, ybk = self.bank()
            for jj in range(4):
                tb, tbk = tabs[jj]
                tbv = tb[:, 0:512].rearrange("p (i t) -> p i t", i=4)
                Rr = tbv[:, 2:3, :].broadcast_to([128, 4, 128]); Ri = tbv[:, 3:4, :].broadcast_to([128, 4, 128])
                kr_ = [zork + "#%d" % jj, zoik + "#%d" % jj, tbk]
                p1, p1k = self.scr(); p2, p2k = self.scr()
                for q in range(4):
                    sl = slice(q * 128, (q + 1) * 128)
                    self.tt("dve", p1[:, sl], zor[:, jj, sl], tbv[:, 2, :], ALU.mult, kr_, [p1k])
                    self.tt("dve", p2[:, sl], zoi[:, jj, sl], tbv[:, 3, :], ALU.mult, kr_, [p2k])
                self.tt("dve", srv[:, jj, :], p1[:, 0:512], p2[:, 0:512], ALU.subtract, [p1k, p2k], [srek])
                for q in range(4):
                    sl = slice(q * 128, (q + 1) * 128)
                    self.tt("dve", p1[:, sl], zor[:, jj, sl], tbv[:, 3, :], ALU.mult, kr_ + [p1k], [p1k])
                    self.tt("dve", p2[:, sl], zoi[:, jj, sl], tbv[:, 2, :], ALU.mult, kr_ + [p2k], [p2k])
                self.tt("dve", p1[:, 0:512], p1[:, 0:512], p2[:, 0:512], ALU.add, [p1k, p2k], [p1k])
                self.ts("dve", nsv[:, jj, :], p1[:, 0:512], -1.0, None, ALU.mult, None, [p1k], [nsik])
                self.unscr((p1, p1k)); self.unscr((p2, p2k))
                self.mm(yb[:, :], csv[:, 0, jj, :], srv[:, jj, :], jj == 0, False, [csk, srek], [ybk])
                self.mm(yb[:, :], csv[:, 1, jj, :], nsv[:, jj, :], False, jj == 3, [csk, nsik], [ybk])
            self.stt("dve", yv[:, 0:512], usb[:, 0:512], self.pvc("s5_d", c), yb[:, :], ALU.mult, ALU.add, [usbk, "pv", ybk], [yvk])
            self.unbank((yb, ybk))
            self.gelu(yv[:, 0:512], yvk, ygv[:, c, :], ygk, tmp[:, 0:512], tmpk)
        wg, wgk = self.wload(self.wb["s5_w_glu"][l].rearrange("(k p) c -> p k c", p=128), [128, 4, 1024])
        for oc in range(4):
            za, zak = self.bank(); zb, zbk = self.bank()
            self.proj(za[:, :], wg, oc * 128, 128, 4, lambda kc: ygv[:, kc, :], [wgk, ygk], [zak])
            self.proj(zb[:, :], wg, 512 + oc * 128, 128, 4, lambda kc: ygv[:, kc, :], [wgk, ygk], [zbk])
            self.act(tmp[:, 0:512], zb[:, :], AF.Sigmoid, [zbk], [tmpk])
            self.tt("dve", self.ys[1][:, oc, :], za[:, :], tmp[:, 0:512], ALU.mult, [zak, tmpk], ["arB"])
            self.unbank((za, zak)); self.unbank((zb, zbk))
        for s_ in [(usb, usbk), (ubf, ubfk), (tmp, tmpk), (yv, yvk), (tc, tck), (bs, bsk), (cs, csk)] + tabs:
            self.unscr(s_)
        for s_ in [(yg, ygk), (sre, srek), (nsi, nsik)]:
            self.unscr(s_, 4)
        for z in Z:
            self.unscr(z, 8)

    def proj_shift(self, slab, slk, c0, m, mu_ap, ccol):
        hf = lambda kc: self.hT[:, kc, :]
        b, bk = self.bank()
        self.proj(b[0:m, :], slab, c0, m, 8, hf, [slk, "hT"], [bk])
        R, Rk = self.scr(4)
        self.cp("act", R[0:m, 1:513], b[0:m, :], [bk], [Rk])
        self.unbank((b, bk))
        ck = "carry%d" % ccol
        self.cp("dve", R[0:m, 0:1], self.carry[0:m, ccol:ccol + 1], [ck], [Rk])
        d, dk = self.scr()
        o, ok_ = self.scr()
        self.tt("dve", d[0:m, 0:512], R[0:m, 0:512], R[0:m, 1:513], ALU.subtract, [Rk], [dk])
        self.stt("dve", o[0:m, 0:512], d[0:m, 0:512], mu_ap, R[0:m, 1:513], ALU.mult, ALU.add, [dk, "pv", Rk], [ok_])
        self.cp("dve", self.carry[0:m, ccol:ccol + 1], R[0:m, 512:513], [Rk], [ck])
        self.unscr((R, Rk), 4); self.unscr((d, dk))
        return o, ok_

    def rwkv(self, l, ti):
        import os
        self.rstage = int(os.environ.get('RSTAGE', '99'))
        slA, slAk = self.win(3072)
        bfv = lambda t, n=256: t[:, 0:n].bitcast(BF16)
        wl, wlk = self.proj_shift(slA, slAk, 0, 64, self.pvc("mu_w", 0, 64), 12)
        twl, twlk = self.scr()
        self.act(bfv(twl)[0:64], wl[0:64, 0:512], AF.Tanh, [wlk], [twlk]); self.unscr((wl, wlk))
        al, alk = self.proj_shift(slA, slAk, 64, 64, self.pvc("mu_a", 0, 64), 13)
        alb, albk = self.scr()
        self.cp("dve", bfv(alb)[0:64], al[0:64, 0:512], [alk], [albk]); self.unscr((al, alk))
        gl, glk = self.proj_shift(slA, slAk, 128, 128, self.pvc("mu_g"), 14)
        sgl, sglk = self.scr()
        self.act(bfv(sgl), gl[:, 0:512], AF.Sigmoid, [glk], [sglk]); self.unscr((gl, glk))
        slr, slrk = self.win(1536); slk_, slkk = self.win(2048); slv, slvk = self.win(2560)
        m3 = lambda ap: ap.rearrange("p (q t) -> p q t", q=8)
        def prep(cc):
            rm, rmk = self.proj_shift(slr, slrk, cc * 128, 128, self.pvc("mu_rkv", cc), cc)
            km, kmk = self.proj_shift(slk_, slkk, cc * 128, 128, self.pvc("mu_rkv", 4 + cc), 4 + cc)
            vm, vmk = self.proj_shift(slv, slvk, cc * 128, 128, self.pvc("mu_rkv", 8 + cc), 8 + cc)
            ew, ewk = self.scr(); cs, csk = self.scr(); t1, t1k = self.scr()
            b, bk = self.bank()
            self.mm(b[:, :], self.w2_sb[:, cc * 128:(cc + 1) * 128], bfv(twl)[0:64], True, True, ["w2_sb", twlk], [bk])
            self.act(ew[:, 0:512], b[:, :], AF.Exp, [bk, "der"], [ewk], scale=-1.0, bias=self.der[:, 4 + cc:5 + cc])
            self.unbank((b, bk))
            self.act(ew[:, 0:512], ew[:, 0:512], AF.Ln, [ewk], [ewk], bias=1.0)
            self.ts("dve", ew[:, 0:512], ew[:, 0:512], -1.0, -0.5, ALU.mult, ALU.add, [ewk], [ewk])
            self.act(ew[:, 0:512], ew[:, 0:512], AF.Exp, [ewk], [ewk])
            self.scan(cs[:, 0:512], self.reset64[:, :], ew[:, 0:512], 0.0, ["reset64", ewk], [csk])
            E1, E1k = self.scr(); E2, E2k = self.scr(); E3, E3k = self.scr(); Ex, Exk = self.scr()
            self.act(E1[:, 0:512], cs[:, 0:512], AF.Exp, [csk], [E1k], scale=-1.0)
            self.act(E2[:, 0:512], cs[:, 0:512], AF.Exp, [csk], [E2k])
            self.tt("dve", t1[:, 0:512], cs[:, 0:512], ew[:, 0:512], ALU.subtract, [csk, ewk], [t1k])
            self.act(Ex[:, 0:512], t1[:, 0:512], AF.Exp, [t1k], [Exk], scale=-1.0)
            self.tt("dve", m3(t1[:, 0:512]), m3(cs[:, 0:512])[:, :, 63:64].broadcast_to([128, 8, 64]), m3(cs[:, 0:512]), ALU.subtract, [csk], [t1k])
            self.act(E3[:, 0:512], t1[:, 0:512], AF.Exp, [t1k], [E3k], scale=-1.0)
            ag, agk = self.scr()
            b, bk = self.bank()
            self.mm(b[:, :], self.a2_sb[:, cc * 128:(cc + 1) * 128], bfv(alb)[0:64], True, True, ["a2_sb", albk], [bk])
            self.act(ag[:, 0:512], b[:, :], AF.Sigmoid, [bk, "pv"], [agk], bias=self.pvc("a0", cc))
            self.unbank((b, bk))
            kk, kkk = self.scr()
            self.ts("dve", kk[:, 0:512], km[:, 0:512], self.pvc("k_k", cc), None, ALU.mult, None, [kmk, "pv"], [kkk])
            self.tt("dve", t1[:, 0:512], kk[:, 0:512], kk[:, 0:512], ALU.mult, [kkk], [t1k])
            b, bk = self.bank()
            self.mm(b[:, :], self.blk64[:, :], t1[:, 0:512], True, True, ["blk64", t1k], [bk])
            self.act(t1[:, 0:512], b[:, :], AF.Sqrt, [bk], [t1k], bias=1e-12)
            self.unbank((b, bk))
            self.recip(t1[:, 0:512], t1[:, 0:512], [t1k], [t1k])
            self.tt("dve", kk[:, 0:512], kk[:, 0:512], t1[:, 0:512], ALU.mult, [kkk, t1k], [kkk])
            self.ts("dve", t1[:, 0:512], ag[:, 0:512], self.pvc("k_a", cc), self.der[:, 8 + cc:9 + cc], ALU.mult, ALU.add, [agk, "pv", "der"], [t1k])
            self.tt("dve", km[:, 0:512], km[:, 0:512], t1[:, 0:512], ALU.mult, [kmk, t1k], [kmk])
            self.tt("dve", ag[:, 0:512], ag[:, 0:512], kk[:, 0:512], ALU.mult, [agk, kkk], [agk])
            AR, ARk = self.scr(); BT, BTk = self.scr(); KT, KTk = self.scr(); BH, BHk = self.scr(); KH, KHk = self.scr(); RK, RKk = self.scr()
            ARv = AR[:, :].bitcast(BF16).rearrange("p (q a t) -> p q a t", q=8, a=2)
            self.stt("dve", ARv[:, :, 0, :], m3(kk[:, 0:512]), -1.0, m3(Ex[:, 0:512]), ALU.mult, ALU.mult, [kkk, Exk], [ARk])
            self.tt("dve", ARv[:, :, 1, :], m3(rm[:, 0:512]), m3(E1[:, 0:512]), ALU.mult, [rmk, E1k], [ARk])
            self.cp("dve", self.pc_t[:, cc, :], m3(E1[:, 0:512])[:, :, 63], [E1k], ["pc%d" % cc])
            self.tt("dve", bfv(BT), ag[:, 0:512], E2[:, 0:512], ALU.mult, [agk, E2k], [BTk])
            self.tt("dve", bfv(KT), km[:, 0:512], E2[:, 0:512], ALU.mult, [kmk, E2k], [KTk])
            self.tt("dve", bfv(BH), ag[:, 0:512], E3[:, 0:512], ALU.mult, [agk, E3k], [BHk])
            self.tt("dve", bfv(KH), km[:, 0:512], E3[:, 0:512], ALU.mult, [kmk, E3k], [KHk])
            self.stt("dve", bfv(RK), rm[:, 0:512], self.pvc("r_k", cc), km[:, 0:512], ALU.mult, ALU.mult, [rmk, "pv", kmk], [RKk])
            for s_ in [(rm, rmk), (km, kmk), (ew, ewk), (cs, csk), (t1, t1k), (E1, E1k), (E2, E2k), (E3, E3k), (Ex, Exk), (ag, agk), (kk, kkk)]:
                self.unscr(s_)
            BHt, BHtk = self.scr(); KHt, KHtk = self.scr(); Vt, Vtk = self.scr(4); Vtb, Vtbk = self.scr()
            tm = lambda t: t[0:64, :].bitcast(BF16).rearrange("p (q c) -> p q c", q=8)
            for (src, srck, dst, dstk) in [(BH, BHk, BHt, BHtk), (KH, KHk, KHt, KHtk)]:
                b, bk = self.bank()
                bb = b[:, :].bitcast(BF16)
                for q in range(8):
                    self.trp(bb[0:64, q * 128:(q + 1) * 128], bfv(src)[:, q * 64:(q + 1) * 64], self.identb[:], [srck, "identb"], [bk])
                self.cp("act", tm(dst), bb[0:64, 0:1024].rearrange("p (q c) -> p q c", q=8), [bk], [dstk])
                self.unbank((b, bk))
            Vtv = Vt[0:64, :].rearrange("p (q c) -> p q c", q=8)
            for hf_ in range(2):
                b, bk = self.bank()
                for q4_ in range(4):
                    q = hf_ * 4 + q4_
                    self.trp(b[0:64, q4_ * 128:(q4_ + 1) * 128], vm[:, q * 64:(q + 1) * 64], self.identf[:], [vmk, "ident"], [bk])
                self.cp("act", Vtv[:, hf_ * 4:hf_ * 4 + 4, :], b[0:64, :].rearrange("p (q c) -> p q c", q=4), [bk], [Vtk])
                self.unbank((b, bk))
            self.cp("dve", tm(Vtb), Vtv, [Vtk], [Vtbk])
            self.unscr((vm, vmk))
            mk = lambda i: self.rmask[:, i:i + 1, :].broadcast_to([64, 8, 64])
            NP = [self.scr() for _ in range(2)]; XP = [self.scr() for _ in range(2)]; PP = [self.scr() for _ in range(2)]; QQ = [self.scr() for _ in range(2)]
            ARB, ARBk = self.scr(); AAK, AAKk = self.scr(); ARK, ARKk = self.scr()
            mt = lambda t: t[0:64, :].bitcast(BF16).rearrange("p (m t) -> p m t", m=16)
            for hf_ in range(2):
                BN = [self.bank(), self.bank()]; BK = [self.bank(), self.bank()]; BX = [self.bank(), self.bank()]
                for q4_ in range(4):
                    q = hf_ * 4 + q4_
                    for e in range(2):
                        rows = slice(e * 64, (e + 1) * 64)
                        arhs = ARv[rows, q, :, :]
                        co = q4_ * 128
                        self.mm(BN[e][0][0:64, co:co + 128], bfv(BT)[rows, q * 64:(q + 1) * 64], arhs, True, True, [BTk, ARk], [BN[e][1]])
                        self.mm(BK[e][0][0:64, co:co + 128], bfv(KT)[rows, q * 64:(q + 1) * 64], arhs, True, True, [KTk, ARk], [BK[e][1]])
                        self.mm(BX[e][0][0:64, q4_ * 64:(q4_ + 1) * 64], ARv[rows, q, 0, :], bfv(BT)[rows, q * 64:(q + 1) * 64], True, True, [ARk, BTk], [BX[e][1]])
                v4 = lambda bnk: bnk[0:64, :].rearrange("p (m a t) -> p m a t", m=4, a=2)
                mk4 = lambda i: self.rmask[:, i:i + 1, :].broadcast_to([64, 4, 64])
                for e in range(2):
                    ms = slice(hf_ * 8 + e, hf_ * 8 + 8, 2)
                    self.tt("dve", mt(NP[0][0])[:, ms, :], v4(BN[e][0])[:, :, 0, :], mk4(0), ALU.mult, [BN[e][1], "rmask"], [NP[0][1]])
                    self.tt("dve", mt(ARB)[:, ms, :], v4(BN[e][0])[:, :, 1, :], mk4(1), ALU.mult, [BN[e][1], "rmask"], [ARBk])
                    self.tt("dve", mt(AAK)[:, ms, :], v4(BK[e][0])[:, :, 0, :], mk4(0), ALU.mult, [BK[e][1], "rmask"], [AAKk])
                    self.tt("dve", mt(ARK)[:, ms, :], v4(BK[e][0])[:, :, 1, :], mk4(1), ALU.mult, [BK[e][1], "rmask"], [ARKk])
                    self.tt("dve", mt(XP[0][0])[:, ms, :], BX[e][0][0:64, 0:256].rearrange("p (m t) -> p m t", m=4), mk4(2), ALU.mult, [BX[e][1], "rmask"], [XP[0][1]])
                for bb_ in BN + BK + BX:
                    self.unbank(bb_)
            idb = self.identb[0:64, 0:64].unsqueeze(1).broadcast_to([64, 16, 64])
            self.tt("dve", mt(PP[0][0]), mt(NP[0][0]), idb, ALU.add, [NP[0][1], "identb"], [PP[0][1]])
            self.tt("dve", mt(QQ[0][0]), mt(XP[0][0]), idb, ALU.add, [XP[0][1], "identb"], [QQ[0][1]])
            for k in range(5):
                c_, n_ = k % 2, (k + 1) % 2
                for hf_ in range(2):
                    hs = slice(hf_ * 8, hf_ * 8 + 8)
                    bn_, bnk_ = self.bank(); bx_, bxk_ = self.bank()
                    for m8 in range(8):
                        m = hf_ * 8 + m8
                        self.mm(bn_[0:64, m8 * 64:(m8 + 1) * 64], mt(XP[c_][0])[:, m, :], mt(NP[c_][0])[:, m, :], True, True, [XP[c_][1], NP[c_][1]], [bnk_])
                        if k < 4:
                            self.mm(bx_[0:64, m8 * 64:(m8 + 1) * 64], mt(NP[c_][0])[:, m, :], mt(XP[c_][0])[:, m, :], True, True, [XP[c_][1], NP[c_][1]], [bxk_])
                    self.cp("act", mt(NP[n_][0])[:, hs, :], bn_[0:64, :].rearrange("p (m t) -> p m t", m=8), [bnk_], [NP[n_][1] + "#%d" % hf_])
                    if k < 4:
                        self.cp("act", mt(XP[n_][0])[:, hs, :], bx_[0:64, :].rearrange("p (m t) -> p m t", m=8), [bxk_], [XP[n_][1] + "#%d" % hf_])
                    self.unbank((bn_, bnk_)); self.unbank((bx_, bxk_))
                for hf_ in range(2):
                    hs = slice(hf_ * 8, hf_ * 8 + 8)
                    bp_, bpk_ = self.bank(); bq_, bqk_ = self.bank()
                    for m8 in range(8):
                        m = hf_ * 8 + m8
                        self.mm(bp_[0:64, m8 * 64:(m8 + 1) * 64], mt(QQ[c_][0])[:, m, :], mt(NP[n_][0])[:, m, :], True, True, [QQ[c_][1], NP[n_][1] + "#%d" % hf_], [bpk_])
                        if k < 4:
                            self.mm(bq_[0:64, m8 * 64:(m8 + 1) * 64], mt(PP[c_][0])[:, m, :], mt(XP[n_][0])[:, m, :], True, True, [PP[c_][1], XP[n_][1] + "#%d" % hf_], [bqk_])
                    self.tt("dve", mt(PP[n_][0])[:, hs, :], bp_[0:64, :].rearrange("p (m t) -> p m t", m=8), mt(PP[c_][0])[:, hs, :], ALU.add, [bpk_, PP[c_][1]], [PP[n_][1] + "#%d" % hf_])
                    if k < 4:
                        self.tt("dve", mt(QQ[n_][0])[:, hs, :], bq_[0:64, :].rearrange("p (m t) -> p m t", m=8), mt(QQ[c_][0])[:, hs, :], ALU.add, [bqk_, QQ[c_][1]], [QQ[n_][1] + "#%d" % hf_])
                    self.unbank((bp_, bpk_)); self.unbank((bq_, bqk_))
            PF, PFk = PP[1]
            for s_ in [(BT, BTk), (KT, KTk), (BH, BHk), (KH, KHk), PP[0]] + NP + XP + QQ:
                self.unscr(s_)
            ytm, ytmk = self.scr(4)
            ytv = ytm[0:64, :].rearrange("p (q c) -> p q c", q=8)
            U = [self.scr(), self.scr()]
            ub = lambda i: U[i][0][0:64, 0:64].bitcast(BF16)
            s0k = "s0_%d" % cc
            return locals()

        def step(L, q):
            cc, ARv, ARk, AAK, AAKk, ARB, ARBk, ARK, ARKk, Vtb, Vtbk, Vt, Vtk, PF, PFk, BHt, BHtk, KHt, KHtk, U, ub, ytm, ytmk, ytv, s0k, RK, RKk, AR, mt, tm = (L[k_] for k_ in ['cc', 'ARv', 'ARk', 'AAK', 'AAKk', 'ARB', 'ARBk', 'ARK', 'ARKk', 'Vtb', 'Vtbk', 'Vt', 'Vtk', 'PF', 'PFk', 'BHt', 'BHtk', 'KHt', 'KHtk', 'U', 'ub', 'ytm', 'ytmk', 'ytv', 's0k', 'RK', 'RKk', 'AR', 'mt', 'tm'])
            b, bk = self.bank()
            self.mm(b[0:64, 0:128], ARv[:, q, 0, :], self.s0bd[:, cc, :], True, False, [ARk, s0k], [bk])
            for e in range(2):
                self.mm(b[0:64, e * 64:(e + 1) * 64], mt(AAK)[:, 2 * q + e, :], tm(Vtb)[:, q, e * 64:(e + 1) * 64], False, e == 1, [AAKk, Vtbk], [bk])
            self.cp("act", ub(0), b[0:64, 0:128], [bk], [U[0][1]])
            self.unbank((b, bk))
            b, bk = self.bank()
            for e in range(2):
                self.mm(b[0:64, e * 64:(e + 1) * 64], mt(PF)[:, 2 * q + e, :], ub(0)[:, e * 64:(e + 1) * 64], True, True, [PFk, U[0][1]], [bk])
            self.cp("act", ub(1), b[0:64, 0:128], [bk], [U[1][1]])
            self.unbank((b, bk))
            cur = 1
            uf, ufk = ub(cur), U[cur][1]
            b, bk = self.bank()
            self.mm(b[0:64, 0:128], ARv[:, q, 1, :], self.s0bd[:, cc, :], True, False, [ARk, s0k], [bk])
            for e in range(2):
                self.mm(b[0:64, e * 64:(e + 1) * 64], mt(ARB)[:, 2 * q + e, :], uf[:, e * 64:(e + 1) * 64], False, False, [ARBk, ufk], [bk])
                self.mm(b[0:64, e * 64:(e + 1) * 64], mt(ARK)[:, 2 * q + e, :], tm(Vtb)[:, q, e * 64:(e + 1) * 64], False, e == 1, [ARKk, Vtbk], [bk])
            self.cp("act", ytv[:, q, :], b[0:64, 0:128], [bk], [ytmk])
            self.unbank((b, bk))
            b, bk = self.bank()
            self.mm(b[:, 0:128], tm(BHt)[:, q, :], uf, True, False, [BHtk, ufk], [bk])
            self.mm(b[:, 0:128], tm(KHt)[:, q, :], tm(Vtb)[:, q, :], False, True, [KHtk, Vtbk], [bk])
            for e in range(2):
                rows = slice(e * 64, (e + 1) * 64)
                self.stt("dve", self.s0f[rows, cc, :], self.s0f[rows, cc, :], self.pc_t[rows, cc, q:q + 1], b[rows, e * 64:(e + 1) * 64], ALU.mult, ALU.add, [s0k + "f", "pc%d" % cc, bk], [s0k + "f"])
                self.cp("dve", self.s0bd[rows, cc, e * 64:(e + 1) * 64], self.s0f[rows, cc, :], [s0k + "f"], [s0k])
            self.unbank((b, bk))

        def post(L):
            cc, ARv, ARk, AAK, AAKk, ARB, ARBk, ARK, ARKk, Vtb, Vtbk, Vt, Vtk, PF, PFk, BHt, BHtk, KHt, KHtk, U, ub, ytm, ytmk, ytv, s0k, RK, RKk, AR, mt, tm = (L[k_] for k_ in ['cc', 'ARv', 'ARk', 'AAK', 'AAKk', 'ARB', 'ARBk', 'ARK', 'ARKk', 'Vtb', 'Vtbk', 'Vt', 'Vtk', 'PF', 'PFk', 'BHt', 'BHtk', 'KHt', 'KHtk', 'U', 'ub', 'ytm', 'ytmk', 'ytv', 's0k', 'RK', 'RKk', 'AR', 'mt', 'tm'])
            g16 = lambda ap: ap.rearrange("p (g v) -> p g v", g=16)
            yv = g16(ytm[0:64, :])
            st_, stk = self.scr(); sq, sqk = self.scr(4)
            self.P.op("dve", lambda e, o=st_[0:64, 0:16], i=yv: e.tensor_reduce(out=o, in_=i, axis=AX.X, op=ALU.add), [ytmk], [stk])
            self.tt("dve", sq[0:64, :], ytm[0:64, :], ytm[0:64, :], ALU.mult, [ytmk], [sqk])
            self.P.op("dve", lambda e, o=st_[0:64, 16:32], i=g16(sq[0:64, :]): e.tensor_reduce(out=o, in_=i, axis=AX.X, op=ALU.add), [sqk], [stk])
            self.ts("dve", st_[0:64, 0:32], st_[0:64, 0:32], 1.0 / 64, None, ALU.mult, None, [stk], [stk])
            self.tt("dve", st_[0:64, 32:48], st_[0:64, 0:16], st_[0:64, 0:16], ALU.mult, [stk], [stk])
            self.tt("dve", st_[0:64, 16:32], st_[0:64, 16:32], st_[0:64, 32:48], ALU.subtract, [stk], [stk])
            self.act(st_[0:64, 16:32], st_[0:64, 16:32], AF.Sqrt, [stk], [stk], bias=64e-5)
            self.recip(st_[0:64, 16:32], st_[0:64, 16:32], [stk], [stk])
            bc = lambda ap: ap.unsqueeze(2).broadcast_to([64, 16, 64])
            self.tt("dve", yv, yv, bc(st_[0:64, 0:16]), ALU.subtract, [ytmk, stk], [ytmk])
            self.tt("dve", yv, yv, bc(st_[0:64, 16:32]), ALU.mult, [ytmk, stk], [ytmk])
            lg = self.lnx[:, cc * 128:(cc + 1) * 128].unsqueeze(1).broadcast_to([64, 8, 128])
            lb = self.lnx[:, 512 + cc * 128:512 + (cc + 1) * 128].unsqueeze(1).broadcast_to([64, 8, 128])
            self.tt("dve", ytv, ytv, lg, ALU.mult, [ytmk, "lnx"], [ytmk])
            self.tt("dve", ytv, ytv, lb, ALU.add, [ytmk, "lnx"], [ytmk])
            b, bk = self.bank()
            for q in range(8):
                self.mm(b[0:64, q * 2:q * 2 + 2], bfv(RK)[:, q * 64:(q + 1) * 64], self.headselb[:, :], True, True, [RKk, "headselb"], [bk])
            self.cp("act", st_[0:64, 0:16], b[0:64, 0:16], [bk], [stk])
            self.unbank((b, bk))
            self.tt("dve", g16(sq[0:64, :]), g16(Vt[0:64, :]), bc(st_[0:64, 0:16]), ALU.mult, [Vtk, stk], [sqk])
            self.tt("dve", ytm[0:64, :], ytm[0:64, :], sq[0:64, :], ALU.add, [ytmk, sqk], [ytmk])
            gg, ggk = self.scr()
            bgt, bgtk = self.bank()
            self.mm(bgt[:, :], self.g2_sb[:, cc * 128:(cc + 1) * 128], bfv(sgl), True, True, ["g2_sb", sglk], [bgtk])
            self.cp("act", gg[:, 0:512], bgt[:, :], [bgtk], [ggk])
            self.unbank((bgt, bgtk))
            yb_, ybk_ = self.scr()
            self.cp("dve", tm(yb_), ytv, [ytmk], [ybk_])
            b, bk = self.bank()
            bb = b[:, :].bitcast(BF16)
            for q in range(8):
                self.trp(bb[:, q * 64:(q + 1) * 64], tm(yb_)[:, q, :], self.identb[0:64, 0:64], [ybk_, "identb"], [bk])
            self.tt("dve", self.ys[2][:, cc, :], bb[:, 0:512], gg[:, 0:512], ALU.mult, [bk, ggk], ["arB"])
            self.unbank((b, bk))
            for s_ in [(gg, ggk), (L["AR"], ARk), (RK, RKk), (BHt, BHtk), (KHt, KHtk), (Vtb, Vtbk), (ARB, ARBk), (AAK, AAKk), (ARK, ARKk),
                       (st_, stk), (yb_, ybk_), (PF, PFk)] + U:
                self.unscr(s_)
            for s_ in [(Vt, Vtk), (ytm, ytmk), (sq, sqk)]:
                self.unscr(s_, 4)

        for pr_ in range(2):
            La = prep(2 * pr_); Lb = prep(2 * pr_ + 1)
            for q in range(8):
                step(La, q); step(Lb, q)
            post(La); post(Lb)
        for s_ in [(twl, twlk), (alb, albk), (sgl, sglk)]:
            self.unscr(s_)

    def rstd_bcast(self, dst, dstk, srcs, nfeat, nparts, ones_lhsT):
        sq, sqk = self.scr()
        b, bk = self.bank()
        for i, (ap, k) in enumerate(srcs):
            self.act(sq[0:ap.shape[0], 0:512], ap, AF.Square, [k], [sqk])
            self.mm(b[0:nparts, :], ones_lhsT(ap.shape[0]), sq[0:ap.shape[0], 0:512], i == 0, i == len(srcs) - 1, [sqk, "ones"], [bk])
        self.act(dst, b[0:nparts, :], AF.Sqrt, [bk], [dstk], scale=1.0 / nfeat, bias=EPS)
        self.recip(dst, dst, [dstk], [dstk])
        self.unbank((b, bk))
        self.unscr((sq, sqk))

    def qk_stages(self, pre, src, srck, gname, out, outk, post):
        st = {}
        S = list(pre)

        def s1():
            st["sq"] = self.scr(); st["rs"] = self.scr()
            self.act(st["sq"][0][0:96, 0:512], src, AF.Square, [srck], [st["sq"][1]])

        def s2():
            st["b"] = self.bank()
            self.mm(st["b"][0][0:96, :], self.onesf[0:96, 0:96], st["sq"][0][0:96, 0:512], True, True, [st["sq"][1], "ones"], [st["b"][1]])

        def s3():
            self.act(st["rs"][0][0:96, 0:512], st["b"][0][0:96, :], AF.Sqrt, [st["b"][1]], [st["rs"][1]], scale=1.0 / 96, bias=EPS)
            self.unbank(st["b"]); self.unscr(st["sq"])

        def s4():
            self.recip(st["rs"][0][0:96, 0:512], st["rs"][0][0:96, 0:512], [st["rs"][1]], [st["rs"][1]])

        def s5():
            self.stt("dve", src, src, self.pvc(gname, 0, 96), st["rs"][0][0:96, 0:512], ALU.mult, ALU.mult, [srck, "pv", st["rs"][1]], [srck])

        def s6():
            st["b"] = self.bank()
            self.mm(st["b"][0][0:96, :], self.prot[:, :], src, True, True, ["prot", srck], [st["b"][1]])

        def s7():
            self.tt("dve", st["rs"][0][0:96, 0:512], st["b"][0][0:96, :], self.rsin[0:96, 0:512], ALU.mult, [st["b"][1], self.rsink], [st["rs"][1]])
            self.unbank(st["b"])

        def s8():
            self.tt("dve", src, src, self.rcos[0:96, 0:512], ALU.mult, [srck, self.rcosk], [srck])

        def s9():
            self.tt("dve", out, src, st["rs"][0][0:96, 0:512], ALU.add, [srck, st["rs"][1]], [outk])
            self.unscr(st["rs"])
        return S + [s1, s2, s3, s4, s5, s6, s7, s8, s9] + list(post)

    @staticmethod
    def interleave(chains):
        n = max(len(c) for c in chains)
        for i in range(n):
            for c in chains:
                if i < len(c):
                    c[i]()

    def mla(self, l, ti):
        t0 = ti * TT
        hf = lambda kc: self.hT[:, kc, :]
        slA, slAk = self.win(3072)
        slB, slBk = self.win(3584, 160)
        (self.rcos, self.rcosk), (self.rsin, self.rsink) = self.scr(), self.scr()
        self.dma("sp", self.rcos[0:96, 0:512], self.cd["ropec"][:, t0:t0 + 512], [], [self.rcosk])
        self.dma("sp", self.rsin[0:96, 0:512], self.cd["ropes"][:, t0:t0 + 512], [], [self.rsink])
        cq = [self.scr() for _ in range(2)]
        for c in range(2):
            b, bk = self.bank()
            self.proj(b[:, :], slA, 256 + c * 128, 128, 8, hf, [slAk, "hT"], [bk])
            self.cp("act", cq[c][0][:, 0:512], b[:, :], [bk], [cq[c][1]])
            self.unbank((b, bk))
        rs, rsk = self.scr()
        self.rstd_bcast(rs[:, 0:512], rsk, [(cq[0][0][:, 0:512], cq[0][1]), (cq[1][0][:, 0:512], cq[1][1])], 256, 128, lambda n: self.onesf[:, :])
        cqn, cqnk = self.scr()
        cqnv = cqn[:, 0:512].bitcast(BF16).rearrange("p (c t) -> p c t", c=2)
        for c in range(2):
            self.stt("dve", cqnv[:, c, :], cq[c][0][:, 0:512], self.pvc("q_norm", c), rs[:, 0:512], ALU.mult, ALU.mult, [cq[c][1], "pv", rsk], [cqnk])
        ckv, ckvk = cq[0]
        b, bk = self.bank()
        self.proj(b[:, :], slB, 0, 128, 8, hf, [slBk, "hT"], [bk])
        self.cp("act", ckv[:, 0:512], b[:, :], [bk], [ckvk])
        self.unbank((b, bk))
        self.rstd_bcast(rs[:, 0:512], rsk, [(ckv[:, 0:512], ckvk)], 128, 128, lambda n: self.onesf[:, :])
        ckvn, ckvnk = self.scr()
        ckvnv = ckvn[:, 0:256].bitcast(BF16)
        self.stt("dve", ckvnv, ckv[:, 0:512], self.pvc("kv_norm"), rs[:, 0:512], ALU.mult, ALU.mult, [ckvk, "pv", rsk], [ckvnk])
        kr, krk = cq[1]
        b, bk = self.bank()
        self.proj(b[0:32, :], slB, 128, 32, 8, hf, [slBk, "hT"], [bk])
        self.cp("act", kr[64:96, 0:512], b[0:32, :], [bk], [krk])
        self.unbank((b, bk))
        self.unscr((rs, rsk))
        vt, vtk = self.scr(8)
        vtv = vt[:, 0:1040].bitcast(BF16).rearrange("p (s h e) -> p s h e", s=4, h=8)
        self.memset("pool", vtv[:, :, :, 64:65], 1.0, [vtk])
        wv = self.wukv_sb[:, :].rearrange("p (h e) -> p h e", h=8)[:, :, 64:128]
        for s in range(4):
            b, bk = self.bank()
            self.mm(b[:, :].rearrange("p (h e) -> p h e", h=8), ckvnv[:, s * 128:(s + 1) * 128], wv, True, True, [ckvnk, "wukv_sb"], [bk])
            self.cp("act" if s % 2 else "dve", vtv[:, s, :, 0:64], b[:, :].rearrange("p (h e) -> p h e", h=8), [bk], [vtk])
            self.unbank((b, bk))
        for h in range(8):
            self.dma("pool", self.vc[h, :, 4 * ti:4 * ti + 4, :], vtv[:, :, h, :], [vtk], ["vc"])
        self.unscr((vt, vtk), 8)
        def kchain(h, kt, ktk, kb_, kbk):
            kbv = kb_[0:96, 0:256].bitcast(BF16)
            st = {}

            def p1():
                st["b"] = self.bank()
                self.mm(st["b"][0][0:64, :], self.wukv_sb[:, h * 128:h * 128 + 64], ckvnv, True, True, ["wukv_sb", ckvnk], [st["b"][1]])

            def p2():
                self.cp("act", kt[0:64, 0:512], st["b"][0][0:64, :], [st["b"][1]], [ktk])
                self.unbank(st["b"])

            def p3():
                self.cp("pool", kt[64:96, 0:512], kr[64:96, 0:512], [krk], [ktk])

            def post():
                self.dma("pool", self.kc[h, :, t0:t0 + 512], kbv, [kbk], ["kc"])
            return self.qk_stages([p1, p2, p3], kt[0:96, 0:512], ktk, "qkn_k", kbv, kbk, [post])
        KB4 = [(self.scr(), self.scr()) for _ in range(4)]
        for hg in range(2):
            self.interleave([kchain(hg * 4 + i, KB4[i][0][0], KB4[i][0][1], KB4[i][1][0], KB4[i][1][1]) for i in range(4)])
        for (a_, b_) in KB4:
            self.unscr(a_); self.unscr(b_)
        nkt = 4 * (ti + 1)
        QB = [self.scr(), self.scr()]
        QF = [self.scr(), self.scr()]
        qbv_ = lambda i: QB[i][0][0:96, 0:256].bitcast(BF16)
        KBUF = [self.scr(8), self.scr(8)]
        VBUF = [self.scr(8), self.scr(8)]
        pts = [self.scr() for _ in range(3)]
        rl, rlk = self.scr()

        def qchain(h):
            i = h % 2
            qf, qfk = QF[i]
            st = {}

            def p1():
                st["b"] = self.bank()
                for c in range(2):
                    self.mm(st["b"][0][0:96, :], self.wuq_sb[:, c, h * 96:(h + 1) * 96], cqnv[:, c, :], c == 0, c == 1, ["wuq_sb", cqnk], [st["b"][1]])

            def p2():
                self.cp("act", qf[0:96, 0:512], st["b"][0][0:96, :], [st["b"][1]], [qfk])
                self.unbank(st["b"])

            def p3():
                kbv2 = KBUF[i][0][0:96, 0:2048].bitcast(BF16)
                vbv = VBUF[i][0][:, 0:1040].bitcast(BF16).rearrange("p (k e) -> p k e", k=32)
                self.dma("sp", kbv2[:, 0:nkt * 128], self.kc[h, :, 0:nkt * 128], ["kc"], [KBUF[i][1]])
                self.dma("sp", vbv[:, 0:nkt, :], self.vc[h, :, 0:nkt, :], ["vc"], [VBUF[i][1]])
            return self.qk_stages([p1, p2, p3], qf[0:96, 0:512], qfk, "qkn_q", qbv_(i), QB[i][1], [])
        for f in qchain(0):
            f()
        for h in range(8):
            i = h % 2
            qbv, qbk = qbv_(i), QB[i][1]
            kbv2 = KBUF[i][0][0:96, 0:2048].bitcast(BF16)
            kbufk = KBUF[i][1]
            vbv = VBUF[i][0][:, 0:1040].bitcast(BF16).rearrange("p (k e) -> p k e", k=32)
            vbufk = VBUF[i][1]
            nxt = qchain(h + 1) if h < 7 else []
            per = -(-len(nxt) // nkt) if nxt else 0
            ob, obk = self.bank()
            for k in range(nkt):
                sb_, sbk = self.bank()
                self.mm(sb_[:, :], kbv2[:, k * 128:(k + 1) * 128], qbv, True, True, [kbufk, qbk], [sbk])
                pt, ptk = pts[k % 3]
                ptv = pt[:, 0:256].bitcast(BF16)
                self.act(ptv, sb_[:, :], AF.Exp, [sbk], [ptk], scale=96.0 ** -0.5)
                self.unbank((sb_, sbk))
                if k >= 4 * ti:
                    self.tt("dve", ptv, ptv, self.amask[:, k - 4 * ti, :], ALU.mult, [ptk, "amask"], [ptk])
                self.mm(ob[0:65, :], vbv[:, k, :], ptv, k == 0, k == nkt - 1, [vbufk, ptk], [obk])
                for f in nxt[k * per:(k + 1) * per]:
                    f()
            self.recip(rl[64:65, 0:512], ob[64:65, :], [obk], [rlk])
            bc, bck = self.bank()
            self.mm(bc[0:64, :], self.onesf[64:65, 0:64], rl[64:65, 0:512], True, True, ["ones", rlk], [bck])
            self.cp("act", rl[0:64, 0:512], bc[0:64, :], [bck], [rlk])
            self.unbank((bc, bck))
            self.tt("dve", self.ys[3][:, h, :], ob[0:64, :], rl[0:64, 0:512], ALU.mult, [obk, rlk], ["arB"])
            self.unbank((ob, obk))
        for s_ in QB + QF:
            self.unscr(s_)
        for s_ in KBUF + VBUF:
            self.unscr(s_, 8)
        for s_ in pts + [(rl, rlk), (cqn, cqnk), (ckvn, ckvnk), cq[0], cq[1], (self.rcos, self.rcosk), (self.rsin, self.rsink)]:
            self.unscr(s_)


def _prep_inputs(inputs):
    pvec, fvec, lrug, bst, cst = host_layout(inputs)
    shared = {"pvec": pvec, "fvec": fvec,
              "lrug": lrug.reshape(DEPTH, -1, 1024), "bst": bst.reshape(DEPTH, -1, 1024), "cst": cst.reshape(DEPTH, -1, 1024)}
    for n in BIGW:
        shared[n] = np.ascontiguousarray(np.asarray(inputs[n], np.float32)).reshape(DEPTH, -1, 1024)
    for n, v in host_consts().items():
        shared["c_" + n] = v
    return shared


def run(inputs, L_RUN=DEPTH, T_RUN=T_FULL, n_cores=8, branches=(0, 1, 2, 3), dbg=False, dbg_tile=0, trace=False):
    inputs = {k: np.asarray(v) for k, v in inputs.items()}
    shared = _prep_inputs(inputs)
    nc = bass.Bass("TRN2", target_bir_lowering=False)
    kb = KB(nc, L_RUN, T_RUN, dbg=dbg)
    kb.dbg_tile = dbg_tile
    kb.build(branches=branches)
    in_maps = []
    for b in range(n_cores):
        m = dict(shared)
        m["x"] = np.ascontiguousarray(inputs["x"][b, :T_RUN].astype(np.float32))
        in_maps.append(m)
    res = run_bass_kernel_spmd(nc, in_maps, core_ids=list(range(n_cores)), trace=trace)
    return res


DEFAULT_BRANCHES = (0, 1, 2, 3)


def kernel(**inputs):
    res = run(inputs, branches=DEFAULT_BRANCHES)
    return np.stack([r["y"] for r in res.results], axis=0).astype(np.float32)
```

```python
import math
from contextlib import ExitStack
import numpy as np
import concourse.bass as bass
import concourse.mybir as mybir
from concourse.bass_utils import run_bass_kernel_spmd

F32 = mybir.dt.float32
BF16 = mybir.dt.bfloat16
AF = mybir.ActivationFunctionType
ALU = mybir.AluOpType
AX = mybir.AxisListType

D = 1024
T_FULL = 4096
DEPTH = 4
C = 512
D_IN = 7840
TT = 512
EPS = 1e-6
ENGS = ("pe", "act", "dve", "pool", "sp")
GELU_K = 1.5957691216057308


class Op:
    __slots__ = ("eng", "fn", "reads", "writes", "dma", "idx", "deps", "sig", "cnt", "sem", "semval")

    def __init__(self, eng, fn, reads, writes, dma):
        self.eng, self.fn, self.reads, self.writes, self.dma = eng, fn, reads, writes, dma
        self.deps = []
        self.sig = False
        self.cnt = 0
        self.sem = None
        self.semval = 0


class Prog:
    NDMA = 48

    def __init__(self, nc):
        self.nc = nc
        self.ops = []

    def op(self, eng, fn, reads=(), writes=(), dma=False):
        o = Op(eng, fn, tuple(reads), tuple(writes), dma)
        o.idx = len(self.ops)
        self.ops.append(o)
        return o

    def finalize(self):
        last_w, readers, children = {}, {}, {}
        dma_k = 0
        dma_last = [None] * self.NDMA
        alias = getattr(self, "alias", {})

        def expand(keys):
            out = []
            for k in keys:
                base, _, sub = k.partition("#")
                for s_ in alias.get(base, (base,)):
                    out.append((s_, sub))
            return out

        def related(s_, sub):
            if sub == "":
                return [(s_, "")] + [(s_, c) for c in children.get(s_, ())]
            return [(s_, sub), (s_, "")]

        for o in self.ops:
            deps = {}
            rd, wr = expand(o.reads), expand(o.writes)
            for (s_, sub) in rd + wr:
                if sub:
                    children.setdefault(s_, set()).add(sub)
            for (s_, sub) in rd:
                for kk in related(s_, sub):
                    w = last_w.get(kk)
                    if w is not None:
                        deps[w.idx] = (w, "raw")
            for (s_, sub) in wr:
                for kk in related(s_, sub):
                    w = last_w.get(kk)
                    if w is not None and w.idx not in deps:
                        deps[w.idx] = (w, "waw")
                    for r in readers.get(kk, ()):
                        if r.idx not in deps and r is not o:
                            deps[r.idx] = (r, "war")
            for kk in rd:
                readers.setdefault(kk, []).append(o)
            for (s_, sub) in wr:
                last_w[(s_, sub)] = o
                readers[(s_, sub)] = []
                if sub == "":
                    for c in children.get(s_, ()):
                        last_w[(s_, c)] = o
                        readers[(s_, c)] = []
            if o.dma:
                k = dma_k % self.NDMA
                dma_k += 1
                prev = dma_last[k]
                o.sem = k
                o.semval = (prev.semval if prev is not None else 0) + 16
                if prev is not None and prev.idx not in deps:
                    deps[prev.idx] = (prev, "raw")
                dma_last[k] = o
            for (p, kind) in deps.values():
                if p.dma:
                    o.deps.append(p)
                elif p.eng == o.eng and not o.dma:
                    if kind == "raw" and o.eng != "pe":
                        o.deps.append(p)
                        p.sig = True
                else:
                    o.deps.append(p)
                    p.sig = True
        cnt = {e: 0 for e in ENGS}
        for o in self.ops:
            if o.sig and not o.dma:
                cnt[o.eng] += 1
                o.cnt = cnt[o.eng]

    def emit(self, final_waits=()):
        nc = self.nc
        with ExitStack() as st:
            esem = {e: st.enter_context(nc.semaphore("s_" + e)) for e in ENGS}
            dsem = [st.enter_context(nc.semaphore("d%d" % i)) for i in range(self.NDMA)]
            block = st.enter_context(nc.Block())
            per = {e: [o for o in self.ops if o.eng == e] for e in ENGS}

            def run(e, engobj, extra_final=()):
                seen_e = {x: 0 for x in ENGS}
                seen_d = {}
                for o in per[e]:
                    for p in o.deps:
                        if p.dma:
                            if seen_d.get(p.sem, 0) < p.semval:
                                engobj.wait_ge(dsem[p.sem], p.semval)
                                seen_d[p.sem] = p.semval
                        elif seen_e[p.eng] < p.cnt:
                            engobj.wait_ge(esem[p.eng], p.cnt)
                            seen_e[p.eng] = p.cnt
                    ins = o.fn(engobj)
                    if o.dma:
                        ins.then_inc(dsem[o.sem], 16)
                    elif o.sig:
                        ins.then_inc(esem[o.eng], 1)
                for p in extra_final:
                    engobj.wait_ge(dsem[p.sem], p.semval)

            @block.tensor
            def _(e):
                run("pe", e)

            @block.scalar
            def _(e):
                run("act", e)

            @block.vector
            def _(e):
                run("dve", e)

            @block.gpsimd
            def _(e):
                run("pool", e)

            @block.sync
            def _(e):
                run("sp", e, extra_final=final_waits)


PV = {}
_o = 0
for _n, _w in [("conv_w", 16), ("conv_b", 4), ("gate_b", 8), ("lam", 4), ("s5_d", 4), ("mu_rkv", 12), ("w0", 4),
               ("a0", 4), ("k_k", 4), ("k_a", 4), ("r_k", 4), ("mu_w", 1), ("mu_a", 1), ("mu_g", 1), ("q_norm", 2),
               ("kv_norm", 1), ("qkn_q", 1), ("qkn_k", 1), ("s5_are", 16), ("s5_aim", 16), ("s5_ldt", 16)]:
    PV[_n] = (_o, _w)
    _o += _w
NPV = _o


def _chunks(v, n):
    return np.ascontiguousarray(v.reshape(n, 128).T)


def host_layout(inp):
    f = np.float32
    L = DEPTH
    pvec = np.zeros((L, 128, NPV), f)
    fvec = np.zeros((L, 4, 1024), f)
    lrug = np.zeros((L, 2, 4, 128, 128), f)
    bst = np.zeros((L, 2, 16, 128, 128), f)
    cst = np.zeros((L, 2, 16, 128, 128), f)
    for l in range(L):
        def put(name, arr):
            o, w = PV[name]
            pvec[l, :arr.shape[0], o:o + w] = arr
        put("conv_w", np.concatenate([_chunks(inp["lru_conv_w"][l, k], 4) for k in range(4)], axis=1))
        put("conv_b", _chunks(inp["lru_conv_b"][l], 4))
        put("gate_b", np.concatenate([_chunks(inp["lru_gate_b"][l, g], 4) for g in range(2)], axis=1))
        put("lam", _chunks(inp["lru_lambda"][l], 4))
        put("s5_d", _chunks(inp["s5_d"][l], 4))
        put("mu_rkv", np.concatenate([_chunks(inp["rwkv_mu_rkv"][l, j], 4) for j in range(3)], axis=1))
        put("w0", _chunks(inp["rwkv_w0"][l], 4))
        put("a0", _chunks(inp["rwkv_a0"][l], 4))
        put("k_k", _chunks(inp["rwkv_k_k"][l], 4))
        put("k_a", _chunks(inp["rwkv_k_a"][l], 4))
        put("r_k", _chunks(inp["rwkv_r_k"][l].reshape(-1), 4))
        put("mu_w", inp["rwkv_mu_w"][l].reshape(64, 1))
        put("mu_a", inp["rwkv_mu_a"][l].reshape(64, 1))
        put("mu_g", inp["rwkv_mu_g"][l].reshape(128, 1))
        put("q_norm", _chunks(inp["mla_q_norm"][l], 2))
        put("kv_norm", inp["mla_kv_norm"][l].reshape(128, 1))
        put("qkn_q", inp["mla_qk_norm_q"][l].reshape(96, 1))
        put("qkn_k", inp["mla_qk_norm_k"][l].reshape(96, 1))
        are = inp["s5_a_re"][l].reshape(16, 2, 64).transpose(1, 2, 0).reshape(128, 16)
        aim = inp["s5_a_im"][l].reshape(16, 2, 64).transpose(1, 2, 0).reshape(128, 16)
        ldt = np.broadcast_to(inp["s5_log_dt"][l].reshape(16, 2, 1), (16, 2, 64)).transpose(1, 2, 0).reshape(128, 16)
        put("s5_are", are)
        put("s5_aim", aim)
        put("s5_ldt", ldt)
        fvec[l, 0] = inp["norm_mix"][l]
        fvec[l, 1] = inp["norm_mlp"][l]
        fvec[l, 2, :512] = inp["rwkv_lnx_g"][l]
        fvec[l, 2, 512:] = inp["rwkv_lnx_b"][l]
        for g in range(2):
            for h in range(8):
                c, e = h // 2, h % 2
                lrug[l, g, c, e * 64:(e + 1) * 64, e * 64:(e + 1) * 64] = inp["lru_gate_w"][l, g, h]
        for ri, (bsrc, csrc) in enumerate([(inp["s5_b_re"][l], inp["s5_c_re"][l]), (inp["s5_b_im"][l], inp["s5_c_im"][l])]):
            for g in range(32):
                j, e = g // 2, g % 2
                r0 = 32 * (j % 4) + e * 16
                bst[l, ri, j, r0:r0 + 16, e * 64:(e + 1) * 64] = bsrc[g].T
                cst[l, ri, j, e * 64:(e + 1) * 64, r0:r0 + 16] = csrc[g].T
    return pvec, fvec, lrug, bst, cst


def host_consts():
    f = np.float32
    c = {}
    c["ident"] = np.eye(128, dtype=f)
    c["ones"] = np.ones((128, 128), f)
    blk = np.zeros((128, 128), f)
    blk[:64, :64] = 1
    blk[64:, 64:] = 1
    c["blk64"] = blk
    hs = np.zeros((128, 2), f)
    hs[:64, 0] = 1
    hs[64:, 1] = 1
    c["headsel"] = hs
    p = np.arange(128)[:, None]
    q = np.arange(512)[None, :]
    c["amask"] = np.stack([(q >= 128 * j + p) for j in range(4)], 1).astype(f)
    j = np.arange(64)[:, None]
    t = np.arange(64)[None, :]
    m = np.zeros((64, 3, 64), f)
    m[:, 0] = (t > j)
    m[:, 1] = (t >= j)
    m[:, 2] = (t < j)
    c["rmask"] = m
    rs = np.ones((128, 512), f)
    rs[:, ::64] = 0
    c["reset64"] = rs
    pos = np.arange(T_FULL, dtype=np.float64)
    inv = 10000.0 ** (-np.arange(0, 32, 2, dtype=np.float64) / 32)
    ang = pos[None, :] * inv[:, None]
    cos = np.ones((96, T_FULL), np.float64)
    sin = np.zeros((96, T_FULL), np.float64)
    cos[64:80] = np.cos(ang); cos[80:96] = np.cos(ang)
    sin[64:80] = np.sin(ang); sin[80:96] = np.sin(ang)
    c["ropec"] = cos.astype(f)
    c["ropes"] = sin.astype(f)
    pr = np.zeros((96, 96), f)
    for i in range(16):
        pr[80 + i, 64 + i] = -1.0
        pr[64 + i, 80 + i] = 1.0
    c["prot"] = pr
    return c


CONST_SHAPES = {"ident": [128, 128], "ones": [128, 128], "blk64": [128, 128], "headsel": [128, 2],
                "amask": [128, 4, 512], "rmask": [64, 3, 64], "reset64": [128, 512],
                "ropec": [96, T_FULL], "ropes": [96, T_FULL], "prot": [96, 96]}

BIGW = {"w_in": [D, D_IN], "s5_w_glu": [C, 2 * C], "w_branch": [4 * C, D], "w_out": [D, D], "w_ff1": [D, 4 * D],
        "w_ff2": [4 * D, D], "mla_w_uq": [256, 768], "mla_w_ukv": [128, 1024], "rwkv_w2": [64, C],
        "rwkv_a2": [64, C], "rwkv_g2": [128, C]}


class KB:
    def __init__(self, nc, L_RUN, T_RUN, dbg=False):
        self.nc, self.L, self.T, self.dbg = nc, L_RUN, T_RUN, dbg
        self.NT = T_RUN // TT
        self.P = Prog(nc)
        self.st = ExitStack()
        self.free_banks = []
        self.scr_free = {}
        self.scr_n = 0
        self.wk = 0
        self.outs = []

    def sb(self, name, shape, dt=F32):
        return self.st.enter_context(self.nc.sbuf_tensor(name, shape, dt))

    def bank(self):
        assert self.free_banks, "out of PSUM banks"
        return self.free_banks.pop(0)

    def unbank(self, b):
        self.free_banks.append(b)

    NSLOT = 42

    def scr(self, kb=2):
        n = kb // 2
        if not hasattr(self, "arena"):
            self.arena = self.sb("arena", [128, self.NSLOT * 512], F32)
            self.slot_used = [False] * self.NSLOT
            self.P.alias = {}
        for st in range(self.NSLOT - n + 1):
            if not any(self.slot_used[st:st + n]):
                for i in range(st, st + n):
                    self.slot_used[i] = True
                key = "arn_%d_%d" % (st, n)
                self.P.alias[key] = tuple("slot%d" % i for i in range(st, st + n))
                return (self.arena[:, st * 512:(st + n) * 512], key)
        raise AssertionError("out of scratch slots")

    def unscr(self, s, kb=2):
        _, st, n = s[1].split("_")
        for i in range(int(st), int(st) + int(n)):
            assert self.slot_used[i]
            self.slot_used[i] = False

    def mm(self, out, lhsT, rhs, start, stop, r, w):
        self.P.op("pe", lambda e: e.matmul(out, lhsT=lhsT, rhs=rhs, start=start, stop=stop), r, w)

    def trp(self, out, in_, ident, r, w):
        self.P.op("pe", lambda e: e.transpose(out=out, in_=in_, identity=ident), r, w)

    def act(self, out, in_, func, r, w, **kw):
        self.P.op("act", lambda e: e.activation(out=out, in_=in_, func=func, **kw), r, w)

    def tt(self, eng, out, in0, in1, op, r, w):
        self.P.op(eng, lambda e: e.tensor_tensor(out=out, in0=in0, in1=in1, op=op), r, w)

    def ts(self, eng, out, in0, s1, s2, op0, op1, r, w):
        if s2 is None:
            self.P.op(eng, lambda e: e.tensor_single_scalar(out=out, in_=in0, scalar=s1, op=op0), r, w)
        else:
            self.P.op(eng, lambda e: e.tensor_scalar(out=out, in0=in0, scalar1=s1, scalar2=s2, op0=op0, op1=op1), r, w)

    def stt(self, eng, out, in0, scalar, in1, op0, op1, r, w):
        self.P.op(eng, lambda e: e.scalar_tensor_tensor(out=out, in0=in0, scalar=scalar, in1=in1, op0=op0, op1=op1), r, w)

    def cp(self, eng, out, in_, r, w):
        if eng == "act":
            self.P.op("act", lambda e: e.copy(out=out, in_=in_), r, w)
        else:
            self.P.op(eng, lambda e: e.tensor_copy(out=out, in_=in_), r, w)

    def memset(self, eng, out, val, w):
        self.P.op(eng, lambda e: e.memset(out, val), [], w)

    def recip(self, out, in_, r, w):
        self.P.op("dve", lambda e: e.reciprocal(out=out, in_=in_), r, w)

    def scan(self, out, d0, d1, init, r, w):
        self.P.op("dve", lambda e: e.tensor_tensor_scan(out=out, data0=d0, data1=d1, initial=init, op0=ALU.mult, op1=ALU.add), r, w)

    def dma(self, eng, out, in_, r, w, **kw):
        return self.P.op(eng, lambda e: e.dma_start(out=out, in_=in_, **kw), r, w, dma=True)

    def wload(self, src, shape):
        i = self.wk % len(self.wring)
        self.wk += 1
        buf, key = self.wring[i]
        n = int(np.prod(shape[1:]))
        if len(shape) == 3:
            view = buf[:shape[0], 0:n].rearrange("p (k c) -> p k c", k=shape[1])
        else:
            view = buf[:shape[0], 0:n]
        self.dma("sp", view, src, [self.cur_wkey], [key])
        return view, key

    def gelu(self, src, srck, out, outk, tmp, tmpk):
        self.tt("dve", tmp, src, src, ALU.mult, [srck], [tmpk])
        self.ts("dve", tmp, tmp, 0.044715, 1.0, ALU.mult, ALU.add, [tmpk], [tmpk])
        self.tt("dve", tmp, tmp, src, ALU.mult, [tmpk, srck], [tmpk])
        self.act(tmp, tmp, AF.Sigmoid, [tmpk], [tmpk], scale=GELU_K)
        self.tt("dve", out, tmp, src, ALU.mult, [tmpk, srck], [outk])

    def build(self, branches=(0, 1, 2, 3)):
        nc, L, T = self.nc, self.L, self.T
        self.branches = branches
        dt = lambda name, shape, dty, kind: nc.dram_tensor(name, shape, dty, kind=kind).ap()
        self.x = dt("x", [T, D], F32, "ExternalInput")
        self.y = dt("y", [T, D], F32, "ExternalOutput")
        self.w32, self.wb = {}, {}
        for n, (r, c) in BIGW.items():
            self.w32[n] = dt(n, [DEPTH, r * c // 1024, 1024], F32, "ExternalInput")
            self.wb[n] = dt("b_" + n, [DEPTH, r, c], BF16, "Internal")
        self.pvec_d = dt("pvec", [DEPTH, 128, NPV], F32, "ExternalInput")
        self.fvec_d = dt("fvec", [DEPTH, 4, 1024], F32, "ExternalInput")
        for n, shp in [("lrug", [DEPTH, 8 * 128 * 128 // 1024, 1024]), ("bst", [DEPTH, 32 * 16, 1024]), ("cst", [DEPTH, 32 * 16, 1024])]:
            self.w32[n] = dt(n, shp, F32, "ExternalInput")
        self.wb["lrug"] = dt("b_lrug", [DEPTH, 8, 128, 128], BF16, "Internal")
        self.wb["bst"] = dt("b_bst", [DEPTH, 32, 128, 128], BF16, "Internal")
        self.wb["cst"] = dt("b_cst", [DEPTH, 32, 128, 128], BF16, "Internal")
        self.cd = {n: dt("c_" + n, s, F32, "ExternalInput") for n, s in CONST_SHAPES.items()}
        self.kc = dt("kcache", [8, 96, T], BF16, "Internal")
        self.vc = dt("vcache", [8, 128, T // 128, 65], BF16, "Internal")
        self.s5tab = dt("s5tab", [16, 128, 4, 128], F32, "Internal")
        if self.dbg:
            self.dbg_ys = dt("dbg_ys", [4, 128, 8, TT], F32, "ExternalOutput")

        for i in range(8):
            t = self.st.enter_context(nc.psum_tensor("bank%d" % i, [128, 512], F32))
            self.free_banks.append((t, "bank%d" % i))
        sb = self.sb
        self.identf = sb("identf", [128, 128]); self.identb = sb("identb", [128, 128], BF16)
        self.onesf = sb("onesf", [128, 128]); self.onesb = sb("onesb", [128, 128], BF16)
        self.blk64 = sb("blk64", [128, 128]); self.headsel = sb("headsel", [128, 2]); self.headselb = sb("headselb", [128, 2], BF16)
        self.amask = sb("amask", [128, 4, 512], BF16); self.rmask = sb("rmask", [64, 3, 64])
        self.reset64 = sb("reset64", [128, 512]); self.prot = sb("prot", [96, 96])
        amf, amfk = self.scr(8)
        for n, tl in [("ident", self.identf), ("ones", self.onesf), ("blk64", self.blk64), ("headsel", self.headsel),
                      ("rmask", self.rmask), ("reset64", self.reset64), ("prot", self.prot)]:
            self.dma("sp", tl[:], self.cd[n], [], [n])
        self.dma("sp", amf[:, 0:2048].rearrange("p (a b) -> p a b", a=4), self.cd["amask"], [], [amfk])
        self.cp("dve", self.amask[:], amf[:, 0:2048].rearrange("p (a b) -> p a b", a=4), [amfk], ["amask"])
        self.unscr((amf, amfk), 8)
        self.cp("dve", self.identb[:], self.identf[:], ["ident"], ["identb"])
        self.cp("dve", self.onesb[:], self.onesf[:], ["ones"], ["onesb"])
        self.cp("dve", self.headselb[:], self.headsel[:], ["headsel"], ["headselb"])
        self.wring = [(sb("wring%d" % i, [128, 4096], BF16), "wring%d" % i) for i in range(3)]
        self.xt = sb("xt", [128, 4, 1024]); self.hT = sb("hT", [128, 8, 512], BF16)
        self.gbc = sb("gbc", [128, 1024]); self.lnx = sb("lnx", [64, 1024])
        self.arA = sb("arA", [128, 4096]); self.arB = sb("arB", [128, 5120])
        self.hn = self.arA[:, 0:2048].bitcast(BF16).rearrange("p (s d) -> p s d", s=4)
        self.macc = self.arA[:, :].rearrange("p (c t) -> p c t", c=8)
        ysb = self.arB[:, :].bitcast(BF16)
        self.ys = [ysb[:, i * 2048:(i + 1) * 2048].rearrange("p (c t) -> p c t", c=4) for i in range(3)]
        self.ys.append(ysb[0:64, 6144:10240].rearrange("p (h t) -> p h t", h=8))
        self.a1T_lo = self.arA[:, :].bitcast(BF16).rearrange("p (c t) -> p c t", c=16)
        self.a1T_hi = ysb[:, 0:8192].rearrange("p (c t) -> p c t", c=16)
        self.pv = sb("pv", [128, NPV]); self.der = sb("der", [128, 16])
        self.lrug_sb = sb("lrug_sb", [128, 8, 128], BF16)
        self.w2_sb = sb("w2_sb", [64, 512], BF16); self.a2_sb = sb("a2_sb", [64, 512], BF16); self.g2_sb = sb("g2_sb", [128, 512], BF16)
        self.wuq_sb = sb("wuq_sb", [128, 2, 768], BF16); self.wukv_sb = sb("wukv_sb", [128, 1024], BF16)
        self.ss = sb("ss", [128, 8])
        self.xh = [sb("xh%d" % c, [128, 515]) for c in range(4)]
        self.hc = sb("hc", [128, 4])
        self.zc = sb("zc", [128, 2, 16]); self.el = sb("el", [128, 2, 16])
        self.carry = sb("carry", [128, 16])
        self.pc_t = sb("pc_t", [128, 4, 8])
        self.s0f = sb("s0f", [128, 4, 64]); self.s0bd = sb("s0bd", [128, 4, 128], BF16)

        for l in range(L):
            for n in list(BIGW) + ["lrug", "bst", "cst"]:
                dstv = self.wb[n][l]
                if n in ("lrug", "bst", "cst"):
                    dstv = dstv.rearrange("a r c -> (a r c)")
                else:
                    dstv = dstv.rearrange("r c -> (r c)")
                dstv = dstv.rearrange("(a b) -> a b", b=1024)
                self.dma("pool", dstv, self.w32[n][l], [], ["wb%d" % l])
        for l in range(L):
            self.layer(l)
        self.P.finalize()
        self.P.emit(final_waits=self.outs)
        self.st.close()

    def layer(self, l):
        self.l = l
        self.cur_wkey = "wb%d" % l
        wk = self.cur_wkey
        pv, der = self.pv, self.der
        self.dma("sp", pv[:], self.pvec_d[l], [], ["pv"])
        self.dma("sp", self.lnx[:], self.fvec_d[l, 2:3, :].broadcast_to([64, 1024]), [], ["lnx"])
        self.dma("sp", self.lrug_sb[:], self.wb["lrug"][l].rearrange("a r c -> r a c"), [wk], ["lrug_sb"])
        self.dma("sp", self.w2_sb[:], self.wb["rwkv_w2"][l], [wk], ["w2_sb"])
        self.dma("sp", self.a2_sb[:], self.wb["rwkv_a2"][l], [wk], ["a2_sb"])
        self.dma("sp", self.g2_sb[:], self.wb["rwkv_g2"][l], [wk], ["g2_sb"])
        self.dma("sp", self.wuq_sb[:], self.wb["mla_w_uq"][l].rearrange("(k p) c -> p k c", p=128), [wk], ["wuq_sb"])
        self.dma("sp", self.wukv_sb[:], self.wb["mla_w_ukv"][l], [wk], ["wukv_sb"])
        o = PV["lam"][0]
        self.act(der[:, 0:4], pv[:, o:o + 4], AF.Exp, ["pv"], ["der"], scale=-1.0)
        self.act(der[:, 0:4], der[:, 0:4], AF.Ln, ["der"], ["der"], bias=1.0)
        self.ts("dve", der[:, 0:4], der[:, 0:4], -8.0, None, ALU.mult, None, ["der"], ["der"])
        o = PV["w0"][0]
        self.ts("dve", der[:, 4:8], pv[:, o:o + 4], -1.0, None, ALU.mult, None, ["pv"], ["der"])
        o = PV["k_a"][0]
        self.ts("dve", der[:, 8:12], pv[:, o:o + 4], -1.0, 1.0, ALU.mult, ALU.add, ["pv"], ["der"])
        for c in range(4):
            self.memset("pool", self.xh[c][:, 0:3], 0.0, ["xh%d" % c])
        self.memset("pool", self.hc[:], 0.0, ["hc"])
        self.memset("pool", self.zc[:], 0.0, ["zc%d" % i for i in range(16)])
        self.memset("pool", self.carry[:], 0.0, ["carry%d" % i for i in range(16)])
        self.memset("pool", self.s0f[:], 0.0, ["s0_%df" % i for i in range(4)])
        self.memset("pool", self.s0bd[:], 0.0, ["s0_%d" % i for i in range(4)])
        if 1 in self.branches:
            self.s5_setup(l)
        for ti in range(self.NT):
            self.tile(l, ti)

    def pvc(self, name, i=0, rows=128):
        o = PV[name][0] + i
        return self.pv[0:rows, o:o + 1]

    def norm_T(self, gi):
        l = self.l
        self.dma("sp", self.gbc[:], self.fvec_d[l, gi:gi + 1, :].broadcast_to([128, 1024]), [], ["gbc"])
        junk, jk = self.scr(4)
        for s in range(4):
            self.act(junk[:, 0:1024], self.xt[:, s, :], AF.Square, ["xt"], [jk, "ss"], accum_out=self.ss[:, s:s + 1])
        self.unscr((junk, jk), 4)
        self.act(self.ss[:, 4:8], self.ss[:, 0:4], AF.Sqrt, ["ss"], ["ss"], scale=1.0 / D, bias=EPS)
        self.recip(self.ss[:, 4:8], self.ss[:, 4:8], ["ss"], ["ss"])
        for s in range(4):
            self.stt("dve", self.hn[:, s, :], self.xt[:, s, :], self.ss[:, 4 + s:5 + s], self.gbc[:], ALU.mult, ALU.mult,
                     ["xt", "ss", "gbc"], ["arA"])
        for kc in range(8):
            b, bk = self.bank()
            bb = b[:, :].bitcast(BF16)
            for s in range(4):
                self.trp(bb[:, s * 128:(s + 1) * 128], self.hn[:, s, kc * 128:(kc + 1) * 128], self.identb[:], ["arA", "identb"], [bk])
            self.cp("act" if kc % 2 else "dve", self.hT[:, kc, :], bb[:, 0:512], [bk], ["hT"])
            self.unbank((b, bk))

    def proj(self, out, slab, c0, m, nk, rhsf, r, w):
        for kc in range(nk):
            self.mm(out, slab[:, kc, c0:c0 + m], rhsf(kc), kc == 0, kc == nk - 1, r, w)

    def win(self, c0, n=512):
        return self.wload(self.wb["w_in"][self.l][:, c0:c0 + n].rearrange("(k p) c -> p k c", p=128), [128, 8, n])

    def tile(self, l, ti):
        t0 = ti * TT
        src = self.x if l == 0 else self.y
        self.dma("sp", self.xt[:], src[t0:t0 + TT, :].rearrange("(s p) d -> p s d", p=128), ["ydram"], ["xt"])
        self.norm_T(0)
        for bi, fn in enumerate([self.lru, self.s5, self.rwkv, self.mla]):
            if bi in self.branches:
                fn(l, ti)
            else:
                self.memset("pool", self.ys[bi], 0.0, ["arB"])
        if self.dbg and l == 0 and ti == self.dbg_tile:
            d, dk = self.scr(8)
            dv = d[:, :].rearrange("p (c t) -> p c t", c=4)
            for bi in range(4):
                np_ = 128 if bi < 3 else 64
                for hf_ in range(1 if bi < 3 else 2):
                    self.cp("dve", dv[0:np_], self.ys[bi][:, hf_ * 4:hf_ * 4 + 4, :], ["arB"], [dk])
                    self.outs.append(self.dma("sp", self.dbg_ys[bi, 0:np_, hf_ * 4:hf_ * 4 + 4, :], dv[0:np_], [dk], ["dbgout"]))
            self.unscr((d, dk), 8)
        self.merge(l, ti)
        self.mlp(l, ti)
        st = self.dma("pool", self.y[t0:t0 + TT, :].rearrange("(s p) d -> p s d", p=128), self.xt[:], ["xt"], ["ydram"])
        if l == self.L - 1:
            self.outs.append(st)

    def merge(self, l, ti):
        wbr = self.wb["w_branch"][l]
        SG = [self.scr() for _ in range(3)]
        TM = [self.scr() for _ in range(3)]
        it = 0
        for n in range(4):
            for half in range(2):
                if n < 3:
                    wsl, wslk = self.wload(wbr[n * 512:(n + 1) * 512, half * 512:(half + 1) * 512].rearrange("(k p) c -> p k c", p=128), [128, 4, 512])
                    nk = 4
                else:
                    wsl, wslk = self.wload(wbr[n * 512:(n + 1) * 512, half * 512:(half + 1) * 512].rearrange("(k p) c -> p k c", p=64), [64, 8, 512])
                    nk = 8
                gsl, gslk = self.win(3744 + n * 1024 + half * 512)
                for dc4 in range(4):
                    dc = half * 4 + dc4
                    sg, sgk = SG[it % 3]; tm, tmk = TM[it % 3]; it += 1
                    bg, bgk = self.bank()
                    self.proj(bg[:, :], gsl, dc4 * 128, 128, 8, lambda kc: self.hT[:, kc, :], [gslk, "hT"], [bgk])
                    self.act(sg[:, 0:512], bg[:, :], AF.Sigmoid, [bgk], [sgk])
                    self.unbank((bg, bgk))
                    bz, bzk = self.bank()
                    ysn = self.ys[n]
                    self.proj(bz[:, :], wsl, dc4 * 128, 128, nk, lambda kc: ysn[:, kc, :], [wslk, "arB"], [bzk])
                    mk_ = "arA#m%d" % dc
                    if n == 0:
                        self.tt("dve", self.macc[:, dc, :], bz[:, :], sg[:, 0:512], ALU.mult, [bzk, sgk], [mk_])
                    else:
                        self.tt("dve", tm[:, 0:512], bz[:, :], sg[:, 0:512], ALU.mult, [bzk, sgk], [tmk])
                        self.tt("dve", self.macc[:, dc, :], self.macc[:, dc, :], tm[:, 0:512], ALU.add, [mk_, tmk], [mk_])
                    self.unbank((bz, bzk))
        for s_ in SG + TM:
            self.unscr(s_)
        for dc in range(8):
            self.cp("act" if dc % 2 else "dve", self.hT[:, dc, :], self.macc[:, dc, :], ["arA"], ["hT"])
        for half in range(2):
            wsl, wslk = self.wload(self.wb["w_out"][l][:, half * 512:(half + 1) * 512].rearrange("(k p) c -> p k c", p=128), [128, 8, 512])
            for s in range(4):
                b, bk = self.bank()
                for kc in range(8):
                    self.mm(b[:, :], self.hT[:, kc, s * 128:(s + 1) * 128], wsl[:, kc, :], kc == 0, kc == 7, ["hT", wslk], [bk])
                self.tt("dve", self.xt[:, s, half * 512:(half + 1) * 512], self.xt[:, s, half * 512:(half + 1) * 512], b[:, :], ALU.add, ["xt", bk], ["xt"])
                self.unbank((b, bk))

    def mlp(self, l, ti):
        self.norm_T(1)
        R1 = [self.scr(), self.scr()]
        for sl in range(8):
            wsl, wslk = self.wload(self.wb["w_ff1"][l][:, sl * 512:(sl + 1) * 512].rearrange("(k p) c -> p k c", p=128), [128, 8, 512])
            for c4 in range(4):
                fc = sl * 4 + c4
                r1, r1k = R1[fc % 2]
                b, bk = self.bank()
                self.proj(b[:, :], wsl, c4 * 128, 128, 8, lambda kc: self.hT[:, kc, :], [wslk, "hT"], [bk])
                self.act(r1[:, 0:512], b[:, :], AF.Relu, [bk], [r1k])
                dst = self.a1T_lo[:, fc, :] if fc < 16 else self.a1T_hi[:, fc - 16, :]
                self.tt("dve", dst, r1[:, 0:512], r1[:, 0:512], ALU.mult, [r1k], ["arA" if fc < 16 else "arB"])
                self.unbank((b, bk))
        self.unscr(R1[0]); self.unscr(R1[1])
        for half in range(2):
            accs = [self.bank() for _ in range(4)]
            for g in range(4):
                wsl, wslk = self.wload(self.wb["w_ff2"][l][g * 1024:(g + 1) * 1024, half * 512:(half + 1) * 512].rearrange("(k p) c -> p k c", p=128), [128, 8, 512])
                for s in range(4):
                    for kc in range(8):
                        fc = g * 8 + kc
                        a = self.a1T_lo[:, fc, s * 128:(s + 1) * 128] if fc < 16 else self.a1T_hi[:, fc - 16, s * 128:(s + 1) * 128]
                        self.mm(accs[s][0][:, :], a, wsl[:, kc, :], fc == 0, fc == 31, ["arA", "arB", wslk], [accs[s][1]])
            for s in range(4):
                self.tt("dve", self.xt[:, s, half * 512:(half + 1) * 512], self.xt[:, s, half * 512:(half + 1) * 512], accs[s][0][:, :], ALU.add, ["xt", accs[s][1]], ["xt"])
                self.unbank(accs[s])

    def lru(self, l, ti):
        slx, slxk = self.win(0)
        slg, slgk = self.win(512)
        S = [self.scr() for _ in range(6)]
        (xc, xck), (rr, rrk), (ii, iik), (aa, aak), (t1, t1k), (hh, hhk) = S
        xcb, xcbk = self.scr()
        xcbv = xcb[:, 0:256].bitcast(BF16)
        hf = lambda kc: self.hT[:, kc, :]
        for c in range(4):
            xh, xhk = self.xh[c], "xh%d" % c
            b, bk = self.bank()
            self.proj(b[:, :], slx, c * 128, 128, 8, hf, [slxk, "hT"], [bk])
            self.cp("act", xh[:, 3:515], b[:, :], [bk], [xhk])
            self.unbank((b, bk))
            self.ts("dve", xc[:, 0:512], xh[:, 3:515], self.pvc("conv_w", 12 + c), self.pvc("conv_b", c), ALU.mult, ALU.add, [xhk, "pv"], [xck])
            for k in range(3):
                self.stt("dve", xc[:, 0:512], xh[:, k:k + 512], self.pvc("conv_w", 4 * k + c), xc[:, 0:512], ALU.mult, ALU.add, [xhk, "pv", xck], [xck])
            self.cp("pool", xh[:, 0:3], xh[:, 512:515], [xhk], [xhk])
            self.cp("dve", xcbv, xc[:, 0:512], [xck], [xcbk])
            for g, (dst, dstk) in enumerate([(rr, rrk), (ii, iik)]):
                b, bk = self.bank()
                self.mm(b[:, :], self.lrug_sb[:, g * 4 + c, :], xcbv, True, True, ["lrug_sb", xcbk], [bk])
                self.act(dst[:, 0:512], b[:, :], AF.Sigmoid, [bk, "pv"], [dstk], bias=self.pvc("gate_b", g * 4 + c))
                self.unbank((b, bk))
            self.act(aa[:, 0:512], rr[:, 0:512], AF.Exp, [rrk, "der"], [aak], scale=self.der[:, c:c + 1])
            self.tt("dve", t1[:, 0:512], aa[:, 0:512], aa[:, 0:512], ALU.mult, [aak], [t1k])
            self.ts("dve", t1[:, 0:512], t1[:, 0:512], -1.0, 1.0, ALU.mult, ALU.add, [t1k], [t1k])
            self.act(t1[:, 0:512], t1[:, 0:512], AF.Sqrt, [t1k], [t1k])
            self.tt("dve", ii[:, 0:512], ii[:, 0:512], xc[:, 0:512], ALU.mult, [iik, xck], [iik])
            self.tt("dve", t1[:, 0:512], t1[:, 0:512], ii[:, 0:512], ALU.mult, [t1k, iik], [t1k])
            self.scan(hh[:, 0:512], aa[:, 0:512], t1[:, 0:512], self.hc[:, c:c + 1], [aak, t1k, "hc"], [hhk])
            self.cp("dve", self.hc[:, c:c + 1], hh[:, 511:512], [hhk], ["hc"])
            b, bk = self.bank()
            self.proj(b[:, :], slg, c * 128, 128, 8, hf, [slgk, "hT"], [bk])
            self.cp("act", rr[:, 0:512], b[:, :], [bk], [rrk])
            self.unbank((b, bk))
            self.gelu(rr[:, 0:512], rrk, ii[:, 0:512], iik, t1[:, 0:512], t1k)
            self.tt("dve", self.ys[0][:, c, :], hh[:, 0:512], ii[:, 0:512], ALU.mult, [hhk, iik], ["arB"])
        for s_ in S:
            self.unscr(s_)
        self.unscr((xcb, xcbk))

    def s5_setup(self, l):
        sm, smk = self.scr()
        v = lambda i: sm[:, i * 16:(i + 1) * 16]
        pvs = lambda n: self.pv[:, PV[n][0]:PV[n][0] + 16]
        R, W = [smk, "pv"], [smk]
        tt = lambda o, a, b, op: self.tt("dve", o, a, b, op, R, W)
        self.act(v(0), pvs("s5_ldt"), AF.Exp, R, W)
        tt(v(1), pvs("s5_aim"), v(0), ALU.mult)
        tt(v(2), pvs("s5_are"), v(0), ALU.mult)
        self.act(v(3), v(2), AF.Exp, R, W)
        self.act(v(4), v(2), AF.Exp, R, W, scale=-1.0)
        self.act(v(5), v(1), AF.Sin, R, W, scale=1.0 / 16)
        self.ts("dve", v(6), v(1), 1.0 / 16, math.pi / 2, ALU.mult, ALU.add, R, W)
        self.act(v(6), v(6), AF.Sin, R, W)

        def csq(c, s_):
            tt(v(7), c, c, ALU.mult); tt(v(8), s_, s_, ALU.mult); tt(v(9), c, s_, ALU.mult)
            tt(c, v(7), v(8), ALU.subtract)
            self.ts("dve", s_, v(9), 2.0, None, ALU.mult, None, R, W)
        for _ in range(4):
            csq(v(6), v(5))
        tt(v(10), v(3), v(6), ALU.mult); tt(v(11), v(3), v(5), ALU.mult)
        tt(v(12), v(4), v(6), ALU.mult); tt(v(13), v(4), v(5), ALU.mult)
        self.ts("dve", v(13), v(13), -1.0, None, ALU.mult, None, R, W)
        self.ts("dve", v(14), v(10), -1.0, None, ALU.add, None, R, W)
        tt(v(7), pvs("s5_are"), pvs("s5_are"), ALU.mult); tt(v(8), pvs("s5_aim"), pvs("s5_aim"), ALU.mult)
        tt(v(7), v(7), v(8), ALU.add)
        self.recip(v(7), v(7), R, W)
        tt(v(8), v(14), pvs("s5_are"), ALU.mult); tt(v(9), v(11), pvs("s5_aim"), ALU.mult); tt(v(8), v(8), v(9), ALU.add)
        tt(v(15), v(8), v(7), ALU.mult)
        tt(v(8), v(11), pvs("s5_are"), ALU.mult); tt(v(9), v(14), pvs("s5_aim"), ALU.mult); tt(v(8), v(8), v(9), ALU.subtract)
        tt(v(14), v(8), v(7), ALU.mult)
        tabs = [self.scr(8) for _ in range(4)]
        tv = [t[0][:, :].rearrange("p (j t) -> p j t", j=16) for t in tabs]
        tk = [t[1] for t in tabs]
        tmp, tmpk = self.scr(8)
        tmv = tmp[:, :].rearrange("p (j t) -> p j t", j=16)
        self.cp("dve", tv[0][:, :, 0], v(15), R, [tk[0]]); self.cp("dve", tv[1][:, :, 0], v(14), R, [tk[1]])
        self.memset("dve", tv[2][:, :, 0], 1.0, [tk[2]]); self.memset("dve", tv[3][:, :, 0], 0.0, [tk[3]])
        for (ar_, ai_, pr, pi) in [(0, 1, v(12), v(13)), (2, 3, v(10), v(11))]:
            for k in range(7):
                n = 1 << k
                prb = pr.unsqueeze(2).broadcast_to([128, 16, n]) if n > 1 else pr.unsqueeze(2)
                pib = pi.unsqueeze(2).broadcast_to([128, 16, n]) if n > 1 else pi.unsqueeze(2)
                Ar, Ai = tv[ar_], tv[ai_]
                kk = [tk[ar_], tk[ai_], smk, tmpk]
                self.tt("dve", Ar[:, :, n:2 * n], Ar[:, :, 0:n], prb, ALU.mult, kk, [tk[ar_]])
                self.tt("dve", tmv[:, :, 0:n], Ai[:, :, 0:n], pib, ALU.mult, kk, [tmpk])
                self.tt("dve", Ar[:, :, n:2 * n], Ar[:, :, n:2 * n], tmv[:, :, 0:n], ALU.subtract, kk, [tk[ar_]])
                self.tt("dve", tmv[:, :, 0:n], Ar[:, :, 0:n], pib, ALU.mult, kk, [tmpk])
                self.tt("dve", Ai[:, :, n:2 * n], Ai[:, :, 0:n], prb, ALU.mult, kk, [tk[ai_]])
                self.tt("dve", Ai[:, :, n:2 * n], Ai[:, :, n:2 * n], tmv[:, :, 0:n], ALU.add, kk, [tk[ai_]])
                csq(pr, pi)
        self.cp("dve", self.el[:, 0, :], v(10), R, ["el"]); self.cp("dve", self.el[:, 1, :], v(11), R, ["el"])
        for i in range(4):
            self.dma("sp", self.s5tab[:, :, i, :].rearrange("j p t -> p j t"), tv[i], [tk[i]], ["s5tab"])
        for t in tabs:
            self.unscr(t, 8)
        self.unscr((tmp, tmpk), 8); self.unscr((sm, smk))

    def s5(self, l, ti):
        import os
        if os.environ.get('S5SKIP'):
            return
        hf = lambda kc: self.hT[:, kc, :]
        slu, sluk = self.win(1024)
        (usb, usbk), (ubf, ubfk), (tmp, tmpk), (yv, yvk), (tc, tck) = [self.scr() for _ in range(5)]
        ubv = ubf[:, 0:256].bitcast(BF16)
        (yg, ygk), (sre, srek), (nsi, nsik) = [self.scr(4) for _ in range(3)]
        ygv = yg[:, :].bitcast(BF16).rearrange("p (c t) -> p c t", c=4)
        srv = sre[:, :].bitcast(BF16).rearrange("p (c t) -> p c t", c=4)
        nsv = nsi[:, :].bitcast(BF16).rearrange("p (c t) -> p c t", c=4)
        Z = [self.scr(8) for _ in range(4)]
        zr, zi, zor, zoi = [z[0][:, :].rearrange("p (j t) -> p j t", j=4) for z in Z]
        zrk, zik, zork, zoik = [z[1] for z in Z]
        (bs, bsk), (cs, csk) = self.scr(), self.scr()
        bsv = bs[:, :].bitcast(BF16).rearrange("p (r j c) -> p r j c", r=2, j=4)
        csv = cs[:, :].bitcast(BF16).rearrange("p (r j c) -> p r j c", r=2, j=4)
        tabs = [self.scr() for _ in range(4)]
        q4 = lambda ap: ap.rearrange("p (q t) -> p q t", q=4)
        for c in range(4):
            b, bk = self.bank()
            self.proj(b[:, :], slu, c * 128, 128, 8, hf, [sluk, "hT"], [bk])
            self.cp("act", usb[:, 0:512], b[:, :], [bk], [usbk])
            self.cp("dve", ubv, usb[:, 0:512], [usbk], [ubfk])
            self.unbank((b, bk))
            for r in range(2):
                self.dma("sp", bsv[:, r], self.wb["bst"][l][r * 16 + 4 * c:r * 16 + 4 * c + 4].rearrange("a r c -> r a c"), [self.cur_wkey], [bsk])
                self.dma("sp", csv[:, r], self.wb["cst"][l][r * 16 + 4 * c:r * 16 + 4 * c + 4].rearrange("a r c -> r a c"), [self.cur_wkey], [csk])
            for jj in range(4):
                j = 4 * c + jj
                tb, tbk = tabs[jj]
                tbv = tb[:, 0:512].rearrange("p (i t) -> p i t", i=4)
                self.dma("sp", tbv, self.s5tab[j], ["s5tab"], [tbk])
                Fr = tbv[:, 0:1, :].broadcast_to([128, 4, 128]); Fi = tbv[:, 1:2, :].broadcast_to([128, 4, 128])
                bre, brek = self.bank(); bim, bimk = self.bank()
                self.mm(bre[:, :], bsv[:, 0, jj, :], ubv, True, True, [bsk, ubfk], [brek])
                self.mm(bim[:, :], bsv[:, 1, jj, :], ubv, True, True, [bsk, ubfk], [bimk])
                self.tt("dve", q4(zr[:, jj, :]), q4(bre[:, :]), Fr, ALU.mult, [brek, tbk], [zrk])
                self.tt("dve", q4(tmp[:, 0:512]), q4(bim[:, :]), Fi, ALU.mult, [bimk, tbk], [tmpk])
                self.tt("dve", zr[:, jj, :], zr[:, jj, :], tmp[:, 0:512], ALU.subtract, [zrk, tmpk], [zrk])
                self.tt("dve", q4(zi[:, jj, :]), q4(bre[:, :]), Fi, ALU.mult, [brek, tbk], [zik])
                self.tt("dve", q4(tmp[:, 0:512]), q4(bim[:, :]), Fr, ALU.mult, [bimk, tbk], [tmpk])
                self.tt("dve", zi[:, jj, :], zi[:, jj, :], tmp[:, 0:512], ALU.add, [zik, tmpk], [zik])
                self.unbank((bre, brek)); self.unbank((bim, bimk))
            for q in range(4):
                for jj in range(4):
                    j = 4 * c + jj
                    zk = "zc%d" % j
                    sl = slice(q * 128, (q + 1) * 128)
                    e = q * 128 + 127
                    self.scan(zor[:, jj, sl], self.onesf[:, 0:128], zr[:, jj, sl], self.zc[:, 0, j:j + 1], ["ones", zrk, zk], [zork + "#%d" % jj])
                    self.scan(zoi[:, jj, sl], self.onesf[:, 0:128], zi[:, jj, sl], self.zc[:, 1, j:j + 1], ["ones", zik, zk], [zoik + "#%d" % jj])
                    tcr = [zork + "#%d" % jj, zoik + "#%d" % jj, "el", tck + "#%d" % jj]
                    self.tt("dve", tc[:, 2 * jj:2 * jj + 1], zoi[:, jj, e:e + 1], self.el[:, 1, j:j + 1], ALU.mult, tcr, [tck + "#%d" % jj])
                    self.tt("dve", tc[:, 2 * jj + 1:2 * jj + 2], zor[:, jj, e:e + 1], self.el[:, 1, j:j + 1], ALU.mult, tcr, [tck + "#%d" % jj])
                    self.stt("dve", self.zc[:, 0, j:j + 1], zor[:, jj, e:e + 1], self.el[:, 0, j:j + 1], tc[:, 2 * jj:2 * jj + 1], ALU.mult, ALU.subtract, tcr, [zk])
                    self.stt("dve", self.zc[:, 1, j:j + 1], zoi[:, jj, e:e + 1], self.el[:, 0, j:j + 1], tc[:, 2 * jj + 1:2 * jj + 2], ALU.mult, ALU.add, tcr, [zk])
            yb, ybk = self.bank()
            for jj in range(4):
                tb, tbk = tabs[jj]
                tbv = tb[:, 0:512].rearrange("p (i t) -> p i t", i=4)
                Rr = tbv[:, 2:3, :].broadcast_to([128, 4, 128]); Ri = tbv[:, 3:4, :].broadcast_to([128, 4, 128])
                kr_ = [zork + "#%d" % jj, zoik + "#%d" % jj, tbk]
                p1, p1k = self.scr(); p2, p2k = self.scr()
                self.tt("dve", q4(p1[:, 0:512]), q4(zor[:, jj, :]), Rr, ALU.mult, kr_, [p1k])
                self.tt("dve", q4(p2[:, 0:512]), q4(zoi[:, jj, :]), Ri, ALU.mult, kr_, [p2k])
                self.tt("dve", srv[:, jj, :], p1[:, 0:512], p2[:, 0:512], ALU.subtract, [p1k, p2k], [srek])
                self.tt("dve", q4(p1[:, 0:512]), q4(zor[:, jj, :]), Ri, ALU.mult, kr_ + [p1k], [p1k])
                self.tt("dve", q4(p2[:, 0:512]), q4(zoi[:, jj, :]), Rr, ALU.mult, kr_ + [p2k], [p2k])
                self.stt("dve", nsv[:, jj, :], p1[:, 0:512], -1.0, p2[:, 0:512], ALU.mult, ALU.subtract, [p1k, p2k], [nsik])
                self.unscr((p1, p1k)); self.unscr((p2, p2k))
                self.mm(yb[:, :], csv[:, 0, jj, :], srv[:, jj, :], jj == 0, False, [csk, srek], [ybk])
                self.mm(yb[:, :], csv[:, 1, jj, :], nsv[:, jj, :], False, jj == 3, [csk, nsik], [ybk])
            self.stt("dve", yv[:, 0:512], usb[:, 0:512], self.pvc("s5_d", c), yb[:, :], ALU.mult, ALU.add, [usbk, "pv", ybk], [yvk])
            self.unbank((yb, ybk))
            self.gelu(yv[:, 0:512], yvk, ygv[:, c, :], ygk, tmp[:, 0:512], tmpk)
        wg, wgk = self.wload(self.wb["s5_w_glu"][l].rearrange("(k p) c -> p k c", p=128), [128, 4, 1024])
        for oc in range(4):
            za, zak = self.bank(); zb, zbk = self.bank()
            self.proj(za[:, :], wg, oc * 128, 128, 4, lambda kc: ygv[:, kc, :], [wgk, ygk], [zak])
            self.proj(zb[:, :], wg, 512 + oc * 128, 128, 4, lambda kc: ygv[:, kc, :], [wgk, ygk], [zbk])
            self.act(tmp[:, 0:512], zb[:, :], AF.Sigmoid, [zbk], [tmpk])
            self.tt("dve", self.ys[1][:, oc, :], za[:, :], tmp[:, 0:512], ALU.mult, [zak, tmpk], ["arB"])
            self.unbank((za, zak)); self.unbank((zb, zbk))
        for s_ in [(usb, usbk), (ubf, ubfk), (tmp, tmpk), (yv, yvk), (tc, tck), (bs, bsk), (cs, csk)] + tabs:
            self.unscr(s_)
        for s_ in [(yg, ygk), (sre, srek), (nsi, nsik)]:
            self.unscr(s_, 4)
        for z in Z:
            self.unscr(z, 8)

    def proj_shift(self, slab, slk, c0, m, mu_ap, ccol):
        hf = lambda kc: self.hT[:, kc, :]
        b, bk = self.bank()
        self.proj(b[0:m, :], slab, c0, m, 8, hf, [slk, "hT"], [bk])
        R, Rk = self.scr(4)
        self.cp("act", R[0:m, 1:513], b[0:m, :], [bk], [Rk])
        self.unbank((b, bk))
        ck = "carry%d" % ccol
        self.cp("dve", R[0:m, 0:1], self.carry[0:m, ccol:ccol + 1], [ck], [Rk])
        d, dk = self.scr()
        o, ok_ = self.scr()
        self.tt("dve", d[0:m, 0:512], R[0:m, 0:512], R[0:m, 1:513], ALU.subtract, [Rk], [dk])
        self.stt("dve", o[0:m, 0:512], d[0:m, 0:512], mu_ap, R[0:m, 1:513], ALU.mult, ALU.add, [dk, "pv", Rk], [ok_])
        self.cp("dve", self.carry[0:m, ccol:ccol + 1], R[0:m, 512:513], [Rk], [ck])
        self.unscr((R, Rk), 4); self.unscr((d, dk))
        return o, ok_

    def rwkv(self, l, ti):
        import os
        self.rstage = int(os.environ.get('RSTAGE', '99'))
        slA, slAk = self.win(3072)
        bfv = lambda t, n=256: t[:, 0:n].bitcast(BF16)
        wl, wlk = self.proj_shift(slA, slAk, 0, 64, self.pvc("mu_w", 0, 64), 12)
        twl, twlk = self.scr()
        self.act(bfv(twl)[0:64], wl[0:64, 0:512], AF.Tanh, [wlk], [twlk]); self.unscr((wl, wlk))
        al, alk = self.proj_shift(slA, slAk, 64, 64, self.pvc("mu_a", 0, 64), 13)
        alb, albk = self.scr()
        self.cp("dve", bfv(alb)[0:64], al[0:64, 0:512], [alk], [albk]); self.unscr((al, alk))
        gl, glk = self.proj_shift(slA, slAk, 128, 128, self.pvc("mu_g"), 14)
        sgl, sglk = self.scr()
        self.act(bfv(sgl), gl[:, 0:512], AF.Sigmoid, [glk], [sglk]); self.unscr((gl, glk))
        slr, slrk = self.win(1536); slk_, slkk = self.win(2048); slv, slvk = self.win(2560)
        m3 = lambda ap: ap.rearrange("p (q t) -> p q t", q=8)
        def prep(cc):
            rm, rmk = self.proj_shift(slr, slrk, cc * 128, 128, self.pvc("mu_rkv", cc), cc)
            km, kmk = self.proj_shift(slk_, slkk, cc * 128, 128, self.pvc("mu_rkv", 4 + cc), 4 + cc)
            vm, vmk = self.proj_shift(slv, slvk, cc * 128, 128, self.pvc("mu_rkv", 8 + cc), 8 + cc)
            ew, ewk = self.scr(); cs, csk = self.scr(); t1, t1k = self.scr()
            b, bk = self.bank()
            self.mm(b[:, :], self.w2_sb[:, cc * 128:(cc + 1) * 128], bfv(twl)[0:64], True, True, ["w2_sb", twlk], [bk])
            self.act(ew[:, 0:512], b[:, :], AF.Exp, [bk, "der"], [ewk], scale=-1.0, bias=self.der[:, 4 + cc:5 + cc])
            self.unbank((b, bk))
            self.act(ew[:, 0:512], ew[:, 0:512], AF.Ln, [ewk], [ewk], bias=1.0)
            self.ts("dve", ew[:, 0:512], ew[:, 0:512], -1.0, -0.5, ALU.mult, ALU.add, [ewk], [ewk])
            self.act(ew[:, 0:512], ew[:, 0:512], AF.Exp, [ewk], [ewk])
            self.scan(cs[:, 0:512], self.reset64[:, :], ew[:, 0:512], 0.0, ["reset64", ewk], [csk])
            E1, E1k = self.scr(); E2, E2k = self.scr(); E3, E3k = self.scr(); Ex, Exk = self.scr()
            self.act(E1[:, 0:512], cs[:, 0:512], AF.Exp, [csk], [E1k], scale=-1.0)
            self.act(E2[:, 0:512], cs[:, 0:512], AF.Exp, [csk], [E2k])
            self.tt("dve", t1[:, 0:512], cs[:, 0:512], ew[:, 0:512], ALU.subtract, [csk, ewk], [t1k])
            self.act(Ex[:, 0:512], t1[:, 0:512], AF.Exp, [t1k], [Exk], scale=-1.0)
            self.tt("dve", m3(t1[:, 0:512]), m3(cs[:, 0:512])[:, :, 63:64].broadcast_to([128, 8, 64]), m3(cs[:, 0:512]), ALU.subtract, [csk], [t1k])
            self.act(E3[:, 0:512], t1[:, 0:512], AF.Exp, [t1k], [E3k], scale=-1.0)
            ag, agk = self.scr()
            b, bk = self.bank()
            self.mm(b[:, :], self.a2_sb[:, cc * 128:(cc + 1) * 128], bfv(alb)[0:64], True, True, ["a2_sb", albk], [bk])
            self.act(ag[:, 0:512], b[:, :], AF.Sigmoid, [bk, "pv"], [agk], bias=self.pvc("a0", cc))
            self.unbank((b, bk))
            kk, kkk = self.scr()
            self.ts("dve", kk[:, 0:512], km[:, 0:512], self.pvc("k_k", cc), None, ALU.mult, None, [kmk, "pv"], [kkk])
            self.tt("dve", t1[:, 0:512], kk[:, 0:512], kk[:, 0:512], ALU.mult, [kkk], [t1k])
            b, bk = self.bank()
            self.mm(b[:, :], self.blk64[:, :], t1[:, 0:512], True, True, ["blk64", t1k], [bk])
            self.act(t1[:, 0:512], b[:, :], AF.Sqrt, [bk], [t1k], bias=1e-12)
            self.unbank((b, bk))
            self.recip(t1[:, 0:512], t1[:, 0:512], [t1k], [t1k])
            self.tt("dve", kk[:, 0:512], kk[:, 0:512], t1[:, 0:512], ALU.mult, [kkk, t1k], [kkk])
            self.ts("dve", t1[:, 0:512], ag[:, 0:512], self.pvc("k_a", cc), self.der[:, 8 + cc:9 + cc], ALU.mult, ALU.add, [agk, "pv", "der"], [t1k])
            self.tt("dve", km[:, 0:512], km[:, 0:512], t1[:, 0:512], ALU.mult, [kmk, t1k], [kmk])
            self.tt("dve", ag[:, 0:512], ag[:, 0:512], kk[:, 0:512], ALU.mult, [agk, kkk], [agk])
            AR, ARk = self.scr(); BT, BTk = self.scr(); KT, KTk = self.scr(); BH, BHk = self.scr(); KH, KHk = self.scr(); RK, RKk = self.scr()
            ARv = AR[:, :].bitcast(BF16).rearrange("p (q a t) -> p q a t", q=8, a=2)
            self.stt("dve", ARv[:, :, 0, :], m3(kk[:, 0:512]), -1.0, m3(Ex[:, 0:512]), ALU.mult, ALU.mult, [kkk, Exk], [ARk])
            self.tt("dve", ARv[:, :, 1, :], m3(rm[:, 0:512]), m3(E1[:, 0:512]), ALU.mult, [rmk, E1k], [ARk])
            self.cp("dve", self.pc_t[:, cc, :], m3(E1[:, 0:512])[:, :, 63], [E1k], ["pc%d" % cc])
            self.tt("dve", bfv(BT), ag[:, 0:512], E2[:, 0:512], ALU.mult, [agk, E2k], [BTk])
            self.tt("dve", bfv(KT), km[:, 0:512], E2[:, 0:512], ALU.mult, [kmk, E2k], [KTk])
            self.tt("dve", bfv(BH), ag[:, 0:512], E3[:, 0:512], ALU.mult, [agk, E3k], [BHk])
            self.tt("dve", bfv(KH), km[:, 0:512], E3[:, 0:512], ALU.mult, [kmk, E3k], [KHk])
            self.stt("dve", bfv(RK), rm[:, 0:512], self.pvc("r_k", cc), km[:, 0:512], ALU.mult, ALU.mult, [rmk, "pv", kmk], [RKk])
            for s_ in [(rm, rmk), (km, kmk), (ew, ewk), (cs, csk), (t1, t1k), (E1, E1k), (E2, E2k), (E3, E3k), (Ex, Exk), (ag, agk), (kk, kkk)]:
                self.unscr(s_)
            BHt, BHtk = self.scr(); KHt, KHtk = self.scr(); Vt, Vtk = self.scr(4); Vtb, Vtbk = self.scr()
            tm = lambda t: t[0:64, :].bitcast(BF16).rearrange("p (q c) -> p q c", q=8)
            for (src, srck, dst, dstk) in [(BH, BHk, BHt, BHtk), (KH, KHk, KHt, KHtk)]:
                b, bk = self.bank()
                bb = b[:, :].bitcast(BF16)
                for q in range(8):
                    self.trp(bb[0:64, q * 128:(q + 1) * 128], bfv(src)[:, q * 64:(q + 1) * 64], self.identb[:], [srck, "identb"], [bk])
                self.cp("act", tm(dst), bb[0:64, 0:1024].rearrange("p (q c) -> p q c", q=8), [bk], [dstk])
                self.unbank((b, bk))
            Vtv = Vt[0:64, :].rearrange("p (q c) -> p q c", q=8)
            for hf_ in range(2):
                b, bk = self.bank()
                for q4_ in range(4):
                    q = hf_ * 4 + q4_
                    self.trp(b[0:64, q4_ * 128:(q4_ + 1) * 128], vm[:, q * 64:(q + 1) * 64], self.identf[:], [vmk, "ident"], [bk])
                self.cp("act", Vtv[:, hf_ * 4:hf_ * 4 + 4, :], b[0:64, :].rearrange("p (q c) -> p q c", q=4), [bk], [Vtk])
                self.unbank((b, bk))
            self.cp("dve", tm(Vtb), Vtv, [Vtk], [Vtbk])
            self.unscr((vm, vmk))
            mk = lambda i: self.rmask[:, i:i + 1, :].broadcast_to([64, 8, 64])
            NP = [self.scr() for _ in range(2)]; XP = [self.scr() for _ in range(2)]; PP = [self.scr() for _ in range(2)]; QQ = [self.scr() for _ in range(2)]
            ARB, ARBk = self.scr(); AAK, AAKk = self.scr(); ARK, ARKk = self.scr()
            mt = lambda t: t[0:64, :].bitcast(BF16).rearrange("p (m t) -> p m t", m=16)
            for hf_ in range(2):
                BN = [self.bank(), self.bank()]; BK = [self.bank(), self.bank()]; BX = [self.bank(), self.bank()]
                for q4_ in range(4):
                    q = hf_ * 4 + q4_
                    for e in range(2):
                        rows = slice(e * 64, (e + 1) * 64)
                        arhs = ARv[rows, q, :, :]
                        co = q4_ * 128
                        self.mm(BN[e][0][0:64, co:co + 128], bfv(BT)[rows, q * 64:(q + 1) * 64], arhs, True, True, [BTk, ARk], [BN[e][1]])
                        self.mm(BK[e][0][0:64, co:co + 128], bfv(KT)[rows, q * 64:(q + 1) * 64], arhs, True, True, [KTk, ARk], [BK[e][1]])
                        self.mm(BX[e][0][0:64, q4_ * 64:(q4_ + 1) * 64], ARv[rows, q, 0, :], bfv(BT)[rows, q * 64:(q + 1) * 64], True, True, [ARk, BTk], [BX[e][1]])
                v4 = lambda bnk: bnk[0:64, :].rearrange("p (m a t) -> p m a t", m=4, a=2)
                mk4 = lambda i: self.rmask[:, i:i + 1, :].broadcast_to([64, 4, 64])
                for e in range(2):
                    ms = slice(hf_ * 8 + e, hf_ * 8 + 8, 2)
                    self.tt("dve", mt(NP[0][0])[:, ms, :], v4(BN[e][0])[:, :, 0, :], mk4(0), ALU.mult, [BN[e][1], "rmask"], [NP[0][1]])
                    self.tt("dve", mt(ARB)[:, ms, :], v4(BN[e][0])[:, :, 1, :], mk4(1), ALU.mult, [BN[e][1], "rmask"], [ARBk])
                    self.tt("dve", mt(AAK)[:, ms, :], v4(BK[e][0])[:, :, 0, :], mk4(0), ALU.mult, [BK[e][1], "rmask"], [AAKk])
                    self.tt("dve", mt(ARK)[:, ms, :], v4(BK[e][0])[:, :, 1, :], mk4(1), ALU.mult, [BK[e][1], "rmask"], [ARKk])
                    self.tt("dve", mt(XP[0][0])[:, ms, :], BX[e][0][0:64, 0:256].rearrange("p (m t) -> p m t", m=4), mk4(2), ALU.mult, [BX[e][1], "rmask"], [XP[0][1]])
                for bb_ in BN + BK + BX:
                    self.unbank(bb_)
            idb = self.identb[0:64, 0:64].unsqueeze(1).broadcast_to([64, 16, 64])
            self.tt("dve", mt(PP[0][0]), mt(NP[0][0]), idb, ALU.add, [NP[0][1], "identb"], [PP[0][1]])
            self.tt("dve", mt(QQ[0][0]), mt(XP[0][0]), idb, ALU.add, [XP[0][1], "identb"], [QQ[0][1]])
            for k in range(5):
                c_, n_ = k % 2, (k + 1) % 2
                for hf_ in range(2):
                    hs = slice(hf_ * 8, hf_ * 8 + 8)
                    bn_, bnk_ = self.bank(); bx_, bxk_ = self.bank()
                    for m8 in range(8):
                        m = hf_ * 8 + m8
                        self.mm(bn_[0:64, m8 * 64:(m8 + 1) * 64], mt(XP[c_][0])[:, m, :], mt(NP[c_][0])[:, m, :], True, True, [XP[c_][1], NP[c_][1]], [bnk_])
                        if k < 4:
                            self.mm(bx_[0:64, m8 * 64:(m8 + 1) * 64], mt(NP[c_][0])[:, m, :], mt(XP[c_][0])[:, m, :], True, True, [XP[c_][1], NP[c_][1]], [bxk_])
                    self.cp("act", mt(NP[n_][0])[:, hs, :], bn_[0:64, :].rearrange("p (m t) -> p m t", m=8), [bnk_], [NP[n_][1] + "#%d" % hf_])
                    if k < 4:
                        self.cp("act", mt(XP[n_][0])[:, hs, :], bx_[0:64, :].rearrange("p (m t) -> p m t", m=8), [bxk_], [XP[n_][1] + "#%d" % hf_])
                    self.unbank((bn_, bnk_)); self.unbank((bx_, bxk_))
                for hf_ in range(2):
                    hs = slice(hf_ * 8, hf_ * 8 + 8)
                    bp_, bpk_ = self.bank(); bq_, bqk_ = self.bank()
                    for m8 in range(8):
                        m = hf_ * 8 + m8
                        self.mm(bp_[0:64, m8 * 64:(m8 + 1) * 64], mt(QQ[c_][0])[:, m, :], mt(NP[n_][0])[:, m, :], True, True, [QQ[c_][1], NP[n_][1] + "#%d" % hf_], [bpk_])
                        if k < 4:
                            self.mm(bq_[0:64, m8 * 64:(m8 + 1) * 64], mt(PP[c_][0])[:, m, :], mt(XP[n_][0])[:, m, :], True, True, [PP[c_][1], XP[n_][1] + "#%d" % hf_], [bqk_])
                    self.tt("dve", mt(PP[n_][0])[:, hs, :], bp_[0:64, :].rearrange("p (m t) -> p m t", m=8), mt(PP[c_][0])[:, hs, :], ALU.add, [bpk_, PP[c_][1]], [PP[n_][1] + "#%d" % hf_])
                    if k < 4:
                        self.tt("dve", mt(QQ[n_][0])[:, hs, :], bq_[0:64, :].rearrange("p (m t) -> p m t", m=8), mt(QQ[c_][0])[:, hs, :], ALU.add, [bqk_, QQ[c_][1]], [QQ[n_][1] + "#%d" % hf_])
                    self.unbank((bp_, bpk_)); self.unbank((bq_, bqk_))
            PF, PFk = PP[1]
            for s_ in [(BT, BTk), (KT, KTk), (BH, BHk), (KH, KHk), PP[0]] + NP + XP + QQ:
                self.unscr(s_)
            ytm, ytmk = self.scr(4)
            ytv = ytm[0:64, :].rearrange("p (q c) -> p q c", q=8)
            U = [self.scr(), self.scr()]
            ub = lambda i: U[i][0][0:64, 0:64].bitcast(BF16)
            s0k = "s0_%d" % cc
            return locals()

        def step(L, q):
            cc, ARv, ARk, AAK, AAKk, ARB, ARBk, ARK, ARKk, Vtb, Vtbk, Vt, Vtk, PF, PFk, BHt, BHtk, KHt, KHtk, U, ub, ytm, ytmk, ytv, s0k, RK, RKk, AR, mt, tm = (L[k_] for k_ in ['cc', 'ARv', 'ARk', 'AAK', 'AAKk', 'ARB', 'ARBk', 'ARK', 'ARKk', 'Vtb', 'Vtbk', 'Vt', 'Vtk', 'PF', 'PFk', 'BHt', 'BHtk', 'KHt', 'KHtk', 'U', 'ub', 'ytm', 'ytmk', 'ytv', 's0k', 'RK', 'RKk', 'AR', 'mt', 'tm'])
            b, bk = self.bank()
            self.mm(b[0:64, 0:128], ARv[:, q, 0, :], self.s0bd[:, cc, :], True, False, [ARk, s0k], [bk])
            for e in range(2):
                self.mm(b[0:64, e * 64:(e + 1) * 64], mt(AAK)[:, 2 * q + e, :], tm(Vtb)[:, q, e * 64:(e + 1) * 64], False, e == 1, [AAKk, Vtbk], [bk])
            self.cp("act", ub(0), b[0:64, 0:128], [bk], [U[0][1]])
            self.unbank((b, bk))
            b, bk = self.bank()
            for e in range(2):
                self.mm(b[0:64, e * 64:(e + 1) * 64], mt(PF)[:, 2 * q + e, :], ub(0)[:, e * 64:(e + 1) * 64], True, True, [PFk, U[0][1]], [bk])
            self.cp("act", ub(1), b[0:64, 0:128], [bk], [U[1][1]])
            self.unbank((b, bk))
            cur = 1
            uf, ufk = ub(cur), U[cur][1]
            b, bk = self.bank()
            self.mm(b[0:64, 0:128], ARv[:, q, 1, :], self.s0bd[:, cc, :], True, False, [ARk, s0k], [bk])
            for e in range(2):
                self.mm(b[0:64, e * 64:(e + 1) * 64], mt(ARB)[:, 2 * q + e, :], uf[:, e * 64:(e + 1) * 64], False, False, [ARBk, ufk], [bk])
                self.mm(b[0:64, e * 64:(e + 1) * 64], mt(ARK)[:, 2 * q + e, :], tm(Vtb)[:, q, e * 64:(e + 1) * 64], False, e == 1, [ARKk, Vtbk], [bk])
            self.cp("act", ytv[:, q, :], b[0:64, 0:128], [bk], [ytmk])
            self.unbank((b, bk))
            b, bk = self.bank()
            self.mm(b[:, 0:128], tm(BHt)[:, q, :], uf, True, False, [BHtk, ufk], [bk])
            self.mm(b[:, 0:128], tm(KHt)[:, q, :], tm(Vtb)[:, q, :], False, True, [KHtk, Vtbk], [bk])
            for e in range(2):
                rows = slice(e * 64, (e + 1) * 64)
                self.stt("dve", self.s0f[rows, cc, :], self.s0f[rows, cc, :], self.pc_t[rows, cc, q:q + 1], b[rows, e * 64:(e + 1) * 64], ALU.mult, ALU.add, [s0k + "f", "pc%d" % cc, bk], [s0k + "f"])
                self.cp("dve", self.s0bd[rows, cc, e * 64:(e + 1) * 64], self.s0f[rows, cc, :], [s0k + "f"], [s0k])
            self.unbank((b, bk))

        def post(L):
            cc, ARv, ARk, AAK, AAKk, ARB, ARBk, ARK, ARKk, Vtb, Vtbk, Vt, Vtk, PF, PFk, BHt, BHtk, KHt, KHtk, U, ub, ytm, ytmk, ytv, s0k, RK, RKk, AR, mt, tm = (L[k_] for k_ in ['cc', 'ARv', 'ARk', 'AAK', 'AAKk', 'ARB', 'ARBk', 'ARK', 'ARKk', 'Vtb', 'Vtbk', 'Vt', 'Vtk', 'PF', 'PFk', 'BHt', 'BHtk', 'KHt', 'KHtk', 'U', 'ub', 'ytm', 'ytmk', 'ytv', 's0k', 'RK', 'RKk', 'AR', 'mt', 'tm'])
            g16 = lambda ap: ap.rearrange("p (g v) -> p g v", g=16)
            yv = g16(ytm[0:64, :])
            st_, stk = self.scr(); sq, sqk = self.scr(4)
            self.P.op("dve", lambda e, o=st_[0:64, 0:16], i=yv: e.tensor_reduce(out=o, in_=i, axis=AX.X, op=ALU.add), [ytmk], [stk])
            self.tt("dve", sq[0:64, :], ytm[0:64, :], ytm[0:64, :], ALU.mult, [ytmk], [sqk])
            self.P.op("dve", lambda e, o=st_[0:64, 16:32], i=g16(sq[0:64, :]): e.tensor_reduce(out=o, in_=i, axis=AX.X, op=ALU.add), [sqk], [stk])
            self.ts("dve", st_[0:64, 0:32], st_[0:64, 0:32], 1.0 / 64, None, ALU.mult, None, [stk], [stk])
            self.tt("dve", st_[0:64, 32:48], st_[0:64, 0:16], st_[0:64, 0:16], ALU.mult, [stk], [stk])
            self.tt("dve", st_[0:64, 16:32], st_[0:64, 16:32], st_[0:64, 32:48], ALU.subtract, [stk], [stk])
            self.act(st_[0:64, 16:32], st_[0:64, 16:32], AF.Sqrt, [stk], [stk], bias=64e-5)
            self.recip(st_[0:64, 16:32], st_[0:64, 16:32], [stk], [stk])
            bc = lambda ap: ap.unsqueeze(2).broadcast_to([64, 16, 64])
            self.tt("dve", yv, yv, bc(st_[0:64, 0:16]), ALU.subtract, [ytmk, stk], [ytmk])
            self.tt("dve", yv, yv, bc(st_[0:64, 16:32]), ALU.mult, [ytmk, stk], [ytmk])
            lg = self.lnx[:, cc * 128:(cc + 1) * 128].unsqueeze(1).broadcast_to([64, 8, 128])
            lb = self.lnx[:, 512 + cc * 128:512 + (cc + 1) * 128].unsqueeze(1).broadcast_to([64, 8, 128])
            self.tt("dve", ytv, ytv, lg, ALU.mult, [ytmk, "lnx"], [ytmk])
            self.tt("dve", ytv, ytv, lb, ALU.add, [ytmk, "lnx"], [ytmk])
            b, bk = self.bank()
            for q in range(8):
                self.mm(b[0:64, q * 2:q * 2 + 2], bfv(RK)[:, q * 64:(q + 1) * 64], self.headselb[:, :], True, True, [RKk, "headselb"], [bk])
            self.cp("act", st_[0:64, 0:16], b[0:64, 0:16], [bk], [stk])
            self.unbank((b, bk))
            self.tt("dve", g16(sq[0:64, :]), g16(Vt[0:64, :]), bc(st_[0:64, 0:16]), ALU.mult, [Vtk, stk], [sqk])
            self.tt("dve", ytm[0:64, :], ytm[0:64, :], sq[0:64, :], ALU.add, [ytmk, sqk], [ytmk])
            gg, ggk = self.scr()
            bgt, bgtk = self.bank()
            self.mm(bgt[:, :], self.g2_sb[:, cc * 128:(cc + 1) * 128], bfv(sgl), True, True, ["g2_sb", sglk], [bgtk])
            self.cp("act", gg[:, 0:512], bgt[:, :], [bgtk], [ggk])
            self.unbank((bgt, bgtk))
            yb_, ybk_ = self.scr()
            self.cp("dve", tm(yb_), ytv, [ytmk], [ybk_])
            b, bk = self.bank()
            bb = b[:, :].bitcast(BF16)
            for q in range(8):
                self.trp(bb[:, q * 64:(q + 1) * 64], tm(yb_)[:, q, :], self.identb[0:64, 0:64], [ybk_, "identb"], [bk])
            self.tt("dve", self.ys[2][:, cc, :], bb[:, 0:512], gg[:, 0:512], ALU.mult, [bk, ggk], ["arB"])
            self.unbank((b, bk))
            for s_ in [(gg, ggk), (L["AR"], ARk), (RK, RKk), (BHt, BHtk), (KHt, KHtk), (Vtb, Vtbk), (ARB, ARBk), (AAK, AAKk), (ARK, ARKk),
                       (st_, stk), (yb_, ybk_), (PF, PFk)] + U:
                self.unscr(s_)
            for s_ in [(Vt, Vtk), (ytm, ytmk), (sq, sqk)]:
                self.unscr(s_, 4)

        for pr_ in range(2):
            La = prep(2 * pr_); Lb = prep(2 * pr_ + 1)
            for q in range(8):
                step(La, q); step(Lb, q)
            post(La); post(Lb)
        for s_ in [(twl, twlk), (alb, albk), (sgl, sglk)]:
            self.unscr(s_)

    def rstd_bcast(self, dst, dstk, srcs, nfeat, nparts, ones_lhsT):
        sq, sqk = self.scr()
        b, bk = self.bank()
        for i, (ap, k) in enumerate(srcs):
            self.act(sq[0:ap.shape[0], 0:512], ap, AF.Square, [k], [sqk])
            self.mm(b[0:nparts, :], ones_lhsT(ap.shape[0]), sq[0:ap.shape[0], 0:512], i == 0, i == len(srcs) - 1, [sqk, "ones"], [bk])
        self.act(dst, b[0:nparts, :], AF.Sqrt, [bk], [dstk], scale=1.0 / nfeat, bias=EPS)
        self.recip(dst, dst, [dstk], [dstk])
        self.unbank((b, bk))
        self.unscr((sq, sqk))

    def qk_stages(self, pre, src, srck, gname, out, outk, post):
        st = {}
        S = list(pre)

        def s1():
            st["sq"] = self.scr(); st["rs"] = self.scr()
            self.act(st["sq"][0][0:96, 0:512], src, AF.Square, [srck], [st["sq"][1]])

        def s2():
            st["b"] = self.bank()
            self.mm(st["b"][0][0:96, :], self.onesf[0:96, 0:96], st["sq"][0][0:96, 0:512], True, True, [st["sq"][1], "ones"], [st["b"][1]])

        def s3():
            self.act(st["rs"][0][0:96, 0:512], st["b"][0][0:96, :], AF.Sqrt, [st["b"][1]], [st["rs"][1]], scale=1.0 / 96, bias=EPS)
            self.unbank(st["b"]); self.unscr(st["sq"])

        def s4():
            self.recip(st["rs"][0][0:96, 0:512], st["rs"][0][0:96, 0:512], [st["rs"][1]], [st["rs"][1]])

        def s5():
            self.stt("dve", src, src, self.pvc(gname, 0, 96), st["rs"][0][0:96, 0:512], ALU.mult, ALU.mult, [srck, "pv", st["rs"][1]], [srck])

        def s6():
            st["b"] = self.bank()
            self.mm(st["b"][0][0:96, :], self.prot[:, :], src, True, True, ["prot", srck], [st["b"][1]])

        def s7():
            self.tt("dve", st["rs"][0][0:96, 0:512], st["b"][0][0:96, :], self.rsin[0:96, 0:512], ALU.mult, [st["b"][1], self.rsink], [st["rs"][1]])
            self.unbank(st["b"])

        def s8():
            self.tt("dve", src, src, self.rcos[0:96, 0:512], ALU.mult, [srck, self.rcosk], [srck])

        def s9():
            self.tt("dve", out, src, st["rs"][0][0:96, 0:512], ALU.add, [srck, st["rs"][1]], [outk])
            self.unscr(st["rs"])
        return S + [s1, s2, s3, s4, s5, s6, s7, s8, s9] + list(post)

    @staticmethod
    def interleave(chains):
        n = max(len(c) for c in chains)
        for i in range(n):
            for c in chains:
                if i < len(c):
                    c[i]()

    def mla(self, l, ti):
        t0 = ti * TT
        hf = lambda kc: self.hT[:, kc, :]
        slA, slAk = self.win(3072)
        slB, slBk = self.win(3584, 160)
        (self.rcos, self.rcosk), (self.rsin, self.rsink) = self.scr(), self.scr()
        self.dma("sp", self.rcos[0:96, 0:512], self.cd["ropec"][:, t0:t0 + 512], [], [self.rcosk])
        self.dma("sp", self.rsin[0:96, 0:512], self.cd["ropes"][:, t0:t0 + 512], [], [self.rsink])
        cq = [self.scr() for _ in range(2)]
        for c in range(2):
            b, bk = self.bank()
            self.proj(b[:, :], slA, 256 + c * 128, 128, 8, hf, [slAk, "hT"], [bk])
            self.cp("act", cq[c][0][:, 0:512], b[:, :], [bk], [cq[c][1]])
            self.unbank((b, bk))
        rs, rsk = self.scr()
        self.rstd_bcast(rs[:, 0:512], rsk, [(cq[0][0][:, 0:512], cq[0][1]), (cq[1][0][:, 0:512], cq[1][1])], 256, 128, lambda n: self.onesf[:, :])
        cqn, cqnk = self.scr()
        cqnv = cqn[:, 0:512].bitcast(BF16).rearrange("p (c t) -> p c t", c=2)
        for c in range(2):
            self.stt("dve", cqnv[:, c, :], cq[c][0][:, 0:512], self.pvc("q_norm", c), rs[:, 0:512], ALU.mult, ALU.mult, [cq[c][1], "pv", rsk], [cqnk])
        ckv, ckvk = cq[0]
        b, bk = self.bank()
        self.proj(b[:, :], slB, 0, 128, 8, hf, [slBk, "hT"], [bk])
        self.cp("act", ckv[:, 0:512], b[:, :], [bk], [ckvk])
        self.unbank((b, bk))
        self.rstd_bcast(rs[:, 0:512], rsk, [(ckv[:, 0:512], ckvk)], 128, 128, lambda n: self.onesf[:, :])
        ckvn, ckvnk = self.scr()
        ckvnv = ckvn[:, 0:256].bitcast(BF16)
        self.stt("dve", ckvnv, ckv[:, 0:512], self.pvc("kv_norm"), rs[:, 0:512], ALU.mult, ALU.mult, [ckvk, "pv", rsk], [ckvnk])
        kr, krk = cq[1]
        b, bk = self.bank()
        self.proj(b[0:32, :], slB, 128, 32, 8, hf, [slBk, "hT"], [bk])
        self.cp("act", kr[64:96, 0:512], b[0:32, :], [bk], [krk])
        self.unbank((b, bk))
        self.unscr((rs, rsk))
        vt, vtk = self.scr(8)
        vtv = vt[:, 0:1040].bitcast(BF16).rearrange("p (s h e) -> p s h e", s=4, h=8)
        self.memset("pool", vtv[:, :, :, 64:65], 1.0, [vtk])
        wv = self.wukv_sb[:, :].rearrange("p (h e) -> p h e", h=8)[:, :, 64:128]
        for s in range(4):
            b, bk = self.bank()
            self.mm(b[:, :].rearrange("p (h e) -> p h e", h=8), ckvnv[:, s * 128:(s + 1) * 128], wv, True, True, [ckvnk, "wukv_sb"], [bk])
            self.cp("act" if s % 2 else "dve", vtv[:, s, :, 0:64], b[:, :].rearrange("p (h e) -> p h e", h=8), [bk], [vtk])
            self.unbank((b, bk))
        for h in range(8):
            self.dma("pool", self.vc[h, :, 4 * ti:4 * ti + 4, :], vtv[:, :, h, :], [vtk], ["vc"])
        self.unscr((vt, vtk), 8)
        def kchain(h, kt, ktk, kb_, kbk):
            kbv = kb_[0:96, 0:256].bitcast(BF16)
            st = {}

            def p1():
                st["b"] = self.bank()
                self.mm(st["b"][0][0:64, :], self.wukv_sb[:, h * 128:h * 128 + 64], ckvnv, True, True, ["wukv_sb", ckvnk], [st["b"][1]])

            def p2():
                self.cp("act", kt[0:64, 0:512], st["b"][0][0:64, :], [st["b"][1]], [ktk])
                self.unbank(st["b"])

            def p3():
                self.cp("pool", kt[64:96, 0:512], kr[64:96, 0:512], [krk], [ktk])

            def post():
                self.dma("pool", self.kc[h, :, t0:t0 + 512], kbv, [kbk], ["kc"])
            return self.qk_stages([p1, p2, p3], kt[0:96, 0:512], ktk, "qkn_k", kbv, kbk, [post])
        KB4 = [(self.scr(), self.scr()) for _ in range(4)]
        for hg in range(2):
            self.interleave([kchain(hg * 4 + i, KB4[i][0][0], KB4[i][0][1], KB4[i][1][0], KB4[i][1][1]) for i in range(4)])
        for (a_, b_) in KB4:
            self.unscr(a_); self.unscr(b_)
        nkt = 4 * (ti + 1)
        QB = [self.scr(), self.scr()]
        QF = [self.scr(), self.scr()]
        qbv_ = lambda i: QB[i][0][0:96, 0:256].bitcast(BF16)
        KBUF = [self.scr(8), self.scr(8)]
        VBUF = [self.scr(8), self.scr(8)]
        pts = [self.scr() for _ in range(3)]
        rl, rlk = self.scr()

        def qchain(h):
            i = h % 2
            qf, qfk = QF[i]
            st = {}

            def p1():
                st["b"] = self.bank()
                for c in range(2):
                    self.mm(st["b"][0][0:96, :], self.wuq_sb[:, c, h * 96:(h + 1) * 96], cqnv[:, c, :], c == 0, c == 1, ["wuq_sb", cqnk], [st["b"][1]])

            def p2():
                self.cp("act", qf[0:96, 0:512], st["b"][0][0:96, :], [st["b"][1]], [qfk])
                self.unbank(st["b"])

            def p3():
                kbv2 = KBUF[i][0][0:96, 0:2048].bitcast(BF16)
                vbv = VBUF[i][0][:, 0:1040].bitcast(BF16).rearrange("p (k e) -> p k e", k=32)
                self.dma("sp", kbv2[:, 0:nkt * 128], self.kc[h, :, 0:nkt * 128], ["kc"], [KBUF[i][1]])
                self.dma("sp", vbv[:, 0:nkt, :], self.vc[h, :, 0:nkt, :], ["vc"], [VBUF[i][1]])
            return self.qk_stages([p1, p2, p3], qf[0:96, 0:512], qfk, "qkn_q", qbv_(i), QB[i][1], [])
        for f in qchain(0):
            f()
        for h in range(8):
            i = h % 2
            qbv, qbk = qbv_(i), QB[i][1]
            kbv2 = KBUF[i][0][0:96, 0:2048].bitcast(BF16)
            kbufk = KBUF[i][1]
            vbv = VBUF[i][0][:, 0:1040].bitcast(BF16).rearrange("p (k e) -> p k e", k=32)
            vbufk = VBUF[i][1]
            nxt = qchain(h + 1) if h < 7 else []
            per = -(-len(nxt) // nkt) if nxt else 0
            ob, obk = self.bank()
            for k in range(nkt):
                sb_, sbk = self.bank()
                self.mm(sb_[:, :], kbv2[:, k * 128:(k + 1) * 128], qbv, True, True, [kbufk, qbk], [sbk])
                pt, ptk = pts[k % 3]
                ptv = pt[:, 0:256].bitcast(BF16)
                self.act(ptv, sb_[:, :], AF.Exp, [sbk], [ptk], scale=96.0 ** -0.5)
                self.unbank((sb_, sbk))
                if k >= 4 * ti:
                    self.tt("dve", ptv, ptv, self.amask[:, k - 4 * ti, :], ALU.mult, [ptk, "amask"], [ptk])
                self.mm(ob[0:65, :], vbv[:, k, :], ptv, k == 0, k == nkt - 1, [vbufk, ptk], [obk])
                for f in nxt[k * per:(k + 1) * per]:
                    f()
            self.recip(rl[64:65, 0:512], ob[64:65, :], [obk], [rlk])
            bc, bck = self.bank()
            self.mm(bc[0:64, :], self.onesf[64:65, 0:64], rl[64:65, 0:512], True, True, ["ones", rlk], [bck])
            self.cp("act", rl[0:64, 0:512], bc[0:64, :], [bck], [rlk])
            self.unbank((bc, bck))
            self.tt("dve", self.ys[3][:, h, :], ob[0:64, :], rl[0:64, 0:512], ALU.mult, [obk, rlk], ["arB"])
            self.unbank((ob, obk))
        for s_ in QB + QF:
            self.unscr(s_)
        for s_ in KBUF + VBUF:
            self.unscr(s_, 8)
        for s_ in pts + [(rl, rlk), (cqn, cqnk), (ckvn, ckvnk), cq[0], cq[1], (self.rcos, self.rcosk), (self.rsin, self.rsink)]:
            self.unscr(s_)


def _prep_inputs(inputs):
    pvec, fvec, lrug, bst, cst = host_layout(inputs)
    shared = {"pvec": pvec, "fvec": fvec,
              "lrug": lrug.reshape(DEPTH, -1, 1024), "bst": bst.reshape(DEPTH, -1, 1024), "cst": cst.reshape(DEPTH, -1, 1024)}
    for n in BIGW:
        shared[n] = np.ascontiguousarray(np.asarray(inputs[n], np.float32)).reshape(DEPTH, -1, 1024)
    for n, v in host_consts().items():
        shared["c_" + n] = v
    return shared


def run(inputs, L_RUN=DEPTH, T_RUN=T_FULL, n_cores=8, branches=(0, 1, 2, 3), dbg=False, dbg_tile=0, trace=False):
    inputs = {k: np.asarray(v) for k, v in inputs.items()}
    shared = _prep_inputs(inputs)
    nc = bass.Bass("TRN2", target_bir_lowering=False)
    kb = KB(nc, L_RUN, T_RUN, dbg=dbg)
    kb.dbg_tile = dbg_tile
    kb.build(branches=branches)
    in_maps = []
    for b in range(n_cores):
        m = dict(shared)
        m["x"] = np.ascontiguousarray(inputs["x"][b, :T_RUN].astype(np.float32))
        in_maps.append(m)
    res = run_bass_kernel_spmd(nc, in_maps, core_ids=list(range(n_cores)), trace=trace)
    return res


DEFAULT_BRANCHES = (0, 1, 2, 3)


def kernel(**inputs):
    res = run(inputs, branches=DEFAULT_BRANCHES)
    return np.stack([r["y"] for r in res.results], axis=0).astype(np.float32)
```
